# Optimizing a Trainium2 kernel written in Bass

```python
import math
import jax, jax.numpy as jnp
from jax import lax
import numpy as np

D_MODEL = 1024
BATCH = 2
SEQ = 16384
DEPTH = 1
DEC_BATCH = 16
DEC_SEQ = 64
PAST_LEN = 4096

CHUNK = 64
D_A = D_MODEL
D_B = D_MODEL
D_MIX = D_A + D_B
CONV_A_W = 3
CONV_B_W = 4
SSM_HEADDIM = 64
SSM_HEADS = D_B // SSM_HEADDIM
SSM_GROUPS = 4
D_STATE = 128
D_XBC = D_B + 2 * SSM_GROUPS * D_STATE
SPLITS = [D_A, D_A, D_A, D_A, D_B, D_XBC, SSM_HEADS]
D_IN_PROJ = sum(SPLITS)
EPS = 1e-5

kernel_name = "hybrid_shortconv_ssd_stream_step"


def rmsnorm(x, w):
    xf = x.astype(jnp.float32)
    y = xf * lax.rsqrt(jnp.mean(xf * xf, axis=-1, keepdims=True) + EPS)
    return (y * w.astype(jnp.float32)).astype(x.dtype)


def gated_rmsnorm(y, z, w):
    return rmsnorm(y * jax.nn.silu(z), w)


def causal_dwconv(u, buf, w):
    K = w.shape[0]
    L = u.shape[1]
    up = jnp.concatenate([buf.astype(u.dtype), u], axis=1)
    y = up[:, 0:L] * w[0]
    for k in range(1, K):
        y = y + up[:, k:k + L] * w[k]
    return y, up[:, L:]


def ssd_chunked(x, dt, a, bmat, cmat, s0):
    bsz, L, H, P = x.shape
    G, N = bmat.shape[2], bmat.shape[3]
    Hg = H // G
    pad = (-L) % CHUNK
    f32 = jnp.float32
    x, dt, bmat, cmat = (t.astype(f32) for t in (x, dt, bmat, cmat))
    if pad:
        pw = lambda t: jnp.pad(t, [(0, 0), (0, pad)] + [(0, 0)] * (t.ndim - 2))
        x, dt, bmat, cmat = pw(x), pw(dt), pw(bmat), pw(cmat)
    nc = (L + pad) // CHUNK
    xdt = (x * dt[..., None]).reshape(bsz, nc, CHUNK, G, Hg, P)
    da = (dt * a.astype(f32)).reshape(bsz, nc, CHUNK, G, Hg)
    bm = bmat.reshape(bsz, nc, CHUNK, G, N)
    cm = cmat.reshape(bsz, nc, CHUNK, G, N)
    acum = jnp.cumsum(da, axis=2)
    seg = acum[:, :, :, None] - acum[:, :, None, :]
    causal = jnp.tril(jnp.ones((CHUNK, CHUNK), bool))[:, :, None, None]
    decay = jnp.exp(jnp.where(causal, seg, -jnp.inf))
    cb = jnp.einsum('bcign,bcjgn->bcijg', cm, bm)
    y_diag = jnp.einsum('bcijg,bcijgh,bcjghp->bcighp', cb, decay, xdt)
    decay_end = jnp.exp(acum[:, :, -1:] - acum)
    states = jnp.einsum('bcjgn,bcjgh,bcjghp->bcghpn', bm, decay_end, xdt)
    chunk_decay = jnp.exp(acum[:, :, -1])

    def step(s, inp):
        st, dec = inp
        return s * dec[..., None, None] + st, s

    s_final, s_in = lax.scan(step, s0.astype(f32).reshape(bsz, G, Hg, P, N),
                             (jnp.moveaxis(states, 1, 0), jnp.moveaxis(chunk_decay, 1, 0)))
    s_in = jnp.moveaxis(s_in, 0, 1)
    y_off = jnp.einsum('bcign,bcghpn,bcigh->bcighp', cm, s_in, jnp.exp(acum))
    y = (y_diag + y_off).reshape(bsz, nc * CHUNK, H, P)[:, :L]
    return y, s_final.reshape(bsz, H, P, N)


def mixer_layer(x, c, buf_a, buf_b, s0, w_mod, b_mod, norm_in_w, w_in, conv_a_w, norm_a_w,
                conv_b_w, conv_b_b, dt_bias, a_log, d_skip, norm_b_w, w_out):
    bsz, L, _ = x.shape
    mod = c @ w_mod + b_mod
    shift, scale, gate = jnp.split(mod, 3, axis=-1)
    h = rmsnorm(x, norm_in_w) * (1 + scale[:, None]) + shift[:, None]
    proj = h @ w_in
    idx = list(np.cumsum(SPLITS)[:-1])
    b_gate, c_gate, h_a, g_a, z, xbc_raw, dt_raw = jnp.split(proj, idx, axis=-1)
    u = c_gate * h_a
    conv_u, new_a = causal_dwconv(u, buf_a, conv_a_w)
    y_a = gated_rmsnorm(b_gate * conv_u, g_a, norm_a_w)
    xbc, new_b = causal_dwconv(xbc_raw, buf_b, conv_b_w)
    xbc = jax.nn.silu(xbc + conv_b_b)
    xs, bs, cs = jnp.split(xbc, [D_B, D_B + SSM_GROUPS * D_STATE], axis=-1)
    xs = xs.reshape(bsz, L, SSM_HEADS, SSM_HEADDIM)
    dt = jax.nn.softplus(dt_raw.astype(jnp.float32) + dt_bias.astype(jnp.float32))
    a = -jnp.exp(a_log.astype(jnp.float32))
    y, s_new = ssd_chunked(xs, dt, a,
                           bs.reshape(bsz, L, SSM_GROUPS, D_STATE),
                           cs.reshape(bsz, L, SSM_GROUPS, D_STATE), s0)
    y = y.astype(x.dtype) + d_skip[:, None] * xs
    y_b = gated_rmsnorm(y.reshape(bsz, L, D_B), z, norm_b_w)
    out = jnp.concatenate([y_a, y_b], axis=-1) @ w_out
    return x + gate[:, None] * out, new_a, new_b, s_new.astype(s0.dtype)


def trunk(x, c, sa, sb, ss, w_mod, b_mod, norm_in_w, w_in, conv_a_w, norm_a_w, conv_b_w,
          conv_b_b, dt_bias, a_log, d_skip, norm_b_w, w_out, norm_f_w):
    new_a, new_b, new_s = [], [], []
    for l in range(DEPTH):
        x, na, nb, ns = mixer_layer(x, c, sa[l], sb[l], ss[l], w_mod[l], b_mod[l], norm_in_w[l],
                                    w_in[l], conv_a_w[l], norm_a_w[l], conv_b_w[l], conv_b_b[l],
                                    dt_bias[l], a_log[l], d_skip[l], norm_b_w[l], w_out[l])
        new_a.append(na)
        new_b.append(nb)
        new_s.append(ns)
    return rmsnorm(x, norm_f_w), jnp.stack(new_a), jnp.stack(new_b), jnp.stack(new_s)


def setup_inputs(seed: int = 0) -> dict:
    key = jax.random.key(seed)
    ks = jax.random.split(key, 24)
    f32 = jnp.float32
    nrm = lambda k, s, sc: jax.random.normal(k, s, f32) * sc
    dt0 = jnp.exp(jax.random.uniform(ks[15], (DEPTH, SSM_HEADS), f32) * (math.log(0.1) - math.log(0.001)) + math.log(0.001))
    return {
        "x_prompt": nrm(ks[0], (BATCH, SEQ, D_MODEL), 1.0),
        "x_sample": nrm(ks[1], (DEC_BATCH, DEC_SEQ, D_MODEL), 1.0),
        "state_conv_a": nrm(ks[2], (DEPTH, DEC_BATCH, CONV_A_W - 1, D_A), 0.5),
        "state_conv_b": nrm(ks[3], (DEPTH, DEC_BATCH, CONV_B_W - 1, D_XBC), 0.5),
        "state_ssm": nrm(ks[4], (DEPTH, DEC_BATCH, SSM_HEADS, SSM_HEADDIM, D_STATE), 0.1),
        "c_prompt": nrm(ks[5], (BATCH, D_MODEL), 1.0),
        "c_sample": nrm(ks[6], (DEC_BATCH, D_MODEL), 1.0),
        "w_mod": nrm(ks[7], (DEPTH, D_MODEL, 3 * D_MODEL), 0.5 * D_MODEL ** -0.5),
        "b_mod": nrm(ks[8], (DEPTH, 3 * D_MODEL), 0.02),
        "norm_in_w": 1.0 + nrm(ks[9], (DEPTH, D_MODEL), 0.02),
        "w_in": nrm(ks[10], (DEPTH, D_MODEL, D_IN_PROJ), D_MODEL ** -0.5),
        "conv_a_w": nrm(ks[11], (DEPTH, CONV_A_W, D_A), CONV_A_W ** -0.5),
        "norm_a_w": 1.0 + nrm(ks[12], (DEPTH, D_A), 0.02),
        "conv_b_w": nrm(ks[13], (DEPTH, CONV_B_W, D_XBC), CONV_B_W ** -0.5),
        "conv_b_b": nrm(ks[14], (DEPTH, D_XBC), 0.02),
        "dt_bias": dt0 + jnp.log(-jnp.expm1(-dt0)),
        "a_log": jnp.log(jax.random.uniform(ks[16], (DEPTH, SSM_HEADS), f32, 1.0, 16.0)),
        "d_skip": 1.0 + nrm(ks[17], (DEPTH, SSM_HEADS), 0.02),
        "norm_b_w": 1.0 + nrm(ks[18], (DEPTH, D_B), 0.02),
        "w_out": nrm(ks[19], (DEPTH, D_MIX, D_MODEL), D_MIX ** -0.5),
        "norm_f_w": 1.0 + nrm(ks[20], (D_MODEL,), 0.02),
    }


def reference(x_prompt, x_sample, state_conv_a, state_conv_b, state_ssm, c_prompt, c_sample,
              w_mod, b_mod, norm_in_w, w_in, conv_a_w, norm_a_w, conv_b_w, conv_b_b,
              dt_bias, a_log, d_skip, norm_b_w, w_out, norm_f_w):
    dt_ = x_prompt.dtype
    za = jnp.zeros((DEPTH, x_prompt.shape[0], CONV_A_W - 1, D_A), dt_)
    zb = jnp.zeros((DEPTH, x_prompt.shape[0], CONV_B_W - 1, D_XBC), dt_)
    zs = jnp.zeros((DEPTH, x_prompt.shape[0], SSM_HEADS, SSM_HEADDIM, D_STATE), state_ssm.dtype)
    y_prompt, conv_a_p, conv_b_p, ssm_p = trunk(
        x_prompt, c_prompt, za, zb, zs, w_mod, b_mod, norm_in_w, w_in, conv_a_w, norm_a_w,
        conv_b_w, conv_b_b, dt_bias, a_log, d_skip, norm_b_w, w_out, norm_f_w)
    y_sample, conv_a_s, conv_b_s, ssm_s = trunk(
        x_sample, c_sample, state_conv_a, state_conv_b, state_ssm, w_mod, b_mod, norm_in_w,
        w_in, conv_a_w, norm_a_w, conv_b_w, conv_b_b, dt_bias, a_log, d_skip, norm_b_w,
        w_out, norm_f_w)
    return (y_prompt, y_sample, conv_a_p, conv_b_p, ssm_p, conv_a_s, conv_b_s, ssm_s)
```

```python
import contextlib
import numpy as np
import concourse.bass as bass
import concourse.mybir as mybir
from concourse.bass_utils import run_bass_kernel_spmd

F32 = mybir.dt.float32
BF16 = mybir.dt.bfloat16
ALU = mybir.AluOpType
AF = mybir.ActivationFunctionType

NCORES = 8
D = 1024
SEQ = 16384
SEGLEN = 4096
T = 512
NT = SEGLEN // T
DIN = 7184
PW = 256
NPIECE = 28
NWB = 6
EPS = 1e-5
NLSEG = 3
LROWS = NLSEG * SEGLEN
XROWS = 128 + LROWS + SEGLEN + 128
YROWS = SEGLEN + 128

PF_NIN = 0
PF_CAW = 8
PF_NAW = 32
PF_CBW = 40
PF_CBB = 104
PF_NBW = 120
PF_DSK = 128
PF_BSH = 136
PF_BSC = 144
NPF = 152
BC_NF = 0
BC_BG = 1024
BC_DTB = 2048
BC_ALOG = 2064
NBC = 2080
C_ID = 0
C_BLK = 128
C_TRI = 256
C_T64 = 384
C_SEL0 = 448
C_SEL1 = 576
NCONST = 704


class Planner:
    ENGS = ("pe", "act", "dve", "pool", "sp")
    SEM_LIMIT = 30000

    def __init__(self, nc):
        self.nc = nc
        self.lists = {e: [] for e in self.ENGS}
        self.cur = {e: [e + "_0", 0] for e in self.ENGS}
        self.gen = {e: 0 for e in self.ENGS}
        self.sem_names = [e + "_0" for e in self.ENGS]
        self.dma_cnt = {}
        self.waited = {e: {} for e in self.ENGS}
        self.bufs = {}
        self.nins = 0
        self.alias = {}

    def _exp(self, keys):
        out = []
        for k in keys:
            out.extend(self.alias.get(k, (k,)))
        return out

    def _need(self, eng, tok, out):
        if tok is None:
            return
        s, v = tok
        if eng == "pe" and s.startswith("pe_"):
            return
        if self.waited[eng].get(s, 0) >= v:
            return
        if out.get(s, 0) < v:
            out[s] = v

    def _deps(self, eng, reads, writes):
        reads, writes = self._exp(reads), self._exp(writes)
        need = {}
        for k in reads:
            b = self.bufs.get(k)
            if b:
                self._need(eng, b[0], need)
        for k in writes:
            b = self.bufs.get(k)
            if b:
                self._need(eng, b[0], need)
                for t in b[1]:
                    self._need(eng, t, need)
        for s, v in need.items():
            self.waited[eng][s] = v
            self.lists[eng].append(("wait", s, v))

    def _mark(self, tok, reads, writes):
        reads, writes = self._exp(reads), self._exp(writes)
        for k in reads:
            b = self.bufs.setdefault(k, [None, []])
            b[1].append(tok)
        for k in writes:
            self.bufs[k] = [tok, []]

    def op(self, eng, fn, reads=(), writes=(), token=True):
        self._deps(eng, reads, writes)
        c = self.cur[eng]
        self.nins += 1
        if token:
            if c[1] >= self.SEM_LIMIT:
                self.gen[eng] += 1
                c[0] = "%s_%d" % (eng, self.gen[eng])
                c[1] = 0
                self.sem_names.append(c[0])
            c[1] += 1
            tok = (c[0], c[1])
            self.lists[eng].append(("ins", fn, c[0], 1))
        else:
            tok = (c[0], c[1] + 1)
            self.lists[eng].append(("ins", fn, None, 0))
        self._mark(tok, reads, writes)
        return tok

    def dma(self, eng, fn, sem, reads=(), writes=()):
        self._deps(eng, reads, writes)
        self.nins += 1
        if sem not in self.dma_cnt:
            self.dma_cnt[sem] = 0
            self.sem_names.append(sem)
        self.dma_cnt[sem] += 16
        tok = (sem, self.dma_cnt[sem])
        self.lists[eng].append(("ins", fn, sem, 16))
        self._mark(tok, reads, writes)
        return tok

    def raw(self, eng, fn, sem, inc, reads=(), writes=()):
        self._deps(eng, reads, writes)
        if sem not in self.dma_cnt:
            self.dma_cnt[sem] = 0
            self.sem_names.append(sem)
        self.dma_cnt[sem] += inc
        tok = (sem, self.dma_cnt[sem])
        self.lists[eng].append(("ins", fn, sem, -inc))
        self._mark(tok, reads, writes)
        return tok

    def wait_tokens(self, eng, toks):
        need = {}
        for t in toks:
            self._need(eng, t, need)
        for s, v in need.items():
            self.waited[eng][s] = v
            self.lists[eng].append(("wait", s, v))

    def emit(self):
        nc = self.nc
        with contextlib.ExitStack() as st:
            sems = {}
            for n in self.sem_names:
                sems[n] = st.enter_context(nc.semaphore(n))
            block = st.enter_context(nc.Block())
            engmap = {"pe": block.tensor, "act": block.scalar, "dve": block.vector,
                      "pool": block.gpsimd, "sp": block.sync}
            for e in self.ENGS:
                lst = self.lists[e]
                if not lst:
                    continue

                def body(engobj, lst=lst):
                    for it in lst:
                        if it[0] == "wait":
                            engobj.wait_ge(sems[it[1]], it[2])
                        else:
                            ins = it[1](engobj)
                            if it[2] is not None:
                                if it[3] < 0:
                                    ins.then_inc(sems[it[2]])
                                else:
                                    ins.then_inc(sems[it[2]], it[3])
                engmap[e](body)


def build_nc(two_phase=True, debug=False):
    nc = bass.Bass("TRN2", target_bir_lowering=False)
    dr = lambda n, s, k, d=F32: nc.dram_tensor(n, list(s), d, kind=k)
    xs_d = dr("xs", [XROWS, D], "ExternalInput").ap()
    wmod_d = dr("w_mod", [D, 3 * D], "ExternalInput").ap()
    win_d = dr("w_in", [D, DIN], "ExternalInput").ap()
    wout_d = dr("w_out", [2 * D, D], "ExternalInput").ap()
    pf_d = dr("pf", [128, NPF], "ExternalInput").ap()
    bc_d = dr("bc", [128, NBC], "ExternalInput").ap()
    cst_d = dr("cst", [128, NCONST], "ExternalInput").ap()
    cT_d = dr("cT", [128, 8 * 3], "ExternalInput").ap()
    cbc_d = dr("cbc", [2, 128, 8 * 128], "ExternalInput").ap()
    sca_d = dr("sca", [128, 8 * 2 * 2], "ExternalInput").ap()
    scb_d = dr("scb", [128, 16 * 2 * 3], "ExternalInput").ap()
    sst_d = dr("sst", [2, 128, D], "ExternalInput").ap()
    msk_d = dr("msk", [128, 16], "ExternalInput").ap()
    y_d = dr("y", [YROWS, D], "ExternalOutput").ap()
    ca_d = dr("ca", [128, 8 * 3 * 2], "ExternalOutput").ap()
    cb_d = dr("cb", [128, 16 * 3 * 3], "ExternalOutput").ap()
    so_d = dr("so", [3, 128, D], "ExternalOutput").ap()
    winbf_d = nc.dram_tensor("winbf", [NPIECE, 128, 8 * PW], BF16)

    pl = Planner(nc)
    pl.alias = {"stg0": ("rhs1", "segs", "eac", "yo"), "stg1": ("stsb", "o1", "ost0", "ost1"), "sqj": ("Mt",)}
    with contextlib.ExitStack() as st:
        def SB(name, shape, dt=F32):
            return st.enter_context(nc.sbuf_tensor("sb_" + name, list(shape), dt))

        cst = SB("cst", [128, NCONST])
        idb = SB("idb", [128, 128], BF16)
        pf = SB("pf", [128, NPF])
        bc = SB("bc", [128, NBC])
        msk = SB("msk", [128, 16])
        cT = SB("cT", [128, 8, 3])
        gam = SB("gam", [128, 3, 8])
        bet = SB("bet", [128, 3, 8])
        gate = SB("gate", [128, 2, D])
        caw = SB("caw", [128, 8, 3])
        cbw = SB("cbw", [128, 16, 4])
        cbb = SB("cbb", [128, 16])
        a_bc = SB("a_bc", [128, 16])
        wout = SB("wout", [128, 16, D], BF16)
        wdt = SB("wdt", [128, 8, 16], BF16)
        wbuf = [SB("wbuf%d" % i, [128, 8, PW], BF16) for i in range(NWB)]
        big = [SB("big%d" % i, [128, 4096]) for i in range(2)]
        stg = [b[:].rearrange("p (a c) -> p a c", a=8) for b in big]
        xt = [SB("xt%d" % i, [128, D]) for i in range(2)]
        hT = SB("hT", [128, 8, T], BF16)
        ubuf = [SB("ubuf%d" % i, [128, 3 + T]) for i in range(2)]
        hsb = [SB("hsb%d" % i, [128, T]) for i in range(2)]
        cu = [SB("cu%d" % i, [128, T]) for i in range(2)]
        tq = [SB("tq%d" % i, [128, T]) for i in range(2)]
        sq2 = [SB("sq2%d" % i, [128, T], BF16) for i in range(2)]
        uhist = SB("uhist", [128, 8, 3])
        xhist = SB("xhist", [128, 16, 3])
        uhist0 = SB("uhist0", [128, 8, 3])
        xhist0 = SB("xhist0", [128, 16, 3])
        ycat = SB("ycat", [128, 16, T], BF16)
        xbf = SB("xbf", [128, 8, T], BF16)
        BT = SB("BT", [128, 4, T], BF16)
        CT = SB("CT", [128, 4, T], BF16)
        sz = SB("sz", [128, 8, T], BF16)
        xdt = SB("xdt", [128, D], BF16)
        xw = SB("xw", [128, D], BF16)
        btok = SB("btok", [128, 512], BF16)
        rhs1 = big[0][:, 0:1024]
        segs = big[0][:, 1024:2048]
        Mt = SB("Mt", [128, D], BF16)
        sqj = Mt
        eac = big[0][:, 2048:3072]
        yo = big[0][:, 3072:4096]
        cbtm = SB("cbtm", [128, 4, 64])
        sm = SB("sm", [128, 8, 64])
        ST = SB("ST", [128, D])
        STb = SB("STb", [128, D], BF16)
        stsb = big[1][:, 0:1024]
        cd = SB("cd", [128, 2, 64])
        stat = SB("stat", [128, 64])
        ost = [big[1][:, 2048:3072], big[1][:, 3072:4096]]
        o1 = big[1][:, 1024:2048]
        casb = SB("casb", [128, 8, 3, 2])
        cbsb = SB("cbsb", [128, 16, 3, 3])
        atot = SB("atot", [128, 16])
        gsel = SB("gsel", [128, 8, 16])
        print("sbuf remaining after alloc:", nc.sbuf_bytes_remaining)

        psA = st.enter_context(nc.psum_tensor("psA", [128, 4, 512], F32))
        psB = st.enter_context(nc.psum_tensor("psB", [128, 2, 512], F32))
        psC = st.enter_context(nc.psum_tensor("psC", [128, 512], F32))
        psT = st.enter_context(nc.psum_tensor("psT", [128, 1024], BF16))

        ident = cst[:, C_ID:C_ID + 128]
        blk = cst[:, C_BLK:C_BLK + 128]
        tri = cst[:, C_TRI:C_TRI + 128]
        t64 = cst[:, C_T64:C_T64 + 64]
        chsel = [cst[:, C_SEL0:C_SEL0 + 128], cst[:, C_SEL1:C_SEL1 + 128]]

        def pfc(off, j):
            return pf[:, off + j:off + j + 1]

        ld = lambda name, dst, src, key: pl.dma("sp", lambda e: e.dma_start(out=dst, in_=src), name, writes=[key])
        ld("l_cst", cst[:], cst_d[:, :], "cst")
        ld("l_pf", pf[:], pf_d[:, :], "pf")
        ld("l_bc", bc[:], bc_d[:, :], "bc")
        ld("l_msk", msk[:], msk_d[:, :], "msk")
        ld("l_cT", cT[:].rearrange("p a b -> p (a b)"), cT_d[:, :], "cT")
        pl.op("dve", lambda e: e.tensor_copy(out=idb[:], in_=ident), reads=["cst"], writes=["idb"])
        pl.op("dve", lambda e: e.tensor_scalar_mul(out=caw[:].rearrange("p a b -> p (a b)"), in0=pf[:, PF_CAW:PF_CAW + 24], scalar1=0.5), reads=["pf"], writes=["caw"])
        pl.op("dve", lambda e: e.tensor_scalar_mul(out=cbw[:].rearrange("p a b -> p (a b)"), in0=pf[:, PF_CBW:PF_CBW + 64], scalar1=0.5), reads=["pf"], writes=["cbw"])
        pl.op("dve", lambda e: e.tensor_scalar_mul(out=cbb[:], in0=pf[:, PF_CBB:PF_CBB + 16], scalar1=0.5), reads=["pf"], writes=["cbb"])
        pl.op("act", lambda e: e.activation(out=a_bc[:], in_=bc[:, BC_ALOG:BC_ALOG + 16], func=AF.Exp), reads=["bc"], writes=["a_bc"])
        pl.op("dve", lambda e: e.tensor_scalar_mul(out=a_bc[:], in0=a_bc[:], scalar1=-1.0), reads=["a_bc"], writes=["a_bc"])

        wmod_v = wmod_d.rearrange("(kc p) c -> p kc c", p=128)
        for piece in range(6):
            s = stg[piece % 2]
            key = "stg%d" % (piece % 2)
            pl.dma("sp", lambda e, s=s, piece=piece: e.dma_start(out=s[:], in_=wmod_v[:, :, piece * 512:(piece + 1) * 512]), "l_" + key, writes=[key])
            if piece < 4:
                for cc in range(4):
                    j = (piece % 2) * 4 + cc
                    for kc in range(8):
                        pl.op("pe", lambda e, s=s, cc=cc, kc=kc, j=j: e.matmul(psC[:, j * 4:j * 4 + 3], lhsT=s[:, kc, cc * 128:(cc + 1) * 128], rhs=cT[:, kc, :], start=(kc == 0), stop=(kc == 7)),
                              reads=[key, "cT"], writes=["psC"], token=(kc == 7))
                if piece % 2 == 1:
                    src = psC[:, 0:32].rearrange("p (j s) -> p s j", s=4)[:, 0:3, :]
                    if piece == 1:
                        pl.op("dve", lambda e, src=src: e.tensor_tensor(out=bet[:], in0=src, in1=pf[:, PF_BSH:PF_BSH + 8].unsqueeze(1).broadcast_to([128, 3, 8]), op=ALU.add), reads=["psC", "pf"], writes=["bet"])
                    else:
                        pl.op("dve", lambda e, src=src: e.tensor_tensor(out=gam[:], in0=src, in1=pf[:, PF_BSC:PF_BSC + 8].unsqueeze(1).broadcast_to([128, 3, 8]), op=ALU.add), reads=["psC", "pf"], writes=["gam"])
                        pl.op("dve", lambda e: e.scalar_tensor_tensor(out=gam[:], in0=gam[:], scalar=1.0, in1=pf[:, PF_NIN:PF_NIN + 8].unsqueeze(1).broadcast_to([128, 3, 8]), op0=ALU.add, op1=ALU.mult), reads=["gam", "pf"], writes=["gam"])
            else:
                half = piece - 4
                for which in range(2):
                    cb_t = xt[which]
                    if half == 0:
                        pl.dma("sp", lambda e, cb_t=cb_t, which=which: e.dma_start(out=cb_t[:], in_=cbc_d[which, :, :]), "l_xt%d" % which, writes=["xt%d" % which])
                    cbv = cb_t[:].rearrange("p (k m) -> p k m", k=8)
                    for kc in range(8):
                        pl.op("pe", lambda e, s=s, kc=kc, cbv=cbv, which=which: e.matmul(psA[:, which, :], lhsT=cbv[:, kc, :], rhs=s[:, kc, :], start=(kc == 0), stop=(kc == 7)),
                              reads=[key, "xt%d" % which], writes=["psA%d" % which], token=(kc == 7))
                    pl.op("dve", lambda e, which=which, half=half: e.tensor_tensor(out=gate[:, which, half * 512:(half + 1) * 512], in0=psA[:, which, :], in1=bc[:, BC_BG + half * 512:BC_BG + (half + 1) * 512], op=ALU.add),
                          reads=["psA%d" % which, "bc"], writes=["gate"])

        win_v = win_d.rearrange("(kc p) c -> p kc c", p=128)
        wout_v = wout_d.rearrange("(kc p) c -> p kc c", p=128)
        castengs = ["dve", "act", "pool"]
        ci = 0

        def cast(dst, src, rk, wk):
            nonlocal ci
            eng = castengs[ci % 3]
            ci += 1
            if eng == "act":
                pl.op("act", lambda e: e.copy(out=dst, in_=src), reads=rk, writes=wk)
            else:
                pl.op(eng, lambda e: e.tensor_copy(out=dst, in_=src), reads=rk, writes=wk)

        porder = [10, 11, 12, 2, 3, 4, 5, 0, 1, 6, 7, 13, 8, 9]
        pl.dma("sp", lambda e: e.dma_start(out=stg[0][:, :, 0:16], in_=win_v[:, :, 7168:7184]), "l_stg0", writes=["stg0"])
        pl.op("dve", lambda e: e.tensor_copy(out=wdt[:], in_=stg[0][:, :, 0:16]), reads=["stg0"], writes=["wdt"])
        wci = 0
        for n, piece in enumerate(porder):
            s = stg[n % 2]
            key = "stg%d" % (n % 2)
            pl.dma("sp", lambda e, s=s, piece=piece: e.dma_start(out=s[:], in_=win_v[:, :, piece * 512:(piece + 1) * 512]), "l_" + key, writes=[key])
            for hp in range(2):
                wb = wbuf[wci % NWB]
                wkey = "wbuf%d" % (wci % NWB)
                wci += 1
                for kh in range(2):
                    cast(wb[:, kh * 4:(kh + 1) * 4, :], s[:, kh * 4:(kh + 1) * 4, hp * PW:(hp + 1) * PW], [key], [wkey])
                sp_ = 2 * piece + hp
                pl.dma("pool", lambda e, wb=wb, sp_=sp_: e.dma_start(out=winbf_d[sp_, :, :], in_=wb[:].rearrange("p a b -> p (a b)")), "s_" + wkey, reads=[wkey], writes=["winbf%d" % sp_])
        for n in range(4):
            kg, ch = n // 2, n % 2
            s = stg[n % 2]
            key = "stg%d" % (n % 2)
            pl.dma("sp", lambda e, s=s, kg=kg, ch=ch: e.dma_start(out=s[:], in_=wout_v[:, kg * 8:(kg + 1) * 8, ch * 512:(ch + 1) * 512]), "l_" + key, writes=[key])
            for kh in range(2):
                cast(wout[:, kg * 8 + kh * 4:kg * 8 + (kh + 1) * 4, ch * 512:(ch + 1) * 512], s[:, kh * 4:(kh + 1) * 4, :], [key], ["wout"])

        wcnt = [0]

        def load_piece(piece):
            i = wcnt[0] % NWB
            wcnt[0] += 1
            wb = wbuf[i]
            pl.dma("sp", lambda e: e.dma_start(out=wb[:].rearrange("p a b -> p (a b)"), in_=winbf_d[piece, :, :]), "l_wbuf%d" % i, reads=["winbf%d" % piece], writes=["wbuf%d" % i])
            return wb, "wbuf%d" % i

        def proj_chunk(wb, wkey, cc, ps_ap, pskey, Tn):
            for kc in range(8):
                pl.op("pe", lambda e, kc=kc: e.matmul(ps_ap, lhsT=wb[:, kc, cc * 128:(cc + 1) * 128], rhs=hT[:, kc, 0:Tn], start=(kc == 0), stop=(kc == 7)),
                      reads=[wkey, "hT"], writes=[pskey], token=(kc == 7))

        def step_a(row0, Tn, slots):
            nsub = Tn // 128
            for p0 in range(0, nsub, 2):
                subs = list(range(p0, min(p0 + 2, nsub)))
                for s in subs:
                    xb = xt[s % 2]
                    xk = "xt%d" % (s % 2)
                    pl.dma("sp", lambda e, xb=xb, s=s: e.dma_start(out=xb[:], in_=xs_d[row0 + s * 128:row0 + (s + 1) * 128, :]), "l_" + xk, writes=[xk])
                    pl.op("pool", lambda e, s=s: e.memset(stat[:, s:s + 1], 0.0), writes=["stat"])
                    pl.op("act", lambda e, xb=xb, s=s: e.activation(out=sqj[:], in_=xb[:], func=AF.Square, accum_out=stat[:, s:s + 1]), reads=[xk, "stat"], writes=["sqj", "stat"])
                    pl.op("act", lambda e, s=s: e.activation(out=stat[:, 8 + s:9 + s], in_=stat[:, s:s + 1], func=AF.Ln, scale=1.0 / D, bias=EPS), reads=["stat"], writes=["stat"])
                    pl.op("act", lambda e, s=s: e.activation(out=stat[:, 16 + s:17 + s], in_=stat[:, 8 + s:9 + s], func=AF.Exp, scale=-0.5), reads=["stat"], writes=["stat"])
                    pl.op("dve", lambda e, xb=xb, s=s: e.tensor_scalar_mul(out=xb[:], in0=xb[:], scalar1=stat[:, 16 + s:17 + s]), reads=[xk, "stat"], writes=[xk])
                w0, w1 = subs[0] * 128, (subs[-1] + 1) * 128
                for kc in range(8):
                    pk = "psB%d" % (kc % 2)
                    for s in subs:
                        xb = xt[s % 2]
                        xk = "xt%d" % (s % 2)
                        pl.op("pe", lambda e, xb=xb, kc=kc, s=s, p0=p0: e.transpose(out=psB[:, kc % 2, (s - p0) * 128:(s - p0 + 1) * 128], in_=xb[:, kc * 128:(kc + 1) * 128], identity=ident), reads=[xk, "cst"], writes=[pk], token=(s == subs[-1]))
                    for (c0, c1, slot) in slots:
                        lo, hi = max(c0, w0), min(c1, w1)
                        if lo >= hi:
                            continue
                        pl.op("act", lambda e, kc=kc, lo=lo, hi=hi, slot=slot, w0=w0: e.activation(out=hT[:, kc, lo:hi], in_=psB[:, kc % 2, lo - w0:hi - w0], func=AF.Identity, scale=gam[:, slot, kc:kc + 1], bias=bet[:, slot, kc:kc + 1]),
                              reads=[pk, "gam", "bet"], writes=["hT"])

        def conv_chunk(eng_first, src, wts, nk, dst, Tn, rk, wk, bias=None, bk=()):
            off = 3 - (nk - 1)
            if bias is None:
                pl.op("act", lambda e: e.activation(out=dst[:, 0:Tn], in_=src[:, off:off + Tn], func=AF.Copy, scale=wts[:, 0:1]), reads=rk, writes=wk)
            else:
                pl.op("act", lambda e: e.activation(out=dst[:, 0:Tn], in_=src[:, off:off + Tn], func=AF.Identity, scale=wts[:, 0:1], bias=bias), reads=rk + list(bk), writes=wk)
            for k in range(1, nk):
                pl.op("dve", lambda e, k=k: e.scalar_tensor_tensor(out=dst[:, 0:Tn], in0=src[:, off + k:off + k + Tn], scalar=wts[:, k:k + 1], in1=dst[:, 0:Tn], op0=ALU.mult, op1=ALU.add), reads=rk + wk, writes=wk)

        def tile(row0, Tn, slots, mode, gidx, yrow0, hist_from, st_mode, out_slot):
            nsub = Tn // 128
            nch = Tn // 64
            step_a(row0, Tn, slots)
            light = (mode != "full")
            if mode in ("full", "halo"):
                wbs = {}
                pendA = None
                for j in range(8):
                    i = j % 2
                    if j % 2 == 0:
                        need = [4 + j // 2, 8 + j // 2] if mode == "halo" else [4 + j // 2, 8 + j // 2, 0 + j // 2, 12 + j // 2]
                        for pc in need:
                            wbs[pc] = load_piece(pc)
                    cc = j % 2
                    wb, wk_ = wbs[4 + j // 2]
                    proj_chunk(wb, wk_, cc, psA[:, 0, 0:Tn], "psA0", Tn)
                    wb, wk_ = wbs[8 + j // 2]
                    proj_chunk(wb, wk_, cc, psA[:, 1, 0:Tn], "psA1", Tn)
                    ub, uk = ubuf[i], "ubuf%d" % i
                    pl.op("act", lambda e, i=i: e.copy(out=hsb[i][:, 0:Tn], in_=psA[:, 1, 0:Tn]), reads=["psA1"], writes=["hsb%d" % i])
                    pl.op("pool", lambda e, ub=ub, j=j: e.tensor_copy(out=ub[:, 0:3], in_=uhist[:, j, :]), reads=["uhist"], writes=[uk])
                    pl.op("dve", lambda e, ub=ub, i=i: e.tensor_tensor(out=ub[:, 3:3 + Tn], in0=psA[:, 0, 0:Tn], in1=hsb[i][:, 0:Tn], op=ALU.mult), reads=["psA0", "hsb%d" % i], writes=[uk])
                    pl.op("pool", lambda e, ub=ub, j=j: e.tensor_copy(out=uhist[:, j, :], in_=ub[:, Tn:Tn + 3]), reads=[uk], writes=["uhist"])
                    if mode == "halo":
                        continue
                    wb, wk_ = wbs[0 + j // 2]
                    proj_chunk(wb, wk_, cc, psA[:, 2, 0:Tn], "psA2", Tn)
                    wb, wk_ = wbs[12 + j // 2]
                    proj_chunk(wb, wk_, cc, psA[:, 3, 0:Tn], "psA3", Tn)
                    pl.op("act", lambda e, i=i: e.activation(out=tq[i][:, 0:Tn], in_=psA[:, 3, 0:Tn], func=AF.Tanh, scale=0.5), reads=["psA3"], writes=["tq%d" % i])
                    pl.op("dve", lambda e, i=i: e.scalar_tensor_tensor(out=tq[i][:, 0:Tn], in0=tq[i][:, 0:Tn], scalar=1.0, in1=psA[:, 3, 0:Tn], op0=ALU.add, op1=ALU.mult), reads=["tq%d" % i, "psA3"], writes=["tq%d" % i])
                    pl.op("dve", lambda e, i=i: e.tensor_tensor(out=tq[i][:, 0:Tn], in0=psA[:, 2, 0:Tn], in1=tq[i][:, 0:Tn], op=ALU.mult), reads=["psA2", "tq%d" % i], writes=["tq%d" % i])
                    c_, ck = cu[i], "cu%d" % i
                    if hist_from == "carry":
                        conv_chunk("dve", ub, caw[:, j, :], 3, c_, Tn, [uk, "caw"], [ck])
                    else:
                        for sidx in range(2):
                            pl.op("dve", lambda e, ub=ub, j=j, sidx=sidx: e.tensor_copy(out=stsb[:, sidx * 128 + 1:sidx * 128 + 3], in_=sca_sb[:, j, sidx, :]), reads=["sca"], writes=["stsb"])
                        for sidx in range(2):
                            base = sidx * 128
                            pl.op("dve", lambda e, ub=ub, base=base, sidx=sidx: e.tensor_copy(out=stsb[:, base + 3:base + 67], in_=ub[:, 3 + sidx * 64:3 + (sidx + 1) * 64]), reads=[uk], writes=["stsb"])
                            off = 1
                            pl.op("dve", lambda e, c_=c_, base=base, sidx=sidx, j=j: e.tensor_scalar_mul(out=c_[:, sidx * 64:(sidx + 1) * 64], in0=stsb[:, base + 1:base + 65], scalar1=caw[:, j, 0:1]), reads=["stsb", "caw"], writes=[ck])
                            for k in (1, 2):
                                pl.op("dve", lambda e, c_=c_, base=base, sidx=sidx, j=j, k=k: e.scalar_tensor_tensor(out=c_[:, sidx * 64:(sidx + 1) * 64], in0=stsb[:, base + 1 + k:base + 65 + k], scalar=caw[:, j, k:k + 1], in1=c_[:, sidx * 64:(sidx + 1) * 64], op0=ALU.mult, op1=ALU.add), reads=["stsb", "caw", ck], writes=[ck])
                            pl.op("dve", lambda e, base=base, sidx=sidx, j=j: e.tensor_copy(out=casb[:, j, 1 + sidx, :], in_=stsb[:, base + 65:base + 67]), reads=["stsb"], writes=["casb"])
                    def stage_b(c_=c_, ck=ck, i=i, j=j):
                        pl.op("dve", lambda e: e.tensor_tensor(out=c_[:, 0:Tn], in0=c_[:, 0:Tn], in1=tq[i][:, 0:Tn], op=ALU.mult), reads=[ck, "tq%d" % i], writes=[ck])
                        pl.op("act", lambda e: e.activation(out=sq2[i][:, 0:Tn], in_=c_[:, 0:Tn], func=AF.Square), reads=[ck], writes=["sq2%d" % i])
                        pl.op("act", lambda e: e.activation(out=ycat[:, j, 0:Tn], in_=c_[:, 0:Tn], func=AF.Copy, scale=pfc(PF_NAW, j)), reads=[ck, "pf"], writes=["ycat"])
                        for s in range(nsub):
                            pl.op("pe", lambda e, s=s: e.matmul(psC[:, 320 + s:321 + s], lhsT=sq2[i][:, s * 128:(s + 1) * 128], rhs=ones_bf[:, 0:1], start=(j == 0 and s == 0), stop=(j == 7), skip_group_check=True),
                                  reads=["sq2%d" % i, "ones"], writes=["psCa"], token=(s == nsub - 1))
                    if pendA is not None:
                        pendA()
                    pendA = stage_b
                if pendA is not None:
                    pendA()
                if mode == "full" and hist_from == "carry":
                    pl.op("dve", lambda e: e.tensor_copy(out=casb[:, :, 0, :], in_=uhist[:, :, 1:3]), reads=["uhist"], writes=["casb"])
                if mode == "halo":
                    pl.op("dve", lambda e: e.tensor_scalar_mul(out=uhist[:].rearrange("p a b -> p (a b)"), in0=uhist[:].rearrange("p a b -> p (a b)"), scalar1=msk[:, 0:1]), reads=["uhist", "msk"], writes=["uhist"])
                    wbs = {}
                    for j in range(12, 16):
                        if j % 2 == 0:
                            wbs[j // 2] = load_piece(20 + j // 2)
                        wb, wk_ = wbs[j // 2]
                        pa = psA[:, j % 4, 0:Tn]
                        pk = "psA%d" % (j % 4)
                        proj_chunk(wb, wk_, j % 2, pa, pk, Tn)
                        pl.op("act", lambda e, j=j, pa=pa: e.copy(out=xhist[:, j, :], in_=pa[:, Tn - 3:Tn]), reads=[pk], writes=["xhist"])
                    pl.op("dve", lambda e: e.tensor_scalar_mul(out=xhist[:, 12:16, :], in0=xhist[:, 12:16, :], scalar1=msk[:, 0:1]), reads=["xhist", "msk"], writes=["xhist"])
                    return

            wbs = {}
            pending = None
            for j in range(16):
                if mode == "light" and j >= 12:
                    break
                if j % 2 == 0:
                    wbs[j // 2] = load_piece(20 + j // 2)
                wb, wk_ = wbs[j // 2]
                i = j % 2
                pa = psA[:, j % 4, 0:Tn]
                pk = "psA%d" % (j % 4)
                proj_chunk(wb, wk_, j % 2, pa, pk, Tn)
                ub, uk = ubuf[i], "ubuf%d" % i
                pl.op("pool", lambda e, ub=ub, j=j: e.tensor_copy(out=ub[:, 0:3], in_=xhist[:, j, :]), reads=["xhist"], writes=[uk])
                pl.op("act", lambda e, ub=ub, pa=pa: e.copy(out=ub[:, 3:3 + Tn], in_=pa), reads=[pk], writes=[uk])
                pl.op("pool", lambda e, ub=ub, j=j: e.tensor_copy(out=xhist[:, j, :], in_=ub[:, Tn:Tn + 3]), reads=[uk], writes=["xhist"])
                c_, ck = cu[i], "cu%d" % i
                if hist_from == "carry":
                    conv_chunk("act", ub, cbw[:, j, :], 4, c_, Tn, [uk, "cbw"], [ck], bias=cbb[:, j:j + 1], bk=["cbb"])
                else:
                    for sidx in range(2):
                        base = sidx * 128
                        pl.op("dve", lambda e, j=j, sidx=sidx, base=base: e.tensor_copy(out=stsb[:, base:base + 3], in_=scb_sb[:, j, sidx, :]), reads=["scb"], writes=["stsb"])
                        pl.op("dve", lambda e, ub=ub, base=base, sidx=sidx: e.tensor_copy(out=stsb[:, base + 3:base + 67], in_=ub[:, 3 + sidx * 64:3 + (sidx + 1) * 64]), reads=[uk], writes=["stsb"])
                        pl.op("dve", lambda e, c_=c_, base=base, sidx=sidx, j=j: e.tensor_scalar_mul(out=c_[:, sidx * 64:(sidx + 1) * 64], in0=stsb[:, base:base + 64], scalar1=cbw[:, j, 0:1]), reads=["stsb", "cbw"], writes=[ck])
                        for k in (1, 2, 3):
                            pl.op("dve", lambda e, c_=c_, base=base, sidx=sidx, j=j, k=k: e.scalar_tensor_tensor(out=c_[:, sidx * 64:(sidx + 1) * 64], in0=stsb[:, base + k:base + 64 + k], scalar=cbw[:, j, k:k + 1], in1=c_[:, sidx * 64:(sidx + 1) * 64], op0=ALU.mult, op1=ALU.add), reads=["stsb", "cbw", ck], writes=[ck])
                        pl.op("dve", lambda e, base=base, sidx=sidx, j=j: e.tensor_copy(out=cbsb[:, j, 1 + sidx, :], in_=stsb[:, base + 64:base + 67]), reads=["stsb"], writes=["cbsb"])
                    pl.op("dve", lambda e, c_=c_, j=j: e.tensor_scalar_add(out=c_[:, 0:Tn], in0=c_[:, 0:Tn], scalar1=cbb[:, j:j + 1]), reads=[ck, "cbb"], writes=[ck])
                if j < 8:
                    dst, dk = xbf[:, j, 0:Tn], "xbf"
                elif j < 12:
                    dst, dk = BT[:, j - 8, 0:Tn], "BT"
                else:
                    dst, dk = CT[:, j - 12, 0:Tn], "CT"

                def stage_b(c_=c_, ck=ck, i=i, dst=dst, dk=dk):
                    pl.op("act", lambda e: e.activation(out=tq[i][:, 0:Tn], in_=c_[:, 0:Tn], func=AF.Tanh), reads=[ck], writes=["tq%d" % i])
                    pl.op("dve", lambda e: e.scalar_tensor_tensor(out=dst, in0=tq[i][:, 0:Tn], scalar=1.0, in1=c_[:, 0:Tn], op0=ALU.add, op1=ALU.mult), reads=["tq%d" % i, ck], writes=[dk])
                if pending is not None:
                    pending()
                pending = stage_b
            if pending is not None:
                pending()
            if mode == "full" and hist_from == "carry":
                pl.op("dve", lambda e: e.tensor_copy(out=cbsb[:, :, 0, :], in_=xhist[:]), reads=["xhist"], writes=["cbsb"])
            if mode == "full":
                wbs = {}
                for j in range(8):
                    if j % 2 == 0:
                        wbs[j // 2] = load_piece(16 + j // 2)
                    wb, wk_ = wbs[j // 2]
                    pa = psA[:, j % 4, 0:Tn]
                    pk = "psA%d" % (j % 4)
                    i = j % 2
                    proj_chunk(wb, wk_, j % 2, pa, pk, Tn)
                    pl.op("act", lambda e, i=i, pa=pa: e.activation(out=tq[i][:, 0:Tn], in_=pa, func=AF.Tanh, scale=0.5), reads=[pk], writes=["tq%d" % i])
                    pl.op("dve", lambda e, i=i, pa=pa, j=j: e.scalar_tensor_tensor(out=sz[:, j, 0:Tn], in0=tq[i][:, 0:Tn], scalar=1.0, in1=pa, op0=ALU.add, op1=ALU.mult), reads=["tq%d" % i, pk], writes=["sz"])

            W = nsub * 16
            SMA = lambda idx: sm[:, idx, 0:W]
            v3 = lambda ap: ap.rearrange("p (b h) -> p b h", h=16)
            for b in range(nsub):
                tsl = slice(b * 128, (b + 1) * 128)
                for kc in range(8):
                    pl.op("pe", lambda e, kc=kc, tsl=tsl, b=b: e.matmul(psC[:, 256 + b * 16:272 + b * 16], lhsT=hT[:, kc, tsl], rhs=wdt[:, kc, :], start=(kc == 0), stop=(kc == 7)), reads=["hT", "wdt"], writes=["psCd"], token=(kc == 7))
            pl.op("dve", lambda e: e.tensor_tensor(out=v3(SMA(0)), in0=v3(psC[:, 256:256 + W]), in1=bc[:, BC_DTB:BC_DTB + 16].unsqueeze(1).broadcast_to([128, nsub, 16]), op=ALU.add), reads=["psCd", "bc"], writes=["sm0"])
            pl.op("act", lambda e: e.activation(out=SMA(1), in_=SMA(0), func=AF.Abs), reads=["sm0"], writes=["sm1"])
            pl.op("act", lambda e: e.activation(out=SMA(1), in_=SMA(1), func=AF.Exp, scale=-1.0), reads=["sm1"], writes=["sm1"])
            pl.op("act", lambda e: e.activation(out=SMA(1), in_=SMA(1), func=AF.Ln, bias=1.0), reads=["sm1"], writes=["sm1"])
            pl.op("dve", lambda e: e.scalar_tensor_tensor(out=SMA(2), in0=SMA(0), scalar=0.0, in1=SMA(1), op0=ALU.max, op1=ALU.add), reads=["sm0", "sm1"], writes=["sm2"])
            pl.op("dve", lambda e: e.tensor_tensor(out=v3(SMA(3)), in0=v3(SMA(2)), in1=a_bc[:].unsqueeze(1).broadcast_to([128, nsub, 16]), op=ALU.mult), reads=["sm2", "a_bc"], writes=["sm3"])
            pl.op("pe", lambda e: e.matmul(psB[:, 0, 0:W], lhsT=tri, rhs=SMA(3), start=True, stop=True), reads=["cst", "sm3"], writes=["psB0"], token=False)
            pl.op("pe", lambda e: e.matmul(psB[:, 0, 64:64 + W], lhsT=blk, rhs=SMA(3), start=True, stop=True), reads=["cst", "sm3"], writes=["psB0"], token=False)
            pl.op("pe", lambda e: e.matmul(psB[:, 0, 128:128 + W], lhsT=chsel[0], rhs=SMA(3), start=True, stop=True), reads=["cst", "sm3"], writes=["psB0"], token=False)
            pl.op("pe", lambda e: e.matmul(psB[:, 0, 192:192 + W], lhsT=chsel[1], rhs=SMA(3), start=True, stop=True), reads=["cst", "sm3"], writes=["psB0"])
            pl.op("dve", lambda e: e.tensor_copy(out=SMA(4), in_=psB[:, 0, 0:W]), reads=["psB0"], writes=["sm4"])
            pl.op("dve", lambda e: e.tensor_tensor(out=SMA(5), in0=psB[:, 0, 64:64 + W], in1=SMA(4), op=ALU.subtract), reads=["psB0", "sm4"], writes=["sm5"])
            pl.op("act", lambda e: e.activation(out=SMA(5), in_=SMA(5), func=AF.Exp), reads=["sm5"], writes=["sm5"])
            pl.op("dve", lambda e: e.tensor_tensor(out=SMA(6), in0=SMA(5), in1=SMA(2), op=ALU.mult), reads=["sm5", "sm2"], writes=["sm6"])
            pl.op("act", lambda e: e.activation(out=cd[:, 0, 0:W], in_=psB[:, 0, 128:128 + W], func=AF.Exp), reads=["psB0"], writes=["cd"])
            pl.op("act", lambda e: e.activation(out=cd[:, 1, 0:W], in_=psB[:, 0, 192:192 + W], func=AF.Exp), reads=["psB0"], writes=["cd"])
            for b in range(nsub):
                tsl = slice(b * 128, (b + 1) * 128)
                SM = lambda idx, b=b: sm[:, idx, b * 16:(b + 1) * 16]
                bc16 = lambda idx, SM=SM: SM(idx).unsqueeze(2).broadcast_to([128, 16, 64])
                b16_2, b16_3, b16_4, b16_6 = bc16(2), bc16(3), bc16(4), bc16(6)
                for j in range(8):
                    pl.op("pe", lambda e, j=j, tsl=tsl: e.transpose(out=psT[:, j * 128:(j + 1) * 128], in_=xbf[:, j, tsl], identity=idb[:]), reads=["xbf", "idb"], writes=["psT"], token=(j == 7))
                bview = lambda ap: ap.rearrange("p (h q) -> p h q", h=16)
                if mode == "full":
                    pl.op("dve", lambda e, b16_2=b16_2: e.tensor_tensor(out=bview(xdt[:]), in0=bview(psT[:, :]), in1=b16_2, op=ALU.mult), reads=["psT", "sm2"], writes=["xdt"])
                pl.op("dve", lambda e, b16_6=b16_6: e.tensor_tensor(out=bview(xw[:]), in0=bview(psT[:, :]), in1=b16_6, op=ALU.mult), reads=["psT", "sm6"], writes=["xw"])
                for g in range(4):
                    pl.op("pe", lambda e, g=g, tsl=tsl: e.transpose(out=psT[:, g * 128:(g + 1) * 128], in_=BT[:, g, tsl], identity=idb[:]), reads=["BT", "idb"], writes=["psT"], token=(g == 3))
                pl.op("act", lambda e: e.copy(out=btok[:], in_=psT[:, 0:512]), reads=["psT"], writes=["btok"])

                if mode == "full":
                    for g in range(4):
                        for c in range(2):
                            cs = slice(b * 128 + c * 64, b * 128 + (c + 1) * 64)
                            pl.op("pe", lambda e, g=g, c=c, cs=cs: e.matmul(psC[c * 64:(c + 1) * 64, g * 64:(g + 1) * 64], lhsT=BT[:, g, cs], rhs=CT[:, g, cs], start=True, stop=True), reads=["BT", "CT"], writes=["psCb"], token=(g == 3 and c == 1))
                    pl.op("dve", lambda e: e.tensor_tensor(out=cbtm[:], in0=psC[:, 0:256].rearrange("p (g i) -> p g i", g=4), in1=t64.unsqueeze(1).broadcast_to([128, 4, 64]), op=ALU.mult), reads=["psCb", "cst"], writes=["cbtm"])
                    pl.op("dve", lambda e, b16_3=b16_3: e.tensor_tensor(out=bview(rhs1[:]), in0=b16_3, in1=t64.unsqueeze(1).broadcast_to([128, 16, 64]), op=ALU.mult), reads=["sm3", "cst"], writes=["rhs1"])
                    for hh in range(2):
                        pl.op("pe", lambda e, hh=hh: e.matmul(psA[:, hh, :], lhsT=blk, rhs=rhs1[:, hh * 512:(hh + 1) * 512], start=True, stop=True), reads=["cst", "rhs1"], writes=["psA%d" % hh])
                    pl.op("dve", lambda e, b16_4=b16_4: e.tensor_tensor(out=bview(segs[:]), in0=psA[:, 0:2, :].rearrange("p a (h q) -> p (a h) q", q=64), in1=b16_4, op=ALU.subtract), reads=["psA0", "psA1", "sm4"], writes=["segs"])
                    pl.op("act", lambda e: e.activation(out=segs[:], in_=segs[:], func=AF.Exp), reads=["segs"], writes=["segs"])
                    for g in range(4):
                        pl.op("dve", lambda e, g=g: e.scalar_tensor_tensor(out=Mt[:, g * 256:(g + 1) * 256].rearrange("p (r q) -> p r q", r=4), in0=segs[:, g * 256:(g + 1) * 256].rearrange("p (r q) -> p r q", r=4), scalar=1.0,
                                                                           in1=cbtm[:, g, :].unsqueeze(1).broadcast_to([128, 4, 64]), op0=ALU.min, op1=ALU.mult), reads=["segs", "cbtm"], writes=["Mt"])
                    pl.op("dve", lambda e, b16_3=b16_3: e.tensor_copy(out=bview(rhs1[:]), in_=b16_3), reads=["sm3"], writes=["rhs1"])
                    for a in range(8):
                        pl.op("pe", lambda e, a=a: e.matmul(psA[:, 2 + a // 4, (a % 4) * 128:(a % 4 + 1) * 128], lhsT=rhs1[:, a * 128:(a + 1) * 128], rhs=tri, start=True, stop=True), reads=["rhs1", "cst"], writes=["psA%d" % (2 + a // 4)], token=(a % 4 == 3))
                    pl.op("act", lambda e: e.activation(out=eac[:], in_=psA[:, 2:4, :].rearrange("p a q -> p (a q)"), func=AF.Exp), reads=["psA2", "psA3"], writes=["eac"])

                for c in range(2):
                    ch = b * 2 + c
                    cs = slice(b * 128 + c * 64, b * 128 + (c + 1) * 64)
                    ps_ = slice(c * 64, (c + 1) * 64)
                    if st_mode[ch] is not None:
                        kind, val = st_mode[ch]
                        if kind == "zero":
                            pl.op("dve", lambda e: e.memset(ST[:], 0.0), writes=["ST"])
                        elif kind == "load":
                            pl.dma("sp", lambda e, val=val: e.dma_start(out=ST[:], in_=sst_d[val, :, :]), "l_ST", writes=["ST"])
                        elif kind == "keep":
                            pass
                        if mode == "full":
                            pl.op("act", lambda e: e.copy(out=STb[:], in_=ST[:]), reads=["ST"], writes=["STb"])
                    pl.op("pool", lambda e, c=c, b=b: e.tensor_tensor(out=bview(ST[:]), in0=bview(ST[:]), in1=cd[:, c, b * 16:(b + 1) * 16].unsqueeze(2).broadcast_to([128, 16, 64]), op=ALU.mult), reads=["ST", "cd"], writes=["ST"])
                    if mode == "full":
                        for a in range(8):
                            for hh in range(2):
                                h = 2 * a + hh
                                pl.op("pe", lambda e, a=a, hh=hh, h=h, c=c, ps_=ps_: e.matmul(psB[hh * 64:(hh + 1) * 64, a // 4, (a % 4) * 128 + c * 64:(a % 4) * 128 + (c + 1) * 64], lhsT=xdt[ps_, h * 64:(h + 1) * 64], rhs=Mt[ps_, h * 64:(h + 1) * 64], start=True, stop=True),
                                      reads=["xdt", "Mt"], writes=["psB%d" % (a // 4)], token=(hh == 1 and a % 4 == 3))
                        for a in range(8):
                            g = a // 2
                            pl.op("pe", lambda e, a=a, g=g, c=c, cs=cs: e.matmul(psA[:, a // 4, (a % 4) * 128 + c * 64:(a % 4) * 128 + (c + 1) * 64], lhsT=STb[:, a * 128:(a + 1) * 128], rhs=CT[:, g, cs], start=True, stop=True),
                                  reads=["STb", "CT"], writes=["psA%d" % (a // 4)], token=(a % 4 == 3))
                    for g in range(4):
                        pl.op("pe", lambda e, g=g, ps_=ps_: e.matmul(psA[:, 2 + g // 2, (g % 2) * 256:(g % 2 + 1) * 256], lhsT=btok[ps_, g * 128:(g + 1) * 128], rhs=xw[ps_, g * 256:(g + 1) * 256], start=True, stop=True),
                              reads=["btok", "xw"], writes=["psA%d" % (2 + g // 2)], token=(g % 2 == 1))
                    pl.op("dve", lambda e: e.tensor_tensor(out=ST[:], in0=ST[:], in1=psA[:, 2:4, :].rearrange("p a q -> p (a q)"), op=ALU.add), reads=["ST", "psA2", "psA3"], writes=["ST"])
                    if mode == "full":
                        pl.op("act", lambda e: e.copy(out=STb[:], in_=ST[:]), reads=["ST"], writes=["STb"])
                        if hist_from != "carry":
                            pl.dma("pool", lambda e, ch=ch: e.dma_start(out=so_d[1 + ch, :, :], in_=ST[:]), "s_ST", reads=["ST"], writes=["so%d" % (1 + ch)])
                if mode == "full":
                    pl.op("dve", lambda e: e.tensor_tensor(out=yo[:], in0=psA[:, 0:2, :].rearrange("p a q -> p (a q)"), in1=eac[:], op=ALU.mult), reads=["psA0", "psA1", "eac"], writes=["yo"])
                    pl.op("dve", lambda e: e.tensor_tensor(out=yo[:], in0=psB[:, :, :].rearrange("p a q -> p (a q)"), in1=yo[:], op=ALU.add), reads=["psB0", "psB1", "yo"], writes=["yo"])
                    for a in range(8):
                        ya = yo[:, a * 128:(a + 1) * 128]
                        pl.op("dve", lambda e, a=a, ya=ya, tsl=tsl: e.scalar_tensor_tensor(out=ya, in0=xbf[:, a, tsl], scalar=pfc(PF_DSK, a), in1=ya, op0=ALU.mult, op1=ALU.add), reads=["xbf", "pf", "yo"], writes=["yo"])
                        pl.op("dve", lambda e, a=a, ya=ya, tsl=tsl: e.scalar_tensor_tensor(out=ya, in0=ya, scalar=0.5, in1=sz[:, a, tsl], op0=ALU.mult, op1=ALU.mult), reads=["yo", "sz"], writes=["yo"])
                    pl.op("act", lambda e: e.activation(out=Mt[:], in_=yo[:], func=AF.Square), reads=["yo"], writes=["Mt"])
                    for a in range(8):
                        pl.op("act", lambda e, a=a, tsl=tsl: e.activation(out=ycat[:, 8 + a, tsl], in_=yo[:, a * 128:(a + 1) * 128], func=AF.Copy, scale=pfc(PF_NBW, a)), reads=["yo", "pf"], writes=["ycat"])
                        pl.op("pe", lambda e, a=a, b=b: e.matmul(psC[:, 324 + b:325 + b], lhsT=Mt[:, a * 128:(a + 1) * 128], rhs=ones_bf[:, 0:1], start=(a == 0), stop=(a == 7)), reads=["Mt", "ones"], writes=["psCs"], token=(a == 7))

            if mode != "full":
                return
            pl.op("act", lambda e: e.activation(out=stat[:, 24:32], in_=psC[:, 320:328], func=AF.Ln, scale=1.0 / D, bias=EPS), reads=["psCa", "psCs"], writes=["stat"])
            pl.op("act", lambda e: e.activation(out=stat[:, 24:32], in_=stat[:, 24:32], func=AF.Exp, scale=-0.5), reads=["stat"], writes=["stat"])
            for s in range(nsub):
                tsl = slice(s * 128, (s + 1) * 128)
                for hf in range(2):
                    for part in range(2):
                        for kc in range(8):
                            pl.op("pe", lambda e, hf=hf, part=part, kc=kc, tsl=tsl: e.matmul(psA[:, part * 2 + hf, :], lhsT=ycat[:, part * 8 + kc, tsl], rhs=wout[:, part * 8 + kc, hf * 512:(hf + 1) * 512], start=(kc == 0), stop=(kc == 7)),
                                  reads=["ycat", "wout"], writes=["psA%d" % (part * 2 + hf)], token=(kc == 7))
                xb = xt[s % 2]
                xk = "xt%d" % (s % 2)
                pl.dma("sp", lambda e, xb=xb, s=s: e.dma_start(out=xb[:], in_=xs_d[row0 + s * 128:row0 + (s + 1) * 128, :]), "l_" + xk, writes=[xk])
                ob = ost[s % 2]
                ok = "ost%d" % (s % 2)
                pl.op("act", lambda e, s=s: e.activation(out=o1[:], in_=psA[:, 0:2, :].rearrange("p a q -> p (a q)"), func=AF.Copy, scale=stat[:, 24 + s:25 + s]), reads=["psA0", "psA1", "stat"], writes=["o1"])
                pl.op("dve", lambda e, s=s: e.scalar_tensor_tensor(out=o1[:], in0=psA[:, 2:4, :].rearrange("p a q -> p (a q)"), scalar=stat[:, 28 + s:29 + s], in1=o1[:], op0=ALU.mult, op1=ALU.add), reads=["psA2", "psA3", "stat", "o1"], writes=["o1"])
                pl.op("dve", lambda e: e.tensor_tensor(out=o1[:], in0=o1[:], in1=gate[:, gidx, :], op=ALU.mult), reads=["o1", "gate"], writes=["o1"])
                pl.op("dve", lambda e, xb=xb: e.tensor_tensor(out=o1[:], in0=o1[:], in1=xb[:], op=ALU.add), reads=["o1", xk], writes=["o1"])
                pl.op("dve", lambda e, s=s: e.memset(stat[:, 32 + s:33 + s], 0.0), writes=["stat"])
                pl.op("act", lambda e, s=s: e.activation(out=sqj[:], in_=o1[:], func=AF.Square, accum_out=stat[:, 32 + s:33 + s]), reads=["o1", "stat"], writes=["sqj", "stat"])
                pl.op("act", lambda e, s=s: e.activation(out=stat[:, 36 + s:37 + s], in_=stat[:, 32 + s:33 + s], func=AF.Ln, scale=1.0 / D, bias=EPS), reads=["stat"], writes=["stat"])
                pl.op("act", lambda e, s=s: e.activation(out=stat[:, 36 + s:37 + s], in_=stat[:, 36 + s:37 + s], func=AF.Exp, scale=-0.5), reads=["stat"], writes=["stat"])
                pl.op("dve", lambda e, s=s, ob=ob: e.scalar_tensor_tensor(out=ob[:], in0=o1[:], scalar=stat[:, 36 + s:37 + s], in1=bc[:, BC_NF:BC_NF + D], op0=ALU.mult, op1=ALU.mult), reads=["o1", "stat", "bc"], writes=[ok])
                pl.dma("pool", lambda e, ob=ob, s=s: e.dma_start(out=y_d[yrow0 + s * 128:yrow0 + (s + 1) * 128, :], in_=ob[:]), "s_" + ok, reads=[ok], writes=["y"])

        ones_bf = SB("ones_bf", [128, 8], BF16)
        pl.op("dve", lambda e: e.memset(ones_bf[:], 1.0), writes=["ones"])
        sca_sb = SB("sca_sb", [128, 8, 2, 2])
        scb_sb = SB("scb_sb", [128, 16, 2, 3])
        ld("l_sca", sca_sb[:].rearrange("p a b c -> p (a b c)"), sca_d[:, :], "sca")
        ld("l_scb", scb_sb[:].rearrange("p a b c -> p (a b c)"), scb_d[:, :], "scb")
        pl.op("dve", lambda e: e.memset(uhist[:], 0.0), writes=["uhist"])
        pl.op("dve", lambda e: e.memset(xhist[:], 0.0), writes=["xhist"])
        pl.op("dve", lambda e: e.memset(atot[:], 0.0), writes=["atot"])
        pslot = [(0, T, 0)]
        for n in range(NLSEG * NT):
            stm = [None] * (T // 64)
            if n == 0:
                stm[0] = ("zero", 0)
            tile(128 + n * T, T, pslot, "light", 0, 0, "carry", stm, None)
            if n % NT == NT - 1:
                m = n // NT
                pl.op("dve", lambda e, m=m: e.tensor_scalar_mul(out=ST[:], in0=ST[:], scalar1=msk[:, 1 + m:2 + m]), reads=["ST", "msk"], writes=["ST"])
                pl.op("dve", lambda e, m=m: e.tensor_scalar_mul(out=xhist[:].rearrange("p a b -> p (a b)"), in0=xhist[:].rearrange("p a b -> p (a b)"), scalar1=msk[:, 1 + m:2 + m]), reads=["xhist", "msk"], writes=["xhist"])
        tile(0, 128, [(0, 128, 0)], "halo", 0, 0, "carry", [None, None], None)
        for n in range(NT):
            stm = [None] * (T // 64)
            if n == 0:
                stm[0] = ("keep", 0)
            tile(128 + LROWS + n * T, T, pslot, "full", 0, n * T, "carry", stm, 0)
        pl.dma("pool", lambda e: e.dma_start(out=so_d[0, :, :], in_=ST[:]), "s_ST", reads=["ST"], writes=["so0"])
        tile(128 + LROWS + SEGLEN, 128, [(0, 64, 1), (64, 128, 2)], "full", 1, SEGLEN, "state", [("load", 0), ("load", 1)], 1)
        pl.dma("pool", lambda e: e.dma_start(out=ca_d[:, :], in_=casb[:].rearrange("p a b c -> p (a b c)")), "s_ca", reads=["casb"], writes=["ca"])
        pl.dma("pool", lambda e: e.dma_start(out=cb_d[:, :], in_=cbsb[:].rearrange("p a b c -> p (a b c)")), "s_cb", reads=["cbsb"], writes=["cb"])
        if debug:
            dbg_list = [("xbf", xbf[:].rearrange("p a b -> p (a b)"), 8 * T, BF16), ("BT", BT[:].rearrange("p a b -> p (a b)"), 4 * T, BF16),
                        ("CT", CT[:].rearrange("p a b -> p (a b)"), 4 * T, BF16), ("sm", sm[:].rearrange("p a b -> p (a b)"), 256, F32),
                        ("ycat", ycat[:].rearrange("p a b -> p (a b)"), 16 * T, BF16), ("yo", yo[:], 1024, F32), ("xw", xw[:], 1024, BF16),
                        ("xdt", xdt[:], 1024, BF16), ("btok", btok[:], 512, BF16), ("Mt", Mt[:], 1024, BF16), ("eac", eac[:], 1024, F32),
                        ("cd", cd[:].rearrange("p a b -> p (a b)"), 32, F32), ("hT", hT[:].rearrange("p a b -> p (a b)"), 8 * T, BF16),
                        ("sz", sz[:].rearrange("p a b -> p (a b)"), 8 * T, BF16), ("stat", stat[:], 64, F32), ("gam", gam[:].rearrange("p a b -> p (a b)"), 24, F32)]
            allk = list(pl.bufs.keys())
            for nm, ap, w, dt_ in dbg_list:
                dd = nc.dram_tensor("dbg_" + nm, [128, w], dt_, kind="ExternalOutput").ap()
                pl.dma("pool", lambda e, dd=dd, ap=ap: e.dma_start(out=dd[:, :], in_=ap), "s_dbg", reads=allk)
        pl.wait_tokens("pool", [(s, c) for s, c in pl.dma_cnt.items() if s.startswith("s_")])
        print("planned instructions:", pl.nins, {e: len(pl.lists[e]) for e in pl.ENGS})
        pl.emit()
    return nc


def _host_consts():
    c = np.zeros((128, NCONST), np.float32)
    k = np.arange(128)
    c[:, C_ID:C_ID + 128] = np.eye(128, dtype=np.float32)
    same = (k[:, None] // 64) == (k[None, :] // 64)
    c[:, C_BLK:C_BLK + 128] = same
    c[:, C_TRI:C_TRI + 128] = same & (k[:, None] <= k[None, :])
    c[:, C_T64:C_T64 + 64] = (k[:, None] % 64) <= np.arange(64)[None, :]
    c[:, C_SEL0:C_SEL0 + 128] = (k[:, None] < 64)
    c[:, C_SEL1:C_SEL1 + 128] = (k[:, None] >= 64)
    return c


def _fm(v, nchunk):
    return np.ascontiguousarray(np.asarray(v, np.float32).reshape(nchunk, 128).T)


_NC_CACHE = {}


def kernel(x_prompt, x_sample, state_conv_a, state_conv_b, state_ssm, c_prompt, c_sample,
           w_mod, b_mod, norm_in_w, w_in, conv_a_w, norm_a_w, conv_b_w, conv_b_b,
           dt_bias, a_log, d_skip, norm_b_w, w_out, norm_f_w, _two_phase=True, _debug=False):
    f = lambda a: np.ascontiguousarray(np.asarray(a, np.float32))
    x_prompt, x_sample = f(x_prompt), f(x_sample)
    state_conv_a, state_conv_b, state_ssm = f(state_conv_a), f(state_conv_b), f(state_ssm)
    c_prompt, c_sample = f(c_prompt), f(c_sample)
    w_mod, b_mod, w_in, w_out = f(w_mod)[0], f(b_mod)[0], f(w_in)[0], f(w_out)[0]
    pf = np.zeros((128, NPF), np.float32)
    pf[:, PF_NIN:PF_NIN + 8] = _fm(f(norm_in_w)[0], 8)
    caw = f(conv_a_w)[0]
    pf[:, PF_CAW:PF_CAW + 24] = np.stack([_fm(caw[k], 8) for k in range(3)], axis=2).reshape(128, 24)
    pf[:, PF_NAW:PF_NAW + 8] = _fm(f(norm_a_w)[0], 8)
    cbw = f(conv_b_w)[0]
    pf[:, PF_CBW:PF_CBW + 64] = np.stack([_fm(cbw[k], 16) for k in range(4)], axis=2).reshape(128, 64)
    pf[:, PF_CBB:PF_CBB + 16] = _fm(f(conv_b_b)[0], 16)
    pf[:, PF_NBW:PF_NBW + 8] = _fm(f(norm_b_w)[0], 8)
    pf[:, PF_DSK:PF_DSK + 8] = _fm(np.repeat(f(d_skip)[0], 64), 8)
    pf[:, PF_BSH:PF_BSH + 8] = _fm(b_mod[0:D], 8)
    pf[:, PF_BSC:PF_BSC + 8] = _fm(b_mod[D:2 * D], 8)
    bcv = np.zeros((128, NBC), np.float32)
    bcv[:, BC_NF:BC_NF + D] = f(norm_f_w)[None, :]
    bcv[:, BC_BG:BC_BG + D] = b_mod[None, 2 * D:3 * D]
    bcv[:, BC_DTB:BC_DTB + 16] = f(dt_bias)[0][None, :]
    bcv[:, BC_ALOG:BC_ALOG + 16] = f(a_log)[0][None, :]
    cst = _host_consts()

    in_maps = []
    for k in range(NCORES):
        seq, seg = k // 4, k % 4
        start = seg * SEGLEN
        xs = np.zeros((XROWS, D), np.float32)
        if seg > 0:
            xs[0:128] = x_prompt[seq, start - 128:start]
        if seg > 0 and LROWS >= start:
            xs[128 + LROWS - start:128 + LROWS] = x_prompt[seq, 0:start]
        xs[128 + LROWS:128 + LROWS + SEGLEN] = x_prompt[seq, start:start + SEGLEN]
        xs[128 + LROWS + SEGLEN:] = x_sample[2 * k:2 * k + 2].reshape(128, D)
        cs = [c_prompt[seq], c_sample[2 * k], c_sample[2 * k + 1]]
        cT = np.stack([_fm(c, 8) for c in cs], axis=2).reshape(128, 24)
        cbc = np.zeros((2, 128, 8, 128), np.float32)
        cbc[0] = _fm(cs[0], 8)[:, :, None]
        cbc[1, :, :, 0:64] = _fm(cs[1], 8)[:, :, None]
        cbc[1, :, :, 64:128] = _fm(cs[2], 8)[:, :, None]
        sca = state_conv_a[0, 2 * k:2 * k + 2]
        sca = sca.reshape(2, 2, 8, 128).transpose(3, 2, 0, 1).reshape(128, 32)
        scb = state_conv_b[0, 2 * k:2 * k + 2]
        scb = scb.reshape(2, 3, 16, 128).transpose(3, 2, 0, 1).reshape(128, 96)
        sst = state_ssm[0, 2 * k:2 * k + 2]
        sst = sst.reshape(2, 1024, 128).transpose(0, 2, 1)
        msk = np.zeros((128, 16), np.float32)
        msk[:, 0] = 1.0 if seg > 0 else 0.0
        for m in range(NLSEG):
            msk[:, 1 + m] = 0.0 if m < NLSEG - seg else 1.0
        in_maps.append({
            "xs": xs, "w_mod": w_mod, "w_in": w_in, "w_out": w_out, "pf": pf, "bc": bcv, "cst": cst,
            "cT": np.ascontiguousarray(cT), "cbc": np.ascontiguousarray(cbc.reshape(2, 128, 1024)),
            "sca": np.ascontiguousarray(sca), "scb": np.ascontiguousarray(scb),
            "sst": np.ascontiguousarray(sst), "msk": msk,
        })
    key = (bool(_two_phase), bool(_debug))
    if key not in _NC_CACHE:
        _NC_CACHE[key] = build_nc(two_phase=key[0], debug=key[1])
    nc = _NC_CACHE[key]
    res = run_bass_kernel_spmd(nc, in_maps, core_ids=list(range(NCORES)))
    R = res.results
    if _debug:
        kernel.last_results = R
    y_prompt = np.zeros((2, SEQ, D), np.float32)
    y_sample = np.zeros((16, 64, D), np.float32)
    ca_p = np.zeros((1, 2, 2, D), np.float32)
    cb_p = np.zeros((1, 2, 3, 2 * D), np.float32)
    ss_p = np.zeros((1, 2, 16, 64, 128), np.float32)
    ca_s = np.zeros((1, 16, 2, D), np.float32)
    cb_s = np.zeros((1, 16, 3, 2 * D), np.float32)
    ss_s = np.zeros((1, 16, 16, 64, 128), np.float32)
    for k in range(NCORES):
        seq, seg = k // 4, k % 4
        r = R[k]
        y_prompt[seq, seg * SEGLEN:(seg + 1) * SEGLEN] = r["y"][0:SEGLEN]
        y_sample[2 * k:2 * k + 2] = r["y"][SEGLEN:].reshape(2, 64, D)
        ca = r["ca"].reshape(128, 8, 3, 2)
        cb = r["cb"].reshape(128, 16, 3, 3)
        so = r["so"]
        for s in range(2):
            ca_s[0, 2 * k + s] = ca[:, :, 1 + s, :].transpose(2, 1, 0).reshape(2, D)
            cb_s[0, 2 * k + s] = cb[:, :, 1 + s, :].transpose(2, 1, 0).reshape(3, 2 * D)
            ss_s[0, 2 * k + s] = so[1 + s].T.reshape(16, 64, 128)
        if seg == 3:
            ca_p[0, seq] = ca[:, :, 0, :].transpose(2, 1, 0).reshape(2, D)
            cb_p[0, seq] = cb[:, :, 0, :].transpose(2, 1, 0).reshape(3, 2 * D)
            ss_p[0, seq] = so[0].T.reshape(16, 64, 128)
    return (y_prompt, y_sample, ca_p, cb_p, ss_p, ca_s, cb_s, ss_s)
```

```python
import contextlib
import numpy as np
import concourse.bass as bass
import concourse.mybir as mybir
from concourse.bass_utils import run_bass_kernel_spmd

F32 = mybir.dt.float32
BF16 = mybir.dt.bfloat16
ALU = mybir.AluOpType
AF = mybir.ActivationFunctionType

NCORES = 8
D = 1024
SEQ = 16384
SEGLEN = 4096
T = 512
NT = SEGLEN // T
DIN = 7184
PW = 128
NPIECE = 56
NWB = 12
EPS = 1e-5
NLSEG = 3
LROWS = NLSEG * SEGLEN
XROWS = 128 + LROWS + SEGLEN + 128
YROWS = SEGLEN + 128

PF_NIN = 0
PF_CAW = 8
PF_NAW = 32
PF_CBW = 40
PF_CBB = 104
PF_NBW = 120
PF_DSK = 128
PF_BSH = 136
PF_BSC = 144
NPF = 152
BC_NF = 0
BC_BG = 1024
BC_DTB = 2048
BC_ALOG = 2064
NBC = 2080
C_ID = 0
C_BLK = 128
C_TRI = 256
C_T64 = 384
C_SEL0 = 448
C_SEL1 = 576
NCONST = 704


class Planner:
    ENGS = ("pe", "act", "dve", "pool", "sp")
    SEM_LIMIT = 30000

    def __init__(self, nc):
        self.nc = nc
        self.lists = {e: [] for e in self.ENGS}
        self.cur = {e: [e + "_0", 0] for e in self.ENGS}
        self.gen = {e: 0 for e in self.ENGS}
        self.sem_names = [e + "_0" for e in self.ENGS]
        self.dma_cnt = {}
        self.waited = {e: {} for e in self.ENGS}
        self.bufs = {}
        self.nins = 0
        self.alias = {}

    def _exp(self, keys):
        out = []
        for k in keys:
            out.extend(self.alias.get(k, (k,)))
        return out

    def _need(self, eng, tok, out):
        if tok is None:
            return
        s, v = tok
        if eng == "pe" and s.startswith("pe_"):
            return
        if self.waited[eng].get(s, 0) >= v:
            return
        if out.get(s, 0) < v:
            out[s] = v

    def _deps(self, eng, reads, writes):
        reads, writes = self._exp(reads), self._exp(writes)
        need = {}
        for k in reads:
            b = self.bufs.get(k)
            if b:
                self._need(eng, b[0], need)
        for k in writes:
            b = self.bufs.get(k)
            if b:
                self._need(eng, b[0], need)
                for t in b[1]:
                    self._need(eng, t, need)
        for s, v in need.items():
            self.waited[eng][s] = v
            self.lists[eng].append(("wait", s, v))

    def _mark(self, tok, reads, writes):
        reads, writes = self._exp(reads), self._exp(writes)
        for k in reads:
            b = self.bufs.setdefault(k, [None, []])
            b[1].append(tok)
        for k in writes:
            self.bufs[k] = [tok, []]

    def op(self, eng, fn, reads=(), writes=(), token=True):
        self._deps(eng, reads, writes)
        c = self.cur[eng]
        self.nins += 1
        if token:
            if c[1] >= self.SEM_LIMIT:
                self.gen[eng] += 1
                c[0] = "%s_%d" % (eng, self.gen[eng])
                c[1] = 0
                self.sem_names.append(c[0])
            c[1] += 1
            tok = (c[0], c[1])
            self.lists[eng].append(("ins", fn, c[0], 1))
        else:
            tok = (c[0], c[1] + 1)
            self.lists[eng].append(("ins", fn, None, 0))
        self._mark(tok, reads, writes)
        return tok

    def dma(self, eng, fn, sem, reads=(), writes=()):
        self._deps(eng, reads, writes)
        self.nins += 1
        if sem not in self.dma_cnt:
            self.dma_cnt[sem] = 0
            self.sem_names.append(sem)
        self.dma_cnt[sem] += 16
        tok = (sem, self.dma_cnt[sem])
        self.lists[eng].append(("ins", fn, sem, 16))
        self._mark(tok, reads, writes)
        return tok

    def raw(self, eng, fn, sem, inc, reads=(), writes=()):
        self._deps(eng, reads, writes)
        if sem not in self.dma_cnt:
            self.dma_cnt[sem] = 0
            self.sem_names.append(sem)
        self.dma_cnt[sem] += inc
        tok = (sem, self.dma_cnt[sem])
        self.lists[eng].append(("ins", fn, sem, -inc))
        self._mark(tok, reads, writes)
        return tok

    def wait_tokens(self, eng, toks):
        need = {}
        for t in toks:
            self._need(eng, t, need)
        for s, v in need.items():
            self.waited[eng][s] = v
            self.lists[eng].append(("wait", s, v))

    def emit(self):
        nc = self.nc
        with contextlib.ExitStack() as st:
            sems = {}
            for n in self.sem_names:
                sems[n] = st.enter_context(nc.semaphore(n))
            block = st.enter_context(nc.Block())
            engmap = {"pe": block.tensor, "act": block.scalar, "dve": block.vector,
                      "pool": block.gpsimd, "sp": block.sync}
            for e in self.ENGS:
                lst = self.lists[e]
                if not lst:
                    continue

                def body(engobj, lst=lst):
                    for it in lst:
                        if it[0] == "wait":
                            engobj.wait_ge(sems[it[1]], it[2])
                        else:
                            ins = it[1](engobj)
                            if it[2] is not None:
                                if it[3] < 0:
                                    ins.then_inc(sems[it[2]])
                                else:
                                    ins.then_inc(sems[it[2]], it[3])
                engmap[e](body)


def build_nc(two_phase=True, debug=False):
    nc = bass.Bass("TRN2", target_bir_lowering=False)
    dr = lambda n, s, k, d=F32: nc.dram_tensor(n, list(s), d, kind=k)
    xs_d = dr("xs", [XROWS, D], "ExternalInput").ap()
    wmod_d = dr("w_mod", [D, 3 * D], "ExternalInput").ap()
    win_d = dr("w_in", [D, DIN], "ExternalInput").ap()
    wout_d = dr("w_out", [2 * D, D], "ExternalInput").ap()
    pf_d = dr("pf", [128, NPF], "ExternalInput").ap()
    bc_d = dr("bc", [128, NBC], "ExternalInput").ap()
    cst_d = dr("cst", [128, NCONST], "ExternalInput").ap()
    cT_d = dr("cT", [128, 8 * 3], "ExternalInput").ap()
    cbc_d = dr("cbc", [2, 128, 8 * 128], "ExternalInput").ap()
    sca_d = dr("sca", [128, 8 * 2 * 2], "ExternalInput").ap()
    scb_d = dr("scb", [128, 16 * 2 * 3], "ExternalInput").ap()
    sst_d = dr("sst", [2, 128, D], "ExternalInput").ap()
    msk_d = dr("msk", [128, 16], "ExternalInput").ap()
    y_d = dr("y", [YROWS, D], "ExternalOutput").ap()
    ca_d = dr("ca", [128, 8 * 3 * 2], "ExternalOutput").ap()
    cb_d = dr("cb", [128, 16 * 3 * 3], "ExternalOutput").ap()
    so_d = dr("so", [3, 128, D], "ExternalOutput").ap()
    winbf_d = nc.dram_tensor("winbf", [NPIECE, 128, 8 * PW], BF16)

    pl = Planner(nc)
    pl.alias = {"stg0": ("rhs1", "segs", "eac", "yo"), "stg1": ("stsb", "o1", "ost0", "ost1"), "sqj": ("Mt",)}
    with contextlib.ExitStack() as st:
        def SB(name, shape, dt=F32):
            return st.enter_context(nc.sbuf_tensor("sb_" + name, list(shape), dt))

        cst = SB("cst", [128, NCONST])
        idb = SB("idb", [128, 128], BF16)
        pf = SB("pf", [128, NPF])
        bc = SB("bc", [128, NBC])
        msk = SB("msk", [128, 16])
        cT = SB("cT", [128, 8, 3])
        gam = SB("gam", [128, 3, 8])
        bet = SB("bet", [128, 3, 8])
        gate = SB("gate", [128, 2, D])
        caw = SB("caw", [128, 8, 3])
        cbw = SB("cbw", [128, 16, 4])
        cbb = SB("cbb", [128, 16])
        a_bc = SB("a_bc", [128, 16])
        wout = SB("wout", [128, 16, D], BF16)
        wdt = SB("wdt", [128, 8, 16], BF16)
        wbuf = [SB("wbuf%d" % i, [128, 8, PW], BF16) for i in range(NWB)]
        big = [SB("big%d" % i, [128, 4096]) for i in range(2)]
        stg = [b[:].rearrange("p (a c) -> p a c", a=8) for b in big]
        xt = [SB("xt%d" % i, [128, D]) for i in range(2)]
        hT = SB("hT", [128, 8, T], BF16)
        ubuf = [SB("ubuf%d" % i, [128, 3 + T]) for i in range(2)]
        hsb = [SB("hsb%d" % i, [128, T]) for i in range(2)]
        cu = [SB("cu%d" % i, [128, T]) for i in range(2)]
        tq = [SB("tq%d" % i, [128, T]) for i in range(2)]
        sq2 = [SB("sq2%d" % i, [128, T], BF16) for i in range(2)]
        uhist = SB("uhist", [128, 8, 3])
        xhist = SB("xhist", [128, 16, 3])
        uhist0 = SB("uhist0", [128, 8, 3])
        xhist0 = SB("xhist0", [128, 16, 3])
        ycat = SB("ycat", [128, 16, T], BF16)
        xbf = SB("xbf", [128, 8, T], BF16)
        BT = SB("BT", [128, 4, T], BF16)
        CT = SB("CT", [128, 4, T], BF16)
        sz = SB("sz", [128, 8, T], BF16)
        xdt = SB("xdt", [128, D], BF16)
        xw = SB("xw", [128, D], BF16)
        btok = SB("btok", [128, 512], BF16)
        rhs1 = big[0][:, 0:1024]
        segs = big[0][:, 1024:2048]
        Mt = SB("Mt", [128, D], BF16)
        sqj = Mt
        eac = big[0][:, 2048:3072]
        yo = big[0][:, 3072:4096]
        cbtm = SB("cbtm", [128, 4, 64])
        sm = SB("sm", [128, 8, 64])
        ST = SB("ST", [128, D])
        STb = SB("STb", [128, D], BF16)
        stsb = big[1][:, 0:1024]
        cd = SB("cd", [128, 2, 64])
        stat = SB("stat", [128, 64])
        ost = [big[1][:, 2048:3072], big[1][:, 3072:4096]]
        o1 = big[1][:, 1024:2048]
        casb = SB("casb", [128, 8, 3, 2])
        cbsb = SB("cbsb", [128, 16, 3, 3])
        atot = SB("atot", [128, 16])
        gsel = SB("gsel", [128, 8, 16])
        print("sbuf remaining after alloc:", nc.sbuf_bytes_remaining)

        psA = st.enter_context(nc.psum_tensor("psA", [128, 4, 512], F32))
        psB = st.enter_context(nc.psum_tensor("psB", [128, 2, 512], F32))
        psC = st.enter_context(nc.psum_tensor("psC", [128, 512], F32))
        psT = st.enter_context(nc.psum_tensor("psT", [128, 1024], BF16))

        ident = cst[:, C_ID:C_ID + 128]
        blk = cst[:, C_BLK:C_BLK + 128]
        tri = cst[:, C_TRI:C_TRI + 128]
        t64 = cst[:, C_T64:C_T64 + 64]
        chsel = [cst[:, C_SEL0:C_SEL0 + 128], cst[:, C_SEL1:C_SEL1 + 128]]

        def pfc(off, j):
            return pf[:, off + j:off + j + 1]

        ld = lambda name, dst, src, key: pl.dma("sp", lambda e: e.dma_start(out=dst, in_=src), name, writes=[key])
        ld("l_cst", cst[:], cst_d[:, :], "cst")
        ld("l_pf", pf[:], pf_d[:, :], "pf")
        ld("l_bc", bc[:], bc_d[:, :], "bc")
        ld("l_msk", msk[:], msk_d[:, :], "msk")
        ld("l_cT", cT[:].rearrange("p a b -> p (a b)"), cT_d[:, :], "cT")
        pl.op("dve", lambda e: e.tensor_copy(out=idb[:], in_=ident), reads=["cst"], writes=["idb"])
        pl.op("dve", lambda e: e.tensor_scalar_mul(out=caw[:].rearrange("p a b -> p (a b)"), in0=pf[:, PF_CAW:PF_CAW + 24], scalar1=1.0), reads=["pf"], writes=["caw"])
        pl.op("dve", lambda e: e.tensor_scalar_mul(out=cbw[:].rearrange("p a b -> p (a b)"), in0=pf[:, PF_CBW:PF_CBW + 64], scalar1=1.0), reads=["pf"], writes=["cbw"])
        pl.op("dve", lambda e: e.tensor_scalar_mul(out=cbb[:], in0=pf[:, PF_CBB:PF_CBB + 16], scalar1=1.0), reads=["pf"], writes=["cbb"])
        pl.op("act", lambda e: e.activation(out=a_bc[:], in_=bc[:, BC_ALOG:BC_ALOG + 16], func=AF.Exp), reads=["bc"], writes=["a_bc"])
        pl.op("dve", lambda e: e.tensor_scalar_mul(out=a_bc[:], in0=a_bc[:], scalar1=-1.0), reads=["a_bc"], writes=["a_bc"])

        wmod_v = wmod_d.rearrange("(kc p) c -> p kc c", p=128)
        for piece in range(6):
            s = stg[piece % 2]
            key = "stg%d" % (piece % 2)
            pl.dma("sp", lambda e, s=s, piece=piece: e.dma_start(out=s[:], in_=wmod_v[:, :, piece * 512:(piece + 1) * 512]), "l_" + key, writes=[key])
            if piece < 4:
                for cc in range(4):
                    j = (piece % 2) * 4 + cc
                    for kc in range(8):
                        pl.op("pe", lambda e, s=s, cc=cc, kc=kc, j=j: e.matmul(psC[:, j * 4:j * 4 + 3], lhsT=s[:, kc, cc * 128:(cc + 1) * 128], rhs=cT[:, kc, :], start=(kc == 0), stop=(kc == 7)),
                              reads=[key, "cT"], writes=["psC"], token=(kc == 7))
                if piece % 2 == 1:
                    src = psC[:, 0:32].rearrange("p (j s) -> p s j", s=4)[:, 0:3, :]
                    if piece == 1:
                        pl.op("dve", lambda e, src=src: e.tensor_tensor(out=bet[:], in0=src, in1=pf[:, PF_BSH:PF_BSH + 8].unsqueeze(1).broadcast_to([128, 3, 8]), op=ALU.add), reads=["psC", "pf"], writes=["bet"])
                    else:
                        pl.op("dve", lambda e, src=src: e.tensor_tensor(out=gam[:], in0=src, in1=pf[:, PF_BSC:PF_BSC + 8].unsqueeze(1).broadcast_to([128, 3, 8]), op=ALU.add), reads=["psC", "pf"], writes=["gam"])
                        pl.op("dve", lambda e: e.scalar_tensor_tensor(out=gam[:], in0=gam[:], scalar=1.0, in1=pf[:, PF_NIN:PF_NIN + 8].unsqueeze(1).broadcast_to([128, 3, 8]), op0=ALU.add, op1=ALU.mult), reads=["gam", "pf"], writes=["gam"])
            else:
                half = piece - 4
                for which in range(2):
                    cb_t = xt[which]
                    if half == 0:
                        pl.dma("sp", lambda e, cb_t=cb_t, which=which: e.dma_start(out=cb_t[:], in_=cbc_d[which, :, :]), "l_xt%d" % which, writes=["xt%d" % which])
                    cbv = cb_t[:].rearrange("p (k m) -> p k m", k=8)
                    for kc in range(8):
                        pl.op("pe", lambda e, s=s, kc=kc, cbv=cbv, which=which: e.matmul(psA[:, which, :], lhsT=cbv[:, kc, :], rhs=s[:, kc, :], start=(kc == 0), stop=(kc == 7)),
                              reads=[key, "xt%d" % which], writes=["psA%d" % which], token=(kc == 7))
                    pl.op("dve", lambda e, which=which, half=half: e.tensor_tensor(out=gate[:, which, half * 512:(half + 1) * 512], in0=psA[:, which, :], in1=bc[:, BC_BG + half * 512:BC_BG + (half + 1) * 512], op=ALU.add),
                          reads=["psA%d" % which, "bc"], writes=["gate"])

        win_v = win_d.rearrange("(kc p) c -> p kc c", p=128)
        wout_v = wout_d.rearrange("(kc p) c -> p kc c", p=128)
        castengs = ["dve", "act", "pool"]
        ci = 0

        def cast(dst, src, rk, wk):
            nonlocal ci
            eng = castengs[ci % 3]
            ci += 1
            if eng == "act":
                pl.op("act", lambda e: e.copy(out=dst, in_=src), reads=rk, writes=wk)
            else:
                pl.op(eng, lambda e: e.tensor_copy(out=dst, in_=src), reads=rk, writes=wk)

        porder = [10, 11, 12, 2, 3, 4, 5, 0, 1, 6, 7, 13, 8, 9]
        pl.dma("sp", lambda e: e.dma_start(out=stg[0][:, :, 0:16], in_=win_v[:, :, 7168:7184]), "l_stg0", writes=["stg0"])
        pl.op("dve", lambda e: e.tensor_copy(out=wdt[:], in_=stg[0][:, :, 0:16]), reads=["stg0"], writes=["wdt"])
        wci = 0
        for n, piece in enumerate(porder):
            s = stg[n % 2]
            key = "stg%d" % (n % 2)
            pl.dma("sp", lambda e, s=s, piece=piece: e.dma_start(out=s[:], in_=win_v[:, :, piece * 512:(piece + 1) * 512]), "l_" + key, writes=[key])
            for hp in range(512 // PW):
                wb = wbuf[wci % NWB]
                wkey = "wbuf%d" % (wci % NWB)
                wci += 1
                for kh in range(2):
                    cast(wb[:, kh * 4:(kh + 1) * 4, :], s[:, kh * 4:(kh + 1) * 4, hp * PW:(hp + 1) * PW], [key], [wkey])
                sp_ = (512 // PW) * piece + hp
                pl.dma("pool", lambda e, wb=wb, sp_=sp_: e.dma_start(out=winbf_d[sp_, :, :], in_=wb[:].rearrange("p a b -> p (a b)")), "s_" + wkey, reads=[wkey], writes=["winbf%d" % sp_])
        for n in range(4):
            kg, ch = n // 2, n % 2
            s = stg[n % 2]
            key = "stg%d" % (n % 2)
            pl.dma("sp", lambda e, s=s, kg=kg, ch=ch: e.dma_start(out=s[:], in_=wout_v[:, kg * 8:(kg + 1) * 8, ch * 512:(ch + 1) * 512]), "l_" + key, writes=[key])
            for kh in range(2):
                cast(wout[:, kg * 8 + kh * 4:kg * 8 + (kh + 1) * 4, ch * 512:(ch + 1) * 512], s[:, kh * 4:(kh + 1) * 4, :], [key], ["wout"])

        wcnt = [0]

        def load_piece(piece):
            i = wcnt[0] % NWB
            wcnt[0] += 1
            wb = wbuf[i]
            pl.dma("sp", lambda e: e.dma_start(out=wb[:].rearrange("p a b -> p (a b)"), in_=winbf_d[piece, :, :]), "l_wbuf%d" % i, reads=["winbf%d" % piece], writes=["wbuf%d" % i])
            return wb, "wbuf%d" % i

        def proj_chunk(wb, wkey, cc, ps_ap, pskey, Tn):
            for kc in range(8):
                pl.op("pe", lambda e, kc=kc: e.matmul(ps_ap, lhsT=wb[:, kc, cc * 128:(cc + 1) * 128], rhs=hT[:, kc, 0:Tn], start=(kc == 0), stop=(kc == 7)),
                      reads=[wkey, "hT"], writes=[pskey], token=(kc == 7))

        def step_a(row0, Tn, slots):
            nsub = Tn // 128
            for p0 in range(0, nsub, 2):
                subs = list(range(p0, min(p0 + 2, nsub)))
                for s in subs:
                    xb = xt[s % 2]
                    xk = "xt%d" % (s % 2)
                    pl.dma("sp", lambda e, xb=xb, s=s: e.dma_start(out=xb[:], in_=xs_d[row0 + s * 128:row0 + (s + 1) * 128, :]), "l_" + xk, writes=[xk])
                    pl.op("pool", lambda e, s=s: e.memset(stat[:, s:s + 1], 0.0), writes=["stat"])
                    pl.op("act", lambda e, xb=xb, s=s: e.activation(out=sqj[:], in_=xb[:], func=AF.Square, accum_out=stat[:, s:s + 1]), reads=[xk, "stat"], writes=["sqj", "stat"])
                    pl.op("act", lambda e, s=s: e.activation(out=stat[:, 8 + s:9 + s], in_=stat[:, s:s + 1], func=AF.Ln, scale=1.0 / D, bias=EPS), reads=["stat"], writes=["stat"])
                    pl.op("act", lambda e, s=s: e.activation(out=stat[:, 16 + s:17 + s], in_=stat[:, 8 + s:9 + s], func=AF.Exp, scale=-0.5), reads=["stat"], writes=["stat"])
                    pl.op("dve", lambda e, xb=xb, s=s: e.tensor_scalar_mul(out=xb[:], in0=xb[:], scalar1=stat[:, 16 + s:17 + s]), reads=[xk, "stat"], writes=[xk])
                w0, w1 = subs[0] * 128, (subs[-1] + 1) * 128
                for kc in range(8):
                    pk = "psB%d" % (kc % 2)
                    for s in subs:
                        xb = xt[s % 2]
                        xk = "xt%d" % (s % 2)
                        pl.op("pe", lambda e, xb=xb, kc=kc, s=s, p0=p0: e.transpose(out=psB[:, kc % 2, (s - p0) * 128:(s - p0 + 1) * 128], in_=xb[:, kc * 128:(kc + 1) * 128], identity=ident), reads=[xk, "cst"], writes=[pk], token=(s == subs[-1]))
                    for (c0, c1, slot) in slots:
                        lo, hi = max(c0, w0), min(c1, w1)
                        if lo >= hi:
                            continue
                        pl.op("act", lambda e, kc=kc, lo=lo, hi=hi, slot=slot, w0=w0: e.activation(out=hT[:, kc, lo:hi], in_=psB[:, kc % 2, lo - w0:hi - w0], func=AF.Identity, scale=gam[:, slot, kc:kc + 1], bias=bet[:, slot, kc:kc + 1]),
                              reads=[pk, "gam", "bet"], writes=["hT"])

        def conv_chunk(eng_first, src, wts, nk, dst, Tn, rk, wk, bias=None, bk=()):
            off = 3 - (nk - 1)
            if bias is None:
                pl.op("act", lambda e: e.activation(out=dst[:, 0:Tn], in_=src[:, off:off + Tn], func=AF.Copy, scale=wts[:, 0:1]), reads=rk, writes=wk)
            else:
                pl.op("act", lambda e: e.activation(out=dst[:, 0:Tn], in_=src[:, off:off + Tn], func=AF.Identity, scale=wts[:, 0:1], bias=bias), reads=rk + list(bk), writes=wk)
            for k in range(1, nk):
                pl.op("dve", lambda e, k=k: e.scalar_tensor_tensor(out=dst[:, 0:Tn], in0=src[:, off + k:off + k + Tn], scalar=wts[:, k:k + 1], in1=dst[:, 0:Tn], op0=ALU.mult, op1=ALU.add), reads=rk + wk, writes=wk)

        def tile(row0, Tn, slots, mode, gidx, yrow0, hist_from, st_mode, out_slot):
            nsub = Tn // 128
            nch = Tn // 64
            step_a(row0, Tn, slots)
            light = (mode != "full")
            if mode in ("full", "halo"):
                wbs = {}
                pendA = None
                for j in range(8):
                    i = j % 2
                    need = [8 + j, 16 + j] if mode == "halo" else [8 + j, 16 + j, 0 + j, 24 + j]
                    for pc in need:
                        wbs[pc] = load_piece(pc)
                    cc = 0
                    wb, wk_ = wbs[8 + j]
                    proj_chunk(wb, wk_, cc, psA[:, 0, 0:Tn], "psA0", Tn)
                    wb, wk_ = wbs[16 + j]
                    proj_chunk(wb, wk_, cc, psA[:, 1, 0:Tn], "psA1", Tn)
                    ub, uk = ubuf[i], "ubuf%d" % i
                    pl.op("act", lambda e, i=i: e.copy(out=hsb[i][:, 0:Tn], in_=psA[:, 1, 0:Tn]), reads=["psA1"], writes=["hsb%d" % i])
                    pl.op("pool", lambda e, ub=ub, j=j: e.tensor_copy(out=ub[:, 0:3], in_=uhist[:, j, :]), reads=["uhist"], writes=[uk])
                    pl.op("dve", lambda e, ub=ub, i=i: e.tensor_tensor(out=ub[:, 3:3 + Tn], in0=psA[:, 0, 0:Tn], in1=hsb[i][:, 0:Tn], op=ALU.mult), reads=["psA0", "hsb%d" % i], writes=[uk])
                    pl.op("pool", lambda e, ub=ub, j=j: e.tensor_copy(out=uhist[:, j, :], in_=ub[:, Tn:Tn + 3]), reads=[uk], writes=["uhist"])
                    if mode == "halo":
                        continue
                    wb, wk_ = wbs[0 + j]
                    proj_chunk(wb, wk_, cc, psA[:, 2, 0:Tn], "psA2", Tn)
                    wb, wk_ = wbs[24 + j]
                    proj_chunk(wb, wk_, cc, psA[:, 3, 0:Tn], "psA3", Tn)
                    pl.op("act", lambda e, i=i: e.activation(out=tq[i][:, 0:Tn], in_=psA[:, 3, 0:Tn], func=AF.Silu), reads=["psA3"], writes=["tq%d" % i])
                    pl.op("dve", lambda e, i=i: e.tensor_tensor(out=tq[i][:, 0:Tn], in0=psA[:, 2, 0:Tn], in1=tq[i][:, 0:Tn], op=ALU.mult), reads=["psA2", "tq%d" % i], writes=["tq%d" % i])
                    c_, ck = cu[i], "cu%d" % i
                    if hist_from == "carry":
                        conv_chunk("dve", ub, caw[:, j, :], 3, c_, Tn, [uk, "caw"], [ck])
                    else:
                        for sidx in range(2):
                            pl.op("dve", lambda e, ub=ub, j=j, sidx=sidx: e.tensor_copy(out=stsb[:, sidx * 128 + 1:sidx * 128 + 3], in_=sca_sb[:, j, sidx, :]), reads=["sca"], writes=["stsb"])
                        for sidx in range(2):
                            base = sidx * 128
                            pl.op("dve", lambda e, ub=ub, base=base, sidx=sidx: e.tensor_copy(out=stsb[:, base + 3:base + 67], in_=ub[:, 3 + sidx * 64:3 + (sidx + 1) * 64]), reads=[uk], writes=["stsb"])
                            off = 1
                            pl.op("dve", lambda e, c_=c_, base=base, sidx=sidx, j=j: e.tensor_scalar_mul(out=c_[:, sidx * 64:(sidx + 1) * 64], in0=stsb[:, base + 1:base + 65], scalar1=caw[:, j, 0:1]), reads=["stsb", "caw"], writes=[ck])
                            for k in (1, 2):
                                pl.op("dve", lambda e, c_=c_, base=base, sidx=sidx, j=j, k=k: e.scalar_tensor_tensor(out=c_[:, sidx * 64:(sidx + 1) * 64], in0=stsb[:, base + 1 + k:base + 65 + k], scalar=caw[:, j, k:k + 1], in1=c_[:, sidx * 64:(sidx + 1) * 64], op0=ALU.mult, op1=ALU.add), reads=["stsb", "caw", ck], writes=[ck])
                            pl.op("dve", lambda e, base=base, sidx=sidx, j=j: e.tensor_copy(out=casb[:, j, 1 + sidx, :], in_=stsb[:, base + 65:base + 67]), reads=["stsb"], writes=["casb"])
                    def stage_b(c_=c_, ck=ck, i=i, j=j):
                        pl.op("dve", lambda e: e.tensor_tensor(out=c_[:, 0:Tn], in0=c_[:, 0:Tn], in1=tq[i][:, 0:Tn], op=ALU.mult), reads=[ck, "tq%d" % i], writes=[ck])
                        pl.op("act", lambda e: e.activation(out=sq2[i][:, 0:Tn], in_=c_[:, 0:Tn], func=AF.Square), reads=[ck], writes=["sq2%d" % i])
                        pl.op("act", lambda e: e.activation(out=ycat[:, j, 0:Tn], in_=c_[:, 0:Tn], func=AF.Copy, scale=pfc(PF_NAW, j)), reads=[ck, "pf"], writes=["ycat"])
                        for s in range(nsub):
                            pl.op("pe", lambda e, s=s: e.matmul(psC[:, 320 + s:321 + s], lhsT=sq2[i][:, s * 128:(s + 1) * 128], rhs=ones_bf[:, 0:1], start=(j == 0 and s == 0), stop=(j == 7), skip_group_check=True),
                                  reads=["sq2%d" % i, "ones"], writes=["psCa"], token=(s == nsub - 1))
                    if pendA is not None:
                        pendA()
                    pendA = stage_b
                if pendA is not None:
                    pendA()
                if mode == "full" and hist_from == "carry":
                    pl.op("dve", lambda e: e.tensor_copy(out=casb[:, :, 0, :], in_=uhist[:, :, 1:3]), reads=["uhist"], writes=["casb"])
                if mode == "halo":
                    pl.op("dve", lambda e: e.tensor_scalar_mul(out=uhist[:].rearrange("p a b -> p (a b)"), in0=uhist[:].rearrange("p a b -> p (a b)"), scalar1=msk[:, 0:1]), reads=["uhist", "msk"], writes=["uhist"])
                    wbs = {}
                    for j in range(12, 16):
                        wb, wk_ = load_piece(40 + j)
                        pa = psA[:, j % 4, 0:Tn]
                        pk = "psA%d" % (j % 4)
                        proj_chunk(wb, wk_, 0, pa, pk, Tn)
                        pl.op("act", lambda e, j=j, pa=pa: e.copy(out=xhist[:, j, :], in_=pa[:, Tn - 3:Tn]), reads=[pk], writes=["xhist"])
                    pl.op("dve", lambda e: e.tensor_scalar_mul(out=xhist[:, 12:16, :], in0=xhist[:, 12:16, :], scalar1=msk[:, 0:1]), reads=["xhist", "msk"], writes=["xhist"])
                    return

            wbs = {}
            pending = None
            for j in range(16):
                if mode == "light" and j >= 12:
                    break
                wb, wk_ = load_piece(40 + j)
                i = j % 2
                pa = psA[:, j % 4, 0:Tn]
                pk = "psA%d" % (j % 4)
                proj_chunk(wb, wk_, 0, pa, pk, Tn)
                ub, uk = ubuf[i], "ubuf%d" % i
                pl.op("pool", lambda e, ub=ub, j=j: e.tensor_copy(out=ub[:, 0:3], in_=xhist[:, j, :]), reads=["xhist"], writes=[uk])
                pl.op("act", lambda e, ub=ub, pa=pa: e.copy(out=ub[:, 3:3 + Tn], in_=pa), reads=[pk], writes=[uk])
                pl.op("pool", lambda e, ub=ub, j=j: e.tensor_copy(out=xhist[:, j, :], in_=ub[:, Tn:Tn + 3]), reads=[uk], writes=["xhist"])
                c_, ck = cu[i], "cu%d" % i
                if hist_from == "carry":
                    conv_chunk("act", ub, cbw[:, j, :], 4, c_, Tn, [uk, "cbw"], [ck], bias=cbb[:, j:j + 1], bk=["cbb"])
                else:
                    for sidx in range(2):
                        base = sidx * 128
                        pl.op("dve", lambda e, j=j, sidx=sidx, base=base: e.tensor_copy(out=stsb[:, base:base + 3], in_=scb_sb[:, j, sidx, :]), reads=["scb"], writes=["stsb"])
                        pl.op("dve", lambda e, ub=ub, base=base, sidx=sidx: e.tensor_copy(out=stsb[:, base + 3:base + 67], in_=ub[:, 3 + sidx * 64:3 + (sidx + 1) * 64]), reads=[uk], writes=["stsb"])
                        pl.op("dve", lambda e, c_=c_, base=base, sidx=sidx, j=j: e.tensor_scalar_mul(out=c_[:, sidx * 64:(sidx + 1) * 64], in0=stsb[:, base:base + 64], scalar1=cbw[:, j, 0:1]), reads=["stsb", "cbw"], writes=[ck])
                        for k in (1, 2, 3):
                            pl.op("dve", lambda e, c_=c_, base=base, sidx=sidx, j=j, k=k: e.scalar_tensor_tensor(out=c_[:, sidx * 64:(sidx + 1) * 64], in0=stsb[:, base + k:base + 64 + k], scalar=cbw[:, j, k:k + 1], in1=c_[:, sidx * 64:(sidx + 1) * 64], op0=ALU.mult, op1=ALU.add), reads=["stsb", "cbw", ck], writes=[ck])
                        pl.op("dve", lambda e, base=base, sidx=sidx, j=j: e.tensor_copy(out=cbsb[:, j, 1 + sidx, :], in_=stsb[:, base + 64:base + 67]), reads=["stsb"], writes=["cbsb"])
                    pl.op("dve", lambda e, c_=c_, j=j: e.tensor_scalar_add(out=c_[:, 0:Tn], in0=c_[:, 0:Tn], scalar1=cbb[:, j:j + 1]), reads=[ck, "cbb"], writes=[ck])
                if j < 8:
                    dst, dk = xbf[:, j, 0:Tn], "xbf"
                elif j < 12:
                    dst, dk = BT[:, j - 8, 0:Tn], "BT"
                else:
                    dst, dk = CT[:, j - 12, 0:Tn], "CT"

                def stage_b(c_=c_, ck=ck, i=i, dst=dst, dk=dk):
                    pl.op("act", lambda e: e.activation(out=dst, in_=c_[:, 0:Tn], func=AF.Silu), reads=[ck], writes=[dk])
                if pending is not None:
                    pending()
                pending = stage_b
            if pending is not None:
                pending()
            if mode == "full" and hist_from == "carry":
                pl.op("dve", lambda e: e.tensor_copy(out=cbsb[:, :, 0, :], in_=xhist[:]), reads=["xhist"], writes=["cbsb"])
            if mode == "full":
                wbs = {}
                for j in range(8):
                    wb, wk_ = load_piece(32 + j)
                    pa = psA[:, j % 4, 0:Tn]
                    pk = "psA%d" % (j % 4)
                    i = j % 2
                    proj_chunk(wb, wk_, 0, pa, pk, Tn)
                    pl.op("act", lambda e, pa=pa, j=j: e.activation(out=sz[:, j, 0:Tn], in_=pa, func=AF.Silu), reads=[pk], writes=["sz"])

            W = nsub * 16
            SMA = lambda idx: sm[:, idx, 0:W]
            v3 = lambda ap: ap.rearrange("p (b h) -> p b h", h=16)
            for b in range(nsub):
                tsl = slice(b * 128, (b + 1) * 128)
                for kc in range(8):
                    pl.op("pe", lambda e, kc=kc, tsl=tsl, b=b: e.matmul(psC[:, 256 + b * 16:272 + b * 16], lhsT=hT[:, kc, tsl], rhs=wdt[:, kc, :], start=(kc == 0), stop=(kc == 7)), reads=["hT", "wdt"], writes=["psCd"], token=(kc == 7))
            pl.op("dve", lambda e: e.tensor_tensor(out=v3(SMA(0)), in0=v3(psC[:, 256:256 + W]), in1=bc[:, BC_DTB:BC_DTB + 16].unsqueeze(1).broadcast_to([128, nsub, 16]), op=ALU.add), reads=["psCd", "bc"], writes=["sm0"])
            pl.op("act", lambda e: e.activation(out=SMA(1), in_=SMA(0), func=AF.Abs), reads=["sm0"], writes=["sm1"])
            pl.op("act", lambda e: e.activation(out=SMA(1), in_=SMA(1), func=AF.Exp, scale=-1.0), reads=["sm1"], writes=["sm1"])
            pl.op("act", lambda e: e.activation(out=SMA(1), in_=SMA(1), func=AF.Ln, bias=1.0), reads=["sm1"], writes=["sm1"])
            pl.op("dve", lambda e: e.scalar_tensor_tensor(out=SMA(2), in0=SMA(0), scalar=0.0, in1=SMA(1), op0=ALU.max, op1=ALU.add), reads=["sm0", "sm1"], writes=["sm2"])
            pl.op("dve", lambda e: e.tensor_tensor(out=v3(SMA(3)), in0=v3(SMA(2)), in1=a_bc[:].unsqueeze(1).broadcast_to([128, nsub, 16]), op=ALU.mult), reads=["sm2", "a_bc"], writes=["sm3"])
            pl.op("pe", lambda e: e.matmul(psB[:, 0, 0:W], lhsT=tri, rhs=SMA(3), start=True, stop=True), reads=["cst", "sm3"], writes=["psB0"], token=False)
            pl.op("pe", lambda e: e.matmul(psB[:, 0, 64:64 + W], lhsT=blk, rhs=SMA(3), start=True, stop=True), reads=["cst", "sm3"], writes=["psB0"], token=False)
            pl.op("pe", lambda e: e.matmul(psB[:, 0, 128:128 + W], lhsT=chsel[0], rhs=SMA(3), start=True, stop=True), reads=["cst", "sm3"], writes=["psB0"], token=False)
            pl.op("pe", lambda e: e.matmul(psB[:, 0, 192:192 + W], lhsT=chsel[1], rhs=SMA(3), start=True, stop=True), reads=["cst", "sm3"], writes=["psB0"])
            pl.op("dve", lambda e: e.tensor_copy(out=SMA(4), in_=psB[:, 0, 0:W]), reads=["psB0"], writes=["sm4"])
            pl.op("dve", lambda e: e.tensor_tensor(out=SMA(5), in0=psB[:, 0, 64:64 + W], in1=SMA(4), op=ALU.subtract), reads=["psB0", "sm4"], writes=["sm5"])
            pl.op("act", lambda e: e.activation(out=SMA(5), in_=SMA(5), func=AF.Exp), reads=["sm5"], writes=["sm5"])
            pl.op("dve", lambda e: e.tensor_tensor(out=SMA(6), in0=SMA(5), in1=SMA(2), op=ALU.mult), reads=["sm5", "sm2"], writes=["sm6"])
            pl.op("act", lambda e: e.activation(out=cd[:, 0, 0:W], in_=psB[:, 0, 128:128 + W], func=AF.Exp), reads=["psB0"], writes=["cd"])
            pl.op("act", lambda e: e.activation(out=cd[:, 1, 0:W], in_=psB[:, 0, 192:192 + W], func=AF.Exp), reads=["psB0"], writes=["cd"])
            for b in range(nsub):
                tsl = slice(b * 128, (b + 1) * 128)
                SM = lambda idx, b=b: sm[:, idx, b * 16:(b + 1) * 16]
                bc16 = lambda idx, SM=SM: SM(idx).unsqueeze(2).broadcast_to([128, 16, 64])
                b16_2, b16_3, b16_4, b16_6 = bc16(2), bc16(3), bc16(4), bc16(6)
                for j in range(8):
                    pl.op("pe", lambda e, j=j, tsl=tsl: e.transpose(out=psT[:, j * 128:(j + 1) * 128], in_=xbf[:, j, tsl], identity=idb[:]), reads=["xbf", "idb"], writes=["psT"], token=(j == 7))
                bview = lambda ap: ap.rearrange("p (h q) -> p h q", h=16)
                if mode == "full":
                    pl.op("dve", lambda e, b16_2=b16_2: e.tensor_tensor(out=bview(xdt[:]), in0=bview(psT[:, :]), in1=b16_2, op=ALU.mult), reads=["psT", "sm2"], writes=["xdt"])
                pl.op("dve", lambda e, b16_6=b16_6: e.tensor_tensor(out=bview(xw[:]), in0=bview(psT[:, :]), in1=b16_6, op=ALU.mult), reads=["psT", "sm6"], writes=["xw"])
                for g in range(4):
                    pl.op("pe", lambda e, g=g, tsl=tsl: e.transpose(out=psT[:, g * 128:(g + 1) * 128], in_=BT[:, g, tsl], identity=idb[:]), reads=["BT", "idb"], writes=["psT"], token=(g == 3))
                pl.op("act", lambda e: e.copy(out=btok[:], in_=psT[:, 0:512]), reads=["psT"], writes=["btok"])

                if mode == "full":
                    for g in range(4):
                        for c in range(2):
                            cs = slice(b * 128 + c * 64, b * 128 + (c + 1) * 64)
                            pl.op("pe", lambda e, g=g, c=c, cs=cs: e.matmul(psC[c * 64:(c + 1) * 64, g * 64:(g + 1) * 64], lhsT=BT[:, g, cs], rhs=CT[:, g, cs], start=True, stop=True), reads=["BT", "CT"], writes=["psCb"], token=(g == 3 and c == 1))
                    pl.op("dve", lambda e: e.tensor_tensor(out=cbtm[:], in0=psC[:, 0:256].rearrange("p (g i) -> p g i", g=4), in1=t64.unsqueeze(1).broadcast_to([128, 4, 64]), op=ALU.mult), reads=["psCb", "cst"], writes=["cbtm"])
                    pl.op("dve", lambda e, b16_3=b16_3: e.tensor_tensor(out=bview(rhs1[:]), in0=b16_3, in1=t64.unsqueeze(1).broadcast_to([128, 16, 64]), op=ALU.mult), reads=["sm3", "cst"], writes=["rhs1"])
                    for hh in range(2):
                        pl.op("pe", lambda e, hh=hh: e.matmul(psA[:, hh, :], lhsT=blk, rhs=rhs1[:, hh * 512:(hh + 1) * 512], start=True, stop=True), reads=["cst", "rhs1"], writes=["psA%d" % hh])
                    pl.op("dve", lambda e, b16_4=b16_4: e.tensor_tensor(out=bview(segs[:]), in0=psA[:, 0:2, :].rearrange("p a (h q) -> p (a h) q", q=64), in1=b16_4, op=ALU.subtract), reads=["psA0", "psA1", "sm4"], writes=["segs"])
                    pl.op("act", lambda e: e.activation(out=segs[:], in_=segs[:], func=AF.Exp), reads=["segs"], writes=["segs"])
                    for g in range(4):
                        pl.op("dve", lambda e, g=g: e.scalar_tensor_tensor(out=Mt[:, g * 256:(g + 1) * 256].rearrange("p (r q) -> p r q", r=4), in0=segs[:, g * 256:(g + 1) * 256].rearrange("p (r q) -> p r q", r=4), scalar=1.0,
                                                                           in1=cbtm[:, g, :].unsqueeze(1).broadcast_to([128, 4, 64]), op0=ALU.min, op1=ALU.mult), reads=["segs", "cbtm"], writes=["Mt"])
                    pl.op("dve", lambda e, b16_3=b16_3: e.tensor_copy(out=bview(rhs1[:]), in_=b16_3), reads=["sm3"], writes=["rhs1"])
                    for a in range(8):
                        pl.op("pe", lambda e, a=a: e.matmul(psA[:, 2 + a // 4, (a % 4) * 128:(a % 4 + 1) * 128], lhsT=rhs1[:, a * 128:(a + 1) * 128], rhs=tri, start=True, stop=True), reads=["rhs1", "cst"], writes=["psA%d" % (2 + a // 4)], token=(a % 4 == 3))
                    pl.op("act", lambda e: e.activation(out=eac[:], in_=psA[:, 2:4, :].rearrange("p a q -> p (a q)"), func=AF.Exp), reads=["psA2", "psA3"], writes=["eac"])

                for c in range(2):
                    ch = b * 2 + c
                    cs = slice(b * 128 + c * 64, b * 128 + (c + 1) * 64)
                    ps_ = slice(c * 64, (c + 1) * 64)
                    if st_mode[ch] is not None:
                        kind, val = st_mode[ch]
                        if kind == "zero":
                            pl.op("dve", lambda e: e.memset(ST[:], 0.0), writes=["ST"])
                        elif kind == "load":
                            pl.dma("sp", lambda e, val=val: e.dma_start(out=ST[:], in_=sst_d[val, :, :]), "l_ST", writes=["ST"])
                        elif kind == "keep":
                            pass
                        if mode == "full":
                            pl.op("act", lambda e: e.copy(out=STb[:], in_=ST[:]), reads=["ST"], writes=["STb"])
                    pl.op("pool", lambda e, c=c, b=b: e.tensor_tensor(out=bview(ST[:]), in0=bview(ST[:]), in1=cd[:, c, b * 16:(b + 1) * 16].unsqueeze(2).broadcast_to([128, 16, 64]), op=ALU.mult), reads=["ST", "cd"], writes=["ST"])
                    if mode == "full":
                        for a in range(8):
                            for hh in range(2):
                                h = 2 * a + hh
                                pl.op("pe", lambda e, a=a, hh=hh, h=h, c=c, ps_=ps_: e.matmul(psB[hh * 64:(hh + 1) * 64, a // 4, (a % 4) * 128 + c * 64:(a % 4) * 128 + (c + 1) * 64], lhsT=xdt[ps_, h * 64:(h + 1) * 64], rhs=Mt[ps_, h * 64:(h + 1) * 64], start=True, stop=True),
                                      reads=["xdt", "Mt"], writes=["psB%d" % (a // 4)], token=(hh == 1 and a % 4 == 3))
                        for a in range(8):
                            g = a // 2
                            pl.op("pe", lambda e, a=a, g=g, c=c, cs=cs: e.matmul(psA[:, a // 4, (a % 4) * 128 + c * 64:(a % 4) * 128 + (c + 1) * 64], lhsT=STb[:, a * 128:(a + 1) * 128], rhs=CT[:, g, cs], start=True, stop=True),
                                  reads=["STb", "CT"], writes=["psA%d" % (a // 4)], token=(a % 4 == 3))
                    for g in range(4):
                        pl.op("pe", lambda e, g=g, ps_=ps_: e.matmul(psA[:, 2 + g // 2, (g % 2) * 256:(g % 2 + 1) * 256], lhsT=btok[ps_, g * 128:(g + 1) * 128], rhs=xw[ps_, g * 256:(g + 1) * 256], start=True, stop=True),
                              reads=["btok", "xw"], writes=["psA%d" % (2 + g // 2)], token=(g % 2 == 1))
                    pl.op("dve", lambda e: e.tensor_tensor(out=ST[:], in0=ST[:], in1=psA[:, 2:4, :].rearrange("p a q -> p (a q)"), op=ALU.add), reads=["ST", "psA2", "psA3"], writes=["ST"])
                    if mode == "full":
                        pl.op("act", lambda e: e.copy(out=STb[:], in_=ST[:]), reads=["ST"], writes=["STb"])
                        if hist_from != "carry":
                            pl.dma("pool", lambda e, ch=ch: e.dma_start(out=so_d[1 + ch, :, :], in_=ST[:]), "s_ST", reads=["ST"], writes=["so%d" % (1 + ch)])
                if mode == "full":
                    pl.op("dve", lambda e: e.tensor_tensor(out=yo[:], in0=psA[:, 0:2, :].rearrange("p a q -> p (a q)"), in1=eac[:], op=ALU.mult), reads=["psA0", "psA1", "eac"], writes=["yo"])
                    pl.op("dve", lambda e: e.tensor_tensor(out=yo[:], in0=psB[:, :, :].rearrange("p a q -> p (a q)"), in1=yo[:], op=ALU.add), reads=["psB0", "psB1", "yo"], writes=["yo"])
                    for a in range(8):
                        ya = yo[:, a * 128:(a + 1) * 128]
                        pl.op("dve", lambda e, a=a, ya=ya, tsl=tsl: e.scalar_tensor_tensor(out=ya, in0=xbf[:, a, tsl], scalar=pfc(PF_DSK, a), in1=ya, op0=ALU.mult, op1=ALU.add), reads=["xbf", "pf", "yo"], writes=["yo"])
                        pl.op("dve", lambda e, a=a, ya=ya, tsl=tsl: e.tensor_tensor(out=ya, in0=ya, in1=sz[:, a, tsl], op=ALU.mult), reads=["yo", "sz"], writes=["yo"])
                    pl.op("act", lambda e: e.activation(out=Mt[:], in_=yo[:], func=AF.Square), reads=["yo"], writes=["Mt"])
                    for a in range(8):
                        pl.op("act", lambda e, a=a, tsl=tsl: e.activation(out=ycat[:, 8 + a, tsl], in_=yo[:, a * 128:(a + 1) * 128], func=AF.Copy, scale=pfc(PF_NBW, a)), reads=["yo", "pf"], writes=["ycat"])
                        pl.op("pe", lambda e, a=a, b=b: e.matmul(psC[:, 324 + b:325 + b], lhsT=Mt[:, a * 128:(a + 1) * 128], rhs=ones_bf[:, 0:1], start=(a == 0), stop=(a == 7)), reads=["Mt", "ones"], writes=["psCs"], token=(a == 7))

            if mode != "full":
                return
            pl.op("act", lambda e: e.activation(out=stat[:, 24:32], in_=psC[:, 320:328], func=AF.Ln, scale=1.0 / D, bias=EPS), reads=["psCa", "psCs"], writes=["stat"])
            pl.op("act", lambda e: e.activation(out=stat[:, 24:32], in_=stat[:, 24:32], func=AF.Exp, scale=-0.5), reads=["stat"], writes=["stat"])
            for s in range(nsub):
                tsl = slice(s * 128, (s + 1) * 128)
                for hf in range(2):
                    for part in range(2):
                        for kc in range(8):
                            pl.op("pe", lambda e, hf=hf, part=part, kc=kc, tsl=tsl: e.matmul(psA[:, part * 2 + hf, :], lhsT=ycat[:, part * 8 + kc, tsl], rhs=wout[:, part * 8 + kc, hf * 512:(hf + 1) * 512], start=(kc == 0), stop=(kc == 7)),
                                  reads=["ycat", "wout"], writes=["psA%d" % (part * 2 + hf)], token=(kc == 7))
                xb = xt[s % 2]
                xk = "xt%d" % (s % 2)
                pl.dma("sp", lambda e, xb=xb, s=s: e.dma_start(out=xb[:], in_=xs_d[row0 + s * 128:row0 + (s + 1) * 128, :]), "l_" + xk, writes=[xk])
                ob = ost[s % 2]
                ok = "ost%d" % (s % 2)
                pl.op("act", lambda e, s=s: e.activation(out=o1[:], in_=psA[:, 0:2, :].rearrange("p a q -> p (a q)"), func=AF.Copy, scale=stat[:, 24 + s:25 + s]), reads=["psA0", "psA1", "stat"], writes=["o1"])
                pl.op("dve", lambda e, s=s: e.scalar_tensor_tensor(out=o1[:], in0=psA[:, 2:4, :].rearrange("p a q -> p (a q)"), scalar=stat[:, 28 + s:29 + s], in1=o1[:], op0=ALU.mult, op1=ALU.add), reads=["psA2", "psA3", "stat", "o1"], writes=["o1"])
                pl.op("dve", lambda e: e.tensor_tensor(out=o1[:], in0=o1[:], in1=gate[:, gidx, :], op=ALU.mult), reads=["o1", "gate"], writes=["o1"])
                pl.op("dve", lambda e, xb=xb: e.tensor_tensor(out=o1[:], in0=o1[:], in1=xb[:], op=ALU.add), reads=["o1", xk], writes=["o1"])
                pl.op("dve", lambda e, s=s: e.memset(stat[:, 32 + s:33 + s], 0.0), writes=["stat"])
                pl.op("act", lambda e, s=s: e.activation(out=sqj[:], in_=o1[:], func=AF.Square, accum_out=stat[:, 32 + s:33 + s]), reads=["o1", "stat"], writes=["sqj", "stat"])
                pl.op("act", lambda e, s=s: e.activation(out=stat[:, 36 + s:37 + s], in_=stat[:, 32 + s:33 + s], func=AF.Ln, scale=1.0 / D, bias=EPS), reads=["stat"], writes=["stat"])
                pl.op("act", lambda e, s=s: e.activation(out=stat[:, 36 + s:37 + s], in_=stat[:, 36 + s:37 + s], func=AF.Exp, scale=-0.5), reads=["stat"], writes=["stat"])
                pl.op("dve", lambda e, s=s, ob=ob: e.scalar_tensor_tensor(out=ob[:], in0=o1[:], scalar=stat[:, 36 + s:37 + s], in1=bc[:, BC_NF:BC_NF + D], op0=ALU.mult, op1=ALU.mult), reads=["o1", "stat", "bc"], writes=[ok])
                pl.dma("pool", lambda e, ob=ob, s=s: e.dma_start(out=y_d[yrow0 + s * 128:yrow0 + (s + 1) * 128, :], in_=ob[:]), "s_" + ok, reads=[ok], writes=["y"])

        ones_bf = SB("ones_bf", [128, 8], BF16)
        pl.op("dve", lambda e: e.memset(ones_bf[:], 1.0), writes=["ones"])
        sca_sb = SB("sca_sb", [128, 8, 2, 2])
        scb_sb = SB("scb_sb", [128, 16, 2, 3])
        ld("l_sca", sca_sb[:].rearrange("p a b c -> p (a b c)"), sca_d[:, :], "sca")
        ld("l_scb", scb_sb[:].rearrange("p a b c -> p (a b c)"), scb_d[:, :], "scb")
        pl.op("dve", lambda e: e.memset(uhist[:], 0.0), writes=["uhist"])
        pl.op("dve", lambda e: e.memset(xhist[:], 0.0), writes=["xhist"])
        pl.op("dve", lambda e: e.memset(atot[:], 0.0), writes=["atot"])
        pslot = [(0, T, 0)]
        for n in range(NLSEG * NT):
            stm = [None] * (T // 64)
            if n == 0:
                stm[0] = ("zero", 0)
            tile(128 + n * T, T, pslot, "light", 0, 0, "carry", stm, None)
            if n % NT == NT - 1:
                m = n // NT
                pl.op("dve", lambda e, m=m: e.tensor_scalar_mul(out=ST[:], in0=ST[:], scalar1=msk[:, 1 + m:2 + m]), reads=["ST", "msk"], writes=["ST"])
                pl.op("dve", lambda e, m=m: e.tensor_scalar_mul(out=xhist[:].rearrange("p a b -> p (a b)"), in0=xhist[:].rearrange("p a b -> p (a b)"), scalar1=msk[:, 1 + m:2 + m]), reads=["xhist", "msk"], writes=["xhist"])
        tile(0, 128, [(0, 128, 0)], "halo", 0, 0, "carry", [None, None], None)
        for n in range(NT):
            stm = [None] * (T // 64)
            if n == 0:
                stm[0] = ("keep", 0)
            tile(128 + LROWS + n * T, T, pslot, "full", 0, n * T, "carry", stm, 0)
        pl.dma("pool", lambda e: e.dma_start(out=so_d[0, :, :], in_=ST[:]), "s_ST", reads=["ST"], writes=["so0"])
        tile(128 + LROWS + SEGLEN, 128, [(0, 64, 1), (64, 128, 2)], "full", 1, SEGLEN, "state", [("load", 0), ("load", 1)], 1)
        pl.dma("pool", lambda e: e.dma_start(out=ca_d[:, :], in_=casb[:].rearrange("p a b c -> p (a b c)")), "s_ca", reads=["casb"], writes=["ca"])
        pl.dma("pool", lambda e: e.dma_start(out=cb_d[:, :], in_=cbsb[:].rearrange("p a b c -> p (a b c)")), "s_cb", reads=["cbsb"], writes=["cb"])
        if debug:
            dbg_list = [("xbf", xbf[:].rearrange("p a b -> p (a b)"), 8 * T, BF16), ("BT", BT[:].rearrange("p a b -> p (a b)"), 4 * T, BF16),
                        ("CT", CT[:].rearrange("p a b -> p (a b)"), 4 * T, BF16), ("sm", sm[:].rearrange("p a b -> p (a b)"), 256, F32),
                        ("ycat", ycat[:].rearrange("p a b -> p (a b)"), 16 * T, BF16), ("yo", yo[:], 1024, F32), ("xw", xw[:], 1024, BF16),
                        ("xdt", xdt[:], 1024, BF16), ("btok", btok[:], 512, BF16), ("Mt", Mt[:], 1024, BF16), ("eac", eac[:], 1024, F32),
                        ("cd", cd[:].rearrange("p a b -> p (a b)"), 32, F32), ("hT", hT[:].rearrange("p a b -> p (a b)"), 8 * T, BF16),
                        ("sz", sz[:].rearrange("p a b -> p (a b)"), 8 * T, BF16), ("stat", stat[:], 64, F32), ("gam", gam[:].rearrange("p a b -> p (a b)"), 24, F32)]
            allk = list(pl.bufs.keys())
            for nm, ap, w, dt_ in dbg_list:
                dd = nc.dram_tensor("dbg_" + nm, [128, w], dt_, kind="ExternalOutput").ap()
                pl.dma("pool", lambda e, dd=dd, ap=ap: e.dma_start(out=dd[:, :], in_=ap), "s_dbg", reads=allk)
        pl.wait_tokens("pool", [(s, c) for s, c in pl.dma_cnt.items() if s.startswith("s_")])
        print("planned instructions:", pl.nins, {e: len(pl.lists[e]) for e in pl.ENGS})
        pl.emit()
    return nc


def _host_consts():
    c = np.zeros((128, NCONST), np.float32)
    k = np.arange(128)
    c[:, C_ID:C_ID + 128] = np.eye(128, dtype=np.float32)
    same = (k[:, None] // 64) == (k[None, :] // 64)
    c[:, C_BLK:C_BLK + 128] = same
    c[:, C_TRI:C_TRI + 128] = same & (k[:, None] <= k[None, :])
    c[:, C_T64:C_T64 + 64] = (k[:, None] % 64) <= np.arange(64)[None, :]
    c[:, C_SEL0:C_SEL0 + 128] = (k[:, None] < 64)
    c[:, C_SEL1:C_SEL1 + 128] = (k[:, None] >= 64)
    return c


def _fm(v, nchunk):
    return np.ascontiguousarray(np.asarray(v, np.float32).reshape(nchunk, 128).T)


_NC_CACHE = {}


def kernel(x_prompt, x_sample, state_conv_a, state_conv_b, state_ssm, c_prompt, c_sample,
           w_mod, b_mod, norm_in_w, w_in, conv_a_w, norm_a_w, conv_b_w, conv_b_b,
           dt_bias, a_log, d_skip, norm_b_w, w_out, norm_f_w, _two_phase=True, _debug=False):
    f = lambda a: np.ascontiguousarray(np.asarray(a, np.float32))
    x_prompt, x_sample = f(x_prompt), f(x_sample)
    state_conv_a, state_conv_b, state_ssm = f(state_conv_a), f(state_conv_b), f(state_ssm)
    c_prompt, c_sample = f(c_prompt), f(c_sample)
    w_mod, b_mod, w_in, w_out = f(w_mod)[0], f(b_mod)[0], f(w_in)[0], f(w_out)[0]
    pf = np.zeros((128, NPF), np.float32)
    pf[:, PF_NIN:PF_NIN + 8] = _fm(f(norm_in_w)[0], 8)
    caw = f(conv_a_w)[0]
    pf[:, PF_CAW:PF_CAW + 24] = np.stack([_fm(caw[k], 8) for k in range(3)], axis=2).reshape(128, 24)
    pf[:, PF_NAW:PF_NAW + 8] = _fm(f(norm_a_w)[0], 8)
    cbw = f(conv_b_w)[0]
    pf[:, PF_CBW:PF_CBW + 64] = np.stack([_fm(cbw[k], 16) for k in range(4)], axis=2).reshape(128, 64)
    pf[:, PF_CBB:PF_CBB + 16] = _fm(f(conv_b_b)[0], 16)
    pf[:, PF_NBW:PF_NBW + 8] = _fm(f(norm_b_w)[0], 8)
    pf[:, PF_DSK:PF_DSK + 8] = _fm(np.repeat(f(d_skip)[0], 64), 8)
    pf[:, PF_BSH:PF_BSH + 8] = _fm(b_mod[0:D], 8)
    pf[:, PF_BSC:PF_BSC + 8] = _fm(b_mod[D:2 * D], 8)
    bcv = np.zeros((128, NBC), np.float32)
    bcv[:, BC_NF:BC_NF + D] = f(norm_f_w)[None, :]
    bcv[:, BC_BG:BC_BG + D] = b_mod[None, 2 * D:3 * D]
    bcv[:, BC_DTB:BC_DTB + 16] = f(dt_bias)[0][None, :]
    bcv[:, BC_ALOG:BC_ALOG + 16] = f(a_log)[0][None, :]
    cst = _host_consts()

    in_maps = []
    for k in range(NCORES):
        seq, seg = k // 4, k % 4
        start = seg * SEGLEN
        xs = np.zeros((XROWS, D), np.float32)
        if seg > 0:
            xs[0:128] = x_prompt[seq, start - 128:start]
        if seg > 0 and LROWS >= start:
            xs[128 + LROWS - start:128 + LROWS] = x_prompt[seq, 0:start]
        xs[128 + LROWS:128 + LROWS + SEGLEN] = x_prompt[seq, start:start + SEGLEN]
        xs[128 + LROWS + SEGLEN:] = x_sample[2 * k:2 * k + 2].reshape(128, D)
        cs = [c_prompt[seq], c_sample[2 * k], c_sample[2 * k + 1]]
        cT = np.stack([_fm(c, 8) for c in cs], axis=2).reshape(128, 24)
        cbc = np.zeros((2, 128, 8, 128), np.float32)
        cbc[0] = _fm(cs[0], 8)[:, :, None]
        cbc[1, :, :, 0:64] = _fm(cs[1], 8)[:, :, None]
        cbc[1, :, :, 64:128] = _fm(cs[2], 8)[:, :, None]
        sca = state_conv_a[0, 2 * k:2 * k + 2]
        sca = sca.reshape(2, 2, 8, 128).transpose(3, 2, 0, 1).reshape(128, 32)
        scb = state_conv_b[0, 2 * k:2 * k + 2]
        scb = scb.reshape(2, 3, 16, 128).transpose(3, 2, 0, 1).reshape(128, 96)
        sst = state_ssm[0, 2 * k:2 * k + 2]
        sst = sst.reshape(2, 1024, 128).transpose(0, 2, 1)
        msk = np.zeros((128, 16), np.float32)
        msk[:, 0] = 1.0 if seg > 0 else 0.0
        for m in range(NLSEG):
            msk[:, 1 + m] = 0.0 if m < NLSEG - seg else 1.0
        in_maps.append({
            "xs": xs, "w_mod": w_mod, "w_in": w_in, "w_out": w_out, "pf": pf, "bc": bcv, "cst": cst,
            "cT": np.ascontiguousarray(cT), "cbc": np.ascontiguousarray(cbc.reshape(2, 128, 1024)),
            "sca": np.ascontiguousarray(sca), "scb": np.ascontiguousarray(scb),
            "sst": np.ascontiguousarray(sst), "msk": msk,
        })
    key = (bool(_two_phase), bool(_debug))
    if key not in _NC_CACHE:
        _NC_CACHE[key] = build_nc(two_phase=key[0], debug=key[1])
    nc = _NC_CACHE[key]
    res = run_bass_kernel_spmd(nc, in_maps, core_ids=list(range(NCORES)))
    R = res.results
    if _debug:
        kernel.last_results = R
    y_prompt = np.zeros((2, SEQ, D), np.float32)
    y_sample = np.zeros((16, 64, D), np.float32)
    ca_p = np.zeros((1, 2, 2, D), np.float32)
    cb_p = np.zeros((1, 2, 3, 2 * D), np.float32)
    ss_p = np.zeros((1, 2, 16, 64, 128), np.float32)
    ca_s = np.zeros((1, 16, 2, D), np.float32)
    cb_s = np.zeros((1, 16, 3, 2 * D), np.float32)
    ss_s = np.zeros((1, 16, 16, 64, 128), np.float32)
    for k in range(NCORES):
        seq, seg = k // 4, k % 4
        r = R[k]
        y_prompt[seq, seg * SEGLEN:(seg + 1) * SEGLEN] = r["y"][0:SEGLEN]
        y_sample[2 * k:2 * k + 2] = r["y"][SEGLEN:].reshape(2, 64, D)
        ca = r["ca"].reshape(128, 8, 3, 2)
        cb = r["cb"].reshape(128, 16, 3, 3)
        so = r["so"]
        for s in range(2):
            ca_s[0, 2 * k + s] = ca[:, :, 1 + s, :].transpose(2, 1, 0).reshape(2, D)
            cb_s[0, 2 * k + s] = cb[:, :, 1 + s, :].transpose(2, 1, 0).reshape(3, 2 * D)
            ss_s[0, 2 * k + s] = so[1 + s].T.reshape(16, 64, 128)
        if seg == 3:
            ca_p[0, seq] = ca[:, :, 0, :].transpose(2, 1, 0).reshape(2, D)
            cb_p[0, seq] = cb[:, :, 0, :].transpose(2, 1, 0).reshape(3, 2 * D)
            ss_p[0, seq] = so[0].T.reshape(16, 64, 128)
    return (y_prompt, y_sample, ca_p, cb_p, ss_p, ca_s, cb_s, ss_s)
```

```python
import contextlib
import numpy as np
import concourse.bass as bass
import concourse.mybir as mybir
from concourse.bass_utils import run_bass_kernel_spmd

F32 = mybir.dt.float32
BF16 = mybir.dt.bfloat16
ALU = mybir.AluOpType
AF = mybir.ActivationFunctionType

NCORES = 8
D = 1024
SEQ = 16384
SEGLEN = 4096
T = 512
NT = SEGLEN // T
DIN = 7184
PW = 128
NPIECE = 56
NWB = 12
EPS = 1e-5
NLSEG = 3
LROWS = NLSEG * SEGLEN
XROWS = 128 + LROWS + SEGLEN + 128
YROWS = SEGLEN + 128

PF_NIN = 0
PF_CAW = 8
PF_NAW = 32
PF_CBW = 40
PF_CBB = 104
PF_NBW = 120
PF_DSK = 128
PF_BSH = 136
PF_BSC = 144
NPF = 152
BC_NF = 0
BC_BG = 1024
BC_DTB = 2048
BC_ALOG = 2064
NBC = 2080
C_ID = 0
C_BLK = 128
C_TRI = 256
C_T64 = 384
C_SEL0 = 448
C_SEL1 = 576
NCONST = 704


class Planner:
    ENGS = ("pe", "act", "dve", "pool", "sp")
    SEM_LIMIT = 30000

    def __init__(self, nc):
        self.nc = nc
        self.lists = {e: [] for e in self.ENGS}
        self.cur = {e: [e + "_0", 0] for e in self.ENGS}
        self.gen = {e: 0 for e in self.ENGS}
        self.sem_names = [e + "_0" for e in self.ENGS]
        self.dma_cnt = {}
        self.waited = {e: {} for e in self.ENGS}
        self.bufs = {}
        self.nins = 0
        self.alias = {}
        self.bank_last = {}

    @staticmethod
    def _bank(k):
        if k.startswith("psC"):
            return "psC"
        if k.startswith("psA") or k.startswith("psB") or k == "psT":
            return k
        return None

    def _bank_deps(self, eng, keys, need):
        banks = set(b for b in (self._bank(k) for k in keys) if b)
        for b in banks:
            for oe, tok in self.bank_last.get(b, {}).items():
                if oe != eng:
                    self._need(eng, tok, need)
        return banks

    def _exp(self, keys):
        out = []
        for k in keys:
            out.extend(self.alias.get(k, (k,)))
        return out

    def _need(self, eng, tok, out):
        if tok is None:
            return
        s, v = tok
        if eng == "pe" and s.startswith("pe_"):
            return
        if self.waited[eng].get(s, 0) >= v:
            return
        if out.get(s, 0) < v:
            out[s] = v

    def _deps(self, eng, reads, writes):
        reads, writes = self._exp(reads), self._exp(writes)
        need = {}
        for k in reads:
            b = self.bufs.get(k)
            if b:
                self._need(eng, b[0], need)
        for k in writes:
            b = self.bufs.get(k)
            if b:
                self._need(eng, b[0], need)
                for t in b[1]:
                    self._need(eng, t, need)
        self._cur_banks = self._bank_deps(eng, list(reads) + list(writes), need)
        self._cur_eng = eng
        for s, v in need.items():
            self.waited[eng][s] = v
            self.lists[eng].append(("wait", s, v))

    def _mark(self, tok, reads, writes):
        reads, writes = self._exp(reads), self._exp(writes)
        for b in self._cur_banks:
            self.bank_last.setdefault(b, {})[self._cur_eng] = tok
        for k in reads:
            b = self.bufs.setdefault(k, [None, []])
            b[1].append(tok)
        for k in writes:
            self.bufs[k] = [tok, []]

    def op(self, eng, fn, reads=(), writes=(), token=True):
        self._deps(eng, reads, writes)
        c = self.cur[eng]
        self.nins += 1
        if token:
            if c[1] >= self.SEM_LIMIT:
                self.gen[eng] += 1
                c[0] = "%s_%d" % (eng, self.gen[eng])
                c[1] = 0
                self.sem_names.append(c[0])
            c[1] += 1
            tok = (c[0], c[1])
            self.lists[eng].append(("ins", fn, c[0], 1))
        else:
            tok = (c[0], c[1] + 1)
            self.lists[eng].append(("ins", fn, None, 0))
        self._mark(tok, reads, writes)
        return tok

    def dma(self, eng, fn, sem, reads=(), writes=()):
        self._deps(eng, reads, writes)
        self.nins += 1
        if sem not in self.dma_cnt:
            self.dma_cnt[sem] = 0
            self.sem_names.append(sem)
        self.dma_cnt[sem] += 16
        tok = (sem, self.dma_cnt[sem])
        self.lists[eng].append(("ins", fn, sem, 16))
        self._mark(tok, reads, writes)
        return tok

    def raw(self, eng, fn, sem, inc, reads=(), writes=()):
        self._deps(eng, reads, writes)
        if sem not in self.dma_cnt:
            self.dma_cnt[sem] = 0
            self.sem_names.append(sem)
        self.dma_cnt[sem] += inc
        tok = (sem, self.dma_cnt[sem])
        self.lists[eng].append(("ins", fn, sem, -inc))
        self._mark(tok, reads, writes)
        return tok

    def wait_tokens(self, eng, toks):
        need = {}
        for t in toks:
            self._need(eng, t, need)
        for s, v in need.items():
            self.waited[eng][s] = v
            self.lists[eng].append(("wait", s, v))

    def emit(self):
        nc = self.nc
        with contextlib.ExitStack() as st:
            sems = {}
            for n in self.sem_names:
                sems[n] = st.enter_context(nc.semaphore(n))
            block = st.enter_context(nc.Block())
            engmap = {"pe": block.tensor, "act": block.scalar, "dve": block.vector,
                      "pool": block.gpsimd, "sp": block.sync}
            for e in self.ENGS:
                lst = self.lists[e]
                if not lst:
                    continue

                def body(engobj, lst=lst):
                    for it in lst:
                        if it[0] == "wait":
                            engobj.wait_ge(sems[it[1]], it[2])
                        else:
                            ins = it[1](engobj)
                            if it[2] is not None:
                                if it[3] < 0:
                                    ins.then_inc(sems[it[2]])
                                else:
                                    ins.then_inc(sems[it[2]], it[3])
                engmap[e](body)


def build_nc(two_phase=True, debug=False):
    nc = bass.Bass("TRN2", target_bir_lowering=False)
    dr = lambda n, s, k, d=F32: nc.dram_tensor(n, list(s), d, kind=k)
    xs_d = dr("xs", [XROWS, D], "ExternalInput").ap()
    wmod_d = dr("w_mod", [D, 3 * D], "ExternalInput").ap()
    win_d = dr("w_in", [D, DIN], "ExternalInput").ap()
    wout_d = dr("w_out", [2 * D, D], "ExternalInput").ap()
    pf_d = dr("pf", [128, NPF], "ExternalInput").ap()
    bc_d = dr("bc", [128, NBC], "ExternalInput").ap()
    cst_d = dr("cst", [128, NCONST], "ExternalInput").ap()
    cT_d = dr("cT", [128, 8 * 3], "ExternalInput").ap()
    cbc_d = dr("cbc", [2, 128, 8 * 128], "ExternalInput").ap()
    sca_d = dr("sca", [128, 8 * 2 * 2], "ExternalInput").ap()
    scb_d = dr("scb", [128, 16 * 2 * 3], "ExternalInput").ap()
    sst_d = dr("sst", [2, 128, D], "ExternalInput").ap()
    msk_d = dr("msk", [128, 16], "ExternalInput").ap()
    y_d = dr("y", [YROWS, D], "ExternalOutput").ap()
    ca_d = dr("ca", [128, 8 * 3 * 2], "ExternalOutput").ap()
    cb_d = dr("cb", [128, 16 * 3 * 3], "ExternalOutput").ap()
    so_d = dr("so", [3, 128, D], "ExternalOutput").ap()
    winbf_d = nc.dram_tensor("winbf", [NPIECE, 128, 8 * PW], BF16)

    pl = Planner(nc)
    pl.alias = {"stg0": ("rhs1", "segs", "eac", "yo"), "stg1": ("stsb", "o1", "ost0", "ost1"), "sqj": ("Mt",)}
    with contextlib.ExitStack() as st:
        def SB(name, shape, dt=F32):
            return st.enter_context(nc.sbuf_tensor("sb_" + name, list(shape), dt))

        cst = SB("cst", [128, NCONST])
        idb = SB("idb", [128, 128], BF16)
        pf = SB("pf", [128, NPF])
        bc = SB("bc", [128, NBC])
        msk = SB("msk", [128, 16])
        cT = SB("cT", [128, 8, 3])
        gam = SB("gam", [128, 3, 8])
        bet = SB("bet", [128, 3, 8])
        gate = SB("gate", [128, 2, D])
        caw = SB("caw", [128, 8, 3])
        cbw = SB("cbw", [128, 16, 4])
        cbb = SB("cbb", [128, 16])
        a_bc = SB("a_bc", [128, 16])
        wout = SB("wout", [128, 16, D], BF16)
        wdt = SB("wdt", [128, 8, 16], BF16)
        wbuf = [SB("wbuf%d" % i, [128, 8, PW], BF16) for i in range(NWB)]
        big = [SB("big%d" % i, [128, 4096]) for i in range(2)]
        stg = [b[:].rearrange("p (a c) -> p a c", a=8) for b in big]
        xt = [SB("xt%d" % i, [128, D]) for i in range(2)]
        hT = SB("hT", [128, 8, T], BF16)
        ubuf = [SB("ubuf%d" % i, [128, 3 + T]) for i in range(2)]
        hsb = [SB("hsb%d" % i, [128, T]) for i in range(2)]
        cu = [SB("cu%d" % i, [128, T]) for i in range(2)]
        tq = [SB("tq%d" % i, [128, T]) for i in range(2)]
        sq2 = [SB("sq2%d" % i, [128, T], BF16) for i in range(2)]
        uhist = SB("uhist", [128, 8, 3])
        xhist = SB("xhist", [128, 16, 3])
        uhist0 = SB("uhist0", [128, 8, 3])
        xhist0 = SB("xhist0", [128, 16, 3])
        ycat = SB("ycat", [128, 16, T], BF16)
        xbf = SB("xbf", [128, 8, T], BF16)
        BT = SB("BT", [128, 4, T], BF16)
        CT = SB("CT", [128, 4, T], BF16)
        sz = SB("sz", [128, 8, T], BF16)
        xdt = SB("xdt", [128, D], BF16)
        xw = SB("xw", [128, D], BF16)
        btok = SB("btok", [128, 512], BF16)
        rhs1 = big[0][:, 0:1024]
        segs = big[0][:, 1024:2048]
        Mt = SB("Mt", [128, D], BF16)
        sqj = Mt
        eac = big[0][:, 2048:3072]
        yo = big[0][:, 3072:4096]
        cbtm = SB("cbtm", [128, 4, 64])
        sm = SB("sm", [128, 8, 64])
        ST = SB("ST", [128, D])
        STb = SB("STb", [128, D], BF16)
        stsb = big[1][:, 0:1024]
        cd = SB("cd", [128, 2, 64])
        stat = SB("stat", [128, 64])
        ost = [big[1][:, 2048:3072], big[1][:, 3072:4096]]
        o1 = big[1][:, 1024:2048]
        casb = SB("casb", [128, 8, 3, 2])
        cbsb = SB("cbsb", [128, 16, 3, 3])
        atot = SB("atot", [128, 16])
        gsel = SB("gsel", [128, 8, 16])
        print("sbuf remaining after alloc:", nc.sbuf_bytes_remaining)

        psA = st.enter_context(nc.psum_tensor("psA", [128, 4, 512], F32))
        psB = st.enter_context(nc.psum_tensor("psB", [128, 2, 512], F32))
        psC = st.enter_context(nc.psum_tensor("psC", [128, 512], F32))
        psT = st.enter_context(nc.psum_tensor("psT", [128, 1024], BF16))

        ident = cst[:, C_ID:C_ID + 128]
        blk = cst[:, C_BLK:C_BLK + 128]
        tri = cst[:, C_TRI:C_TRI + 128]
        t64 = cst[:, C_T64:C_T64 + 64]
        chsel = [cst[:, C_SEL0:C_SEL0 + 128], cst[:, C_SEL1:C_SEL1 + 128]]

        def pfc(off, j):
            return pf[:, off + j:off + j + 1]

        ld = lambda name, dst, src, key: pl.dma("sp", lambda e: e.dma_start(out=dst, in_=src), name, writes=[key])
        ld("l_cst", cst[:], cst_d[:, :], "cst")
        ld("l_pf", pf[:], pf_d[:, :], "pf")
        ld("l_bc", bc[:], bc_d[:, :], "bc")
        ld("l_msk", msk[:], msk_d[:, :], "msk")
        ld("l_cT", cT[:].rearrange("p a b -> p (a b)"), cT_d[:, :], "cT")
        pl.op("dve", lambda e: e.tensor_copy(out=idb[:], in_=ident), reads=["cst"], writes=["idb"])
        pl.op("dve", lambda e: e.tensor_scalar_mul(out=caw[:].rearrange("p a b -> p (a b)"), in0=pf[:, PF_CAW:PF_CAW + 24], scalar1=1.0), reads=["pf"], writes=["caw"])
        pl.op("dve", lambda e: e.tensor_scalar_mul(out=cbw[:].rearrange("p a b -> p (a b)"), in0=pf[:, PF_CBW:PF_CBW + 64], scalar1=1.0), reads=["pf"], writes=["cbw"])
        pl.op("dve", lambda e: e.tensor_scalar_mul(out=cbb[:], in0=pf[:, PF_CBB:PF_CBB + 16], scalar1=1.0), reads=["pf"], writes=["cbb"])
        pl.op("act", lambda e: e.activation(out=a_bc[:], in_=bc[:, BC_ALOG:BC_ALOG + 16], func=AF.Exp), reads=["bc"], writes=["a_bc"])
        pl.op("dve", lambda e: e.tensor_scalar_mul(out=a_bc[:], in0=a_bc[:], scalar1=-1.0), reads=["a_bc"], writes=["a_bc"])

        wmod_v = wmod_d.rearrange("(kc p) c -> p kc c", p=128)
        for piece in range(6):
            s = stg[piece % 2]
            key = "stg%d" % (piece % 2)
            pl.dma("sp", lambda e, s=s, piece=piece: e.dma_start(out=s[:], in_=wmod_v[:, :, piece * 512:(piece + 1) * 512]), "l_" + key, writes=[key])
            if piece < 4:
                for cc in range(4):
                    j = (piece % 2) * 4 + cc
                    for kc in range(8):
                        pl.op("pe", lambda e, s=s, cc=cc, kc=kc, j=j: e.matmul(psC[:, j * 4:j * 4 + 3], lhsT=s[:, kc, cc * 128:(cc + 1) * 128], rhs=cT[:, kc, :], start=(kc == 0), stop=(kc == 7)),
                              reads=[key, "cT"], writes=["psC"], token=(kc == 7))
                if piece % 2 == 1:
                    src = psC[:, 0:32].rearrange("p (j s) -> p s j", s=4)[:, 0:3, :]
                    if piece == 1:
                        pl.op("dve", lambda e, src=src: e.tensor_tensor(out=bet[:], in0=src, in1=pf[:, PF_BSH:PF_BSH + 8].unsqueeze(1).broadcast_to([128, 3, 8]), op=ALU.add), reads=["psC", "pf"], writes=["bet"])
                    else:
                        pl.op("dve", lambda e, src=src: e.tensor_tensor(out=gam[:], in0=src, in1=pf[:, PF_BSC:PF_BSC + 8].unsqueeze(1).broadcast_to([128, 3, 8]), op=ALU.add), reads=["psC", "pf"], writes=["gam"])
                        pl.op("dve", lambda e: e.scalar_tensor_tensor(out=gam[:], in0=gam[:], scalar=1.0, in1=pf[:, PF_NIN:PF_NIN + 8].unsqueeze(1).broadcast_to([128, 3, 8]), op0=ALU.add, op1=ALU.mult), reads=["gam", "pf"], writes=["gam"])
            else:
                half = piece - 4
                for which in range(2):
                    cb_t = xt[which]
                    if half == 0:
                        pl.dma("sp", lambda e, cb_t=cb_t, which=which: e.dma_start(out=cb_t[:], in_=cbc_d[which, :, :]), "l_xt%d" % which, writes=["xt%d" % which])
                    cbv = cb_t[:].rearrange("p (k m) -> p k m", k=8)
                    for kc in range(8):
                        pl.op("pe", lambda e, s=s, kc=kc, cbv=cbv, which=which: e.matmul(psA[:, which, :], lhsT=cbv[:, kc, :], rhs=s[:, kc, :], start=(kc == 0), stop=(kc == 7)),
                              reads=[key, "xt%d" % which], writes=["psA%d" % which], token=(kc == 7))
                    pl.op("dve", lambda e, which=which, half=half: e.tensor_tensor(out=gate[:, which, half * 512:(half + 1) * 512], in0=psA[:, which, :], in1=bc[:, BC_BG + half * 512:BC_BG + (half + 1) * 512], op=ALU.add),
                          reads=["psA%d" % which, "bc"], writes=["gate"])

        win_v = win_d.rearrange("(kc p) c -> p kc c", p=128)
        wout_v = wout_d.rearrange("(kc p) c -> p kc c", p=128)
        castengs = ["dve", "act", "pool"]
        ci = 0

        def cast(dst, src, rk, wk):
            nonlocal ci
            eng = castengs[ci % 3]
            ci += 1
            if eng == "act":
                pl.op("act", lambda e: e.copy(out=dst, in_=src), reads=rk, writes=wk)
            else:
                pl.op(eng, lambda e: e.tensor_copy(out=dst, in_=src), reads=rk, writes=wk)

        porder = [10, 11, 12, 2, 3, 4, 5, 0, 1, 6, 7, 13, 8, 9]
        pl.dma("sp", lambda e: e.dma_start(out=stg[0][:, :, 0:16], in_=win_v[:, :, 7168:7184]), "l_stg0", writes=["stg0"])
        pl.op("dve", lambda e: e.tensor_copy(out=wdt[:], in_=stg[0][:, :, 0:16]), reads=["stg0"], writes=["wdt"])
        wci = 0
        for n, piece in enumerate(porder):
            s = stg[n % 2]
            key = "stg%d" % (n % 2)
            pl.dma("sp", lambda e, s=s, piece=piece: e.dma_start(out=s[:], in_=win_v[:, :, piece * 512:(piece + 1) * 512]), "l_" + key, writes=[key])
            for hp in range(512 // PW):
                wb = wbuf[wci % NWB]
                wkey = "wbuf%d" % (wci % NWB)
                wci += 1
                for kh in range(2):
                    cast(wb[:, kh * 4:(kh + 1) * 4, :], s[:, kh * 4:(kh + 1) * 4, hp * PW:(hp + 1) * PW], [key], [wkey])
                sp_ = (512 // PW) * piece + hp
                pl.dma("pool", lambda e, wb=wb, sp_=sp_: e.dma_start(out=winbf_d[sp_, :, :], in_=wb[:].rearrange("p a b -> p (a b)")), "s_" + wkey, reads=[wkey], writes=["winbf%d" % sp_])
        for n in range(4):
            kg, ch = n // 2, n % 2
            s = stg[n % 2]
            key = "stg%d" % (n % 2)
            pl.dma("sp", lambda e, s=s, kg=kg, ch=ch: e.dma_start(out=s[:], in_=wout_v[:, kg * 8:(kg + 1) * 8, ch * 512:(ch + 1) * 512]), "l_" + key, writes=[key])
            for kh in range(2):
                cast(wout[:, kg * 8 + kh * 4:kg * 8 + (kh + 1) * 4, ch * 512:(ch + 1) * 512], s[:, kh * 4:(kh + 1) * 4, :], [key], ["wout"])

        wcnt = [0]

        def load_piece(piece):
            i = wcnt[0] % NWB
            wcnt[0] += 1
            wb = wbuf[i]
            pl.dma("sp", lambda e: e.dma_start(out=wb[:].rearrange("p a b -> p (a b)"), in_=winbf_d[piece, :, :]), "l_wbuf%d" % i, reads=["winbf%d" % piece], writes=["wbuf%d" % i])
            return wb, "wbuf%d" % i

        def proj_chunk(wb, wkey, cc, ps_ap, pskey, Tn):
            for kc in range(8):
                pl.op("pe", lambda e, kc=kc: e.matmul(ps_ap, lhsT=wb[:, kc, cc * 128:(cc + 1) * 128], rhs=hT[:, kc, 0:Tn], start=(kc == 0), stop=(kc == 7)),
                      reads=[wkey, "hT"], writes=[pskey], token=(kc == 7))

        pref = {}

        def load_x(row0, s):
            xb = xt[s % 2]
            xk = "xt%d" % (s % 2)
            pl.dma("sp", lambda e: e.dma_start(out=xb[:], in_=xs_d[row0 + s * 128:row0 + (s + 1) * 128, :]), "l_" + xk, writes=[xk])

        def step_a(row0, Tn, slots):
            nsub = Tn // 128
            for p0 in range(0, nsub, 2):
                subs = list(range(p0, min(p0 + 2, nsub)))
                for s in subs:
                    xb = xt[s % 2]
                    xk = "xt%d" % (s % 2)
                    if not pref.pop((row0, s), False):
                        load_x(row0, s)
                    pl.op("pool", lambda e, s=s: e.memset(stat[:, s:s + 1], 0.0), writes=["stat"])
                    pl.op("act", lambda e, xb=xb, s=s: e.activation(out=sqj[:], in_=xb[:], func=AF.Square, accum_out=stat[:, s:s + 1]), reads=[xk, "stat"], writes=["sqj", "stat"])
                    pl.op("act", lambda e, s=s: e.activation(out=stat[:, 8 + s:9 + s], in_=stat[:, s:s + 1], func=AF.Ln, scale=1.0 / D, bias=EPS), reads=["stat"], writes=["stat"])
                    pl.op("act", lambda e, s=s: e.activation(out=stat[:, 16 + s:17 + s], in_=stat[:, 8 + s:9 + s], func=AF.Exp, scale=-0.5), reads=["stat"], writes=["stat"])
                    pl.op("dve", lambda e, xb=xb, s=s: e.tensor_scalar_mul(out=xb[:], in0=xb[:], scalar1=stat[:, 16 + s:17 + s]), reads=[xk, "stat"], writes=[xk])
                w0, w1 = subs[0] * 128, (subs[-1] + 1) * 128
                for kc in range(8):
                    pk = "psB%d" % (kc % 2)
                    for s in subs:
                        xb = xt[s % 2]
                        xk = "xt%d" % (s % 2)
                        pl.op("pe", lambda e, xb=xb, kc=kc, s=s, p0=p0: e.transpose(out=psB[:, kc % 2, (s - p0) * 128:(s - p0 + 1) * 128], in_=xb[:, kc * 128:(kc + 1) * 128], identity=ident), reads=[xk, "cst"], writes=[pk], token=(s == subs[-1]))
                    for (c0, c1, slot) in slots:
                        lo, hi = max(c0, w0), min(c1, w1)
                        if lo >= hi:
                            continue
                        pl.op("act", lambda e, kc=kc, lo=lo, hi=hi, slot=slot, w0=w0: e.activation(out=hT[:, kc, lo:hi], in_=psB[:, kc % 2, lo - w0:hi - w0], func=AF.Identity, scale=gam[:, slot, kc:kc + 1], bias=bet[:, slot, kc:kc + 1]),
                              reads=[pk, "gam", "bet"], writes=["hT"])

        def conv_chunk(eng_first, src, wts, nk, dst, Tn, rk, wk, bias=None, bk=()):
            off = 3 - (nk - 1)
            if bias is None:
                pl.op("act", lambda e: e.activation(out=dst[:, 0:Tn], in_=src[:, off:off + Tn], func=AF.Copy, scale=wts[:, 0:1]), reads=rk, writes=wk)
            else:
                pl.op("act", lambda e: e.activation(out=dst[:, 0:Tn], in_=src[:, off:off + Tn], func=AF.Identity, scale=wts[:, 0:1], bias=bias), reads=rk + list(bk), writes=wk)
            for k in range(1, nk):
                pl.op("dve", lambda e, k=k: e.scalar_tensor_tensor(out=dst[:, 0:Tn], in0=src[:, off + k:off + k + Tn], scalar=wts[:, k:k + 1], in1=dst[:, 0:Tn], op0=ALU.mult, op1=ALU.add), reads=rk + wk, writes=wk)

        def tile(row0, Tn, slots, mode, gidx, yrow0, hist_from, st_mode, out_slot, next_row0=None):
            nsub = Tn // 128
            nch = Tn // 64
            step_a(row0, Tn, slots)
            light = (mode != "full")
            if mode == "light" and next_row0 is not None:
                for s_ in range(2):
                    load_x(next_row0, s_)
                    pref[(next_row0, s_)] = True
            if mode in ("full", "halo"):
                wbs = {}
                pendA = None
                for j in range(8):
                    i = j % 2
                    need = [8 + j, 16 + j] if mode == "halo" else [8 + j, 16 + j, 0 + j, 24 + j]
                    for pc in need:
                        wbs[pc] = load_piece(pc)
                    cc = 0
                    wb, wk_ = wbs[8 + j]
                    proj_chunk(wb, wk_, cc, psA[:, 0, 0:Tn], "psA0", Tn)
                    wb, wk_ = wbs[16 + j]
                    proj_chunk(wb, wk_, cc, psA[:, 1, 0:Tn], "psA1", Tn)
                    ub, uk = ubuf[i], "ubuf%d" % i
                    pl.op("act", lambda e, i=i: e.copy(out=hsb[i][:, 0:Tn], in_=psA[:, 1, 0:Tn]), reads=["psA1"], writes=["hsb%d" % i])
                    pl.op("pool", lambda e, ub=ub, j=j: e.tensor_copy(out=ub[:, 0:3], in_=uhist[:, j, :]), reads=["uhist"], writes=[uk])
                    pl.op("dve", lambda e, ub=ub, i=i: e.tensor_tensor(out=ub[:, 3:3 + Tn], in0=psA[:, 0, 0:Tn], in1=hsb[i][:, 0:Tn], op=ALU.mult), reads=["psA0", "hsb%d" % i], writes=[uk])
                    pl.op("pool", lambda e, ub=ub, j=j: e.tensor_copy(out=uhist[:, j, :], in_=ub[:, Tn:Tn + 3]), reads=[uk], writes=["uhist"])
                    if mode == "halo":
                        continue
                    wb, wk_ = wbs[0 + j]
                    proj_chunk(wb, wk_, cc, psA[:, 2, 0:Tn], "psA2", Tn)
                    wb, wk_ = wbs[24 + j]
                    proj_chunk(wb, wk_, cc, psA[:, 3, 0:Tn], "psA3", Tn)
                    pl.op("act", lambda e, i=i: e.activation(out=tq[i][:, 0:Tn], in_=psA[:, 3, 0:Tn], func=AF.Silu), reads=["psA3"], writes=["tq%d" % i])
                    pl.op("dve", lambda e, i=i: e.tensor_tensor(out=tq[i][:, 0:Tn], in0=psA[:, 2, 0:Tn], in1=tq[i][:, 0:Tn], op=ALU.mult), reads=["psA2", "tq%d" % i], writes=["tq%d" % i])
                    c_, ck = cu[i], "cu%d" % i
                    if hist_from == "carry":
                        conv_chunk("dve", ub, caw[:, j, :], 3, c_, Tn, [uk, "caw"], [ck])
                    else:
                        for sidx in range(2):
                            pl.op("dve", lambda e, ub=ub, j=j, sidx=sidx: e.tensor_copy(out=stsb[:, sidx * 128 + 1:sidx * 128 + 3], in_=sca_sb[:, j, sidx, :]), reads=["sca"], writes=["stsb"])
                        for sidx in range(2):
                            base = sidx * 128
                            pl.op("dve", lambda e, ub=ub, base=base, sidx=sidx: e.tensor_copy(out=stsb[:, base + 3:base + 67], in_=ub[:, 3 + sidx * 64:3 + (sidx + 1) * 64]), reads=[uk], writes=["stsb"])
                            off = 1
                            pl.op("dve", lambda e, c_=c_, base=base, sidx=sidx, j=j: e.tensor_scalar_mul(out=c_[:, sidx * 64:(sidx + 1) * 64], in0=stsb[:, base + 1:base + 65], scalar1=caw[:, j, 0:1]), reads=["stsb", "caw"], writes=[ck])
                            for k in (1, 2):
                                pl.op("dve", lambda e, c_=c_, base=base, sidx=sidx, j=j, k=k: e.scalar_tensor_tensor(out=c_[:, sidx * 64:(sidx + 1) * 64], in0=stsb[:, base + 1 + k:base + 65 + k], scalar=caw[:, j, k:k + 1], in1=c_[:, sidx * 64:(sidx + 1) * 64], op0=ALU.mult, op1=ALU.add), reads=["stsb", "caw", ck], writes=[ck])
                            pl.op("dve", lambda e, base=base, sidx=sidx, j=j: e.tensor_copy(out=casb[:, j, 1 + sidx, :], in_=stsb[:, base + 65:base + 67]), reads=["stsb"], writes=["casb"])
                    def stage_b(c_=c_, ck=ck, i=i, j=j):
                        pl.op("dve", lambda e: e.tensor_tensor(out=c_[:, 0:Tn], in0=c_[:, 0:Tn], in1=tq[i][:, 0:Tn], op=ALU.mult), reads=[ck, "tq%d" % i], writes=[ck])
                        pl.op("act", lambda e: e.activation(out=sq2[i][:, 0:Tn], in_=c_[:, 0:Tn], func=AF.Square), reads=[ck], writes=["sq2%d" % i])
                        pl.op("act", lambda e: e.activation(out=ycat[:, j, 0:Tn], in_=c_[:, 0:Tn], func=AF.Copy, scale=pfc(PF_NAW, j)), reads=[ck, "pf"], writes=["ycat"])
                        for s in range(nsub):
                            pl.op("pe", lambda e, s=s: e.matmul(psC[:, 320 + s:321 + s], lhsT=sq2[i][:, s * 128:(s + 1) * 128], rhs=ones_bf[:, 0:1], start=(j == 0 and s == 0), stop=(j == 7), skip_group_check=True),
                                  reads=["sq2%d" % i, "ones"], writes=["psCa"], token=(s == nsub - 1))
                    if pendA is not None:
                        pendA()
                    pendA = stage_b
                if pendA is not None:
                    pendA()
                if mode == "full" and hist_from == "carry":
                    pl.op("dve", lambda e: e.tensor_copy(out=casb[:, :, 0, :], in_=uhist[:, :, 1:3]), reads=["uhist"], writes=["casb"])
                if mode == "halo":
                    pl.op("dve", lambda e: e.tensor_scalar_mul(out=uhist[:].rearrange("p a b -> p (a b)"), in0=uhist[:].rearrange("p a b -> p (a b)"), scalar1=msk[:, 0:1]), reads=["uhist", "msk"], writes=["uhist"])
                    wbs = {}
                    for j in range(12, 16):
                        wb, wk_ = load_piece(40 + j)
                        pa = psA[:, j % 4, 0:Tn]
                        pk = "psA%d" % (j % 4)
                        proj_chunk(wb, wk_, 0, pa, pk, Tn)
                        pl.op("act", lambda e, j=j, pa=pa: e.copy(out=xhist[:, j, :], in_=pa[:, Tn - 3:Tn]), reads=[pk], writes=["xhist"])
                    pl.op("dve", lambda e: e.tensor_scalar_mul(out=xhist[:, 12:16, :], in0=xhist[:, 12:16, :], scalar1=msk[:, 0:1]), reads=["xhist", "msk"], writes=["xhist"])
                    return

            wbs = {}
            pending = None
            for j in range(16):
                if mode == "light" and j >= 12:
                    break
                wb, wk_ = load_piece(40 + j)
                i = j % 2
                pa = psA[:, j % 4, 0:Tn]
                pk = "psA%d" % (j % 4)
                proj_chunk(wb, wk_, 0, pa, pk, Tn)
                ub, uk = ubuf[i], "ubuf%d" % i
                pl.op("pool", lambda e, ub=ub, j=j: e.tensor_copy(out=ub[:, 0:3], in_=xhist[:, j, :]), reads=["xhist"], writes=[uk])
                pl.op("act", lambda e, ub=ub, pa=pa: e.copy(out=ub[:, 3:3 + Tn], in_=pa), reads=[pk], writes=[uk])
                pl.op("pool", lambda e, ub=ub, j=j: e.tensor_copy(out=xhist[:, j, :], in_=ub[:, Tn:Tn + 3]), reads=[uk], writes=["xhist"])
                c_, ck = cu[i], "cu%d" % i
                if hist_from == "carry":
                    conv_chunk("act", ub, cbw[:, j, :], 4, c_, Tn, [uk, "cbw"], [ck], bias=cbb[:, j:j + 1], bk=["cbb"])
                else:
                    for sidx in range(2):
                        base = sidx * 128
                        pl.op("dve", lambda e, j=j, sidx=sidx, base=base: e.tensor_copy(out=stsb[:, base:base + 3], in_=scb_sb[:, j, sidx, :]), reads=["scb"], writes=["stsb"])
                        pl.op("dve", lambda e, ub=ub, base=base, sidx=sidx: e.tensor_copy(out=stsb[:, base + 3:base + 67], in_=ub[:, 3 + sidx * 64:3 + (sidx + 1) * 64]), reads=[uk], writes=["stsb"])
                        pl.op("dve", lambda e, c_=c_, base=base, sidx=sidx, j=j: e.tensor_scalar_mul(out=c_[:, sidx * 64:(sidx + 1) * 64], in0=stsb[:, base:base + 64], scalar1=cbw[:, j, 0:1]), reads=["stsb", "cbw"], writes=[ck])
                        for k in (1, 2, 3):
                            pl.op("dve", lambda e, c_=c_, base=base, sidx=sidx, j=j, k=k: e.scalar_tensor_tensor(out=c_[:, sidx * 64:(sidx + 1) * 64], in0=stsb[:, base + k:base + 64 + k], scalar=cbw[:, j, k:k + 1], in1=c_[:, sidx * 64:(sidx + 1) * 64], op0=ALU.mult, op1=ALU.add), reads=["stsb", "cbw", ck], writes=[ck])
                        pl.op("dve", lambda e, base=base, sidx=sidx, j=j: e.tensor_copy(out=cbsb[:, j, 1 + sidx, :], in_=stsb[:, base + 64:base + 67]), reads=["stsb"], writes=["cbsb"])
                    pl.op("dve", lambda e, c_=c_, j=j: e.tensor_scalar_add(out=c_[:, 0:Tn], in0=c_[:, 0:Tn], scalar1=cbb[:, j:j + 1]), reads=[ck, "cbb"], writes=[ck])
                if j < 8:
                    dst, dk = xbf[:, j, 0:Tn], "xbf"
                elif j < 12:
                    dst, dk = BT[:, j - 8, 0:Tn], "BT"
                else:
                    dst, dk = CT[:, j - 12, 0:Tn], "CT"

                def stage_b(c_=c_, ck=ck, i=i, dst=dst, dk=dk):
                    pl.op("act", lambda e: e.activation(out=dst, in_=c_[:, 0:Tn], func=AF.Silu), reads=[ck], writes=[dk])
                if pending is not None:
                    pending()
                pending = stage_b
            if pending is not None:
                pending()
            if mode == "full" and hist_from == "carry":
                pl.op("dve", lambda e: e.tensor_copy(out=cbsb[:, :, 0, :], in_=xhist[:]), reads=["xhist"], writes=["cbsb"])
            if mode == "full":
                wbs = {}
                for j in range(8):
                    wb, wk_ = load_piece(32 + j)
                    pa = psA[:, j % 4, 0:Tn]
                    pk = "psA%d" % (j % 4)
                    i = j % 2
                    proj_chunk(wb, wk_, 0, pa, pk, Tn)
                    pl.op("act", lambda e, pa=pa, j=j: e.activation(out=sz[:, j, 0:Tn], in_=pa, func=AF.Silu), reads=[pk], writes=["sz"])

            W = nsub * 16
            SMA = lambda idx: sm[:, idx, 0:W]
            v3 = lambda ap: ap.rearrange("p (b h) -> p b h", h=16)
            for b in range(nsub):
                tsl = slice(b * 128, (b + 1) * 128)
                for kc in range(8):
                    pl.op("pe", lambda e, kc=kc, tsl=tsl, b=b: e.matmul(psC[:, 256 + b * 16:272 + b * 16], lhsT=hT[:, kc, tsl], rhs=wdt[:, kc, :], start=(kc == 0), stop=(kc == 7)), reads=["hT", "wdt"], writes=["psCd"], token=(kc == 7))
            pl.op("dve", lambda e: e.tensor_tensor(out=v3(SMA(0)), in0=v3(psC[:, 256:256 + W]), in1=bc[:, BC_DTB:BC_DTB + 16].unsqueeze(1).broadcast_to([128, nsub, 16]), op=ALU.add), reads=["psCd", "bc"], writes=["sm0"])
            pl.op("act", lambda e: e.activation(out=SMA(1), in_=SMA(0), func=AF.Abs), reads=["sm0"], writes=["sm1"])
            pl.op("act", lambda e: e.activation(out=SMA(1), in_=SMA(1), func=AF.Exp, scale=-1.0), reads=["sm1"], writes=["sm1"])
            pl.op("act", lambda e: e.activation(out=SMA(1), in_=SMA(1), func=AF.Ln, bias=1.0), reads=["sm1"], writes=["sm1"])
            pl.op("dve", lambda e: e.scalar_tensor_tensor(out=SMA(2), in0=SMA(0), scalar=0.0, in1=SMA(1), op0=ALU.max, op1=ALU.add), reads=["sm0", "sm1"], writes=["sm2"])
            pl.op("dve", lambda e: e.tensor_tensor(out=v3(SMA(3)), in0=v3(SMA(2)), in1=a_bc[:].unsqueeze(1).broadcast_to([128, nsub, 16]), op=ALU.mult), reads=["sm2", "a_bc"], writes=["sm3"])
            pl.op("pe", lambda e: e.matmul(psB[:, 0, 0:W], lhsT=tri, rhs=SMA(3), start=True, stop=True), reads=["cst", "sm3"], writes=["psB0"], token=False)
            pl.op("pe", lambda e: e.matmul(psB[:, 0, 64:64 + W], lhsT=blk, rhs=SMA(3), start=True, stop=True), reads=["cst", "sm3"], writes=["psB0"], token=False)
            pl.op("pe", lambda e: e.matmul(psB[:, 0, 128:128 + W], lhsT=chsel[0], rhs=SMA(3), start=True, stop=True), reads=["cst", "sm3"], writes=["psB0"], token=False)
            pl.op("pe", lambda e: e.matmul(psB[:, 0, 192:192 + W], lhsT=chsel[1], rhs=SMA(3), start=True, stop=True), reads=["cst", "sm3"], writes=["psB0"])
            pl.op("dve", lambda e: e.tensor_copy(out=SMA(4), in_=psB[:, 0, 0:W]), reads=["psB0"], writes=["sm4"])
            pl.op("dve", lambda e: e.tensor_tensor(out=SMA(5), in0=psB[:, 0, 64:64 + W], in1=SMA(4), op=ALU.subtract), reads=["psB0", "sm4"], writes=["sm5"])
            if mode == "light":
                sel0, sel1 = psB[:, 0, 128:128 + W], psB[:, 0, 192:192 + W]
                pl.op("dve", lambda e: e.tensor_copy(out=SMA(7), in_=sel1), reads=["psB0"], writes=["sm7"])
                pl.op("dve", lambda e: e.tensor_tensor(out=SMA(0), in0=sel0, in1=SMA(7), op=ALU.add), reads=["psB0", "sm7"], writes=["sm0"])
                pl.op("dve", lambda e: e.memset(SMA(1), 0.0), writes=["sm1"])
                for b in range(nsub - 2, -1, -1):
                    pl.op("dve", lambda e, b=b: e.tensor_tensor(out=sm[:, 1, b * 16:(b + 1) * 16], in0=sm[:, 1, (b + 1) * 16:(b + 2) * 16], in1=sm[:, 0, (b + 1) * 16:(b + 2) * 16], op=ALU.add), reads=["sm1", "sm0"], writes=["sm1"])
                pl.op("dve", lambda e: e.tensor_tensor(out=cd[:, 1, 0:16], in0=sm[:, 1, 0:16], in1=sm[:, 0, 0:16], op=ALU.add), reads=["sm1", "sm0"], writes=["cd"])
                pl.op("act", lambda e: e.activation(out=cd[:, 0, 0:16], in_=cd[:, 1, 0:16], func=AF.Exp), reads=["cd"], writes=["cd"])
                pl.op("dve", lambda e: e.scalar_tensor_tensor(out=SMA(1), in0=SMA(7), scalar=chsel[0][:, 0:1], in1=SMA(1), op0=ALU.mult, op1=ALU.add), reads=["sm7", "cst", "sm1"], writes=["sm1"])
                pl.op("dve", lambda e: e.tensor_tensor(out=SMA(5), in0=SMA(5), in1=SMA(1), op=ALU.add), reads=["sm5", "sm1"], writes=["sm5"])
            pl.op("act", lambda e: e.activation(out=SMA(5), in_=SMA(5), func=AF.Exp), reads=["sm5"], writes=["sm5"])
            pl.op("dve", lambda e: e.tensor_tensor(out=SMA(6), in0=SMA(5), in1=SMA(2), op=ALU.mult), reads=["sm5", "sm2"], writes=["sm6"])
            if mode != "light":
                pl.op("act", lambda e: e.activation(out=cd[:, 0, 0:W], in_=psB[:, 0, 128:128 + W], func=AF.Exp), reads=["psB0"], writes=["cd"])
                pl.op("act", lambda e: e.activation(out=cd[:, 1, 0:W], in_=psB[:, 0, 192:192 + W], func=AF.Exp), reads=["psB0"], writes=["cd"])
            else:
                if st_mode[0] is not None and st_mode[0][0] == "zero":
                    pl.op("dve", lambda e: e.memset(ST[:], 0.0), writes=["ST"])
                pl.op("pool", lambda e: e.tensor_tensor(out=ST[:].rearrange("p (h q) -> p h q", h=16), in0=ST[:].rearrange("p (h q) -> p h q", h=16), in1=cd[:, 0, 0:16].unsqueeze(2).broadcast_to([128, 16, 64]), op=ALU.mult), reads=["ST", "cd"], writes=["ST"])
            for b in range(nsub):
                tsl = slice(b * 128, (b + 1) * 128)
                SM = lambda idx, b=b: sm[:, idx, b * 16:(b + 1) * 16]
                bc16 = lambda idx, SM=SM: SM(idx).unsqueeze(2).broadcast_to([128, 16, 64])
                b16_2, b16_3, b16_4, b16_6 = bc16(2), bc16(3), bc16(4), bc16(6)
                for j in range(8):
                    pl.op("pe", lambda e, j=j, tsl=tsl: e.transpose(out=psT[:, j * 128:(j + 1) * 128], in_=xbf[:, j, tsl], identity=idb[:]), reads=["xbf", "idb"], writes=["psT"], token=(j == 7))
                bview = lambda ap: ap.rearrange("p (h q) -> p h q", h=16)
                if mode == "full":
                    pl.op("dve", lambda e, b16_2=b16_2: e.tensor_tensor(out=bview(xdt[:]), in0=bview(psT[:, :]), in1=b16_2, op=ALU.mult), reads=["psT", "sm2"], writes=["xdt"])
                pl.op("dve", lambda e, b16_6=b16_6: e.tensor_tensor(out=bview(xw[:]), in0=bview(psT[:, :]), in1=b16_6, op=ALU.mult), reads=["psT", "sm6"], writes=["xw"])
                for g in range(4):
                    pl.op("pe", lambda e, g=g, tsl=tsl: e.transpose(out=psT[:, g * 128:(g + 1) * 128], in_=BT[:, g, tsl], identity=idb[:]), reads=["BT", "idb"], writes=["psT"], token=(g == 3))
                pl.op("act", lambda e: e.copy(out=btok[:], in_=psT[:, 0:512]), reads=["psT"], writes=["btok"])

                if mode == "full":
                    for g in range(4):
                        for c in range(2):
                            cs = slice(b * 128 + c * 64, b * 128 + (c + 1) * 64)
                            pl.op("pe", lambda e, g=g, c=c, cs=cs: e.matmul(psC[c * 64:(c + 1) * 64, g * 64:(g + 1) * 64], lhsT=BT[:, g, cs], rhs=CT[:, g, cs], start=True, stop=True), reads=["BT", "CT"], writes=["psCb"], token=(g == 3 and c == 1))
                    pl.op("dve", lambda e: e.tensor_tensor(out=cbtm[:], in0=psC[:, 0:256].rearrange("p (g i) -> p g i", g=4), in1=t64.unsqueeze(1).broadcast_to([128, 4, 64]), op=ALU.mult), reads=["psCb", "cst"], writes=["cbtm"])
                    pl.op("dve", lambda e, b16_3=b16_3: e.tensor_tensor(out=bview(rhs1[:]), in0=b16_3, in1=t64.unsqueeze(1).broadcast_to([128, 16, 64]), op=ALU.mult), reads=["sm3", "cst"], writes=["rhs1"])
                    for hh in range(2):
                        pl.op("pe", lambda e, hh=hh: e.matmul(psA[:, hh, :], lhsT=blk, rhs=rhs1[:, hh * 512:(hh + 1) * 512], start=True, stop=True), reads=["cst", "rhs1"], writes=["psA%d" % hh])
                    pl.op("dve", lambda e, b16_4=b16_4: e.tensor_tensor(out=bview(segs[:]), in0=psA[:, 0:2, :].rearrange("p a (h q) -> p (a h) q", q=64), in1=b16_4, op=ALU.subtract), reads=["psA0", "psA1", "sm4"], writes=["segs"])
                    pl.op("act", lambda e: e.activation(out=segs[:], in_=segs[:], func=AF.Exp), reads=["segs"], writes=["segs"])
                    for g in range(4):
                        pl.op("dve", lambda e, g=g: e.scalar_tensor_tensor(out=Mt[:, g * 256:(g + 1) * 256].rearrange("p (r q) -> p r q", r=4), in0=segs[:, g * 256:(g + 1) * 256].rearrange("p (r q) -> p r q", r=4), scalar=1.0,
                                                                           in1=cbtm[:, g, :].unsqueeze(1).broadcast_to([128, 4, 64]), op0=ALU.min, op1=ALU.mult), reads=["segs", "cbtm"], writes=["Mt"])
                    pl.op("dve", lambda e, b16_3=b16_3: e.tensor_copy(out=bview(rhs1[:]), in_=b16_3), reads=["sm3"], writes=["rhs1"])
                    for a in range(8):
                        pl.op("pe", lambda e, a=a: e.matmul(psA[:, 2 + a // 4, (a % 4) * 128:(a % 4 + 1) * 128], lhsT=rhs1[:, a * 128:(a + 1) * 128], rhs=tri, start=True, stop=True), reads=["rhs1", "cst"], writes=["psA%d" % (2 + a // 4)], token=(a % 4 == 3))
                    pl.op("act", lambda e: e.activation(out=eac[:], in_=psA[:, 2:4, :].rearrange("p a q -> p (a q)"), func=AF.Exp), reads=["psA2", "psA3"], writes=["eac"])

                for c in range(2):
                    ch = b * 2 + c
                    cs = slice(b * 128 + c * 64, b * 128 + (c + 1) * 64)
                    ps_ = slice(c * 64, (c + 1) * 64)
                    if mode != "light" and st_mode[ch] is not None:
                        kind, val = st_mode[ch]
                        if kind == "zero":
                            pl.op("dve", lambda e: e.memset(ST[:], 0.0), writes=["ST"])
                        elif kind == "load":
                            pl.dma("sp", lambda e, val=val: e.dma_start(out=ST[:], in_=sst_d[val, :, :]), "l_ST", writes=["ST"])
                        elif kind == "keep":
                            pass
                        if mode == "full":
                            pl.op("act", lambda e: e.copy(out=STb[:], in_=ST[:]), reads=["ST"], writes=["STb"])
                    if mode != "light":
                        pl.op("pool", lambda e, c=c, b=b: e.tensor_tensor(out=bview(ST[:]), in0=bview(ST[:]), in1=cd[:, c, b * 16:(b + 1) * 16].unsqueeze(2).broadcast_to([128, 16, 64]), op=ALU.mult), reads=["ST", "cd"], writes=["ST"])
                    if mode == "full":
                        for a in range(8):
                            for hh in range(2):
                                h = 2 * a + hh
                                pl.op("pe", lambda e, a=a, hh=hh, h=h, c=c, ps_=ps_: e.matmul(psB[hh * 64:(hh + 1) * 64, a // 4, (a % 4) * 128 + c * 64:(a % 4) * 128 + (c + 1) * 64], lhsT=xdt[ps_, h * 64:(h + 1) * 64], rhs=Mt[ps_, h * 64:(h + 1) * 64], start=True, stop=True),
                                      reads=["xdt", "Mt"], writes=["psB%d" % (a // 4)], token=(hh == 1 and a % 4 == 3))
                        for a in range(8):
                            g = a // 2
                            pl.op("pe", lambda e, a=a, g=g, c=c, cs=cs: e.matmul(psA[:, a // 4, (a % 4) * 128 + c * 64:(a % 4) * 128 + (c + 1) * 64], lhsT=STb[:, a * 128:(a + 1) * 128], rhs=CT[:, g, cs], start=True, stop=True),
                                  reads=["STb", "CT"], writes=["psA%d" % (a // 4)], token=(a % 4 == 3))
                    first = (b == 0 and c == 0)
                    last = (b == nsub - 1 and c == 1)
                    for g in range(4):
                        if mode == "light":
                            bk = (2 if c == 0 else 0) + g // 2
                            pl.op("pe", lambda e, g=g, ps_=ps_, bk=bk, b=b, last=last: e.matmul(psA[:, bk, (g % 2) * 256:(g % 2 + 1) * 256], lhsT=btok[ps_, g * 128:(g + 1) * 128], rhs=xw[ps_, g * 256:(g + 1) * 256], start=(b == 0 and g % 2 == 0), stop=(b == nsub - 1), skip_group_check=True),
                                  reads=["btok", "xw"], writes=["psA%d" % bk], token=(g % 2 == 1))
                        else:
                            pl.op("pe", lambda e, g=g, ps_=ps_: e.matmul(psA[:, 2 + g // 2, (g % 2) * 256:(g % 2 + 1) * 256], lhsT=btok[ps_, g * 128:(g + 1) * 128], rhs=xw[ps_, g * 256:(g + 1) * 256], start=True, stop=True),
                                  reads=["btok", "xw"], writes=["psA%d" % (2 + g // 2)], token=(g % 2 == 1))
                    if mode != "light" or last:
                        pl.op("dve", lambda e: e.tensor_tensor(out=ST[:], in0=ST[:], in1=psA[:, 2:4, :].rearrange("p a q -> p (a q)"), op=ALU.add), reads=["ST", "psA2", "psA3"], writes=["ST"])
                        if mode == "light":
                            pl.op("dve", lambda e: e.tensor_tensor(out=ST[:], in0=ST[:], in1=psA[:, 0:2, :].rearrange("p a q -> p (a q)"), op=ALU.add), reads=["ST", "psA0", "psA1"], writes=["ST"])
                    if mode == "full":
                        pl.op("act", lambda e: e.copy(out=STb[:], in_=ST[:]), reads=["ST"], writes=["STb"])
                        if hist_from != "carry":
                            pl.dma("pool", lambda e, ch=ch: e.dma_start(out=so_d[1 + ch, :, :], in_=ST[:]), "s_ST", reads=["ST"], writes=["so%d" % (1 + ch)])
                if mode == "full":
                    pl.op("dve", lambda e: e.tensor_tensor(out=yo[:], in0=psA[:, 0:2, :].rearrange("p a q -> p (a q)"), in1=eac[:], op=ALU.mult), reads=["psA0", "psA1", "eac"], writes=["yo"])
                    pl.op("dve", lambda e: e.tensor_tensor(out=yo[:], in0=psB[:, :, :].rearrange("p a q -> p (a q)"), in1=yo[:], op=ALU.add), reads=["psB0", "psB1", "yo"], writes=["yo"])
                    for a in range(8):
                        ya = yo[:, a * 128:(a + 1) * 128]
                        pl.op("dve", lambda e, a=a, ya=ya, tsl=tsl: e.scalar_tensor_tensor(out=ya, in0=xbf[:, a, tsl], scalar=pfc(PF_DSK, a), in1=ya, op0=ALU.mult, op1=ALU.add), reads=["xbf", "pf", "yo"], writes=["yo"])
                        pl.op("dve", lambda e, a=a, ya=ya, tsl=tsl: e.tensor_tensor(out=ya, in0=ya, in1=sz[:, a, tsl], op=ALU.mult), reads=["yo", "sz"], writes=["yo"])
                    pl.op("act", lambda e: e.activation(out=Mt[:], in_=yo[:], func=AF.Square), reads=["yo"], writes=["Mt"])
                    for a in range(8):
                        pl.op("act", lambda e, a=a, tsl=tsl: e.activation(out=ycat[:, 8 + a, tsl], in_=yo[:, a * 128:(a + 1) * 128], func=AF.Copy, scale=pfc(PF_NBW, a)), reads=["yo", "pf"], writes=["ycat"])
                        pl.op("pe", lambda e, a=a, b=b: e.matmul(psC[:, 324 + b:325 + b], lhsT=Mt[:, a * 128:(a + 1) * 128], rhs=ones_bf[:, 0:1], start=(a == 0), stop=(a == 7)), reads=["Mt", "ones"], writes=["psCs"], token=(a == 7))

            if mode != "full":
                return
            pl.op("act", lambda e: e.activation(out=stat[:, 24:32], in_=psC[:, 320:328], func=AF.Ln, scale=1.0 / D, bias=EPS), reads=["psCa", "psCs"], writes=["stat"])
            pl.op("act", lambda e: e.activation(out=stat[:, 24:32], in_=stat[:, 24:32], func=AF.Exp, scale=-0.5), reads=["stat"], writes=["stat"])
            for s in range(nsub):
                tsl = slice(s * 128, (s + 1) * 128)
                for hf in range(2):
                    for part in range(2):
                        for kc in range(8):
                            pl.op("pe", lambda e, hf=hf, part=part, kc=kc, tsl=tsl: e.matmul(psA[:, part * 2 + hf, :], lhsT=ycat[:, part * 8 + kc, tsl], rhs=wout[:, part * 8 + kc, hf * 512:(hf + 1) * 512], start=(kc == 0), stop=(kc == 7)),
                                  reads=["ycat", "wout"], writes=["psA%d" % (part * 2 + hf)], token=(kc == 7))
                xb = xt[s % 2]
                xk = "xt%d" % (s % 2)
                pl.dma("sp", lambda e, xb=xb, s=s: e.dma_start(out=xb[:], in_=xs_d[row0 + s * 128:row0 + (s + 1) * 128, :]), "l_" + xk, writes=[xk])
                ob = ost[s % 2]
                ok = "ost%d" % (s % 2)
                pl.op("act", lambda e, s=s: e.activation(out=o1[:], in_=psA[:, 0:2, :].rearrange("p a q -> p (a q)"), func=AF.Copy, scale=stat[:, 24 + s:25 + s]), reads=["psA0", "psA1", "stat"], writes=["o1"])
                pl.op("dve", lambda e, s=s: e.scalar_tensor_tensor(out=o1[:], in0=psA[:, 2:4, :].rearrange("p a q -> p (a q)"), scalar=stat[:, 28 + s:29 + s], in1=o1[:], op0=ALU.mult, op1=ALU.add), reads=["psA2", "psA3", "stat", "o1"], writes=["o1"])
                pl.op("dve", lambda e: e.tensor_tensor(out=o1[:], in0=o1[:], in1=gate[:, gidx, :], op=ALU.mult), reads=["o1", "gate"], writes=["o1"])
                pl.op("dve", lambda e, xb=xb: e.tensor_tensor(out=o1[:], in0=o1[:], in1=xb[:], op=ALU.add), reads=["o1", xk], writes=["o1"])
                pl.op("dve", lambda e, s=s: e.memset(stat[:, 32 + s:33 + s], 0.0), writes=["stat"])
                pl.op("act", lambda e, s=s: e.activation(out=sqj[:], in_=o1[:], func=AF.Square, accum_out=stat[:, 32 + s:33 + s]), reads=["o1", "stat"], writes=["sqj", "stat"])
                pl.op("act", lambda e, s=s: e.activation(out=stat[:, 36 + s:37 + s], in_=stat[:, 32 + s:33 + s], func=AF.Ln, scale=1.0 / D, bias=EPS), reads=["stat"], writes=["stat"])
                pl.op("act", lambda e, s=s: e.activation(out=stat[:, 36 + s:37 + s], in_=stat[:, 36 + s:37 + s], func=AF.Exp, scale=-0.5), reads=["stat"], writes=["stat"])
                pl.op("dve", lambda e, s=s, ob=ob: e.scalar_tensor_tensor(out=ob[:], in0=o1[:], scalar=stat[:, 36 + s:37 + s], in1=bc[:, BC_NF:BC_NF + D], op0=ALU.mult, op1=ALU.mult), reads=["o1", "stat", "bc"], writes=[ok])
                pl.dma("pool", lambda e, ob=ob, s=s: e.dma_start(out=y_d[yrow0 + s * 128:yrow0 + (s + 1) * 128, :], in_=ob[:]), "s_" + ok, reads=[ok], writes=["y"])

        ones_bf = SB("ones_bf", [128, 8], BF16)
        pl.op("dve", lambda e: e.memset(ones_bf[:], 1.0), writes=["ones"])
        sca_sb = SB("sca_sb", [128, 8, 2, 2])
        scb_sb = SB("scb_sb", [128, 16, 2, 3])
        ld("l_sca", sca_sb[:].rearrange("p a b c -> p (a b c)"), sca_d[:, :], "sca")
        ld("l_scb", scb_sb[:].rearrange("p a b c -> p (a b c)"), scb_d[:, :], "scb")
        pl.op("dve", lambda e: e.memset(uhist[:], 0.0), writes=["uhist"])
        pl.op("dve", lambda e: e.memset(xhist[:], 0.0), writes=["xhist"])
        pl.op("dve", lambda e: e.memset(atot[:], 0.0), writes=["atot"])
        pslot = [(0, T, 0)]
        for n in range(NLSEG * NT):
            stm = [None] * (T // 64)
            if n == 0:
                stm[0] = ("zero", 0)
            nxt = 128 + (n + 1) * T if n + 1 < NLSEG * NT else None
            tile(128 + n * T, T, pslot, "light", 0, 0, "carry", stm, None, next_row0=nxt)
            if n % NT == NT - 1:
                m = n // NT
                pl.op("dve", lambda e, m=m: e.tensor_scalar_mul(out=ST[:], in0=ST[:], scalar1=msk[:, 1 + m:2 + m]), reads=["ST", "msk"], writes=["ST"])
                pl.op("dve", lambda e, m=m: e.tensor_scalar_mul(out=xhist[:].rearrange("p a b -> p (a b)"), in0=xhist[:].rearrange("p a b -> p (a b)"), scalar1=msk[:, 1 + m:2 + m]), reads=["xhist", "msk"], writes=["xhist"])
        tile(0, 128, [(0, 128, 0)], "halo", 0, 0, "carry", [None, None], None)
        for n in range(NT):
            stm = [None] * (T // 64)
            if n == 0:
                stm[0] = ("keep", 0)
            tile(128 + LROWS + n * T, T, pslot, "full", 0, n * T, "carry", stm, 0)
        pl.dma("pool", lambda e: e.dma_start(out=so_d[0, :, :], in_=ST[:]), "s_ST", reads=["ST"], writes=["so0"])
        tile(128 + LROWS + SEGLEN, 128, [(0, 64, 1), (64, 128, 2)], "full", 1, SEGLEN, "state", [("load", 0), ("load", 1)], 1)
        pl.dma("pool", lambda e: e.dma_start(out=ca_d[:, :], in_=casb[:].rearrange("p a b c -> p (a b c)")), "s_ca", reads=["casb"], writes=["ca"])
        pl.dma("pool", lambda e: e.dma_start(out=cb_d[:, :], in_=cbsb[:].rearrange("p a b c -> p (a b c)")), "s_cb", reads=["cbsb"], writes=["cb"])
        if debug:
            dbg_list = [("xbf", xbf[:].rearrange("p a b -> p (a b)"), 8 * T, BF16), ("BT", BT[:].rearrange("p a b -> p (a b)"), 4 * T, BF16),
                        ("CT", CT[:].rearrange("p a b -> p (a b)"), 4 * T, BF16), ("sm", sm[:].rearrange("p a b -> p (a b)"), 256, F32),
                        ("ycat", ycat[:].rearrange("p a b -> p (a b)"), 16 * T, BF16), ("yo", yo[:], 1024, F32), ("xw", xw[:], 1024, BF16),
                        ("xdt", xdt[:], 1024, BF16), ("btok", btok[:], 512, BF16), ("Mt", Mt[:], 1024, BF16), ("eac", eac[:], 1024, F32),
                        ("cd", cd[:].rearrange("p a b -> p (a b)"), 32, F32), ("hT", hT[:].rearrange("p a b -> p (a b)"), 8 * T, BF16),
                        ("sz", sz[:].rearrange("p a b -> p (a b)"), 8 * T, BF16), ("stat", stat[:], 64, F32), ("gam", gam[:].rearrange("p a b -> p (a b)"), 24, F32)]
            allk = list(pl.bufs.keys())
            for nm, ap, w, dt_ in dbg_list:
                dd = nc.dram_tensor("dbg_" + nm, [128, w], dt_, kind="ExternalOutput").ap()
                pl.dma("pool", lambda e, dd=dd, ap=ap: e.dma_start(out=dd[:, :], in_=ap), "s_dbg", reads=allk)
        pl.wait_tokens("pool", [(s, c) for s, c in pl.dma_cnt.items() if s.startswith("s_")])
        print("planned instructions:", pl.nins, {e: len(pl.lists[e]) for e in pl.ENGS})
        pl.emit()
    return nc


def _host_consts():
    c = np.zeros((128, NCONST), np.float32)
    k = np.arange(128)
    c[:, C_ID:C_ID + 128] = np.eye(128, dtype=np.float32)
    same = (k[:, None] // 64) == (k[None, :] // 64)
    c[:, C_BLK:C_BLK + 128] = same
    c[:, C_TRI:C_TRI + 128] = same & (k[:, None] <= k[None, :])
    c[:, C_T64:C_T64 + 64] = (k[:, None] % 64) <= np.arange(64)[None, :]
    c[:, C_SEL0:C_SEL0 + 128] = (k[:, None] < 64)
    c[:, C_SEL1:C_SEL1 + 128] = (k[:, None] >= 64)
    return c


def _fm(v, nchunk):
    return np.ascontiguousarray(np.asarray(v, np.float32).reshape(nchunk, 128).T)


_NC_CACHE = {}


def kernel(x_prompt, x_sample, state_conv_a, state_conv_b, state_ssm, c_prompt, c_sample,
           w_mod, b_mod, norm_in_w, w_in, conv_a_w, norm_a_w, conv_b_w, conv_b_b,
           dt_bias, a_log, d_skip, norm_b_w, w_out, norm_f_w, _two_phase=True, _debug=False):
    f = lambda a: np.ascontiguousarray(np.asarray(a, np.float32))
    x_prompt, x_sample = f(x_prompt), f(x_sample)
    state_conv_a, state_conv_b, state_ssm = f(state_conv_a), f(state_conv_b), f(state_ssm)
    c_prompt, c_sample = f(c_prompt), f(c_sample)
    w_mod, b_mod, w_in, w_out = f(w_mod)[0], f(b_mod)[0], f(w_in)[0], f(w_out)[0]
    pf = np.zeros((128, NPF), np.float32)
    pf[:, PF_NIN:PF_NIN + 8] = _fm(f(norm_in_w)[0], 8)
    caw = f(conv_a_w)[0]
    pf[:, PF_CAW:PF_CAW + 24] = np.stack([_fm(caw[k], 8) for k in range(3)], axis=2).reshape(128, 24)
    pf[:, PF_NAW:PF_NAW + 8] = _fm(f(norm_a_w)[0], 8)
    cbw = f(conv_b_w)[0]
    pf[:, PF_CBW:PF_CBW + 64] = np.stack([_fm(cbw[k], 16) for k in range(4)], axis=2).reshape(128, 64)
    pf[:, PF_CBB:PF_CBB + 16] = _fm(f(conv_b_b)[0], 16)
    pf[:, PF_NBW:PF_NBW + 8] = _fm(f(norm_b_w)[0], 8)
    pf[:, PF_DSK:PF_DSK + 8] = _fm(np.repeat(f(d_skip)[0], 64), 8)
    pf[:, PF_BSH:PF_BSH + 8] = _fm(b_mod[0:D], 8)
    pf[:, PF_BSC:PF_BSC + 8] = _fm(b_mod[D:2 * D], 8)
    bcv = np.zeros((128, NBC), np.float32)
    bcv[:, BC_NF:BC_NF + D] = f(norm_f_w)[None, :]
    bcv[:, BC_BG:BC_BG + D] = b_mod[None, 2 * D:3 * D]
    bcv[:, BC_DTB:BC_DTB + 16] = f(dt_bias)[0][None, :]
    bcv[:, BC_ALOG:BC_ALOG + 16] = f(a_log)[0][None, :]
    cst = _host_consts()

    in_maps = []
    for k in range(NCORES):
        seq, seg = k // 4, k % 4
        start = seg * SEGLEN
        xs = np.zeros((XROWS, D), np.float32)
        if seg > 0:
            xs[0:128] = x_prompt[seq, start - 128:start]
        if seg > 0 and LROWS >= start:
            xs[128 + LROWS - start:128 + LROWS] = x_prompt[seq, 0:start]
        xs[128 + LROWS:128 + LROWS + SEGLEN] = x_prompt[seq, start:start + SEGLEN]
        xs[128 + LROWS + SEGLEN:] = x_sample[2 * k:2 * k + 2].reshape(128, D)
        cs = [c_prompt[seq], c_sample[2 * k], c_sample[2 * k + 1]]
        cT = np.stack([_fm(c, 8) for c in cs], axis=2).reshape(128, 24)
        cbc = np.zeros((2, 128, 8, 128), np.float32)
        cbc[0] = _fm(cs[0], 8)[:, :, None]
        cbc[1, :, :, 0:64] = _fm(cs[1], 8)[:, :, None]
        cbc[1, :, :, 64:128] = _fm(cs[2], 8)[:, :, None]
        sca = state_conv_a[0, 2 * k:2 * k + 2]
        sca = sca.reshape(2, 2, 8, 128).transpose(3, 2, 0, 1).reshape(128, 32)
        scb = state_conv_b[0, 2 * k:2 * k + 2]
        scb = scb.reshape(2, 3, 16, 128).transpose(3, 2, 0, 1).reshape(128, 96)
        sst = state_ssm[0, 2 * k:2 * k + 2]
        sst = sst.reshape(2, 1024, 128).transpose(0, 2, 1)
        msk = np.zeros((128, 16), np.float32)
        msk[:, 0] = 1.0 if seg > 0 else 0.0
        for m in range(NLSEG):
            msk[:, 1 + m] = 0.0 if m < NLSEG - seg else 1.0
        in_maps.append({
            "xs": xs, "w_mod": w_mod, "w_in": w_in, "w_out": w_out, "pf": pf, "bc": bcv, "cst": cst,
            "cT": np.ascontiguousarray(cT), "cbc": np.ascontiguousarray(cbc.reshape(2, 128, 1024)),
            "sca": np.ascontiguousarray(sca), "scb": np.ascontiguousarray(scb),
            "sst": np.ascontiguousarray(sst), "msk": msk,
        })
    key = (bool(_two_phase), bool(_debug))
    if key not in _NC_CACHE:
        _NC_CACHE[key] = build_nc(two_phase=key[0], debug=key[1])
    nc = _NC_CACHE[key]
    res = run_bass_kernel_spmd(nc, in_maps, core_ids=list(range(NCORES)))
    R = res.results
    if _debug:
        kernel.last_results = R
    y_prompt = np.zeros((2, SEQ, D), np.float32)
    y_sample = np.zeros((16, 64, D), np.float32)
    ca_p = np.zeros((1, 2, 2, D), np.float32)
    cb_p = np.zeros((1, 2, 3, 2 * D), np.float32)
    ss_p = np.zeros((1, 2, 16, 64, 128), np.float32)
    ca_s = np.zeros((1, 16, 2, D), np.float32)
    cb_s = np.zeros((1, 16, 3, 2 * D), np.float32)
    ss_s = np.zeros((1, 16, 16, 64, 128), np.float32)
    for k in range(NCORES):
        seq, seg = k // 4, k % 4
        r = R[k]
        y_prompt[seq, seg * SEGLEN:(seg + 1) * SEGLEN] = r["y"][0:SEGLEN]
        y_sample[2 * k:2 * k + 2] = r["y"][SEGLEN:].reshape(2, 64, D)
        ca = r["ca"].reshape(128, 8, 3, 2)
        cb = r["cb"].reshape(128, 16, 3, 3)
        so = r["so"]
        for s in range(2):
            ca_s[0, 2 * k + s] = ca[:, :, 1 + s, :].transpose(2, 1, 0).reshape(2, D)
            cb_s[0, 2 * k + s] = cb[:, :, 1 + s, :].transpose(2, 1, 0).reshape(3, 2 * D)
            ss_s[0, 2 * k + s] = so[1 + s].T.reshape(16, 64, 128)
        if seg == 3:
            ca_p[0, seq] = ca[:, :, 0, :].transpose(2, 1, 0).reshape(2, D)
            cb_p[0, seq] = cb[:, :, 0, :].transpose(2, 1, 0).reshape(3, 2 * D)
            ss_p[0, seq] = so[0].T.reshape(16, 64, 128)
    return (y_prompt, y_sample, ca_p, cb_p, ss_p, ca_s, cb_s, ss_s)
```

```python
import contextlib
import numpy as np
import concourse.bass as bass
import concourse.mybir as mybir
from concourse.bass_utils import run_bass_kernel_spmd

F32 = mybir.dt.float32
BF16 = mybir.dt.bfloat16
ALU = mybir.AluOpType
AF = mybir.ActivationFunctionType

NCORES = 8
D = 1024
SEQ = 16384
SEGLEN = 4096
T = 512
NT = SEGLEN // T
DIN = 7184
PW = 128
NPIECE = 56
NWB = 12
EPS = 1e-5
NLSEG = 3
LROWS = NLSEG * SEGLEN
XROWS = 128 + LROWS + SEGLEN + 128
YROWS = SEGLEN + 128

PF_NIN = 0
PF_CAW = 8
PF_NAW = 32
PF_CBW = 40
PF_CBB = 104
PF_NBW = 120
PF_DSK = 128
PF_BSH = 136
PF_BSC = 144
NPF = 152
BC_NF = 0
BC_BG = 1024
BC_DTB = 2048
BC_ALOG = 2064
NBC = 2080
C_ID = 0
C_BLK = 128
C_TRI = 256
C_T64 = 384
C_SEL0 = 448
C_SEL1 = 576
NCONST = 704


class Planner:
    ENGS = ("pe", "act", "dve", "pool", "sp")
    SEM_LIMIT = 30000

    def __init__(self, nc):
        self.nc = nc
        self.lists = {e: [] for e in self.ENGS}
        self.cur = {e: [e + "_0", 0] for e in self.ENGS}
        self.gen = {e: 0 for e in self.ENGS}
        self.sem_names = [e + "_0" for e in self.ENGS]
        self.dma_cnt = {}
        self.waited = {e: {} for e in self.ENGS}
        self.bufs = {}
        self.nins = 0
        self.alias = {}
        self.bank_last = {}

    @staticmethod
    def _bank(k):
        if k.startswith("psC"):
            return "psC"
        if k.startswith("psA") or k.startswith("psB") or k == "psT":
            return k
        return None

    def _bank_deps(self, eng, keys, need):
        banks = set(b for b in (self._bank(k) for k in keys) if b)
        for b in banks:
            for oe, tok in self.bank_last.get(b, {}).items():
                if oe != eng:
                    self._need(eng, tok, need)
        return banks

    def _exp(self, keys):
        out = []
        for k in keys:
            out.extend(self.alias.get(k, (k,)))
        return out

    def _need(self, eng, tok, out):
        if tok is None:
            return
        s, v = tok
        if eng == "pe" and s.startswith("pe_"):
            return
        if self.waited[eng].get(s, 0) >= v:
            return
        if out.get(s, 0) < v:
            out[s] = v

    def _deps(self, eng, reads, writes):
        reads, writes = self._exp(reads), self._exp(writes)
        need = {}
        for k in reads:
            b = self.bufs.get(k)
            if b:
                self._need(eng, b[0], need)
        for k in writes:
            b = self.bufs.get(k)
            if b:
                self._need(eng, b[0], need)
                for t in b[1]:
                    self._need(eng, t, need)
        self._cur_banks = self._bank_deps(eng, list(reads) + list(writes), need)
        self._cur_eng = eng
        for s, v in need.items():
            self.waited[eng][s] = v
            self.lists[eng].append(("wait", s, v))

    def _mark(self, tok, reads, writes):
        reads, writes = self._exp(reads), self._exp(writes)
        for b in self._cur_banks:
            self.bank_last.setdefault(b, {})[self._cur_eng] = tok
        for k in reads:
            b = self.bufs.setdefault(k, [None, []])
            b[1].append(tok)
        for k in writes:
            self.bufs[k] = [tok, []]

    def op(self, eng, fn, reads=(), writes=(), token=True):
        self._deps(eng, reads, writes)
        c = self.cur[eng]
        self.nins += 1
        if token:
            if c[1] >= self.SEM_LIMIT:
                self.gen[eng] += 1
                c[0] = "%s_%d" % (eng, self.gen[eng])
                c[1] = 0
                self.sem_names.append(c[0])
            c[1] += 1
            tok = (c[0], c[1])
            self.lists[eng].append(("ins", fn, c[0], 1))
        else:
            tok = (c[0], c[1] + 1)
            self.lists[eng].append(("ins", fn, None, 0))
        self._mark(tok, reads, writes)
        return tok

    def dma(self, eng, fn, sem, reads=(), writes=()):
        self._deps(eng, reads, writes)
        self.nins += 1
        if sem not in self.dma_cnt:
            self.dma_cnt[sem] = 0
            self.sem_names.append(sem)
        self.dma_cnt[sem] += 16
        tok = (sem, self.dma_cnt[sem])
        self.lists[eng].append(("ins", fn, sem, 16))
        self._mark(tok, reads, writes)
        return tok

    def raw(self, eng, fn, sem, inc, reads=(), writes=()):
        self._deps(eng, reads, writes)
        if sem not in self.dma_cnt:
            self.dma_cnt[sem] = 0
            self.sem_names.append(sem)
        self.dma_cnt[sem] += inc
        tok = (sem, self.dma_cnt[sem])
        self.lists[eng].append(("ins", fn, sem, -inc))
        self._mark(tok, reads, writes)
        return tok

    def wait_tokens(self, eng, toks):
        need = {}
        for t in toks:
            self._need(eng, t, need)
        for s, v in need.items():
            self.waited[eng][s] = v
            self.lists[eng].append(("wait", s, v))

    def emit(self):
        nc = self.nc
        with contextlib.ExitStack() as st:
            sems = {}
            for n in self.sem_names:
                sems[n] = st.enter_context(nc.semaphore(n))
            block = st.enter_context(nc.Block())
            engmap = {"pe": block.tensor, "act": block.scalar, "dve": block.vector,
                      "pool": block.gpsimd, "sp": block.sync}
            for e in self.ENGS:
                lst = self.lists[e]
                if not lst:
                    continue

                def body(engobj, lst=lst):
                    for it in lst:
                        if it[0] == "wait":
                            engobj.wait_ge(sems[it[1]], it[2])
                        else:
                            ins = it[1](engobj)
                            if it[2] is not None:
                                if it[3] < 0:
                                    ins.then_inc(sems[it[2]])
                                else:
                                    ins.then_inc(sems[it[2]], it[3])
                engmap[e](body)


def build_nc(two_phase=True, debug=False):
    nc = bass.Bass("TRN2", target_bir_lowering=False)
    dr = lambda n, s, k, d=F32: nc.dram_tensor(n, list(s), d, kind=k)
    xs_d = dr("xs", [XROWS, D], "ExternalInput").ap()
    wmod_d = dr("w_mod", [D, 3 * D], "ExternalInput").ap()
    win_d = dr("w_in", [D, DIN], "ExternalInput").ap()
    wout_d = dr("w_out", [2 * D, D], "ExternalInput").ap()
    pf_d = dr("pf", [128, NPF], "ExternalInput").ap()
    bc_d = dr("bc", [128, NBC], "ExternalInput").ap()
    cst_d = dr("cst", [128, NCONST], "ExternalInput").ap()
    cT_d = dr("cT", [128, 8 * 3], "ExternalInput").ap()
    cbc_d = dr("cbc", [2, 128, 8 * 128], "ExternalInput").ap()
    sca_d = dr("sca", [128, 8 * 2 * 2], "ExternalInput").ap()
    scb_d = dr("scb", [128, 16 * 2 * 3], "ExternalInput").ap()
    sst_d = dr("sst", [2, 128, D], "ExternalInput").ap()
    msk_d = dr("msk", [128, 16], "ExternalInput").ap()
    y_d = dr("y", [YROWS, D], "ExternalOutput").ap()
    ca_d = dr("ca", [128, 8 * 3 * 2], "ExternalOutput").ap()
    cb_d = dr("cb", [128, 16 * 3 * 3], "ExternalOutput").ap()
    so_d = dr("so", [3, 128, D], "ExternalOutput").ap()
    winbf_d = nc.dram_tensor("winbf", [NPIECE, 128, 8 * PW], BF16)

    pl = Planner(nc)
    pl.alias = {"stg0": ("rhs1", "segs", "eac", "yo"), "stg1": ("stsb", "o1", "ost0", "ost1"), "sqj": ("Mt",)}
    with contextlib.ExitStack() as st:
        def SB(name, shape, dt=F32):
            return st.enter_context(nc.sbuf_tensor("sb_" + name, list(shape), dt))

        cst = SB("cst", [128, NCONST])
        idb = SB("idb", [128, 128], BF16)
        pf = SB("pf", [128, NPF])
        bc = SB("bc", [128, NBC])
        msk = SB("msk", [128, 16])
        cT = SB("cT", [128, 8, 3])
        gam = SB("gam", [128, 3, 8])
        bet = SB("bet", [128, 3, 8])
        gate = SB("gate", [128, 2, D])
        caw = SB("caw", [128, 8, 3])
        cbw = SB("cbw", [128, 16, 4])
        cbb = SB("cbb", [128, 16])
        a_bc = SB("a_bc", [128, 16])
        wout = SB("wout", [128, 16, D], BF16)
        wdt = SB("wdt", [128, 8, 16], BF16)
        wbuf = [SB("wbuf%d" % i, [128, 8, PW], BF16) for i in range(NWB)]
        big = [SB("big%d" % i, [128, 4096]) for i in range(2)]
        stg = [b[:].rearrange("p (a c) -> p a c", a=8) for b in big]
        xt = [SB("xt%d" % i, [128, D]) for i in range(2)]
        hT = SB("hT", [128, 8, T], BF16)
        ubuf = [SB("ubuf%d" % i, [128, 3 + T]) for i in range(2)]
        hsb = [SB("hsb%d" % i, [128, T]) for i in range(2)]
        cu = [SB("cu%d" % i, [128, T]) for i in range(2)]
        tq = [SB("tq%d" % i, [128, T]) for i in range(2)]
        sq2 = [SB("sq2%d" % i, [128, T], BF16) for i in range(2)]
        uhist = SB("uhist", [128, 8, 3])
        xhist = SB("xhist", [128, 16, 3])
        uhist0 = SB("uhist0", [128, 8, 3])
        xhist0 = SB("xhist0", [128, 16, 3])
        ycat = SB("ycat", [128, 16, T], BF16)
        xbf = SB("xbf", [128, 8, T], BF16)
        BT = SB("BT", [128, 4, T], BF16)
        CT = SB("CT", [128, 4, T], BF16)
        sz = SB("sz", [128, 8, T], BF16)
        xdt = SB("xdt", [128, D], BF16)
        xw = SB("xw", [128, D], BF16)
        btok = SB("btok", [128, 512], BF16)
        rhs1 = big[0][:, 0:1024]
        segs = big[0][:, 1024:2048]
        Mt = SB("Mt", [128, D], BF16)
        sqj = Mt
        eac = big[0][:, 2048:3072]
        yo = big[0][:, 3072:4096]
        cbtm = SB("cbtm", [128, 4, 64])
        sm = SB("sm", [128, 8, 64])
        ST = SB("ST", [128, D])
        STb = SB("STb", [128, D], BF16)
        stsb = big[1][:, 0:1024]
        cd = SB("cd", [128, 2, 64])
        stat = SB("stat", [128, 64])
        ost = [big[1][:, 2048:3072], big[1][:, 3072:4096]]
        o1 = big[1][:, 1024:2048]
        casb = SB("casb", [128, 8, 3, 2])
        cbsb = SB("cbsb", [128, 16, 3, 3])
        atot = SB("atot", [128, 16])
        gsel = SB("gsel", [128, 8, 16])
        print("sbuf remaining after alloc:", nc.sbuf_bytes_remaining)

        psA = st.enter_context(nc.psum_tensor("psA", [128, 4, 512], F32))
        psB = st.enter_context(nc.psum_tensor("psB", [128, 2, 512], F32))
        psC = st.enter_context(nc.psum_tensor("psC", [128, 512], F32))
        psT = st.enter_context(nc.psum_tensor("psT", [128, 1024], BF16))

        ident = cst[:, C_ID:C_ID + 128]
        blk = cst[:, C_BLK:C_BLK + 128]
        tri = cst[:, C_TRI:C_TRI + 128]
        t64 = cst[:, C_T64:C_T64 + 64]
        chsel = [cst[:, C_SEL0:C_SEL0 + 128], cst[:, C_SEL1:C_SEL1 + 128]]

        def pfc(off, j):
            return pf[:, off + j:off + j + 1]

        ld = lambda name, dst, src, key: pl.dma("sp", lambda e: e.dma_start(out=dst, in_=src), name, writes=[key])
        ld("l_cst", cst[:], cst_d[:, :], "cst")
        ld("l_pf", pf[:], pf_d[:, :], "pf")
        ld("l_bc", bc[:], bc_d[:, :], "bc")
        ld("l_msk", msk[:], msk_d[:, :], "msk")
        ld("l_cT", cT[:].rearrange("p a b -> p (a b)"), cT_d[:, :], "cT")
        pl.op("dve", lambda e: e.tensor_copy(out=idb[:], in_=ident), reads=["cst"], writes=["idb"])
        pl.op("dve", lambda e: e.tensor_scalar_mul(out=caw[:].rearrange("p a b -> p (a b)"), in0=pf[:, PF_CAW:PF_CAW + 24], scalar1=1.0), reads=["pf"], writes=["caw"])
        pl.op("dve", lambda e: e.tensor_scalar_mul(out=cbw[:].rearrange("p a b -> p (a b)"), in0=pf[:, PF_CBW:PF_CBW + 64], scalar1=1.0), reads=["pf"], writes=["cbw"])
        pl.op("dve", lambda e: e.tensor_scalar_mul(out=cbb[:], in0=pf[:, PF_CBB:PF_CBB + 16], scalar1=1.0), reads=["pf"], writes=["cbb"])
        pl.op("act", lambda e: e.activation(out=a_bc[:], in_=bc[:, BC_ALOG:BC_ALOG + 16], func=AF.Exp), reads=["bc"], writes=["a_bc"])
        pl.op("dve", lambda e: e.tensor_scalar_mul(out=a_bc[:], in0=a_bc[:], scalar1=-1.0), reads=["a_bc"], writes=["a_bc"])

        wmod_v = wmod_d.rearrange("(kc p) c -> p kc c", p=128)
        for piece in range(6):
            s = stg[piece % 2]
            key = "stg%d" % (piece % 2)
            pl.dma("sp", lambda e, s=s, piece=piece: e.dma_start(out=s[:], in_=wmod_v[:, :, piece * 512:(piece + 1) * 512]), "l_" + key, writes=[key])
            if piece < 4:
                for cc in range(4):
                    j = (piece % 2) * 4 + cc
                    for kc in range(8):
                        pl.op("pe", lambda e, s=s, cc=cc, kc=kc, j=j: e.matmul(psC[:, j * 4:j * 4 + 3], lhsT=s[:, kc, cc * 128:(cc + 1) * 128], rhs=cT[:, kc, :], start=(kc == 0), stop=(kc == 7)),
                              reads=[key, "cT"], writes=["psC"], token=(kc == 7))
                if piece % 2 == 1:
                    src = psC[:, 0:32].rearrange("p (j s) -> p s j", s=4)[:, 0:3, :]
                    if piece == 1:
                        pl.op("dve", lambda e, src=src: e.tensor_tensor(out=bet[:], in0=src, in1=pf[:, PF_BSH:PF_BSH + 8].unsqueeze(1).broadcast_to([128, 3, 8]), op=ALU.add), reads=["psC", "pf"], writes=["bet"])
                    else:
                        pl.op("dve", lambda e, src=src: e.tensor_tensor(out=gam[:], in0=src, in1=pf[:, PF_BSC:PF_BSC + 8].unsqueeze(1).broadcast_to([128, 3, 8]), op=ALU.add), reads=["psC", "pf"], writes=["gam"])
                        pl.op("dve", lambda e: e.scalar_tensor_tensor(out=gam[:], in0=gam[:], scalar=1.0, in1=pf[:, PF_NIN:PF_NIN + 8].unsqueeze(1).broadcast_to([128, 3, 8]), op0=ALU.add, op1=ALU.mult), reads=["gam", "pf"], writes=["gam"])
            else:
                half = piece - 4
                for which in range(2):
                    cb_t = xt[which]
                    if half == 0:
                        pl.dma("sp", lambda e, cb_t=cb_t, which=which: e.dma_start(out=cb_t[:], in_=cbc_d[which, :, :]), "l_xt%d" % which, writes=["xt%d" % which])
                    cbv = cb_t[:].rearrange("p (k m) -> p k m", k=8)
                    for kc in range(8):
                        pl.op("pe", lambda e, s=s, kc=kc, cbv=cbv, which=which: e.matmul(psA[:, which, :], lhsT=cbv[:, kc, :], rhs=s[:, kc, :], start=(kc == 0), stop=(kc == 7)),
                              reads=[key, "xt%d" % which], writes=["psA%d" % which], token=(kc == 7))
                    pl.op("dve", lambda e, which=which, half=half: e.tensor_tensor(out=gate[:, which, half * 512:(half + 1) * 512], in0=psA[:, which, :], in1=bc[:, BC_BG + half * 512:BC_BG + (half + 1) * 512], op=ALU.add),
                          reads=["psA%d" % which, "bc"], writes=["gate"])

        win_v = win_d.rearrange("(kc p) c -> p kc c", p=128)
        wout_v = wout_d.rearrange("(kc p) c -> p kc c", p=128)
        castengs = ["dve", "act", "pool"]
        ci = 0

        def cast(dst, src, rk, wk):
            nonlocal ci
            eng = castengs[ci % 3]
            ci += 1
            if eng == "act":
                pl.op("act", lambda e: e.copy(out=dst, in_=src), reads=rk, writes=wk)
            else:
                pl.op(eng, lambda e: e.tensor_copy(out=dst, in_=src), reads=rk, writes=wk)

        porder = [10, 11, 12, 2, 3, 4, 5, 0, 1, 6, 7, 13, 8, 9]
        pl.dma("sp", lambda e: e.dma_start(out=stg[0][:, :, 0:16], in_=win_v[:, :, 7168:7184]), "l_stg0", writes=["stg0"])
        pl.op("dve", lambda e: e.tensor_copy(out=wdt[:], in_=stg[0][:, :, 0:16]), reads=["stg0"], writes=["wdt"])
        wci = 0
        for n, piece in enumerate(porder):
            s = stg[n % 2]
            key = "stg%d" % (n % 2)
            pl.dma("sp", lambda e, s=s, piece=piece: e.dma_start(out=s[:], in_=win_v[:, :, piece * 512:(piece + 1) * 512]), "l_" + key, writes=[key])
            for hp in range(512 // PW):
                wb = wbuf[wci % NWB]
                wkey = "wbuf%d" % (wci % NWB)
                wci += 1
                for kh in range(2):
                    cast(wb[:, kh * 4:(kh + 1) * 4, :], s[:, kh * 4:(kh + 1) * 4, hp * PW:(hp + 1) * PW], [key], [wkey])
                sp_ = (512 // PW) * piece + hp
                pl.dma("pool", lambda e, wb=wb, sp_=sp_: e.dma_start(out=winbf_d[sp_, :, :], in_=wb[:].rearrange("p a b -> p (a b)")), "s_" + wkey, reads=[wkey], writes=["winbf%d" % sp_])
        for n in range(4):
            kg, ch = n // 2, n % 2
            s = stg[n % 2]
            key = "stg%d" % (n % 2)
            pl.dma("sp", lambda e, s=s, kg=kg, ch=ch: e.dma_start(out=s[:], in_=wout_v[:, kg * 8:(kg + 1) * 8, ch * 512:(ch + 1) * 512]), "l_" + key, writes=[key])
            for kh in range(2):
                cast(wout[:, kg * 8 + kh * 4:kg * 8 + (kh + 1) * 4, ch * 512:(ch + 1) * 512], s[:, kh * 4:(kh + 1) * 4, :], [key], ["wout"])

        wcnt = [0]

        def load_piece(piece):
            i = wcnt[0] % NWB
            wcnt[0] += 1
            wb = wbuf[i]
            pl.dma("sp", lambda e: e.dma_start(out=wb[:].rearrange("p a b -> p (a b)"), in_=winbf_d[piece, :, :]), "l_wbuf%d" % i, reads=["winbf%d" % piece], writes=["wbuf%d" % i])
            return wb, "wbuf%d" % i

        hcur = [hT, "hT"]

        def proj_chunk(wb, wkey, cc, ps_ap, pskey, Tn):
            hb, hk = hcur
            for kc in range(8):
                pl.op("pe", lambda e, kc=kc: e.matmul(ps_ap, lhsT=wb[:, kc, cc * 128:(cc + 1) * 128], rhs=hb[:, kc, 0:Tn], start=(kc == 0), stop=(kc == 7)),
                      reads=[wkey, hk], writes=[pskey], token=(kc == 7))

        pref = {}

        def load_x(row0, s):
            xb = xt[s % 2]
            xk = "xt%d" % (s % 2)
            pl.dma("sp", lambda e: e.dma_start(out=xb[:], in_=xs_d[row0 + s * 128:row0 + (s + 1) * 128, :]), "l_" + xk, writes=[xk])

        def step_a(row0, Tn, slots, hb=None, hk="hT"):
            for p0 in range(0, Tn // 128, 2):
                step_a_pair(row0, p0, Tn, slots, hT if hb is None else hb, hk)

        def step_a_pair(row0, p0, Tn, slots, hb, hk):
            nsub = Tn // 128
            if True:
                subs = list(range(p0, min(p0 + 2, nsub)))
                for s in subs:
                    xb = xt[s % 2]
                    xk = "xt%d" % (s % 2)
                    if not pref.pop((row0, s), False):
                        load_x(row0, s)
                    pl.op("pool", lambda e, s=s: e.memset(stat[:, s:s + 1], 0.0), writes=["stat"])
                    pl.op("act", lambda e, xb=xb, s=s: e.activation(out=sqj[:], in_=xb[:], func=AF.Square, accum_out=stat[:, s:s + 1]), reads=[xk, "stat"], writes=["sqj", "stat"])
                    pl.op("act", lambda e, s=s: e.activation(out=stat[:, 8 + s:9 + s], in_=stat[:, s:s + 1], func=AF.Ln, scale=1.0 / D, bias=EPS), reads=["stat"], writes=["stat"])
                    pl.op("act", lambda e, s=s: e.activation(out=stat[:, 16 + s:17 + s], in_=stat[:, 8 + s:9 + s], func=AF.Exp, scale=-0.5), reads=["stat"], writes=["stat"])
                    pl.op("dve", lambda e, xb=xb, s=s: e.tensor_scalar_mul(out=xb[:], in0=xb[:], scalar1=stat[:, 16 + s:17 + s]), reads=[xk, "stat"], writes=[xk])
                w0, w1 = subs[0] * 128, (subs[-1] + 1) * 128
                for kc in range(8):
                    pk = "psB%d" % (kc % 2)
                    for s in subs:
                        xb = xt[s % 2]
                        xk = "xt%d" % (s % 2)
                        pl.op("pe", lambda e, xb=xb, kc=kc, s=s, p0=p0: e.transpose(out=psB[:, kc % 2, (s - p0) * 128:(s - p0 + 1) * 128], in_=xb[:, kc * 128:(kc + 1) * 128], identity=ident), reads=[xk, "cst"], writes=[pk], token=(s == subs[-1]))
                    for (c0, c1, slot) in slots:
                        lo, hi = max(c0, w0), min(c1, w1)
                        if lo >= hi:
                            continue
                        pl.op("act", lambda e, kc=kc, lo=lo, hi=hi, slot=slot, w0=w0: e.activation(out=hb[:, kc, lo:hi], in_=psB[:, kc % 2, lo - w0:hi - w0], func=AF.Identity, scale=gam[:, slot, kc:kc + 1], bias=bet[:, slot, kc:kc + 1]),
                              reads=[pk, "gam", "bet"], writes=[hk])

        def conv_chunk(eng_first, src, wts, nk, dst, Tn, rk, wk, bias=None, bk=()):
            off = 3 - (nk - 1)
            if bias is None:
                pl.op("act", lambda e: e.activation(out=dst[:, 0:Tn], in_=src[:, off:off + Tn], func=AF.Copy, scale=wts[:, 0:1]), reads=rk, writes=wk)
            else:
                pl.op("act", lambda e: e.activation(out=dst[:, 0:Tn], in_=src[:, off:off + Tn], func=AF.Identity, scale=wts[:, 0:1], bias=bias), reads=rk + list(bk), writes=wk)
            for k in range(1, nk):
                pl.op("dve", lambda e, k=k: e.scalar_tensor_tensor(out=dst[:, 0:Tn], in0=src[:, off + k:off + k + Tn], scalar=wts[:, k:k + 1], in1=dst[:, 0:Tn], op0=ALU.mult, op1=ALU.add), reads=rk + wk, writes=wk)

        def tile(row0, Tn, slots, mode, gidx, yrow0, hist_from, st_mode, out_slot, next_row0=None, h_idx=0, pre_a=False, next_a=None):
            nsub = Tn // 128
            nch = Tn // 64
            hcur[0], hcur[1] = ((hT, "hT"), (ycat[:, 0:8, :], "ycat"))[h_idx]
            if not pre_a:
                step_a(row0, Tn, slots, hcur[0], hcur[1])
            light = (mode != "full")
            if mode in ("full", "halo"):
                wbs = {}
                pendA = None
                for j in range(8):
                    i = j % 2
                    need = [8 + j, 16 + j] if mode == "halo" else [8 + j, 16 + j, 0 + j, 24 + j]
                    for pc in need:
                        wbs[pc] = load_piece(pc)
                    cc = 0
                    wb, wk_ = wbs[8 + j]
                    proj_chunk(wb, wk_, cc, psA[:, 0, 0:Tn], "psA0", Tn)
                    wb, wk_ = wbs[16 + j]
                    proj_chunk(wb, wk_, cc, psA[:, 1, 0:Tn], "psA1", Tn)
                    ub, uk = ubuf[i], "ubuf%d" % i
                    pl.op("act", lambda e, i=i: e.copy(out=hsb[i][:, 0:Tn], in_=psA[:, 1, 0:Tn]), reads=["psA1"], writes=["hsb%d" % i])
                    pl.op("pool", lambda e, ub=ub, j=j: e.tensor_copy(out=ub[:, 0:3], in_=uhist[:, j, :]), reads=["uhist"], writes=[uk])
                    pl.op("dve", lambda e, ub=ub, i=i: e.tensor_tensor(out=ub[:, 3:3 + Tn], in0=psA[:, 0, 0:Tn], in1=hsb[i][:, 0:Tn], op=ALU.mult), reads=["psA0", "hsb%d" % i], writes=[uk])
                    pl.op("pool", lambda e, ub=ub, j=j: e.tensor_copy(out=uhist[:, j, :], in_=ub[:, Tn:Tn + 3]), reads=[uk], writes=["uhist"])
                    if mode == "halo":
                        continue
                    wb, wk_ = wbs[0 + j]
                    proj_chunk(wb, wk_, cc, psA[:, 2, 0:Tn], "psA2", Tn)
                    wb, wk_ = wbs[24 + j]
                    proj_chunk(wb, wk_, cc, psA[:, 3, 0:Tn], "psA3", Tn)
                    pl.op("act", lambda e, i=i: e.activation(out=tq[i][:, 0:Tn], in_=psA[:, 3, 0:Tn], func=AF.Silu), reads=["psA3"], writes=["tq%d" % i])
                    pl.op("dve", lambda e, i=i: e.tensor_tensor(out=tq[i][:, 0:Tn], in0=psA[:, 2, 0:Tn], in1=tq[i][:, 0:Tn], op=ALU.mult), reads=["psA2", "tq%d" % i], writes=["tq%d" % i])
                    c_, ck = cu[i], "cu%d" % i
                    if hist_from == "carry":
                        conv_chunk("dve", ub, caw[:, j, :], 3, c_, Tn, [uk, "caw"], [ck])
                    else:
                        for sidx in range(2):
                            pl.op("dve", lambda e, ub=ub, j=j, sidx=sidx: e.tensor_copy(out=stsb[:, sidx * 128 + 1:sidx * 128 + 3], in_=sca_sb[:, j, sidx, :]), reads=["sca"], writes=["stsb"])
                        for sidx in range(2):
                            base = sidx * 128
                            pl.op("dve", lambda e, ub=ub, base=base, sidx=sidx: e.tensor_copy(out=stsb[:, base + 3:base + 67], in_=ub[:, 3 + sidx * 64:3 + (sidx + 1) * 64]), reads=[uk], writes=["stsb"])
                            off = 1
                            pl.op("dve", lambda e, c_=c_, base=base, sidx=sidx, j=j: e.tensor_scalar_mul(out=c_[:, sidx * 64:(sidx + 1) * 64], in0=stsb[:, base + 1:base + 65], scalar1=caw[:, j, 0:1]), reads=["stsb", "caw"], writes=[ck])
                            for k in (1, 2):
                                pl.op("dve", lambda e, c_=c_, base=base, sidx=sidx, j=j, k=k: e.scalar_tensor_tensor(out=c_[:, sidx * 64:(sidx + 1) * 64], in0=stsb[:, base + 1 + k:base + 65 + k], scalar=caw[:, j, k:k + 1], in1=c_[:, sidx * 64:(sidx + 1) * 64], op0=ALU.mult, op1=ALU.add), reads=["stsb", "caw", ck], writes=[ck])
                            pl.op("dve", lambda e, base=base, sidx=sidx, j=j: e.tensor_copy(out=casb[:, j, 1 + sidx, :], in_=stsb[:, base + 65:base + 67]), reads=["stsb"], writes=["casb"])
                    def stage_b(c_=c_, ck=ck, i=i, j=j):
                        pl.op("dve", lambda e: e.tensor_tensor(out=c_[:, 0:Tn], in0=c_[:, 0:Tn], in1=tq[i][:, 0:Tn], op=ALU.mult), reads=[ck, "tq%d" % i], writes=[ck])
                        pl.op("act", lambda e: e.activation(out=sq2[i][:, 0:Tn], in_=c_[:, 0:Tn], func=AF.Square), reads=[ck], writes=["sq2%d" % i])
                        pl.op("act", lambda e: e.activation(out=ycat[:, j, 0:Tn], in_=c_[:, 0:Tn], func=AF.Copy, scale=pfc(PF_NAW, j)), reads=[ck, "pf"], writes=["ycat"])
                        for s in range(nsub):
                            pl.op("pe", lambda e, s=s: e.matmul(psC[:, 320 + s:321 + s], lhsT=sq2[i][:, s * 128:(s + 1) * 128], rhs=ones_bf[:, 0:1], start=(j == 0 and s == 0), stop=(j == 7), skip_group_check=True),
                                  reads=["sq2%d" % i, "ones"], writes=["psCa"], token=(s == nsub - 1))
                    if pendA is not None:
                        pendA()
                    pendA = stage_b
                if pendA is not None:
                    pendA()
                if mode == "full" and hist_from == "carry":
                    pl.op("dve", lambda e: e.tensor_copy(out=casb[:, :, 0, :], in_=uhist[:, :, 1:3]), reads=["uhist"], writes=["casb"])
                if mode == "halo":
                    pl.op("dve", lambda e: e.tensor_scalar_mul(out=uhist[:].rearrange("p a b -> p (a b)"), in0=uhist[:].rearrange("p a b -> p (a b)"), scalar1=msk[:, 0:1]), reads=["uhist", "msk"], writes=["uhist"])
                    wbs = {}
                    for j in range(12, 16):
                        wb, wk_ = load_piece(40 + j)
                        pa = psA[:, j % 4, 0:Tn]
                        pk = "psA%d" % (j % 4)
                        proj_chunk(wb, wk_, 0, pa, pk, Tn)
                        pl.op("act", lambda e, j=j, pa=pa: e.copy(out=xhist[:, j, :], in_=pa[:, Tn - 3:Tn]), reads=[pk], writes=["xhist"])
                    pl.op("dve", lambda e: e.tensor_scalar_mul(out=xhist[:, 12:16, :], in0=xhist[:, 12:16, :], scalar1=msk[:, 0:1]), reads=["xhist", "msk"], writes=["xhist"])
                    return

            wbs = {}
            pending = None
            for j in range(16):
                if mode == "light" and j >= 12:
                    break
                wb, wk_ = load_piece(40 + j)
                i = j % 2
                pa = psA[:, j % 4, 0:Tn]
                pk = "psA%d" % (j % 4)
                proj_chunk(wb, wk_, 0, pa, pk, Tn)
                ub, uk = ubuf[i], "ubuf%d" % i
                pl.op("pool", lambda e, ub=ub, j=j: e.tensor_copy(out=ub[:, 0:3], in_=xhist[:, j, :]), reads=["xhist"], writes=[uk])
                pl.op("act", lambda e, ub=ub, pa=pa: e.copy(out=ub[:, 3:3 + Tn], in_=pa), reads=[pk], writes=[uk])
                pl.op("pool", lambda e, ub=ub, j=j: e.tensor_copy(out=xhist[:, j, :], in_=ub[:, Tn:Tn + 3]), reads=[uk], writes=["xhist"])
                c_, ck = cu[i], "cu%d" % i
                if hist_from == "carry":
                    conv_chunk("act", ub, cbw[:, j, :], 4, c_, Tn, [uk, "cbw"], [ck], bias=cbb[:, j:j + 1], bk=["cbb"])
                else:
                    for sidx in range(2):
                        base = sidx * 128
                        pl.op("dve", lambda e, j=j, sidx=sidx, base=base: e.tensor_copy(out=stsb[:, base:base + 3], in_=scb_sb[:, j, sidx, :]), reads=["scb"], writes=["stsb"])
                        pl.op("dve", lambda e, ub=ub, base=base, sidx=sidx: e.tensor_copy(out=stsb[:, base + 3:base + 67], in_=ub[:, 3 + sidx * 64:3 + (sidx + 1) * 64]), reads=[uk], writes=["stsb"])
                        pl.op("dve", lambda e, c_=c_, base=base, sidx=sidx, j=j: e.tensor_scalar_mul(out=c_[:, sidx * 64:(sidx + 1) * 64], in0=stsb[:, base:base + 64], scalar1=cbw[:, j, 0:1]), reads=["stsb", "cbw"], writes=[ck])
                        for k in (1, 2, 3):
                            pl.op("dve", lambda e, c_=c_, base=base, sidx=sidx, j=j, k=k: e.scalar_tensor_tensor(out=c_[:, sidx * 64:(sidx + 1) * 64], in0=stsb[:, base + k:base + 64 + k], scalar=cbw[:, j, k:k + 1], in1=c_[:, sidx * 64:(sidx + 1) * 64], op0=ALU.mult, op1=ALU.add), reads=["stsb", "cbw", ck], writes=[ck])
                        pl.op("dve", lambda e, base=base, sidx=sidx, j=j: e.tensor_copy(out=cbsb[:, j, 1 + sidx, :], in_=stsb[:, base + 64:base + 67]), reads=["stsb"], writes=["cbsb"])
                    pl.op("dve", lambda e, c_=c_, j=j: e.tensor_scalar_add(out=c_[:, 0:Tn], in0=c_[:, 0:Tn], scalar1=cbb[:, j:j + 1]), reads=[ck, "cbb"], writes=[ck])
                if j < 8:
                    dst, dk = xbf[:, j, 0:Tn], "xbf"
                elif j < 12:
                    dst, dk = BT[:, j - 8, 0:Tn], "BT"
                else:
                    dst, dk = CT[:, j - 12, 0:Tn], "CT"

                def stage_b(c_=c_, ck=ck, i=i, dst=dst, dk=dk):
                    pl.op("act", lambda e: e.activation(out=dst, in_=c_[:, 0:Tn], func=AF.Silu), reads=[ck], writes=[dk])
                if pending is not None:
                    pending()
                pending = stage_b
            if pending is not None:
                pending()
            if mode == "full" and hist_from == "carry":
                pl.op("dve", lambda e: e.tensor_copy(out=cbsb[:, :, 0, :], in_=xhist[:]), reads=["xhist"], writes=["cbsb"])
            if mode == "full":
                wbs = {}
                for j in range(8):
                    wb, wk_ = load_piece(32 + j)
                    pa = psA[:, j % 4, 0:Tn]
                    pk = "psA%d" % (j % 4)
                    i = j % 2
                    proj_chunk(wb, wk_, 0, pa, pk, Tn)
                    pl.op("act", lambda e, pa=pa, j=j: e.activation(out=sz[:, j, 0:Tn], in_=pa, func=AF.Silu), reads=[pk], writes=["sz"])

            W = nsub * 16
            SMA = lambda idx: sm[:, idx, 0:W]
            v3 = lambda ap: ap.rearrange("p (b h) -> p b h", h=16)
            for b in range(nsub):
                tsl = slice(b * 128, (b + 1) * 128)
                for kc in range(8):
                    pl.op("pe", lambda e, kc=kc, tsl=tsl, b=b, hb=hcur[0]: e.matmul(psC[:, 256 + b * 16:272 + b * 16], lhsT=hb[:, kc, tsl], rhs=wdt[:, kc, :], start=(kc == 0), stop=(kc == 7)), reads=[hcur[1], "wdt"], writes=["psCd"], token=(kc == 7))
            pl.op("dve", lambda e: e.tensor_tensor(out=v3(SMA(0)), in0=v3(psC[:, 256:256 + W]), in1=bc[:, BC_DTB:BC_DTB + 16].unsqueeze(1).broadcast_to([128, nsub, 16]), op=ALU.add), reads=["psCd", "bc"], writes=["sm0"])
            pl.op("act", lambda e: e.activation(out=SMA(1), in_=SMA(0), func=AF.Abs), reads=["sm0"], writes=["sm1"])
            pl.op("act", lambda e: e.activation(out=SMA(1), in_=SMA(1), func=AF.Exp, scale=-1.0), reads=["sm1"], writes=["sm1"])
            pl.op("act", lambda e: e.activation(out=SMA(1), in_=SMA(1), func=AF.Ln, bias=1.0), reads=["sm1"], writes=["sm1"])
            pl.op("dve", lambda e: e.scalar_tensor_tensor(out=SMA(2), in0=SMA(0), scalar=0.0, in1=SMA(1), op0=ALU.max, op1=ALU.add), reads=["sm0", "sm1"], writes=["sm2"])
            pl.op("dve", lambda e: e.tensor_tensor(out=v3(SMA(3)), in0=v3(SMA(2)), in1=a_bc[:].unsqueeze(1).broadcast_to([128, nsub, 16]), op=ALU.mult), reads=["sm2", "a_bc"], writes=["sm3"])
            pl.op("pe", lambda e: e.matmul(psB[:, 0, 0:W], lhsT=tri, rhs=SMA(3), start=True, stop=True), reads=["cst", "sm3"], writes=["psB0"], token=False)
            pl.op("pe", lambda e: e.matmul(psB[:, 0, 64:64 + W], lhsT=blk, rhs=SMA(3), start=True, stop=True), reads=["cst", "sm3"], writes=["psB0"], token=False)
            pl.op("pe", lambda e: e.matmul(psB[:, 0, 128:128 + W], lhsT=chsel[0], rhs=SMA(3), start=True, stop=True), reads=["cst", "sm3"], writes=["psB0"], token=False)
            pl.op("pe", lambda e: e.matmul(psB[:, 0, 192:192 + W], lhsT=chsel[1], rhs=SMA(3), start=True, stop=True), reads=["cst", "sm3"], writes=["psB0"])
            pl.op("dve", lambda e: e.tensor_copy(out=SMA(4), in_=psB[:, 0, 0:W]), reads=["psB0"], writes=["sm4"])
            pl.op("dve", lambda e: e.tensor_tensor(out=SMA(5), in0=psB[:, 0, 64:64 + W], in1=SMA(4), op=ALU.subtract), reads=["psB0", "sm4"], writes=["sm5"])
            if mode == "light":
                sel0, sel1 = psB[:, 0, 128:128 + W], psB[:, 0, 192:192 + W]
                pl.op("dve", lambda e: e.tensor_copy(out=SMA(7), in_=sel1), reads=["psB0"], writes=["sm7"])
                pl.op("dve", lambda e: e.tensor_tensor(out=SMA(0), in0=sel0, in1=SMA(7), op=ALU.add), reads=["psB0", "sm7"], writes=["sm0"])
                pl.op("dve", lambda e: e.memset(SMA(1), 0.0), writes=["sm1"])
                for b in range(nsub - 2, -1, -1):
                    pl.op("dve", lambda e, b=b: e.tensor_tensor(out=sm[:, 1, b * 16:(b + 1) * 16], in0=sm[:, 1, (b + 1) * 16:(b + 2) * 16], in1=sm[:, 0, (b + 1) * 16:(b + 2) * 16], op=ALU.add), reads=["sm1", "sm0"], writes=["sm1"])
                pl.op("dve", lambda e: e.tensor_tensor(out=cd[:, 1, 0:16], in0=sm[:, 1, 0:16], in1=sm[:, 0, 0:16], op=ALU.add), reads=["sm1", "sm0"], writes=["cd"])
                pl.op("act", lambda e: e.activation(out=cd[:, 0, 0:16], in_=cd[:, 1, 0:16], func=AF.Exp), reads=["cd"], writes=["cd"])
                pl.op("dve", lambda e: e.scalar_tensor_tensor(out=SMA(1), in0=SMA(7), scalar=chsel[0][:, 0:1], in1=SMA(1), op0=ALU.mult, op1=ALU.add), reads=["sm7", "cst", "sm1"], writes=["sm1"])
                pl.op("dve", lambda e: e.tensor_tensor(out=SMA(5), in0=SMA(5), in1=SMA(1), op=ALU.add), reads=["sm5", "sm1"], writes=["sm5"])
            pl.op("act", lambda e: e.activation(out=SMA(5), in_=SMA(5), func=AF.Exp), reads=["sm5"], writes=["sm5"])
            pl.op("dve", lambda e: e.tensor_tensor(out=SMA(6), in0=SMA(5), in1=SMA(2), op=ALU.mult), reads=["sm5", "sm2"], writes=["sm6"])
            if mode != "light":
                pl.op("act", lambda e: e.activation(out=cd[:, 0, 0:W], in_=psB[:, 0, 128:128 + W], func=AF.Exp), reads=["psB0"], writes=["cd"])
                pl.op("act", lambda e: e.activation(out=cd[:, 1, 0:W], in_=psB[:, 0, 192:192 + W], func=AF.Exp), reads=["psB0"], writes=["cd"])
            else:
                if st_mode[0] is not None and st_mode[0][0] == "zero":
                    pl.op("dve", lambda e: e.memset(ST[:], 0.0), writes=["ST"])
                pl.op("pool", lambda e: e.tensor_tensor(out=ST[:].rearrange("p (h q) -> p h q", h=16), in0=ST[:].rearrange("p (h q) -> p h q", h=16), in1=cd[:, 0, 0:16].unsqueeze(2).broadcast_to([128, 16, 64]), op=ALU.mult), reads=["ST", "cd"], writes=["ST"])
            for b in range(nsub):
                tsl = slice(b * 128, (b + 1) * 128)
                SM = lambda idx, b=b: sm[:, idx, b * 16:(b + 1) * 16]
                bc16 = lambda idx, SM=SM: SM(idx).unsqueeze(2).broadcast_to([128, 16, 64])
                b16_2, b16_3, b16_4, b16_6 = bc16(2), bc16(3), bc16(4), bc16(6)
                for j in range(8):
                    pl.op("pe", lambda e, j=j, tsl=tsl: e.transpose(out=psT[:, j * 128:(j + 1) * 128], in_=xbf[:, j, tsl], identity=idb[:]), reads=["xbf", "idb"], writes=["psT"], token=(j == 7))
                bview = lambda ap: ap.rearrange("p (h q) -> p h q", h=16)
                if mode == "full":
                    pl.op("dve", lambda e, b16_2=b16_2: e.tensor_tensor(out=bview(xdt[:]), in0=bview(psT[:, :]), in1=b16_2, op=ALU.mult), reads=["psT", "sm2"], writes=["xdt"])
                if mode == "light":
                    xw_, xwk = ((xw, "xw"), (xdt, "xdt"))[b % 2]
                    bt_, btk = ((btok[:], "btok"), (Mt[:, 0:512], "Mt"))[b % 2]
                    bps, bpk = psC[:, 0:256].bitcast(BF16), "psCb"
                else:
                    xw_, xwk, bt_, btk, bps, bpk = xw, "xw", btok[:], "btok", psT[:, 0:512], "psT"
                pl.op("dve", lambda e, b16_6=b16_6, xw_=xw_: e.tensor_tensor(out=bview(xw_[:]), in0=bview(psT[:, :]), in1=b16_6, op=ALU.mult), reads=["psT", "sm6"], writes=[xwk])
                for g in range(4):
                    pl.op("pe", lambda e, g=g, tsl=tsl, bps=bps: e.transpose(out=bps[:, g * 128:(g + 1) * 128], in_=BT[:, g, tsl], identity=idb[:]), reads=["BT", "idb"], writes=[bpk], token=(g == 3))
                pl.op("act", lambda e, bt_=bt_, bps=bps: e.copy(out=bt_, in_=bps), reads=[bpk], writes=[btk])

                if mode == "full":
                    for g in range(4):
                        for c in range(2):
                            cs = slice(b * 128 + c * 64, b * 128 + (c + 1) * 64)
                            pl.op("pe", lambda e, g=g, c=c, cs=cs: e.matmul(psC[c * 64:(c + 1) * 64, g * 64:(g + 1) * 64], lhsT=BT[:, g, cs], rhs=CT[:, g, cs], start=True, stop=True), reads=["BT", "CT"], writes=["psCb"], token=(g == 3 and c == 1))
                    pl.op("dve", lambda e: e.tensor_tensor(out=cbtm[:], in0=psC[:, 0:256].rearrange("p (g i) -> p g i", g=4), in1=t64.unsqueeze(1).broadcast_to([128, 4, 64]), op=ALU.mult), reads=["psCb", "cst"], writes=["cbtm"])
                    pl.op("dve", lambda e, b16_3=b16_3: e.tensor_tensor(out=bview(rhs1[:]), in0=b16_3, in1=t64.unsqueeze(1).broadcast_to([128, 16, 64]), op=ALU.mult), reads=["sm3", "cst"], writes=["rhs1"])
                    for hh in range(2):
                        pl.op("pe", lambda e, hh=hh: e.matmul(psA[:, hh, :], lhsT=blk, rhs=rhs1[:, hh * 512:(hh + 1) * 512], start=True, stop=True), reads=["cst", "rhs1"], writes=["psA%d" % hh])
                    pl.op("dve", lambda e, b16_4=b16_4: e.tensor_tensor(out=bview(segs[:]), in0=psA[:, 0:2, :].rearrange("p a (h q) -> p (a h) q", q=64), in1=b16_4, op=ALU.subtract), reads=["psA0", "psA1", "sm4"], writes=["segs"])
                    pl.op("act", lambda e: e.activation(out=segs[:], in_=segs[:], func=AF.Exp), reads=["segs"], writes=["segs"])
                    for g in range(4):
                        pl.op("dve", lambda e, g=g: e.scalar_tensor_tensor(out=Mt[:, g * 256:(g + 1) * 256].rearrange("p (r q) -> p r q", r=4), in0=segs[:, g * 256:(g + 1) * 256].rearrange("p (r q) -> p r q", r=4), scalar=1.0,
                                                                           in1=cbtm[:, g, :].unsqueeze(1).broadcast_to([128, 4, 64]), op0=ALU.min, op1=ALU.mult), reads=["segs", "cbtm"], writes=["Mt"])
                    pl.op("dve", lambda e, b16_3=b16_3: e.tensor_copy(out=bview(rhs1[:]), in_=b16_3), reads=["sm3"], writes=["rhs1"])
                    for a in range(8):
                        pl.op("pe", lambda e, a=a: e.matmul(psA[:, 2 + a // 4, (a % 4) * 128:(a % 4 + 1) * 128], lhsT=rhs1[:, a * 128:(a + 1) * 128], rhs=tri, start=True, stop=True), reads=["rhs1", "cst"], writes=["psA%d" % (2 + a // 4)], token=(a % 4 == 3))
                    pl.op("act", lambda e: e.activation(out=eac[:], in_=psA[:, 2:4, :].rearrange("p a q -> p (a q)"), func=AF.Exp), reads=["psA2", "psA3"], writes=["eac"])

                for c in range(2):
                    ch = b * 2 + c
                    cs = slice(b * 128 + c * 64, b * 128 + (c + 1) * 64)
                    ps_ = slice(c * 64, (c + 1) * 64)
                    if mode != "light" and st_mode[ch] is not None:
                        kind, val = st_mode[ch]
                        if kind == "zero":
                            pl.op("dve", lambda e: e.memset(ST[:], 0.0), writes=["ST"])
                        elif kind == "load":
                            pl.dma("sp", lambda e, val=val: e.dma_start(out=ST[:], in_=sst_d[val, :, :]), "l_ST", writes=["ST"])
                        elif kind == "keep":
                            pass
                        if mode == "full":
                            pl.op("act", lambda e: e.copy(out=STb[:], in_=ST[:]), reads=["ST"], writes=["STb"])
                    if mode != "light":
                        pl.op("pool", lambda e, c=c, b=b: e.tensor_tensor(out=bview(ST[:]), in0=bview(ST[:]), in1=cd[:, c, b * 16:(b + 1) * 16].unsqueeze(2).broadcast_to([128, 16, 64]), op=ALU.mult), reads=["ST", "cd"], writes=["ST"])
                    if mode == "full":
                        for a in range(8):
                            for hh in range(2):
                                h = 2 * a + hh
                                pl.op("pe", lambda e, a=a, hh=hh, h=h, c=c, ps_=ps_: e.matmul(psB[hh * 64:(hh + 1) * 64, a // 4, (a % 4) * 128 + c * 64:(a % 4) * 128 + (c + 1) * 64], lhsT=xdt[ps_, h * 64:(h + 1) * 64], rhs=Mt[ps_, h * 64:(h + 1) * 64], start=True, stop=True),
                                      reads=["xdt", "Mt"], writes=["psB%d" % (a // 4)], token=(hh == 1 and a % 4 == 3))
                        for a in range(8):
                            g = a // 2
                            pl.op("pe", lambda e, a=a, g=g, c=c, cs=cs: e.matmul(psA[:, a // 4, (a % 4) * 128 + c * 64:(a % 4) * 128 + (c + 1) * 64], lhsT=STb[:, a * 128:(a + 1) * 128], rhs=CT[:, g, cs], start=True, stop=True),
                                  reads=["STb", "CT"], writes=["psA%d" % (a // 4)], token=(a % 4 == 3))
                    first = (b == 0 and c == 0)
                    last = (b == nsub - 1 and c == 1)
                    for g in range(4):
                        if mode == "light":
                            bk = (2 if c == 0 else 0) + g // 2
                            pl.op("pe", lambda e, g=g, ps_=ps_, bk=bk, b=b, last=last, bt_=bt_, xw_=xw_: e.matmul(psA[:, bk, (g % 2) * 256:(g % 2 + 1) * 256], lhsT=bt_[ps_, g * 128:(g + 1) * 128], rhs=xw_[ps_, g * 256:(g + 1) * 256], start=(b == 0 and g % 2 == 0), stop=(b == nsub - 1), skip_group_check=True),
                                  reads=[btk, xwk], writes=["psA%d" % bk], token=(g % 2 == 1))
                        else:
                            pl.op("pe", lambda e, g=g, ps_=ps_: e.matmul(psA[:, 2 + g // 2, (g % 2) * 256:(g % 2 + 1) * 256], lhsT=btok[ps_, g * 128:(g + 1) * 128], rhs=xw[ps_, g * 256:(g + 1) * 256], start=True, stop=True),
                                  reads=["btok", "xw"], writes=["psA%d" % (2 + g // 2)], token=(g % 2 == 1))
                    if mode != "light" or last:
                        pl.op("dve", lambda e: e.tensor_tensor(out=ST[:], in0=ST[:], in1=psA[:, 2:4, :].rearrange("p a q -> p (a q)"), op=ALU.add), reads=["ST", "psA2", "psA3"], writes=["ST"])
                        if mode == "light":
                            pl.op("dve", lambda e: e.tensor_tensor(out=ST[:], in0=ST[:], in1=psA[:, 0:2, :].rearrange("p a q -> p (a q)"), op=ALU.add), reads=["ST", "psA0", "psA1"], writes=["ST"])
                    if mode == "full":
                        pl.op("act", lambda e: e.copy(out=STb[:], in_=ST[:]), reads=["ST"], writes=["STb"])
                        if hist_from != "carry":
                            pl.dma("pool", lambda e, ch=ch: e.dma_start(out=so_d[1 + ch, :, :], in_=ST[:]), "s_ST", reads=["ST"], writes=["so%d" % (1 + ch)])
                if next_a is not None and b < len(next_a):
                    next_a[b]()
                if mode == "full":
                    pl.op("dve", lambda e: e.tensor_tensor(out=yo[:], in0=psA[:, 0:2, :].rearrange("p a q -> p (a q)"), in1=eac[:], op=ALU.mult), reads=["psA0", "psA1", "eac"], writes=["yo"])
                    pl.op("dve", lambda e: e.tensor_tensor(out=yo[:], in0=psB[:, :, :].rearrange("p a q -> p (a q)"), in1=yo[:], op=ALU.add), reads=["psB0", "psB1", "yo"], writes=["yo"])
                    for a in range(8):
                        ya = yo[:, a * 128:(a + 1) * 128]
                        pl.op("dve", lambda e, a=a, ya=ya, tsl=tsl: e.scalar_tensor_tensor(out=ya, in0=xbf[:, a, tsl], scalar=pfc(PF_DSK, a), in1=ya, op0=ALU.mult, op1=ALU.add), reads=["xbf", "pf", "yo"], writes=["yo"])
                        pl.op("dve", lambda e, a=a, ya=ya, tsl=tsl: e.tensor_tensor(out=ya, in0=ya, in1=sz[:, a, tsl], op=ALU.mult), reads=["yo", "sz"], writes=["yo"])
                    pl.op("act", lambda e: e.activation(out=Mt[:], in_=yo[:], func=AF.Square), reads=["yo"], writes=["Mt"])
                    for a in range(8):
                        pl.op("act", lambda e, a=a, tsl=tsl: e.activation(out=ycat[:, 8 + a, tsl], in_=yo[:, a * 128:(a + 1) * 128], func=AF.Copy, scale=pfc(PF_NBW, a)), reads=["yo", "pf"], writes=["ycat"])
                        pl.op("pe", lambda e, a=a, b=b: e.matmul(psC[:, 324 + b:325 + b], lhsT=Mt[:, a * 128:(a + 1) * 128], rhs=ones_bf[:, 0:1], start=(a == 0), stop=(a == 7)), reads=["Mt", "ones"], writes=["psCs"], token=(a == 7))

            if mode != "full":
                return
            pl.op("act", lambda e: e.activation(out=stat[:, 24:32], in_=psC[:, 320:328], func=AF.Ln, scale=1.0 / D, bias=EPS), reads=["psCa", "psCs"], writes=["stat"])
            pl.op("act", lambda e: e.activation(out=stat[:, 24:32], in_=stat[:, 24:32], func=AF.Exp, scale=-0.5), reads=["stat"], writes=["stat"])
            for s in range(nsub):
                tsl = slice(s * 128, (s + 1) * 128)
                for hf in range(2):
                    for part in range(2):
                        for kc in range(8):
                            pl.op("pe", lambda e, hf=hf, part=part, kc=kc, tsl=tsl: e.matmul(psA[:, part * 2 + hf, :], lhsT=ycat[:, part * 8 + kc, tsl], rhs=wout[:, part * 8 + kc, hf * 512:(hf + 1) * 512], start=(kc == 0), stop=(kc == 7)),
                                  reads=["ycat", "wout"], writes=["psA%d" % (part * 2 + hf)], token=(kc == 7))
                xb = xt[s % 2]
                xk = "xt%d" % (s % 2)
                pl.dma("sp", lambda e, xb=xb, s=s: e.dma_start(out=xb[:], in_=xs_d[row0 + s * 128:row0 + (s + 1) * 128, :]), "l_" + xk, writes=[xk])
                ob = ost[s % 2]
                ok = "ost%d" % (s % 2)
                pl.op("act", lambda e, s=s: e.activation(out=o1[:], in_=psA[:, 0:2, :].rearrange("p a q -> p (a q)"), func=AF.Copy, scale=stat[:, 24 + s:25 + s]), reads=["psA0", "psA1", "stat"], writes=["o1"])
                pl.op("dve", lambda e, s=s: e.scalar_tensor_tensor(out=o1[:], in0=psA[:, 2:4, :].rearrange("p a q -> p (a q)"), scalar=stat[:, 28 + s:29 + s], in1=o1[:], op0=ALU.mult, op1=ALU.add), reads=["psA2", "psA3", "stat", "o1"], writes=["o1"])
                pl.op("dve", lambda e: e.tensor_tensor(out=o1[:], in0=o1[:], in1=gate[:, gidx, :], op=ALU.mult), reads=["o1", "gate"], writes=["o1"])
                pl.op("dve", lambda e, xb=xb: e.tensor_tensor(out=o1[:], in0=o1[:], in1=xb[:], op=ALU.add), reads=["o1", xk], writes=["o1"])
                pl.op("dve", lambda e, s=s: e.memset(stat[:, 32 + s:33 + s], 0.0), writes=["stat"])
                pl.op("act", lambda e, s=s: e.activation(out=sqj[:], in_=o1[:], func=AF.Square, accum_out=stat[:, 32 + s:33 + s]), reads=["o1", "stat"], writes=["sqj", "stat"])
                pl.op("act", lambda e, s=s: e.activation(out=stat[:, 36 + s:37 + s], in_=stat[:, 32 + s:33 + s], func=AF.Ln, scale=1.0 / D, bias=EPS), reads=["stat"], writes=["stat"])
                pl.op("act", lambda e, s=s: e.activation(out=stat[:, 36 + s:37 + s], in_=stat[:, 36 + s:37 + s], func=AF.Exp, scale=-0.5), reads=["stat"], writes=["stat"])
                pl.op("dve", lambda e, s=s, ob=ob: e.scalar_tensor_tensor(out=ob[:], in0=o1[:], scalar=stat[:, 36 + s:37 + s], in1=bc[:, BC_NF:BC_NF + D], op0=ALU.mult, op1=ALU.mult), reads=["o1", "stat", "bc"], writes=[ok])
                pl.dma("pool", lambda e, ob=ob, s=s: e.dma_start(out=y_d[yrow0 + s * 128:yrow0 + (s + 1) * 128, :], in_=ob[:]), "s_" + ok, reads=[ok], writes=["y"])

        ones_bf = SB("ones_bf", [128, 8], BF16)
        pl.op("dve", lambda e: e.memset(ones_bf[:], 1.0), writes=["ones"])
        sca_sb = SB("sca_sb", [128, 8, 2, 2])
        scb_sb = SB("scb_sb", [128, 16, 2, 3])
        ld("l_sca", sca_sb[:].rearrange("p a b c -> p (a b c)"), sca_d[:, :], "sca")
        ld("l_scb", scb_sb[:].rearrange("p a b c -> p (a b c)"), scb_d[:, :], "scb")
        pl.op("dve", lambda e: e.memset(uhist[:], 0.0), writes=["uhist"])
        pl.op("dve", lambda e: e.memset(xhist[:], 0.0), writes=["xhist"])
        pl.op("dve", lambda e: e.memset(atot[:], 0.0), writes=["atot"])
        pslot = [(0, T, 0)]
        hsel = ((hT, "hT"), (ycat[:, 0:8, :], "ycat"))
        for n in range(NLSEG * NT):
            stm = [None] * (T // 64)
            if n == 0:
                stm[0] = ("zero", 0)
            nxa = None
            if n + 1 < NLSEG * NT:
                r1 = 128 + (n + 1) * T
                hb1, hk1 = hsel[(n + 1) % 2]
                nxa = [(lambda r1=r1, p0=p0, hb1=hb1, hk1=hk1: step_a_pair(r1, p0, T, pslot, hb1, hk1)) for p0 in (0, 2)]
            tile(128 + n * T, T, pslot, "light", 0, 0, "carry", stm, None, h_idx=n % 2, pre_a=(n > 0), next_a=nxa)
            if n % NT == NT - 1:
                m = n // NT
                pl.op("dve", lambda e, m=m: e.tensor_scalar_mul(out=ST[:], in0=ST[:], scalar1=msk[:, 1 + m:2 + m]), reads=["ST", "msk"], writes=["ST"])
                pl.op("dve", lambda e, m=m: e.tensor_scalar_mul(out=xhist[:].rearrange("p a b -> p (a b)"), in0=xhist[:].rearrange("p a b -> p (a b)"), scalar1=msk[:, 1 + m:2 + m]), reads=["xhist", "msk"], writes=["xhist"])
        tile(0, 128, [(0, 128, 0)], "halo", 0, 0, "carry", [None, None], None)
        for n in range(NT):
            stm = [None] * (T // 64)
            if n == 0:
                stm[0] = ("keep", 0)
            tile(128 + LROWS + n * T, T, pslot, "full", 0, n * T, "carry", stm, 0)
        pl.dma("pool", lambda e: e.dma_start(out=so_d[0, :, :], in_=ST[:]), "s_ST", reads=["ST"], writes=["so0"])
        tile(128 + LROWS + SEGLEN, 128, [(0, 64, 1), (64, 128, 2)], "full", 1, SEGLEN, "state", [("load", 0), ("load", 1)], 1)
        pl.dma("pool", lambda e: e.dma_start(out=ca_d[:, :], in_=casb[:].rearrange("p a b c -> p (a b c)")), "s_ca", reads=["casb"], writes=["ca"])
        pl.dma("pool", lambda e: e.dma_start(out=cb_d[:, :], in_=cbsb[:].rearrange("p a b c -> p (a b c)")), "s_cb", reads=["cbsb"], writes=["cb"])
        if debug:
            dbg_list = [("xbf", xbf[:].rearrange("p a b -> p (a b)"), 8 * T, BF16), ("BT", BT[:].rearrange("p a b -> p (a b)"), 4 * T, BF16),
                        ("CT", CT[:].rearrange("p a b -> p (a b)"), 4 * T, BF16), ("sm", sm[:].rearrange("p a b -> p (a b)"), 256, F32),
                        ("ycat", ycat[:].rearrange("p a b -> p (a b)"), 16 * T, BF16), ("yo", yo[:], 1024, F32), ("xw", xw[:], 1024, BF16),
                        ("xdt", xdt[:], 1024, BF16), ("btok", btok[:], 512, BF16), ("Mt", Mt[:], 1024, BF16), ("eac", eac[:], 1024, F32),
                        ("cd", cd[:].rearrange("p a b -> p (a b)"), 32, F32), ("hT", hT[:].rearrange("p a b -> p (a b)"), 8 * T, BF16),
                        ("sz", sz[:].rearrange("p a b -> p (a b)"), 8 * T, BF16), ("stat", stat[:], 64, F32), ("gam", gam[:].rearrange("p a b -> p (a b)"), 24, F32)]
            allk = list(pl.bufs.keys())
            for nm, ap, w, dt_ in dbg_list:
                dd = nc.dram_tensor("dbg_" + nm, [128, w], dt_, kind="ExternalOutput").ap()
                pl.dma("pool", lambda e, dd=dd, ap=ap: e.dma_start(out=dd[:, :], in_=ap), "s_dbg", reads=allk)
        pl.wait_tokens("pool", [(s, c) for s, c in pl.dma_cnt.items() if s.startswith("s_")])
        print("planned instructions:", pl.nins, {e: len(pl.lists[e]) for e in pl.ENGS})
        pl.emit()
    return nc


def _host_consts():
    c = np.zeros((128, NCONST), np.float32)
    k = np.arange(128)
    c[:, C_ID:C_ID + 128] = np.eye(128, dtype=np.float32)
    same = (k[:, None] // 64) == (k[None, :] // 64)
    c[:, C_BLK:C_BLK + 128] = same
    c[:, C_TRI:C_TRI + 128] = same & (k[:, None] <= k[None, :])
    c[:, C_T64:C_T64 + 64] = (k[:, None] % 64) <= np.arange(64)[None, :]
    c[:, C_SEL0:C_SEL0 + 128] = (k[:, None] < 64)
    c[:, C_SEL1:C_SEL1 + 128] = (k[:, None] >= 64)
    return c


def _fm(v, nchunk):
    return np.ascontiguousarray(np.asarray(v, np.float32).reshape(nchunk, 128).T)


_NC_CACHE = {}


def kernel(x_prompt, x_sample, state_conv_a, state_conv_b, state_ssm, c_prompt, c_sample,
           w_mod, b_mod, norm_in_w, w_in, conv_a_w, norm_a_w, conv_b_w, conv_b_b,
           dt_bias, a_log, d_skip, norm_b_w, w_out, norm_f_w, _two_phase=True, _debug=False):
    f = lambda a: np.ascontiguousarray(np.asarray(a, np.float32))
    x_prompt, x_sample = f(x_prompt), f(x_sample)
    state_conv_a, state_conv_b, state_ssm = f(state_conv_a), f(state_conv_b), f(state_ssm)
    c_prompt, c_sample = f(c_prompt), f(c_sample)
    w_mod, b_mod, w_in, w_out = f(w_mod)[0], f(b_mod)[0], f(w_in)[0], f(w_out)[0]
    pf = np.zeros((128, NPF), np.float32)
    pf[:, PF_NIN:PF_NIN + 8] = _fm(f(norm_in_w)[0], 8)
    caw = f(conv_a_w)[0]
    pf[:, PF_CAW:PF_CAW + 24] = np.stack([_fm(caw[k], 8) for k in range(3)], axis=2).reshape(128, 24)
    pf[:, PF_NAW:PF_NAW + 8] = _fm(f(norm_a_w)[0], 8)
    cbw = f(conv_b_w)[0]
    pf[:, PF_CBW:PF_CBW + 64] = np.stack([_fm(cbw[k], 16) for k in range(4)], axis=2).reshape(128, 64)
    pf[:, PF_CBB:PF_CBB + 16] = _fm(f(conv_b_b)[0], 16)
    pf[:, PF_NBW:PF_NBW + 8] = _fm(f(norm_b_w)[0], 8)
    pf[:, PF_DSK:PF_DSK + 8] = _fm(np.repeat(f(d_skip)[0], 64), 8)
    pf[:, PF_BSH:PF_BSH + 8] = _fm(b_mod[0:D], 8)
    pf[:, PF_BSC:PF_BSC + 8] = _fm(b_mod[D:2 * D], 8)
    bcv = np.zeros((128, NBC), np.float32)
    bcv[:, BC_NF:BC_NF + D] = f(norm_f_w)[None, :]
    bcv[:, BC_BG:BC_BG + D] = b_mod[None, 2 * D:3 * D]
    bcv[:, BC_DTB:BC_DTB + 16] = f(dt_bias)[0][None, :]
    bcv[:, BC_ALOG:BC_ALOG + 16] = f(a_log)[0][None, :]
    cst = _host_consts()

    in_maps = []
    for k in range(NCORES):
        seq, seg = k // 4, k % 4
        start = seg * SEGLEN
        xs = np.zeros((XROWS, D), np.float32)
        if seg > 0:
            xs[0:128] = x_prompt[seq, start - 128:start]
        if seg > 0 and LROWS >= start:
            xs[128 + LROWS - start:128 + LROWS] = x_prompt[seq, 0:start]
        xs[128 + LROWS:128 + LROWS + SEGLEN] = x_prompt[seq, start:start + SEGLEN]
        xs[128 + LROWS + SEGLEN:] = x_sample[2 * k:2 * k + 2].reshape(128, D)
        cs = [c_prompt[seq], c_sample[2 * k], c_sample[2 * k + 1]]
        cT = np.stack([_fm(c, 8) for c in cs], axis=2).reshape(128, 24)
        cbc = np.zeros((2, 128, 8, 128), np.float32)
        cbc[0] = _fm(cs[0], 8)[:, :, None]
        cbc[1, :, :, 0:64] = _fm(cs[1], 8)[:, :, None]
        cbc[1, :, :, 64:128] = _fm(cs[2], 8)[:, :, None]
        sca = state_conv_a[0, 2 * k:2 * k + 2]
        sca = sca.reshape(2, 2, 8, 128).transpose(3, 2, 0, 1).reshape(128, 32)
        scb = state_conv_b[0, 2 * k:2 * k + 2]
        scb = scb.reshape(2, 3, 16, 128).transpose(3, 2, 0, 1).reshape(128, 96)
        sst = state_ssm[0, 2 * k:2 * k + 2]
        sst = sst.reshape(2, 1024, 128).transpose(0, 2, 1)
        msk = np.zeros((128, 16), np.float32)
        msk[:, 0] = 1.0 if seg > 0 else 0.0
        for m in range(NLSEG):
            msk[:, 1 + m] = 0.0 if m < NLSEG - seg else 1.0
        in_maps.append({
            "xs": xs, "w_mod": w_mod, "w_in": w_in, "w_out": w_out, "pf": pf, "bc": bcv, "cst": cst,
            "cT": np.ascontiguousarray(cT), "cbc": np.ascontiguousarray(cbc.reshape(2, 128, 1024)),
            "sca": np.ascontiguousarray(sca), "scb": np.ascontiguousarray(scb),
            "sst": np.ascontiguousarray(sst), "msk": msk,
        })
    key = (bool(_two_phase), bool(_debug))
    if key not in _NC_CACHE:
        _NC_CACHE[key] = build_nc(two_phase=key[0], debug=key[1])
    nc = _NC_CACHE[key]
    res = run_bass_kernel_spmd(nc, in_maps, core_ids=list(range(NCORES)))
    R = res.results
    if _debug:
        kernel.last_results = R
    y_prompt = np.zeros((2, SEQ, D), np.float32)
    y_sample = np.zeros((16, 64, D), np.float32)
    ca_p = np.zeros((1, 2, 2, D), np.float32)
    cb_p = np.zeros((1, 2, 3, 2 * D), np.float32)
    ss_p = np.zeros((1, 2, 16, 64, 128), np.float32)
    ca_s = np.zeros((1, 16, 2, D), np.float32)
    cb_s = np.zeros((1, 16, 3, 2 * D), np.float32)
    ss_s = np.zeros((1, 16, 16, 64, 128), np.float32)
    for k in range(NCORES):
        seq, seg = k // 4, k % 4
        r = R[k]
        y_prompt[seq, seg * SEGLEN:(seg + 1) * SEGLEN] = r["y"][0:SEGLEN]
        y_sample[2 * k:2 * k + 2] = r["y"][SEGLEN:].reshape(2, 64, D)
        ca = r["ca"].reshape(128, 8, 3, 2)
        cb = r["cb"].reshape(128, 16, 3, 3)
        so = r["so"]
        for s in range(2):
            ca_s[0, 2 * k + s] = ca[:, :, 1 + s, :].transpose(2, 1, 0).reshape(2, D)
            cb_s[0, 2 * k + s] = cb[:, :, 1 + s, :].transpose(2, 1, 0).reshape(3, 2 * D)
            ss_s[0, 2 * k + s] = so[1 + s].T.reshape(16, 64, 128)
        if seg == 3:
            ca_p[0, seq] = ca[:, :, 0, :].transpose(2, 1, 0).reshape(2, D)
            cb_p[0, seq] = cb[:, :, 0, :].transpose(2, 1, 0).reshape(3, 2 * D)
            ss_p[0, seq] = so[0].T.reshape(16, 64, 128)
    return (y_prompt, y_sample, ca_p, cb_p, ss_p, ca_s, cb_s, ss_s)
```

```python
import contextlib
import numpy as np
import concourse.bass as bass
import concourse.mybir as mybir
from concourse.bass_utils import run_bass_kernel_spmd

F32 = mybir.dt.float32
BF16 = mybir.dt.bfloat16
ALU = mybir.AluOpType
AF = mybir.ActivationFunctionType

NCORES = 8
D = 1024
SEQ = 16384
SEGLEN = 4096
T = 512
NT = SEGLEN // T
DIN = 7184
PW = 128
NPIECE = 56
NWB = 12
EPS = 1e-5
NLSEG = 3
LROWS = NLSEG * SEGLEN
XROWS = 128 + LROWS + SEGLEN + 128
YROWS = SEGLEN + 128

PF_NIN = 0
PF_CAW = 8
PF_NAW = 32
PF_CBW = 40
PF_CBB = 104
PF_NBW = 120
PF_DSK = 128
PF_BSH = 136
PF_BSC = 144
NPF = 152
BC_NF = 0
BC_BG = 1024
BC_DTB = 2048
BC_ALOG = 2064
NBC = 2080
C_ID = 0
C_BLK = 128
C_TRI = 256
C_T64 = 384
C_SEL0 = 448
C_SEL1 = 576
NCONST = 704


class Planner:
    ENGS = ("pe", "act", "dve", "pool", "sp")
    SEM_LIMIT = 30000

    def __init__(self, nc):
        self.nc = nc
        self.lists = {e: [] for e in self.ENGS}
        self.cur = {e: [e + "_0", 0] for e in self.ENGS}
        self.gen = {e: 0 for e in self.ENGS}
        self.sem_names = [e + "_0" for e in self.ENGS]
        self.dma_cnt = {}
        self.waited = {e: {} for e in self.ENGS}
        self.bufs = {}
        self.nins = 0
        self.alias = {}
        self.bank_last = {}

    @staticmethod
    def _bank(k):
        if k.startswith("psC"):
            return "psC"
        if k.startswith("psA") or k.startswith("psB") or k == "psT":
            return k
        return None

    def _bank_deps(self, eng, keys, need):
        banks = set(b for b in (self._bank(k) for k in keys) if b)
        for b in banks:
            for oe, tok in self.bank_last.get(b, {}).items():
                if oe != eng:
                    self._need(eng, tok, need)
        return banks

    def _exp(self, keys):
        out = []
        for k in keys:
            out.extend(self.alias.get(k, (k,)))
        return out

    def _need(self, eng, tok, out):
        if tok is None:
            return
        s, v = tok
        if eng == "pe" and s.startswith("pe_"):
            return
        if self.waited[eng].get(s, 0) >= v:
            return
        if out.get(s, 0) < v:
            out[s] = v

    def _deps(self, eng, reads, writes):
        reads, writes = self._exp(reads), self._exp(writes)
        need = {}
        for k in reads:
            b = self.bufs.get(k)
            if b:
                self._need(eng, b[0], need)
        for k in writes:
            b = self.bufs.get(k)
            if b:
                self._need(eng, b[0], need)
                for t in b[1]:
                    self._need(eng, t, need)
        self._cur_banks = self._bank_deps(eng, list(reads) + list(writes), need)
        self._cur_eng = eng
        for s, v in need.items():
            self.waited[eng][s] = v
            self.lists[eng].append(("wait", s, v))

    def _mark(self, tok, reads, writes):
        reads, writes = self._exp(reads), self._exp(writes)
        for b in self._cur_banks:
            self.bank_last.setdefault(b, {})[self._cur_eng] = tok
        for k in reads:
            b = self.bufs.setdefault(k, [None, []])
            b[1].append(tok)
        for k in writes:
            self.bufs[k] = [tok, []]

    def op(self, eng, fn, reads=(), writes=(), token=True):
        self._deps(eng, reads, writes)
        c = self.cur[eng]
        self.nins += 1
        if token:
            if c[1] >= self.SEM_LIMIT:
                self.gen[eng] += 1
                c[0] = "%s_%d" % (eng, self.gen[eng])
                c[1] = 0
                self.sem_names.append(c[0])
            c[1] += 1
            tok = (c[0], c[1])
            self.lists[eng].append(("ins", fn, c[0], 1))
        else:
            tok = (c[0], c[1] + 1)
            self.lists[eng].append(("ins", fn, None, 0))
        self._mark(tok, reads, writes)
        return tok

    def dma(self, eng, fn, sem, reads=(), writes=()):
        self._deps(eng, reads, writes)
        self.nins += 1
        if sem not in self.dma_cnt:
            self.dma_cnt[sem] = 0
            self.sem_names.append(sem)
        self.dma_cnt[sem] += 16
        tok = (sem, self.dma_cnt[sem])
        self.lists[eng].append(("ins", fn, sem, 16))
        self._mark(tok, reads, writes)
        return tok

    def raw(self, eng, fn, sem, inc, reads=(), writes=()):
        self._deps(eng, reads, writes)
        if sem not in self.dma_cnt:
            self.dma_cnt[sem] = 0
            self.sem_names.append(sem)
        self.dma_cnt[sem] += inc
        tok = (sem, self.dma_cnt[sem])
        self.lists[eng].append(("ins", fn, sem, -inc))
        self._mark(tok, reads, writes)
        return tok

    def wait_tokens(self, eng, toks):
        need = {}
        for t in toks:
            self._need(eng, t, need)
        for s, v in need.items():
            self.waited[eng][s] = v
            self.lists[eng].append(("wait", s, v))

    def emit(self):
        nc = self.nc
        with contextlib.ExitStack() as st:
            sems = {}
            for n in self.sem_names:
                sems[n] = st.enter_context(nc.semaphore(n))
            block = st.enter_context(nc.Block())
            engmap = {"pe": block.tensor, "act": block.scalar, "dve": block.vector,
                      "pool": block.gpsimd, "sp": block.sync}
            for e in self.ENGS:
                lst = self.lists[e]
                if not lst:
                    continue

                def body(engobj, lst=lst):
                    for it in lst:
                        if it[0] == "wait":
                            engobj.wait_ge(sems[it[1]], it[2])
                        else:
                            ins = it[1](engobj)
                            if it[2] is not None:
                                if it[3] < 0:
                                    ins.then_inc(sems[it[2]])
                                else:
                                    ins.then_inc(sems[it[2]], it[3])
                engmap[e](body)


def build_nc(two_phase=True, debug=False):
    nc = bass.Bass("TRN2", target_bir_lowering=False)
    dr = lambda n, s, k, d=F32: nc.dram_tensor(n, list(s), d, kind=k)
    xs_d = dr("xs", [XROWS, D], "ExternalInput").ap()
    wmod_d = dr("w_mod", [D, 3 * D], "ExternalInput").ap()
    win_d = dr("w_in", [D, DIN], "ExternalInput").ap()
    wout_d = dr("w_out", [2 * D, D], "ExternalInput").ap()
    pf_d = dr("pf", [128, NPF], "ExternalInput").ap()
    bc_d = dr("bc", [128, NBC], "ExternalInput").ap()
    cst_d = dr("cst", [128, NCONST], "ExternalInput").ap()
    cT_d = dr("cT", [128, 8 * 3], "ExternalInput").ap()
    cbc_d = dr("cbc", [2, 128, 8 * 128], "ExternalInput").ap()
    sca_d = dr("sca", [128, 8 * 2 * 2], "ExternalInput").ap()
    scb_d = dr("scb", [128, 16 * 2 * 3], "ExternalInput").ap()
    sst_d = dr("sst", [2, 128, D], "ExternalInput").ap()
    msk_d = dr("msk", [128, 16], "ExternalInput").ap()
    y_d = dr("y", [YROWS, D], "ExternalOutput").ap()
    ca_d = dr("ca", [128, 8 * 3 * 2], "ExternalOutput").ap()
    cb_d = dr("cb", [128, 16 * 3 * 3], "ExternalOutput").ap()
    so_d = dr("so", [3, 128, D], "ExternalOutput").ap()
    winbf_d = nc.dram_tensor("winbf", [NPIECE, 128, 8 * PW], BF16)

    pl = Planner(nc)
    pl.alias = {"stg0": ("rhs1", "segs", "eac", "yo"), "stg1": ("stsb", "o1", "ost0", "ost1"), "sqj": ("Mt",)}
    with contextlib.ExitStack() as st:
        def SB(name, shape, dt=F32):
            return st.enter_context(nc.sbuf_tensor("sb_" + name, list(shape), dt))

        cst = SB("cst", [128, NCONST])
        idb = SB("idb", [128, 128], BF16)
        pf = SB("pf", [128, NPF])
        bc = SB("bc", [128, NBC])
        msk = SB("msk", [128, 16])
        cT = SB("cT", [128, 8, 3])
        gam = SB("gam", [128, 3, 8])
        bet = SB("bet", [128, 3, 8])
        gate = SB("gate", [128, 2, D])
        caw = SB("caw", [128, 8, 3])
        cbw = SB("cbw", [128, 16, 4])
        cbb = SB("cbb", [128, 16])
        a_bc = SB("a_bc", [128, 16])
        wout = SB("wout", [128, 16, D], BF16)
        wdt = SB("wdt", [128, 8, 16], BF16)
        wbuf = [SB("wbuf%d" % i, [128, 8, PW], BF16) for i in range(NWB)]
        big = [SB("big%d" % i, [128, 4096]) for i in range(2)]
        stg = [b[:].rearrange("p (a c) -> p a c", a=8) for b in big]
        xt = [SB("xt%d" % i, [128, D]) for i in range(2)]
        hT = SB("hT", [128, 8, T], BF16)
        ubuf = [SB("ubuf%d" % i, [128, 3 + T]) for i in range(2)]
        hsb = [SB("hsb%d" % i, [128, T]) for i in range(2)]
        cu = [SB("cu%d" % i, [128, T]) for i in range(2)]
        tq = [SB("tq%d" % i, [128, T]) for i in range(2)]
        sq2 = [SB("sq2%d" % i, [128, T], BF16) for i in range(2)]
        uhist = SB("uhist", [128, 8, 3])
        xhist = SB("xhist", [128, 16, 3])
        uhist0 = SB("uhist0", [128, 8, 3])
        xhist0 = SB("xhist0", [128, 16, 3])
        ycat = SB("ycat", [128, 16, T], BF16)
        xbf = SB("xbf", [128, 8, T], BF16)
        BT = SB("BT", [128, 4, T], BF16)
        CT = SB("CT", [128, 4, T], BF16)
        sz = SB("sz", [128, 8, T], BF16)
        xdt = SB("xdt", [128, D], BF16)
        xw = SB("xw", [128, D], BF16)
        btok = SB("btok", [128, 512], BF16)
        rhs1 = big[0][:, 0:1024]
        segs = big[0][:, 1024:2048]
        Mt = SB("Mt", [128, D], BF16)
        sqj = Mt
        eac = big[0][:, 2048:3072]
        yo = big[0][:, 3072:4096]
        cbtm = SB("cbtm", [128, 4, 64])
        sm = SB("sm", [128, 8, 64])
        ST = SB("ST", [128, D])
        STb = SB("STb", [128, D], BF16)
        stsb = big[1][:, 0:1024]
        cd = SB("cd", [128, 2, 64])
        stat = SB("stat", [128, 64])
        ost = [big[1][:, 2048:3072], big[1][:, 3072:4096]]
        o1 = big[1][:, 1024:2048]
        casb = SB("casb", [128, 8, 3, 2])
        cbsb = SB("cbsb", [128, 16, 3, 3])
        atot = SB("atot", [128, 16])
        gsel = SB("gsel", [128, 8, 16])
        print("sbuf remaining after alloc:", nc.sbuf_bytes_remaining)

        psA = st.enter_context(nc.psum_tensor("psA", [128, 4, 512], F32))
        psB = st.enter_context(nc.psum_tensor("psB", [128, 2, 512], F32))
        psC = st.enter_context(nc.psum_tensor("psC", [128, 512], F32))
        psT = st.enter_context(nc.psum_tensor("psT", [128, 1024], BF16))

        ident = cst[:, C_ID:C_ID + 128]
        blk = cst[:, C_BLK:C_BLK + 128]
        tri = cst[:, C_TRI:C_TRI + 128]
        t64 = cst[:, C_T64:C_T64 + 64]
        chsel = [cst[:, C_SEL0:C_SEL0 + 128], cst[:, C_SEL1:C_SEL1 + 128]]

        def pfc(off, j):
            return pf[:, off + j:off + j + 1]

        ld = lambda name, dst, src, key: pl.dma("sp", lambda e: e.dma_start(out=dst, in_=src), name, writes=[key])
        ld("l_cst", cst[:], cst_d[:, :], "cst")
        ld("l_pf", pf[:], pf_d[:, :], "pf")
        ld("l_bc", bc[:], bc_d[:, :], "bc")
        ld("l_msk", msk[:], msk_d[:, :], "msk")
        ld("l_cT", cT[:].rearrange("p a b -> p (a b)"), cT_d[:, :], "cT")
        pl.op("dve", lambda e: e.tensor_copy(out=idb[:], in_=ident), reads=["cst"], writes=["idb"])
        pl.op("dve", lambda e: e.tensor_scalar_mul(out=caw[:].rearrange("p a b -> p (a b)"), in0=pf[:, PF_CAW:PF_CAW + 24], scalar1=1.0), reads=["pf"], writes=["caw"])
        pl.op("dve", lambda e: e.tensor_scalar_mul(out=cbw[:].rearrange("p a b -> p (a b)"), in0=pf[:, PF_CBW:PF_CBW + 64], scalar1=1.0), reads=["pf"], writes=["cbw"])
        pl.op("dve", lambda e: e.tensor_scalar_mul(out=cbb[:], in0=pf[:, PF_CBB:PF_CBB + 16], scalar1=1.0), reads=["pf"], writes=["cbb"])
        pl.op("act", lambda e: e.activation(out=a_bc[:], in_=bc[:, BC_ALOG:BC_ALOG + 16], func=AF.Exp), reads=["bc"], writes=["a_bc"])
        pl.op("dve", lambda e: e.tensor_scalar_mul(out=a_bc[:], in0=a_bc[:], scalar1=-1.0), reads=["a_bc"], writes=["a_bc"])

        wmod_v = wmod_d.rearrange("(kc p) c -> p kc c", p=128)
        for piece in range(6):
            s = stg[piece % 2]
            key = "stg%d" % (piece % 2)
            pl.dma("sp", lambda e, s=s, piece=piece: e.dma_start(out=s[:], in_=wmod_v[:, :, piece * 512:(piece + 1) * 512]), "l_" + key, writes=[key])
            if piece < 4:
                for cc in range(4):
                    j = (piece % 2) * 4 + cc
                    for kc in range(8):
                        pl.op("pe", lambda e, s=s, cc=cc, kc=kc, j=j: e.matmul(psC[:, j * 4:j * 4 + 3], lhsT=s[:, kc, cc * 128:(cc + 1) * 128], rhs=cT[:, kc, :], start=(kc == 0), stop=(kc == 7)),
                              reads=[key, "cT"], writes=["psC"], token=(kc == 7))
                if piece % 2 == 1:
                    src = psC[:, 0:32].rearrange("p (j s) -> p s j", s=4)[:, 0:3, :]
                    if piece == 1:
                        pl.op("dve", lambda e, src=src: e.tensor_tensor(out=bet[:], in0=src, in1=pf[:, PF_BSH:PF_BSH + 8].unsqueeze(1).broadcast_to([128, 3, 8]), op=ALU.add), reads=["psC", "pf"], writes=["bet"])
                    else:
                        pl.op("dve", lambda e, src=src: e.tensor_tensor(out=gam[:], in0=src, in1=pf[:, PF_BSC:PF_BSC + 8].unsqueeze(1).broadcast_to([128, 3, 8]), op=ALU.add), reads=["psC", "pf"], writes=["gam"])
                        pl.op("dve", lambda e: e.scalar_tensor_tensor(out=gam[:], in0=gam[:], scalar=1.0, in1=pf[:, PF_NIN:PF_NIN + 8].unsqueeze(1).broadcast_to([128, 3, 8]), op0=ALU.add, op1=ALU.mult), reads=["gam", "pf"], writes=["gam"])
            else:
                half = piece - 4
                for which in range(2):
                    cb_t = xt[which]
                    if half == 0:
                        pl.dma("sp", lambda e, cb_t=cb_t, which=which: e.dma_start(out=cb_t[:], in_=cbc_d[which, :, :]), "l_xt%d" % which, writes=["xt%d" % which])
                    cbv = cb_t[:].rearrange("p (k m) -> p k m", k=8)
                    for kc in range(8):
                        pl.op("pe", lambda e, s=s, kc=kc, cbv=cbv, which=which: e.matmul(psA[:, which, :], lhsT=cbv[:, kc, :], rhs=s[:, kc, :], start=(kc == 0), stop=(kc == 7)),
                              reads=[key, "xt%d" % which], writes=["psA%d" % which], token=(kc == 7))
                    pl.op("dve", lambda e, which=which, half=half: e.tensor_tensor(out=gate[:, which, half * 512:(half + 1) * 512], in0=psA[:, which, :], in1=bc[:, BC_BG + half * 512:BC_BG + (half + 1) * 512], op=ALU.add),
                          reads=["psA%d" % which, "bc"], writes=["gate"])

        win_v = win_d.rearrange("(kc p) c -> p kc c", p=128)
        wout_v = wout_d.rearrange("(kc p) c -> p kc c", p=128)
        castengs = ["dve", "act", "pool"]
        ci = 0

        def cast(dst, src, rk, wk):
            nonlocal ci
            eng = castengs[ci % 3]
            ci += 1
            if eng == "act":
                pl.op("act", lambda e: e.copy(out=dst, in_=src), reads=rk, writes=wk)
            else:
                pl.op(eng, lambda e: e.tensor_copy(out=dst, in_=src), reads=rk, writes=wk)

        porder = [10, 11, 12, 2, 3, 4, 5, 0, 1, 6, 7, 13, 8, 9]
        pl.dma("sp", lambda e: e.dma_start(out=stg[0][:, :, 0:16], in_=win_v[:, :, 7168:7184]), "l_stg0", writes=["stg0"])
        pl.op("dve", lambda e: e.tensor_copy(out=wdt[:], in_=stg[0][:, :, 0:16]), reads=["stg0"], writes=["wdt"])
        wci = 0
        for n, piece in enumerate(porder):
            s = stg[n % 2]
            key = "stg%d" % (n % 2)
            pl.dma("sp", lambda e, s=s, piece=piece: e.dma_start(out=s[:], in_=win_v[:, :, piece * 512:(piece + 1) * 512]), "l_" + key, writes=[key])
            for hp in range(512 // PW):
                wb = wbuf[wci % NWB]
                wkey = "wbuf%d" % (wci % NWB)
                wci += 1
                for kh in range(2):
                    cast(wb[:, kh * 4:(kh + 1) * 4, :], s[:, kh * 4:(kh + 1) * 4, hp * PW:(hp + 1) * PW], [key], [wkey])
                sp_ = (512 // PW) * piece + hp
                pl.dma("pool", lambda e, wb=wb, sp_=sp_: e.dma_start(out=winbf_d[sp_, :, :], in_=wb[:].rearrange("p a b -> p (a b)")), "s_" + wkey, reads=[wkey], writes=["winbf%d" % sp_])
        for n in range(4):
            kg, ch = n // 2, n % 2
            s = stg[n % 2]
            key = "stg%d" % (n % 2)
            pl.dma("sp", lambda e, s=s, kg=kg, ch=ch: e.dma_start(out=s[:], in_=wout_v[:, kg * 8:(kg + 1) * 8, ch * 512:(ch + 1) * 512]), "l_" + key, writes=[key])
            for kh in range(2):
                cast(wout[:, kg * 8 + kh * 4:kg * 8 + (kh + 1) * 4, ch * 512:(ch + 1) * 512], s[:, kh * 4:(kh + 1) * 4, :], [key], ["wout"])

        wcnt = [0]

        def load_piece(piece):
            i = wcnt[0] % NWB
            wcnt[0] += 1
            wb = wbuf[i]
            pl.dma("sp", lambda e: e.dma_start(out=wb[:].rearrange("p a b -> p (a b)"), in_=winbf_d[piece, :, :]), "l_wbuf%d" % i, reads=["winbf%d" % piece], writes=["wbuf%d" % i])
            return wb, "wbuf%d" % i

        hcur = [hT, "hT"]

        def proj_chunk(wb, wkey, cc, ps_ap, pskey, Tn):
            hb, hk = hcur
            for kc in range(8):
                pl.op("pe", lambda e, kc=kc: e.matmul(ps_ap, lhsT=wb[:, kc, cc * 128:(cc + 1) * 128], rhs=hb[:, kc, 0:Tn], start=(kc == 0), stop=(kc == 7)),
                      reads=[wkey, hk], writes=[pskey], token=(kc == 7))

        pref = {}

        def load_x(row0, s):
            xb = xt[s % 2]
            xk = "xt%d" % (s % 2)
            pl.dma("sp", lambda e: e.dma_start(out=xb[:], in_=xs_d[row0 + s * 128:row0 + (s + 1) * 128, :]), "l_" + xk, writes=[xk])

        def step_a(row0, Tn, slots, hb=None, hk="hT"):
            for p0 in range(0, Tn // 128, 2):
                step_a_pair(row0, p0, Tn, slots, hT if hb is None else hb, hk)

        def step_a_pair(row0, p0, Tn, slots, hb, hk):
            nsub = Tn // 128
            if True:
                subs = list(range(p0, min(p0 + 2, nsub)))
                for s in subs:
                    xb = xt[s % 2]
                    xk = "xt%d" % (s % 2)
                    if not pref.pop((row0, s), False):
                        load_x(row0, s)
                    pl.op("pool", lambda e, s=s: e.memset(stat[:, s:s + 1], 0.0), writes=["stat"])
                    pl.op("act", lambda e, xb=xb, s=s: e.activation(out=sqj[:], in_=xb[:], func=AF.Square, accum_out=stat[:, s:s + 1]), reads=[xk, "stat"], writes=["sqj", "stat"])
                    pl.op("act", lambda e, s=s: e.activation(out=stat[:, 8 + s:9 + s], in_=stat[:, s:s + 1], func=AF.Ln, scale=1.0 / D, bias=EPS), reads=["stat"], writes=["stat"])
                    pl.op("act", lambda e, s=s: e.activation(out=stat[:, 16 + s:17 + s], in_=stat[:, 8 + s:9 + s], func=AF.Exp, scale=-0.5), reads=["stat"], writes=["stat"])
                    pl.op("dve", lambda e, xb=xb, s=s: e.tensor_scalar_mul(out=xb[:], in0=xb[:], scalar1=stat[:, 16 + s:17 + s]), reads=[xk, "stat"], writes=[xk])
                w0, w1 = subs[0] * 128, (subs[-1] + 1) * 128
                for kc in range(8):
                    pk = "psB%d" % (kc % 2)
                    for s in subs:
                        xb = xt[s % 2]
                        xk = "xt%d" % (s % 2)
                        pl.op("pe", lambda e, xb=xb, kc=kc, s=s, p0=p0: e.transpose(out=psB[:, kc % 2, (s - p0) * 128:(s - p0 + 1) * 128], in_=xb[:, kc * 128:(kc + 1) * 128], identity=ident), reads=[xk, "cst"], writes=[pk], token=(s == subs[-1]))
                    for (c0, c1, slot) in slots:
                        lo, hi = max(c0, w0), min(c1, w1)
                        if lo >= hi:
                            continue
                        pl.op("act", lambda e, kc=kc, lo=lo, hi=hi, slot=slot, w0=w0: e.activation(out=hb[:, kc, lo:hi], in_=psB[:, kc % 2, lo - w0:hi - w0], func=AF.Identity, scale=gam[:, slot, kc:kc + 1], bias=bet[:, slot, kc:kc + 1]),
                              reads=[pk, "gam", "bet"], writes=[hk])

        def conv_chunk(eng_first, src, wts, nk, dst, Tn, rk, wk, bias=None, bk=()):
            off = 3 - (nk - 1)
            if bias is None:
                pl.op("act", lambda e: e.activation(out=dst[:, 0:Tn], in_=src[:, off:off + Tn], func=AF.Copy, scale=wts[:, 0:1]), reads=rk, writes=wk)
            else:
                pl.op("act", lambda e: e.activation(out=dst[:, 0:Tn], in_=src[:, off:off + Tn], func=AF.Identity, scale=wts[:, 0:1], bias=bias), reads=rk + list(bk), writes=wk)
            for k in range(1, nk):
                pl.op("dve", lambda e, k=k: e.scalar_tensor_tensor(out=dst[:, 0:Tn], in0=src[:, off + k:off + k + Tn], scalar=wts[:, k:k + 1], in1=dst[:, 0:Tn], op0=ALU.mult, op1=ALU.add), reads=rk + wk, writes=wk)

        def tile(row0, Tn, slots, mode, gidx, yrow0, hist_from, st_mode, out_slot, next_row0=None, h_idx=0, pre_a=False, next_a=None):
            nsub = Tn // 128
            nch = Tn // 64
            hcur[0], hcur[1] = ((hT, "hT"), (ycat[:, 0:8, :], "ycat"))[h_idx]
            if not pre_a:
                step_a(row0, Tn, slots, hcur[0], hcur[1])
            light = (mode != "full")
            if mode in ("full", "halo"):
                wbs = {}
                pendA = None
                for j in range(8):
                    i = j % 2
                    need = [8 + j, 16 + j] if mode == "halo" else [8 + j, 16 + j, 0 + j, 24 + j]
                    for pc in need:
                        wbs[pc] = load_piece(pc)
                    cc = 0
                    if j % 2 == 0:
                        (pc_, pck), (ph_, phk), (pb_, pbk) = (psA[:, 0, 0:Tn], "psA0"), (psA[:, 1, 0:Tn], "psA1"), (psA[:, 2, 0:Tn], "psA2")
                    else:
                        (pc_, pck), (ph_, phk), (pb_, pbk) = (psA[:, 0, 0:Tn], "psA0"), (psA[:, 1, 0:Tn], "psA1"), (psA[:, 2, 0:Tn], "psA2")
                    wb, wk_ = wbs[8 + j]
                    proj_chunk(wb, wk_, cc, pc_, pck, Tn)
                    wb, wk_ = wbs[16 + j]
                    proj_chunk(wb, wk_, cc, ph_, phk, Tn)
                    ub, uk = ubuf[i], "ubuf%d" % i
                    pl.op("act", lambda e, i=i, ph_=ph_: e.copy(out=hsb[i][:, 0:Tn], in_=ph_), reads=[phk], writes=["hsb%d" % i])
                    pl.op("pool", lambda e, ub=ub, j=j: e.tensor_copy(out=ub[:, 0:3], in_=uhist[:, j, :]), reads=["uhist"], writes=[uk])
                    pl.op("dve", lambda e, ub=ub, i=i, pc_=pc_: e.tensor_tensor(out=ub[:, 3:3 + Tn], in0=pc_, in1=hsb[i][:, 0:Tn], op=ALU.mult), reads=[pck, "hsb%d" % i], writes=[uk])
                    pl.op("pool", lambda e, ub=ub, j=j: e.tensor_copy(out=uhist[:, j, :], in_=ub[:, Tn:Tn + 3]), reads=[uk], writes=["uhist"])
                    if mode == "halo":
                        continue
                    wb, wk_ = wbs[0 + j]
                    proj_chunk(wb, wk_, cc, pb_, pbk, Tn)
                    wb, wk_ = wbs[24 + j]
                    proj_chunk(wb, wk_, cc, psA[:, 3, 0:Tn], "psA3", Tn)
                    wbz, wkz = load_piece(32 + j)
                    proj_chunk(wbz, wkz, 0, psB[:, j % 2, 0:Tn], "psB%d" % (j % 2), Tn)
                    pl.op("act", lambda e, j=j: e.activation(out=sz[:, j, 0:Tn], in_=psB[:, j % 2, 0:Tn], func=AF.Silu), reads=["psB%d" % (j % 2)], writes=["sz"])
                    pl.op("act", lambda e, i=i: e.activation(out=tq[i][:, 0:Tn], in_=psA[:, 3, 0:Tn], func=AF.Silu), reads=["psA3"], writes=["tq%d" % i])
                    pl.op("dve", lambda e, i=i, pb_=pb_: e.tensor_tensor(out=tq[i][:, 0:Tn], in0=pb_, in1=tq[i][:, 0:Tn], op=ALU.mult), reads=[pbk, "tq%d" % i], writes=["tq%d" % i])
                    c_, ck = cu[i], "cu%d" % i
                    if hist_from == "carry":
                        conv_chunk("dve", ub, caw[:, j, :], 3, c_, Tn, [uk, "caw"], [ck])
                    else:
                        for sidx in range(2):
                            pl.op("dve", lambda e, ub=ub, j=j, sidx=sidx: e.tensor_copy(out=stsb[:, sidx * 128 + 1:sidx * 128 + 3], in_=sca_sb[:, j, sidx, :]), reads=["sca"], writes=["stsb"])
                        for sidx in range(2):
                            base = sidx * 128
                            pl.op("dve", lambda e, ub=ub, base=base, sidx=sidx: e.tensor_copy(out=stsb[:, base + 3:base + 67], in_=ub[:, 3 + sidx * 64:3 + (sidx + 1) * 64]), reads=[uk], writes=["stsb"])
                            off = 1
                            pl.op("dve", lambda e, c_=c_, base=base, sidx=sidx, j=j: e.tensor_scalar_mul(out=c_[:, sidx * 64:(sidx + 1) * 64], in0=stsb[:, base + 1:base + 65], scalar1=caw[:, j, 0:1]), reads=["stsb", "caw"], writes=[ck])
                            for k in (1, 2):
                                pl.op("dve", lambda e, c_=c_, base=base, sidx=sidx, j=j, k=k: e.scalar_tensor_tensor(out=c_[:, sidx * 64:(sidx + 1) * 64], in0=stsb[:, base + 1 + k:base + 65 + k], scalar=caw[:, j, k:k + 1], in1=c_[:, sidx * 64:(sidx + 1) * 64], op0=ALU.mult, op1=ALU.add), reads=["stsb", "caw", ck], writes=[ck])
                            pl.op("dve", lambda e, base=base, sidx=sidx, j=j: e.tensor_copy(out=casb[:, j, 1 + sidx, :], in_=stsb[:, base + 65:base + 67]), reads=["stsb"], writes=["casb"])
                    def stage_b(c_=c_, ck=ck, i=i, j=j):
                        pl.op("dve", lambda e: e.tensor_tensor(out=c_[:, 0:Tn], in0=c_[:, 0:Tn], in1=tq[i][:, 0:Tn], op=ALU.mult), reads=[ck, "tq%d" % i], writes=[ck])
                        pl.op("act", lambda e: e.activation(out=sq2[i][:, 0:Tn], in_=c_[:, 0:Tn], func=AF.Square), reads=[ck], writes=["sq2%d" % i])
                        pl.op("act", lambda e: e.activation(out=ycat[:, j, 0:Tn], in_=c_[:, 0:Tn], func=AF.Copy, scale=pfc(PF_NAW, j)), reads=[ck, "pf"], writes=["ycat"])
                        for s in range(nsub):
                            pl.op("pe", lambda e, s=s: e.matmul(psC[:, 320 + s:321 + s], lhsT=sq2[i][:, s * 128:(s + 1) * 128], rhs=ones_bf[:, 0:1], start=(j == 0 and s == 0), stop=(j == 7), skip_group_check=True),
                                  reads=["sq2%d" % i, "ones"], writes=["psCa"], token=(s == nsub - 1))
                    if pendA is not None:
                        pendA()
                    pendA = stage_b
                if pendA is not None:
                    pendA()
                if mode == "full" and hist_from == "carry":
                    pl.op("dve", lambda e: e.tensor_copy(out=casb[:, :, 0, :], in_=uhist[:, :, 1:3]), reads=["uhist"], writes=["casb"])
                if mode == "halo":
                    pl.op("dve", lambda e: e.tensor_scalar_mul(out=uhist[:].rearrange("p a b -> p (a b)"), in0=uhist[:].rearrange("p a b -> p (a b)"), scalar1=msk[:, 0:1]), reads=["uhist", "msk"], writes=["uhist"])
                    wbs = {}
                    for j in range(12, 16):
                        wb, wk_ = load_piece(40 + j)
                        pa = psA[:, j % 4, 0:Tn]
                        pk = "psA%d" % (j % 4)
                        proj_chunk(wb, wk_, 0, pa, pk, Tn)
                        pl.op("act", lambda e, j=j, pa=pa: e.copy(out=xhist[:, j, :], in_=pa[:, Tn - 3:Tn]), reads=[pk], writes=["xhist"])
                    pl.op("dve", lambda e: e.tensor_scalar_mul(out=xhist[:, 12:16, :], in0=xhist[:, 12:16, :], scalar1=msk[:, 0:1]), reads=["xhist", "msk"], writes=["xhist"])
                    return

            wbs = {}
            pending = None
            for j in range(16):
                if mode == "light" and j >= 12:
                    break
                wb, wk_ = load_piece(40 + j)
                i = j % 2
                pa = psA[:, j % 4, 0:Tn]
                pk = "psA%d" % (j % 4)
                proj_chunk(wb, wk_, 0, pa, pk, Tn)
                ub, uk = ubuf[i], "ubuf%d" % i
                pl.op("pool", lambda e, ub=ub, j=j: e.tensor_copy(out=ub[:, 0:3], in_=xhist[:, j, :]), reads=["xhist"], writes=[uk])
                pl.op("act", lambda e, ub=ub, pa=pa: e.copy(out=ub[:, 3:3 + Tn], in_=pa), reads=[pk], writes=[uk])
                pl.op("pool", lambda e, ub=ub, j=j: e.tensor_copy(out=xhist[:, j, :], in_=ub[:, Tn:Tn + 3]), reads=[uk], writes=["xhist"])
                c_, ck = cu[i], "cu%d" % i
                if hist_from == "carry":
                    conv_chunk("act", ub, cbw[:, j, :], 4, c_, Tn, [uk, "cbw"], [ck], bias=cbb[:, j:j + 1], bk=["cbb"])
                else:
                    for sidx in range(2):
                        base = sidx * 128
                        pl.op("dve", lambda e, j=j, sidx=sidx, base=base: e.tensor_copy(out=stsb[:, base:base + 3], in_=scb_sb[:, j, sidx, :]), reads=["scb"], writes=["stsb"])
                        pl.op("dve", lambda e, ub=ub, base=base, sidx=sidx: e.tensor_copy(out=stsb[:, base + 3:base + 67], in_=ub[:, 3 + sidx * 64:3 + (sidx + 1) * 64]), reads=[uk], writes=["stsb"])
                        pl.op("dve", lambda e, c_=c_, base=base, sidx=sidx, j=j: e.tensor_scalar_mul(out=c_[:, sidx * 64:(sidx + 1) * 64], in0=stsb[:, base:base + 64], scalar1=cbw[:, j, 0:1]), reads=["stsb", "cbw"], writes=[ck])
                        for k in (1, 2, 3):
                            pl.op("dve", lambda e, c_=c_, base=base, sidx=sidx, j=j, k=k: e.scalar_tensor_tensor(out=c_[:, sidx * 64:(sidx + 1) * 64], in0=stsb[:, base + k:base + 64 + k], scalar=cbw[:, j, k:k + 1], in1=c_[:, sidx * 64:(sidx + 1) * 64], op0=ALU.mult, op1=ALU.add), reads=["stsb", "cbw", ck], writes=[ck])
                        pl.op("dve", lambda e, base=base, sidx=sidx, j=j: e.tensor_copy(out=cbsb[:, j, 1 + sidx, :], in_=stsb[:, base + 64:base + 67]), reads=["stsb"], writes=["cbsb"])
                    pl.op("dve", lambda e, c_=c_, j=j: e.tensor_scalar_add(out=c_[:, 0:Tn], in0=c_[:, 0:Tn], scalar1=cbb[:, j:j + 1]), reads=[ck, "cbb"], writes=[ck])
                if j < 8:
                    dst, dk = xbf[:, j, 0:Tn], "xbf"
                elif j < 12:
                    dst, dk = BT[:, j - 8, 0:Tn], "BT"
                else:
                    dst, dk = CT[:, j - 12, 0:Tn], "CT"

                def stage_b(c_=c_, ck=ck, i=i, dst=dst, dk=dk):
                    pl.op("act", lambda e: e.activation(out=dst, in_=c_[:, 0:Tn], func=AF.Silu), reads=[ck], writes=[dk])
                if pending is not None:
                    pending()
                pending = stage_b
            if pending is not None:
                pending()
            if mode == "full" and hist_from == "carry":
                pl.op("dve", lambda e: e.tensor_copy(out=cbsb[:, :, 0, :], in_=xhist[:]), reads=["xhist"], writes=["cbsb"])
            W = nsub * 16
            SMA = lambda idx: sm[:, idx, 0:W]
            v3 = lambda ap: ap.rearrange("p (b h) -> p b h", h=16)
            for b in range(nsub):
                tsl = slice(b * 128, (b + 1) * 128)
                for kc in range(8):
                    pl.op("pe", lambda e, kc=kc, tsl=tsl, b=b, hb=hcur[0]: e.matmul(psC[:, 256 + b * 16:272 + b * 16], lhsT=hb[:, kc, tsl], rhs=wdt[:, kc, :], start=(kc == 0), stop=(kc == 7)), reads=[hcur[1], "wdt"], writes=["psCd"], token=(kc == 7))
            pl.op("dve", lambda e: e.tensor_tensor(out=v3(SMA(0)), in0=v3(psC[:, 256:256 + W]), in1=bc[:, BC_DTB:BC_DTB + 16].unsqueeze(1).broadcast_to([128, nsub, 16]), op=ALU.add), reads=["psCd", "bc"], writes=["sm0"])
            pl.op("act", lambda e: e.activation(out=SMA(1), in_=SMA(0), func=AF.Abs), reads=["sm0"], writes=["sm1"])
            pl.op("act", lambda e: e.activation(out=SMA(1), in_=SMA(1), func=AF.Exp, scale=-1.0), reads=["sm1"], writes=["sm1"])
            pl.op("act", lambda e: e.activation(out=SMA(1), in_=SMA(1), func=AF.Ln, bias=1.0), reads=["sm1"], writes=["sm1"])
            pl.op("dve", lambda e: e.scalar_tensor_tensor(out=SMA(2), in0=SMA(0), scalar=0.0, in1=SMA(1), op0=ALU.max, op1=ALU.add), reads=["sm0", "sm1"], writes=["sm2"])
            pl.op("dve", lambda e: e.tensor_tensor(out=v3(SMA(3)), in0=v3(SMA(2)), in1=a_bc[:].unsqueeze(1).broadcast_to([128, nsub, 16]), op=ALU.mult), reads=["sm2", "a_bc"], writes=["sm3"])
            pl.op("pe", lambda e: e.matmul(psB[:, 0, 0:W], lhsT=tri, rhs=SMA(3), start=True, stop=True), reads=["cst", "sm3"], writes=["psB0"], token=False)
            pl.op("pe", lambda e: e.matmul(psB[:, 0, 64:64 + W], lhsT=blk, rhs=SMA(3), start=True, stop=True), reads=["cst", "sm3"], writes=["psB0"], token=False)
            pl.op("pe", lambda e: e.matmul(psB[:, 0, 128:128 + W], lhsT=chsel[0], rhs=SMA(3), start=True, stop=True), reads=["cst", "sm3"], writes=["psB0"], token=False)
            pl.op("pe", lambda e: e.matmul(psB[:, 0, 192:192 + W], lhsT=chsel[1], rhs=SMA(3), start=True, stop=True), reads=["cst", "sm3"], writes=["psB0"])
            pl.op("dve", lambda e: e.tensor_copy(out=SMA(4), in_=psB[:, 0, 0:W]), reads=["psB0"], writes=["sm4"])
            pl.op("dve", lambda e: e.tensor_tensor(out=SMA(5), in0=psB[:, 0, 64:64 + W], in1=SMA(4), op=ALU.subtract), reads=["psB0", "sm4"], writes=["sm5"])
            if mode == "light":
                sel0, sel1 = psB[:, 0, 128:128 + W], psB[:, 0, 192:192 + W]
                pl.op("dve", lambda e: e.tensor_copy(out=SMA(7), in_=sel1), reads=["psB0"], writes=["sm7"])
                pl.op("dve", lambda e: e.tensor_tensor(out=SMA(0), in0=sel0, in1=SMA(7), op=ALU.add), reads=["psB0", "sm7"], writes=["sm0"])
                pl.op("dve", lambda e: e.memset(SMA(1), 0.0), writes=["sm1"])
                for b in range(nsub - 2, -1, -1):
                    pl.op("dve", lambda e, b=b: e.tensor_tensor(out=sm[:, 1, b * 16:(b + 1) * 16], in0=sm[:, 1, (b + 1) * 16:(b + 2) * 16], in1=sm[:, 0, (b + 1) * 16:(b + 2) * 16], op=ALU.add), reads=["sm1", "sm0"], writes=["sm1"])
                pl.op("dve", lambda e: e.tensor_tensor(out=cd[:, 1, 0:16], in0=sm[:, 1, 0:16], in1=sm[:, 0, 0:16], op=ALU.add), reads=["sm1", "sm0"], writes=["cd"])
                pl.op("act", lambda e: e.activation(out=cd[:, 0, 0:16], in_=cd[:, 1, 0:16], func=AF.Exp), reads=["cd"], writes=["cd"])
                pl.op("dve", lambda e: e.scalar_tensor_tensor(out=SMA(1), in0=SMA(7), scalar=chsel[0][:, 0:1], in1=SMA(1), op0=ALU.mult, op1=ALU.add), reads=["sm7", "cst", "sm1"], writes=["sm1"])
                pl.op("dve", lambda e: e.tensor_tensor(out=SMA(5), in0=SMA(5), in1=SMA(1), op=ALU.add), reads=["sm5", "sm1"], writes=["sm5"])
            pl.op("act", lambda e: e.activation(out=SMA(5), in_=SMA(5), func=AF.Exp), reads=["sm5"], writes=["sm5"])
            pl.op("dve", lambda e: e.tensor_tensor(out=SMA(6), in0=SMA(5), in1=SMA(2), op=ALU.mult), reads=["sm5", "sm2"], writes=["sm6"])
            if mode != "light":
                pl.op("act", lambda e: e.activation(out=cd[:, 0, 0:W], in_=psB[:, 0, 128:128 + W], func=AF.Exp), reads=["psB0"], writes=["cd"])
                pl.op("act", lambda e: e.activation(out=cd[:, 1, 0:W], in_=psB[:, 0, 192:192 + W], func=AF.Exp), reads=["psB0"], writes=["cd"])
            else:
                if st_mode[0] is not None and st_mode[0][0] == "zero":
                    pl.op("dve", lambda e: e.memset(ST[:], 0.0), writes=["ST"])
                pl.op("pool", lambda e: e.tensor_tensor(out=ST[:].rearrange("p (h q) -> p h q", h=16), in0=ST[:].rearrange("p (h q) -> p h q", h=16), in1=cd[:, 0, 0:16].unsqueeze(2).broadcast_to([128, 16, 64]), op=ALU.mult), reads=["ST", "cd"], writes=["ST"])
            for b in range(nsub):
                tsl = slice(b * 128, (b + 1) * 128)
                SM = lambda idx, b=b: sm[:, idx, b * 16:(b + 1) * 16]
                bc16 = lambda idx, SM=SM: SM(idx).unsqueeze(2).broadcast_to([128, 16, 64])
                b16_2, b16_3, b16_4, b16_6 = bc16(2), bc16(3), bc16(4), bc16(6)
                for j in range(8):
                    pl.op("pe", lambda e, j=j, tsl=tsl: e.transpose(out=psT[:, j * 128:(j + 1) * 128], in_=xbf[:, j, tsl], identity=idb[:]), reads=["xbf", "idb"], writes=["psT"], token=(j == 7))
                bview = lambda ap: ap.rearrange("p (h q) -> p h q", h=16)
                if mode == "full":
                    pl.op("dve", lambda e, b16_2=b16_2: e.tensor_tensor(out=bview(xdt[:]), in0=bview(psT[:, :]), in1=b16_2, op=ALU.mult), reads=["psT", "sm2"], writes=["xdt"])
                if mode == "light":
                    xw_, xwk = ((xw, "xw"), (xdt, "xdt"))[b % 2]
                    bt_, btk = ((btok[:], "btok"), (Mt[:, 0:512], "Mt"))[b % 2]
                    bps, bpk = psC[:, 0:256].bitcast(BF16), "psCb"
                else:
                    xw_, xwk, bt_, btk, bps, bpk = xw, "xw", btok[:], "btok", psT[:, 0:512], "psT"
                pl.op("dve", lambda e, b16_6=b16_6, xw_=xw_: e.tensor_tensor(out=bview(xw_[:]), in0=bview(psT[:, :]), in1=b16_6, op=ALU.mult), reads=["psT", "sm6"], writes=[xwk])
                for g in range(4):
                    pl.op("pe", lambda e, g=g, tsl=tsl, bps=bps: e.transpose(out=bps[:, g * 128:(g + 1) * 128], in_=BT[:, g, tsl], identity=idb[:]), reads=["BT", "idb"], writes=[bpk], token=(g == 3))
                pl.op("act", lambda e, bt_=bt_, bps=bps: e.copy(out=bt_, in_=bps), reads=[bpk], writes=[btk])

                if mode == "full":
                    for g in range(4):
                        for c in range(2):
                            cs = slice(b * 128 + c * 64, b * 128 + (c + 1) * 64)
                            pl.op("pe", lambda e, g=g, c=c, cs=cs: e.matmul(psC[c * 64:(c + 1) * 64, g * 64:(g + 1) * 64], lhsT=BT[:, g, cs], rhs=CT[:, g, cs], start=True, stop=True), reads=["BT", "CT"], writes=["psCb"], token=(g == 3 and c == 1))
                    pl.op("dve", lambda e: e.tensor_tensor(out=cbtm[:], in0=psC[:, 0:256].rearrange("p (g i) -> p g i", g=4), in1=t64.unsqueeze(1).broadcast_to([128, 4, 64]), op=ALU.mult), reads=["psCb", "cst"], writes=["cbtm"])
                    pl.op("dve", lambda e, b16_3=b16_3: e.tensor_tensor(out=bview(rhs1[:]), in0=b16_3, in1=t64.unsqueeze(1).broadcast_to([128, 16, 64]), op=ALU.mult), reads=["sm3", "cst"], writes=["rhs1"])
                    for hh in range(2):
                        pl.op("pe", lambda e, hh=hh: e.matmul(psA[:, hh, :], lhsT=blk, rhs=rhs1[:, hh * 512:(hh + 1) * 512], start=True, stop=True), reads=["cst", "rhs1"], writes=["psA%d" % hh])
                    pl.op("dve", lambda e, b16_4=b16_4: e.tensor_tensor(out=bview(segs[:]), in0=psA[:, 0:2, :].rearrange("p a (h q) -> p (a h) q", q=64), in1=b16_4, op=ALU.subtract), reads=["psA0", "psA1", "sm4"], writes=["segs"])
                    pl.op("dve", lambda e: e.tensor_scalar_min(out=segs[:], in0=segs[:], scalar1=0.0), reads=["segs"], writes=["segs"])
                    pl.op("act", lambda e: e.activation(out=segs[:], in_=segs[:], func=AF.Exp), reads=["segs"], writes=["segs"])
                    for g in range(4):
                        pl.op("dve", lambda e, g=g: e.scalar_tensor_tensor(out=Mt[:, g * 256:(g + 1) * 256].rearrange("p (r q) -> p r q", r=4), in0=segs[:, g * 256:(g + 1) * 256].rearrange("p (r q) -> p r q", r=4), scalar=1.0,
                                                                           in1=cbtm[:, g, :].unsqueeze(1).broadcast_to([128, 4, 64]), op0=ALU.min, op1=ALU.mult), reads=["segs", "cbtm"], writes=["Mt"])
                    pl.op("pool", lambda e, tsl=tsl: e.tensor_tensor(out=segs[:].rearrange("p (a t) -> p a t", a=8), in0=xbf[:, :, tsl], in1=pf[:, PF_DSK:PF_DSK + 8].unsqueeze(2).broadcast_to([128, 8, 128]), op=ALU.mult), reads=["xbf", "pf", "segs"], writes=["segs", "segs2"])
                    pl.op("dve", lambda e, b16_3=b16_3: e.tensor_copy(out=bview(rhs1[:]), in_=b16_3), reads=["sm3"], writes=["rhs1"])
                    for a in range(8):
                        pl.op("pe", lambda e, a=a: e.matmul(psA[:, 2 + a // 4, (a % 4) * 128:(a % 4 + 1) * 128], lhsT=rhs1[:, a * 128:(a + 1) * 128], rhs=tri, start=True, stop=True), reads=["rhs1", "cst"], writes=["psA%d" % (2 + a // 4)], token=(a % 4 == 3))
                    pl.op("act", lambda e: e.activation(out=eac[:], in_=psA[:, 2:4, :].rearrange("p a q -> p (a q)"), func=AF.Exp), reads=["psA2", "psA3"], writes=["eac"])

                for c in range(2):
                    ch = b * 2 + c
                    cs = slice(b * 128 + c * 64, b * 128 + (c + 1) * 64)
                    ps_ = slice(c * 64, (c + 1) * 64)
                    if mode != "light" and st_mode[ch] is not None:
                        kind, val = st_mode[ch]
                        if kind == "zero":
                            pl.op("dve", lambda e: e.memset(ST[:], 0.0), writes=["ST"])
                        elif kind == "load":
                            pl.dma("sp", lambda e, val=val: e.dma_start(out=ST[:], in_=sst_d[val, :, :]), "l_ST", writes=["ST"])
                        elif kind == "keep":
                            pass
                        if mode == "full":
                            pl.op("act", lambda e: e.copy(out=STb[:], in_=ST[:]), reads=["ST"], writes=["STb"])
                    if mode != "light":
                        pl.op("pool", lambda e, c=c, b=b: e.tensor_tensor(out=bview(ST[:]), in0=bview(ST[:]), in1=cd[:, c, b * 16:(b + 1) * 16].unsqueeze(2).broadcast_to([128, 16, 64]), op=ALU.mult), reads=["ST", "cd"], writes=["ST"])
                    if mode == "full":
                        for a in range(8):
                            for hh in range(2):
                                h = 2 * a + hh
                                pl.op("pe", lambda e, a=a, hh=hh, h=h, c=c, ps_=ps_: e.matmul(psB[hh * 64:(hh + 1) * 64, a // 4, (a % 4) * 128 + c * 64:(a % 4) * 128 + (c + 1) * 64], lhsT=xdt[ps_, h * 64:(h + 1) * 64], rhs=Mt[ps_, h * 64:(h + 1) * 64], start=True, stop=True),
                                      reads=["xdt", "Mt"], writes=["psB%d" % (a // 4)], token=(hh == 1 and a % 4 == 3))
                        for a in range(8):
                            g = a // 2
                            pl.op("pe", lambda e, a=a, g=g, c=c, cs=cs: e.matmul(psA[:, a // 4, (a % 4) * 128 + c * 64:(a % 4) * 128 + (c + 1) * 64], lhsT=STb[:, a * 128:(a + 1) * 128], rhs=CT[:, g, cs], start=True, stop=True),
                                  reads=["STb", "CT"], writes=["psA%d" % (a // 4)], token=(a % 4 == 3))
                    first = (b == 0 and c == 0)
                    last = (b == nsub - 1 and c == 1)
                    for g in range(4):
                        if mode == "light":
                            bk = (2 if c == 0 else 0) + g // 2
                            pl.op("pe", lambda e, g=g, ps_=ps_, bk=bk, b=b, last=last, bt_=bt_, xw_=xw_: e.matmul(psA[:, bk, (g % 2) * 256:(g % 2 + 1) * 256], lhsT=bt_[ps_, g * 128:(g + 1) * 128], rhs=xw_[ps_, g * 256:(g + 1) * 256], start=(b == 0 and g % 2 == 0), stop=(b == nsub - 1), skip_group_check=True),
                                  reads=[btk, xwk], writes=["psA%d" % bk], token=(g % 2 == 1))
                        else:
                            pl.op("pe", lambda e, g=g, ps_=ps_: e.matmul(psA[:, 2 + g // 2, (g % 2) * 256:(g % 2 + 1) * 256], lhsT=btok[ps_, g * 128:(g + 1) * 128], rhs=xw[ps_, g * 256:(g + 1) * 256], start=True, stop=True),
                                  reads=["btok", "xw"], writes=["psA%d" % (2 + g // 2)], token=(g % 2 == 1))
                    if mode != "light" or last:
                        pl.op("dve", lambda e: e.tensor_tensor(out=ST[:], in0=ST[:], in1=psA[:, 2:4, :].rearrange("p a q -> p (a q)"), op=ALU.add), reads=["ST", "psA2", "psA3"], writes=["ST"])
                        if mode == "light":
                            pl.op("dve", lambda e: e.tensor_tensor(out=ST[:], in0=ST[:], in1=psA[:, 0:2, :].rearrange("p a q -> p (a q)"), op=ALU.add), reads=["ST", "psA0", "psA1"], writes=["ST"])
                    if mode == "full":
                        pl.op("act", lambda e: e.copy(out=STb[:], in_=ST[:]), reads=["ST"], writes=["STb"])
                        if hist_from != "carry":
                            pl.dma("pool", lambda e, ch=ch: e.dma_start(out=so_d[1 + ch, :, :], in_=ST[:]), "s_ST", reads=["ST"], writes=["so%d" % (1 + ch)])
                if next_a is not None and b < len(next_a):
                    next_a[b]()
                if mode == "full":
                    pl.op("dve", lambda e: e.tensor_tensor(out=yo[:], in0=psA[:, 0:2, :].rearrange("p a q -> p (a q)"), in1=eac[:], op=ALU.mult), reads=["psA0", "psA1", "eac"], writes=["yo"])
                    pl.op("dve", lambda e: e.tensor_tensor(out=yo[:], in0=psB[:, :, :].rearrange("p a q -> p (a q)"), in1=yo[:], op=ALU.add), reads=["psB0", "psB1", "yo"], writes=["yo"])
                    v8 = lambda ap: ap.rearrange("p (a t) -> p a t", a=8)
                    pl.op("dve", lambda e: e.tensor_tensor(out=yo[:], in0=yo[:], in1=segs[:], op=ALU.add), reads=["yo", "segs2"], writes=["yo"])
                    pl.op("dve", lambda e, tsl=tsl: e.tensor_tensor(out=v8(yo[:]), in0=v8(yo[:]), in1=sz[:, :, tsl], op=ALU.mult), reads=["yo", "sz"], writes=["yo"])
                    pl.op("act", lambda e: e.activation(out=Mt[:], in_=yo[:], func=AF.Square), reads=["yo"], writes=["Mt"])
                    pl.op("pool", lambda e, tsl=tsl: e.tensor_tensor(out=ycat[:, 8:16, tsl], in0=v8(yo[:]), in1=pf[:, PF_NBW:PF_NBW + 8].unsqueeze(2).broadcast_to([128, 8, 128]), op=ALU.mult), reads=["yo", "pf"], writes=["ycat"])
                    for a in range(8):
                        pl.op("pe", lambda e, a=a, b=b: e.matmul(psC[:, 324 + b:325 + b], lhsT=Mt[:, a * 128:(a + 1) * 128], rhs=ones_bf[:, 0:1], start=(a == 0), stop=(a == 7)), reads=["Mt", "ones"], writes=["psCs"], token=(a == 7))

            if mode != "full":
                return
            pl.op("act", lambda e: e.activation(out=stat[:, 24:32], in_=psC[:, 320:328], func=AF.Ln, scale=1.0 / D, bias=EPS), reads=["psCa", "psCs"], writes=["stat"])
            pl.op("act", lambda e: e.activation(out=stat[:, 24:32], in_=stat[:, 24:32], func=AF.Exp, scale=-0.5), reads=["stat"], writes=["stat"])
            for s in range(nsub):
                tsl = slice(s * 128, (s + 1) * 128)
                for hf in range(2):
                    for part in range(2):
                        for kc in range(8):
                            pl.op("pe", lambda e, hf=hf, part=part, kc=kc, tsl=tsl: e.matmul(psA[:, part * 2 + hf, :], lhsT=ycat[:, part * 8 + kc, tsl], rhs=wout[:, part * 8 + kc, hf * 512:(hf + 1) * 512], start=(kc == 0), stop=(kc == 7)),
                                  reads=["ycat", "wout"], writes=["psA%d" % (part * 2 + hf)], token=(kc == 7))
                xb = xt[s % 2]
                xk = "xt%d" % (s % 2)
                pl.dma("sp", lambda e, xb=xb, s=s: e.dma_start(out=xb[:], in_=xs_d[row0 + s * 128:row0 + (s + 1) * 128, :]), "l_" + xk, writes=[xk])
                ob = ost[s % 2]
                ok = "ost%d" % (s % 2)
                pl.op("act", lambda e, s=s: e.activation(out=o1[:], in_=psA[:, 0:2, :].rearrange("p a q -> p (a q)"), func=AF.Copy, scale=stat[:, 24 + s:25 + s]), reads=["psA0", "psA1", "stat"], writes=["o1"])
                pl.op("dve", lambda e, s=s: e.scalar_tensor_tensor(out=o1[:], in0=psA[:, 2:4, :].rearrange("p a q -> p (a q)"), scalar=stat[:, 28 + s:29 + s], in1=o1[:], op0=ALU.mult, op1=ALU.add), reads=["psA2", "psA3", "stat", "o1"], writes=["o1"])
                pl.op("dve", lambda e: e.tensor_tensor(out=o1[:], in0=o1[:], in1=gate[:, gidx, :], op=ALU.mult), reads=["o1", "gate"], writes=["o1"])
                pl.op("dve", lambda e, xb=xb: e.tensor_tensor(out=o1[:], in0=o1[:], in1=xb[:], op=ALU.add), reads=["o1", xk], writes=["o1"])
                pl.op("dve", lambda e, s=s: e.memset(stat[:, 32 + s:33 + s], 0.0), writes=["stat"])
                pl.op("act", lambda e, s=s: e.activation(out=sqj[:], in_=o1[:], func=AF.Square, accum_out=stat[:, 32 + s:33 + s]), reads=["o1", "stat"], writes=["sqj", "stat"])
                pl.op("act", lambda e, s=s: e.activation(out=stat[:, 36 + s:37 + s], in_=stat[:, 32 + s:33 + s], func=AF.Ln, scale=1.0 / D, bias=EPS), reads=["stat"], writes=["stat"])
                pl.op("act", lambda e, s=s: e.activation(out=stat[:, 36 + s:37 + s], in_=stat[:, 36 + s:37 + s], func=AF.Exp, scale=-0.5), reads=["stat"], writes=["stat"])
                pl.op("dve", lambda e, s=s, ob=ob: e.scalar_tensor_tensor(out=ob[:], in0=o1[:], scalar=stat[:, 36 + s:37 + s], in1=bc[:, BC_NF:BC_NF + D], op0=ALU.mult, op1=ALU.mult), reads=["o1", "stat", "bc"], writes=[ok])
                pl.dma("pool", lambda e, ob=ob, s=s: e.dma_start(out=y_d[yrow0 + s * 128:yrow0 + (s + 1) * 128, :], in_=ob[:]), "s_" + ok, reads=[ok], writes=["y"])

        ones_bf = SB("ones_bf", [128, 8], BF16)
        pl.op("dve", lambda e: e.memset(ones_bf[:], 1.0), writes=["ones"])
        sca_sb = SB("sca_sb", [128, 8, 2, 2])
        scb_sb = SB("scb_sb", [128, 16, 2, 3])
        ld("l_sca", sca_sb[:].rearrange("p a b c -> p (a b c)"), sca_d[:, :], "sca")
        ld("l_scb", scb_sb[:].rearrange("p a b c -> p (a b c)"), scb_d[:, :], "scb")
        pl.op("dve", lambda e: e.memset(uhist[:], 0.0), writes=["uhist"])
        pl.op("dve", lambda e: e.memset(xhist[:], 0.0), writes=["xhist"])
        pl.op("dve", lambda e: e.memset(atot[:], 0.0), writes=["atot"])
        pslot = [(0, T, 0)]
        hsel = ((hT, "hT"), (ycat[:, 0:8, :], "ycat"))
        for n in range(NLSEG * NT):
            stm = [None] * (T // 64)
            if n == 0:
                stm[0] = ("zero", 0)
            nxa = None
            if n + 1 < NLSEG * NT:
                r1 = 128 + (n + 1) * T
                hb1, hk1 = hsel[(n + 1) % 2]
                nxa = [(lambda r1=r1, p0=p0, hb1=hb1, hk1=hk1: step_a_pair(r1, p0, T, pslot, hb1, hk1)) for p0 in (0, 2)]
            tile(128 + n * T, T, pslot, "light", 0, 0, "carry", stm, None, h_idx=n % 2, pre_a=(n > 0), next_a=nxa)
            if n % NT == NT - 1:
                m = n // NT
                pl.op("dve", lambda e, m=m: e.tensor_scalar_mul(out=ST[:], in0=ST[:], scalar1=msk[:, 1 + m:2 + m]), reads=["ST", "msk"], writes=["ST"])
                pl.op("dve", lambda e, m=m: e.tensor_scalar_mul(out=xhist[:].rearrange("p a b -> p (a b)"), in0=xhist[:].rearrange("p a b -> p (a b)"), scalar1=msk[:, 1 + m:2 + m]), reads=["xhist", "msk"], writes=["xhist"])
        tile(0, 128, [(0, 128, 0)], "halo", 0, 0, "carry", [None, None], None)
        for n in range(NT):
            stm = [None] * (T // 64)
            if n == 0:
                stm[0] = ("keep", 0)
            tile(128 + LROWS + n * T, T, pslot, "full", 0, n * T, "carry", stm, 0)
        pl.dma("pool", lambda e: e.dma_start(out=so_d[0, :, :], in_=ST[:]), "s_ST", reads=["ST"], writes=["so0"])
        tile(128 + LROWS + SEGLEN, 128, [(0, 64, 1), (64, 128, 2)], "full", 1, SEGLEN, "state", [("load", 0), ("load", 1)], 1)
        pl.dma("pool", lambda e: e.dma_start(out=ca_d[:, :], in_=casb[:].rearrange("p a b c -> p (a b c)")), "s_ca", reads=["casb"], writes=["ca"])
        pl.dma("pool", lambda e: e.dma_start(out=cb_d[:, :], in_=cbsb[:].rearrange("p a b c -> p (a b c)")), "s_cb", reads=["cbsb"], writes=["cb"])
        if debug:
            dbg_list = [("xbf", xbf[:].rearrange("p a b -> p (a b)"), 8 * T, BF16), ("BT", BT[:].rearrange("p a b -> p (a b)"), 4 * T, BF16),
                        ("CT", CT[:].rearrange("p a b -> p (a b)"), 4 * T, BF16), ("sm", sm[:].rearrange("p a b -> p (a b)"), 256, F32),
                        ("ycat", ycat[:].rearrange("p a b -> p (a b)"), 16 * T, BF16), ("yo", yo[:], 1024, F32), ("xw", xw[:], 1024, BF16),
                        ("xdt", xdt[:], 1024, BF16), ("btok", btok[:], 512, BF16), ("Mt", Mt[:], 1024, BF16), ("eac", eac[:], 1024, F32),
                        ("cd", cd[:].rearrange("p a b -> p (a b)"), 32, F32), ("hT", hT[:].rearrange("p a b -> p (a b)"), 8 * T, BF16),
                        ("sz", sz[:].rearrange("p a b -> p (a b)"), 8 * T, BF16), ("stat", stat[:], 64, F32), ("gam", gam[:].rearrange("p a b -> p (a b)"), 24, F32)]
            allk = list(pl.bufs.keys())
            for nm, ap, w, dt_ in dbg_list:
                dd = nc.dram_tensor("dbg_" + nm, [128, w], dt_, kind="ExternalOutput").ap()
                pl.dma("pool", lambda e, dd=dd, ap=ap: e.dma_start(out=dd[:, :], in_=ap), "s_dbg", reads=allk)
        pl.wait_tokens("pool", [(s, c) for s, c in pl.dma_cnt.items() if s.startswith("s_")])
        print("planned instructions:", pl.nins, {e: len(pl.lists[e]) for e in pl.ENGS})
        pl.emit()
    return nc


def _host_consts():
    c = np.zeros((128, NCONST), np.float32)
    k = np.arange(128)
    c[:, C_ID:C_ID + 128] = np.eye(128, dtype=np.float32)
    same = (k[:, None] // 64) == (k[None, :] // 64)
    c[:, C_BLK:C_BLK + 128] = same
    c[:, C_TRI:C_TRI + 128] = same & (k[:, None] <= k[None, :])
    c[:, C_T64:C_T64 + 64] = (k[:, None] % 64) <= np.arange(64)[None, :]
    c[:, C_SEL0:C_SEL0 + 128] = (k[:, None] < 64)
    c[:, C_SEL1:C_SEL1 + 128] = (k[:, None] >= 64)
    return c


def _fm(v, nchunk):
    return np.ascontiguousarray(np.asarray(v, np.float32).reshape(nchunk, 128).T)


_NC_CACHE = {}


def kernel(x_prompt, x_sample, state_conv_a, state_conv_b, state_ssm, c_prompt, c_sample,
           w_mod, b_mod, norm_in_w, w_in, conv_a_w, norm_a_w, conv_b_w, conv_b_b,
           dt_bias, a_log, d_skip, norm_b_w, w_out, norm_f_w, _two_phase=True, _debug=False):
    f = lambda a: np.ascontiguousarray(np.asarray(a, np.float32))
    x_prompt, x_sample = f(x_prompt), f(x_sample)
    state_conv_a, state_conv_b, state_ssm = f(state_conv_a), f(state_conv_b), f(state_ssm)
    c_prompt, c_sample = f(c_prompt), f(c_sample)
    w_mod, b_mod, w_in, w_out = f(w_mod)[0], f(b_mod)[0], f(w_in)[0], f(w_out)[0]
    pf = np.zeros((128, NPF), np.float32)
    pf[:, PF_NIN:PF_NIN + 8] = _fm(f(norm_in_w)[0], 8)
    caw = f(conv_a_w)[0]
    pf[:, PF_CAW:PF_CAW + 24] = np.stack([_fm(caw[k], 8) for k in range(3)], axis=2).reshape(128, 24)
    pf[:, PF_NAW:PF_NAW + 8] = _fm(f(norm_a_w)[0], 8)
    cbw = f(conv_b_w)[0]
    pf[:, PF_CBW:PF_CBW + 64] = np.stack([_fm(cbw[k], 16) for k in range(4)], axis=2).reshape(128, 64)
    pf[:, PF_CBB:PF_CBB + 16] = _fm(f(conv_b_b)[0], 16)
    pf[:, PF_NBW:PF_NBW + 8] = _fm(f(norm_b_w)[0], 8)
    pf[:, PF_DSK:PF_DSK + 8] = _fm(np.repeat(f(d_skip)[0], 64), 8)
    pf[:, PF_BSH:PF_BSH + 8] = _fm(b_mod[0:D], 8)
    pf[:, PF_BSC:PF_BSC + 8] = _fm(b_mod[D:2 * D], 8)
    bcv = np.zeros((128, NBC), np.float32)
    bcv[:, BC_NF:BC_NF + D] = f(norm_f_w)[None, :]
    bcv[:, BC_BG:BC_BG + D] = b_mod[None, 2 * D:3 * D]
    bcv[:, BC_DTB:BC_DTB + 16] = f(dt_bias)[0][None, :]
    bcv[:, BC_ALOG:BC_ALOG + 16] = f(a_log)[0][None, :]
    cst = _host_consts()

    in_maps = []
    for k in range(NCORES):
        seq, seg = k // 4, k % 4
        start = seg * SEGLEN
        xs = np.zeros((XROWS, D), np.float32)
        if seg > 0:
            xs[0:128] = x_prompt[seq, start - 128:start]
        if seg > 0 and LROWS >= start:
            xs[128 + LROWS - start:128 + LROWS] = x_prompt[seq, 0:start]
        xs[128 + LROWS:128 + LROWS + SEGLEN] = x_prompt[seq, start:start + SEGLEN]
        xs[128 + LROWS + SEGLEN:] = x_sample[2 * k:2 * k + 2].reshape(128, D)
        cs = [c_prompt[seq], c_sample[2 * k], c_sample[2 * k + 1]]
        cT = np.stack([_fm(c, 8) for c in cs], axis=2).reshape(128, 24)
        cbc = np.zeros((2, 128, 8, 128), np.float32)
        cbc[0] = _fm(cs[0], 8)[:, :, None]
        cbc[1, :, :, 0:64] = _fm(cs[1], 8)[:, :, None]
        cbc[1, :, :, 64:128] = _fm(cs[2], 8)[:, :, None]
        sca = state_conv_a[0, 2 * k:2 * k + 2]
        sca = sca.reshape(2, 2, 8, 128).transpose(3, 2, 0, 1).reshape(128, 32)
        scb = state_conv_b[0, 2 * k:2 * k + 2]
        scb = scb.reshape(2, 3, 16, 128).transpose(3, 2, 0, 1).reshape(128, 96)
        sst = state_ssm[0, 2 * k:2 * k + 2]
        sst = sst.reshape(2, 1024, 128).transpose(0, 2, 1)
        msk = np.zeros((128, 16), np.float32)
        msk[:, 0] = 1.0 if seg > 0 else 0.0
        for m in range(NLSEG):
            msk[:, 1 + m] = 0.0 if m < NLSEG - seg else 1.0
        in_maps.append({
            "xs": xs, "w_mod": w_mod, "w_in": w_in, "w_out": w_out, "pf": pf, "bc": bcv, "cst": cst,
            "cT": np.ascontiguousarray(cT), "cbc": np.ascontiguousarray(cbc.reshape(2, 128, 1024)),
            "sca": np.ascontiguousarray(sca), "scb": np.ascontiguousarray(scb),
            "sst": np.ascontiguousarray(sst), "msk": msk,
        })
    key = (bool(_two_phase), bool(_debug))
    if key not in _NC_CACHE:
        _NC_CACHE[key] = build_nc(two_phase=key[0], debug=key[1])
    nc = _NC_CACHE[key]
    res = run_bass_kernel_spmd(nc, in_maps, core_ids=list(range(NCORES)))
    R = res.results
    if _debug:
        kernel.last_results = R
    y_prompt = np.zeros((2, SEQ, D), np.float32)
    y_sample = np.zeros((16, 64, D), np.float32)
    ca_p = np.zeros((1, 2, 2, D), np.float32)
    cb_p = np.zeros((1, 2, 3, 2 * D), np.float32)
    ss_p = np.zeros((1, 2, 16, 64, 128), np.float32)
    ca_s = np.zeros((1, 16, 2, D), np.float32)
    cb_s = np.zeros((1, 16, 3, 2 * D), np.float32)
    ss_s = np.zeros((1, 16, 16, 64, 128), np.float32)
    for k in range(NCORES):
        seq, seg = k // 4, k % 4
        r = R[k]
        y_prompt[seq, seg * SEGLEN:(seg + 1) * SEGLEN] = r["y"][0:SEGLEN]
        y_sample[2 * k:2 * k + 2] = r["y"][SEGLEN:].reshape(2, 64, D)
        ca = r["ca"].reshape(128, 8, 3, 2)
        cb = r["cb"].reshape(128, 16, 3, 3)
        so = r["so"]
        for s in range(2):
            ca_s[0, 2 * k + s] = ca[:, :, 1 + s, :].transpose(2, 1, 0).reshape(2, D)
            cb_s[0, 2 * k + s] = cb[:, :, 1 + s, :].transpose(2, 1, 0).reshape(3, 2 * D)
            ss_s[0, 2 * k + s] = so[1 + s].T.reshape(16, 64, 128)
        if seg == 3:
            ca_p[0, seq] = ca[:, :, 0, :].transpose(2, 1, 0).reshape(2, D)
            cb_p[0, seq] = cb[:, :, 0, :].transpose(2, 1, 0).reshape(3, 2 * D)
            ss_p[0, seq] = so[0].T.reshape(16, 64, 128)
    return (y_prompt, y_sample, ca_p, cb_p, ss_p, ca_s, cb_s, ss_s)
```

```python
import contextlib
import numpy as np
import concourse.bass as bass
import concourse.mybir as mybir
from concourse.bass_utils import run_bass_kernel_spmd

F32 = mybir.dt.float32
BF16 = mybir.dt.bfloat16
ALU = mybir.AluOpType
AF = mybir.ActivationFunctionType

NCORES = 8
D = 1024
SEQ = 16384
SEGLEN = 4096
T = 512
NT = SEGLEN // T
DIN = 7184
PW = 128
NPIECE = 56
NWB = 12
EPS = 1e-5
NLSEG = 3
LROWS = NLSEG * SEGLEN
XROWS = 128 + LROWS + SEGLEN + 128
YROWS = SEGLEN + 128

PF_NIN = 0
PF_CAW = 8
PF_NAW = 32
PF_CBW = 40
PF_CBB = 104
PF_NBW = 120
PF_DSK = 128
PF_BSH = 136
PF_BSC = 144
NPF = 152
BC_NF = 0
BC_BG = 1024
BC_DTB = 2048
BC_ALOG = 2064
NBC = 2080
C_ID = 0
C_BLK = 128
C_TRI = 256
C_T64 = 384
C_SEL0 = 448
C_SEL1 = 576
NCONST = 704


class Planner:
    ENGS = ("pe", "act", "dve", "pool", "sp")
    SEM_LIMIT = 30000

    def __init__(self, nc):
        self.nc = nc
        self.lists = {e: [] for e in self.ENGS}
        self.cur = {e: [e + "_0", 0] for e in self.ENGS}
        self.gen = {e: 0 for e in self.ENGS}
        self.sem_names = [e + "_0" for e in self.ENGS]
        self.dma_cnt = {}
        self.waited = {e: {} for e in self.ENGS}
        self.bufs = {}
        self.nins = 0
        self.alias = {}
        self.bank_last = {}

    @staticmethod
    def _bank(k):
        if k.startswith("psC"):
            return "psC"
        if k.startswith("psA") or k.startswith("psB") or k == "psT":
            return k
        return None

    def _bank_deps(self, eng, keys, need):
        banks = set(b for b in (self._bank(k) for k in keys) if b)
        for b in banks:
            for oe, tok in self.bank_last.get(b, {}).items():
                if oe != eng:
                    self._need(eng, tok, need)
        return banks

    def _exp(self, keys):
        out = []
        for k in keys:
            out.extend(self.alias.get(k, (k,)))
        return out

    def _need(self, eng, tok, out):
        if tok is None:
            return
        s, v = tok
        if eng == "pe" and s.startswith("pe_"):
            return
        if self.waited[eng].get(s, 0) >= v:
            return
        if out.get(s, 0) < v:
            out[s] = v

    def _deps(self, eng, reads, writes):
        reads, writes = self._exp(reads), self._exp(writes)
        need = {}
        for k in reads:
            b = self.bufs.get(k)
            if b:
                self._need(eng, b[0], need)
        for k in writes:
            b = self.bufs.get(k)
            if b:
                self._need(eng, b[0], need)
                for t in b[1]:
                    self._need(eng, t, need)
        self._cur_banks = self._bank_deps(eng, list(reads) + list(writes), need)
        self._cur_eng = eng
        for s, v in need.items():
            self.waited[eng][s] = v
            self.lists[eng].append(("wait", s, v))

    def _mark(self, tok, reads, writes):
        reads, writes = self._exp(reads), self._exp(writes)
        for b in self._cur_banks:
            self.bank_last.setdefault(b, {})[self._cur_eng] = tok
        for k in reads:
            b = self.bufs.setdefault(k, [None, []])
            b[1].append(tok)
        for k in writes:
            self.bufs[k] = [tok, []]

    def op(self, eng, fn, reads=(), writes=(), token=True):
        self._deps(eng, reads, writes)
        c = self.cur[eng]
        self.nins += 1
        if token:
            if c[1] >= self.SEM_LIMIT:
                self.gen[eng] += 1
                c[0] = "%s_%d" % (eng, self.gen[eng])
                c[1] = 0
                self.sem_names.append(c[0])
            c[1] += 1
            tok = (c[0], c[1])
            self.lists[eng].append(("ins", fn, c[0], 1))
        else:
            tok = (c[0], c[1] + 1)
            self.lists[eng].append(("ins", fn, None, 0))
        self._mark(tok, reads, writes)
        return tok

    def dma(self, eng, fn, sem, reads=(), writes=()):
        self._deps(eng, reads, writes)
        self.nins += 1
        if sem not in self.dma_cnt:
            self.dma_cnt[sem] = 0
            self.sem_names.append(sem)
        self.dma_cnt[sem] += 16
        tok = (sem, self.dma_cnt[sem])
        self.lists[eng].append(("ins", fn, sem, 16))
        self._mark(tok, reads, writes)
        return tok

    def raw(self, eng, fn, sem, inc, reads=(), writes=()):
        self._deps(eng, reads, writes)
        if sem not in self.dma_cnt:
            self.dma_cnt[sem] = 0
            self.sem_names.append(sem)
        self.dma_cnt[sem] += inc
        tok = (sem, self.dma_cnt[sem])
        self.lists[eng].append(("ins", fn, sem, -inc))
        self._mark(tok, reads, writes)
        return tok

    def wait_tokens(self, eng, toks):
        need = {}
        for t in toks:
            self._need(eng, t, need)
        for s, v in need.items():
            self.waited[eng][s] = v
            self.lists[eng].append(("wait", s, v))

    def emit(self):
        nc = self.nc
        with contextlib.ExitStack() as st:
            sems = {}
            for n in self.sem_names:
                sems[n] = st.enter_context(nc.semaphore(n))
            block = st.enter_context(nc.Block())
            engmap = {"pe": block.tensor, "act": block.scalar, "dve": block.vector,
                      "pool": block.gpsimd, "sp": block.sync}
            for e in self.ENGS:
                lst = self.lists[e]
                if not lst:
                    continue

                def body(engobj, lst=lst):
                    for it in lst:
                        if it[0] == "wait":
                            engobj.wait_ge(sems[it[1]], it[2])
                        else:
                            ins = it[1](engobj)
                            if it[2] is not None:
                                if it[3] < 0:
                                    ins.then_inc(sems[it[2]])
                                else:
                                    ins.then_inc(sems[it[2]], it[3])
                engmap[e](body)


def build_nc(two_phase=True, debug=False):
    nc = bass.Bass("TRN2", target_bir_lowering=False)
    dr = lambda n, s, k, d=F32: nc.dram_tensor(n, list(s), d, kind=k)
    xs_d = dr("xs", [XROWS, D], "ExternalInput").ap()
    wmod_d = dr("w_mod", [D, 3 * D], "ExternalInput").ap()
    win_d = dr("w_in", [D, DIN], "ExternalInput").ap()
    wout_d = dr("w_out", [2 * D, D], "ExternalInput").ap()
    pf_d = dr("pf", [128, NPF], "ExternalInput").ap()
    bc_d = dr("bc", [128, NBC], "ExternalInput").ap()
    cst_d = dr("cst", [128, NCONST], "ExternalInput").ap()
    cT_d = dr("cT", [128, 8 * 3], "ExternalInput").ap()
    cbc_d = dr("cbc", [2, 128, 8 * 128], "ExternalInput").ap()
    sca_d = dr("sca", [128, 8 * 2 * 2], "ExternalInput").ap()
    scb_d = dr("scb", [128, 16 * 2 * 3], "ExternalInput").ap()
    sst_d = dr("sst", [2, 128, D], "ExternalInput").ap()
    msk_d = dr("msk", [128, 16], "ExternalInput").ap()
    y_d = dr("y", [YROWS, D], "ExternalOutput").ap()
    ca_d = dr("ca", [128, 8 * 3 * 2], "ExternalOutput").ap()
    cb_d = dr("cb", [128, 16 * 3 * 3], "ExternalOutput").ap()
    so_d = dr("so", [3, 128, D], "ExternalOutput").ap()
    winbf_d = nc.dram_tensor("winbf", [NPIECE, 128, 8 * PW], BF16)

    pl = Planner(nc)
    pl.alias = {"stg0": ("rhs1", "segs", "eac", "yo"), "stg1": ("stsb", "o1", "ost0", "ost1"), "sqj": ("Mt",)}
    with contextlib.ExitStack() as st:
        def SB(name, shape, dt=F32):
            return st.enter_context(nc.sbuf_tensor("sb_" + name, list(shape), dt))

        cst = SB("cst", [128, NCONST])
        idb = SB("idb", [128, 128], BF16)
        pf = SB("pf", [128, NPF])
        bc = SB("bc", [128, NBC])
        msk = SB("msk", [128, 16])
        cT = SB("cT", [128, 8, 3])
        gam = SB("gam", [128, 3, 8])
        bet = SB("bet", [128, 3, 8])
        gate = SB("gate", [128, 2, D])
        caw = SB("caw", [128, 8, 3])
        cbw = SB("cbw", [128, 16, 4])
        cbb = SB("cbb", [128, 16])
        a_bc = SB("a_bc", [128, 16])
        wout = SB("wout", [128, 16, D], BF16)
        wdt = SB("wdt", [128, 8, 16], BF16)
        wbuf = [SB("wbuf%d" % i, [128, 8, PW], BF16) for i in range(NWB)]
        big = [SB("big%d" % i, [128, 4096]) for i in range(2)]
        stg = [b[:].rearrange("p (a c) -> p a c", a=8) for b in big]
        xt = [SB("xt%d" % i, [128, D]) for i in range(2)]
        hT = SB("hT", [128, 8, T], BF16)
        ubuf = [SB("ubuf%d" % i, [128, 3 + T]) for i in range(2)]
        hsb = [SB("hsb%d" % i, [128, T]) for i in range(2)]
        cu = [SB("cu%d" % i, [128, T]) for i in range(2)]
        tq = [SB("tq%d" % i, [128, T]) for i in range(2)]
        sq2 = [SB("sq2%d" % i, [128, T], BF16) for i in range(2)]
        uhist = SB("uhist", [128, 8, 3])
        xhist = SB("xhist", [128, 16, 3])
        uhist0 = SB("uhist0", [128, 8, 3])
        xhist0 = SB("xhist0", [128, 16, 3])
        ycat = SB("ycat", [128, 16, T], BF16)
        xbf = SB("xbf", [128, 8, T], BF16)
        BT = SB("BT", [128, 4, T], BF16)
        CT = SB("CT", [128, 4, T], BF16)
        sz = SB("sz", [128, 8, T], BF16)
        xdt = SB("xdt", [128, D], BF16)
        xw = SB("xw", [128, D], BF16)
        btok = SB("btok", [128, 512], BF16)
        rhs1 = big[0][:, 0:1024]
        segs = big[0][:, 1024:2048]
        Mt = SB("Mt", [128, D], BF16)
        sqj = Mt
        eac = big[0][:, 2048:3072]
        yo = big[0][:, 3072:4096]
        cbtm = SB("cbtm", [128, 4, 64])
        sm = SB("sm", [128, 8, 64])
        ST = SB("ST", [128, D])
        STb = SB("STb", [128, D], BF16)
        stsb = big[1][:, 0:1024]
        cd = SB("cd", [128, 2, 64])
        stat = SB("stat", [128, 64])
        ost = [big[1][:, 2048:3072], big[1][:, 3072:4096]]
        o1 = big[1][:, 1024:2048]
        casb = SB("casb", [128, 8, 3, 2])
        cbsb = SB("cbsb", [128, 16, 3, 3])
        atot = SB("atot", [128, 16])
        gsel = SB("gsel", [128, 8, 16])
        print("sbuf remaining after alloc:", nc.sbuf_bytes_remaining)

        psA = st.enter_context(nc.psum_tensor("psA", [128, 4, 512], F32))
        psB = st.enter_context(nc.psum_tensor("psB", [128, 2, 512], F32))
        psC = st.enter_context(nc.psum_tensor("psC", [128, 512], F32))
        psT = st.enter_context(nc.psum_tensor("psT", [128, 1024], BF16))

        ident = cst[:, C_ID:C_ID + 128]
        blk = cst[:, C_BLK:C_BLK + 128]
        tri = cst[:, C_TRI:C_TRI + 128]
        t64 = cst[:, C_T64:C_T64 + 64]
        chsel = [cst[:, C_SEL0:C_SEL0 + 128], cst[:, C_SEL1:C_SEL1 + 128]]

        def pfc(off, j):
            return pf[:, off + j:off + j + 1]

        ld = lambda name, dst, src, key: pl.dma("sp", lambda e: e.dma_start(out=dst, in_=src), name, writes=[key])
        ld("l_cst", cst[:], cst_d[:, :], "cst")
        ld("l_pf", pf[:], pf_d[:, :], "pf")
        ld("l_bc", bc[:], bc_d[:, :], "bc")
        ld("l_msk", msk[:], msk_d[:, :], "msk")
        ld("l_cT", cT[:].rearrange("p a b -> p (a b)"), cT_d[:, :], "cT")
        pl.op("dve", lambda e: e.tensor_copy(out=idb[:], in_=ident), reads=["cst"], writes=["idb"])
        pl.op("dve", lambda e: e.tensor_scalar_mul(out=caw[:].rearrange("p a b -> p (a b)"), in0=pf[:, PF_CAW:PF_CAW + 24], scalar1=1.0), reads=["pf"], writes=["caw"])
        pl.op("dve", lambda e: e.tensor_scalar_mul(out=cbw[:].rearrange("p a b -> p (a b)"), in0=pf[:, PF_CBW:PF_CBW + 64], scalar1=1.0), reads=["pf"], writes=["cbw"])
        pl.op("dve", lambda e: e.tensor_scalar_mul(out=cbb[:], in0=pf[:, PF_CBB:PF_CBB + 16], scalar1=1.0), reads=["pf"], writes=["cbb"])
        pl.op("act", lambda e: e.activation(out=a_bc[:], in_=bc[:, BC_ALOG:BC_ALOG + 16], func=AF.Exp), reads=["bc"], writes=["a_bc"])
        pl.op("dve", lambda e: e.tensor_scalar_mul(out=a_bc[:], in0=a_bc[:], scalar1=-1.0), reads=["a_bc"], writes=["a_bc"])

        wmod_v = wmod_d.rearrange("(kc p) c -> p kc c", p=128)
        for piece in range(6):
            s = stg[piece % 2]
            key = "stg%d" % (piece % 2)
            pl.dma("sp", lambda e, s=s, piece=piece: e.dma_start(out=s[:], in_=wmod_v[:, :, piece * 512:(piece + 1) * 512]), "l_" + key, writes=[key])
            if piece < 4:
                for cc in range(4):
                    j = (piece % 2) * 4 + cc
                    for kc in range(8):
                        pl.op("pe", lambda e, s=s, cc=cc, kc=kc, j=j: e.matmul(psC[:, j * 4:j * 4 + 3], lhsT=s[:, kc, cc * 128:(cc + 1) * 128], rhs=cT[:, kc, :], start=(kc == 0), stop=(kc == 7)),
                              reads=[key, "cT"], writes=["psC"], token=(kc == 7))
                if piece % 2 == 1:
                    src = psC[:, 0:32].rearrange("p (j s) -> p s j", s=4)[:, 0:3, :]
                    if piece == 1:
                        pl.op("dve", lambda e, src=src: e.tensor_tensor(out=bet[:], in0=src, in1=pf[:, PF_BSH:PF_BSH + 8].unsqueeze(1).broadcast_to([128, 3, 8]), op=ALU.add), reads=["psC", "pf"], writes=["bet"])
                    else:
                        pl.op("dve", lambda e, src=src: e.tensor_tensor(out=gam[:], in0=src, in1=pf[:, PF_BSC:PF_BSC + 8].unsqueeze(1).broadcast_to([128, 3, 8]), op=ALU.add), reads=["psC", "pf"], writes=["gam"])
                        pl.op("dve", lambda e: e.scalar_tensor_tensor(out=gam[:], in0=gam[:], scalar=1.0, in1=pf[:, PF_NIN:PF_NIN + 8].unsqueeze(1).broadcast_to([128, 3, 8]), op0=ALU.add, op1=ALU.mult), reads=["gam", "pf"], writes=["gam"])
            else:
                half = piece - 4
                for which in range(2):
                    cb_t = xt[which]
                    if half == 0:
                        pl.dma("sp", lambda e, cb_t=cb_t, which=which: e.dma_start(out=cb_t[:], in_=cbc_d[which, :, :]), "l_xt%d" % which, writes=["xt%d" % which])
                    cbv = cb_t[:].rearrange("p (k m) -> p k m", k=8)
                    for kc in range(8):
                        pl.op("pe", lambda e, s=s, kc=kc, cbv=cbv, which=which: e.matmul(psA[:, which, :], lhsT=cbv[:, kc, :], rhs=s[:, kc, :], start=(kc == 0), stop=(kc == 7)),
                              reads=[key, "xt%d" % which], writes=["psA%d" % which], token=(kc == 7))
                    pl.op("dve", lambda e, which=which, half=half: e.tensor_tensor(out=gate[:, which, half * 512:(half + 1) * 512], in0=psA[:, which, :], in1=bc[:, BC_BG + half * 512:BC_BG + (half + 1) * 512], op=ALU.add),
                          reads=["psA%d" % which, "bc"], writes=["gate"])

        win_v = win_d.rearrange("(kc p) c -> p kc c", p=128)
        wout_v = wout_d.rearrange("(kc p) c -> p kc c", p=128)
        castengs = ["dve", "act", "pool"]
        ci = 0

        def cast(dst, src, rk, wk):
            nonlocal ci
            eng = castengs[ci % 3]
            ci += 1
            if eng == "act":
                pl.op("act", lambda e: e.copy(out=dst, in_=src), reads=rk, writes=wk)
            else:
                pl.op(eng, lambda e: e.tensor_copy(out=dst, in_=src), reads=rk, writes=wk)

        porder = [10, 11, 12, 2, 3, 4, 5, 0, 1, 6, 7, 13, 8, 9]
        pl.dma("sp", lambda e: e.dma_start(out=stg[0][:, :, 0:16], in_=win_v[:, :, 7168:7184]), "l_stg0", writes=["stg0"])
        pl.op("dve", lambda e: e.tensor_copy(out=wdt[:], in_=stg[0][:, :, 0:16]), reads=["stg0"], writes=["wdt"])
        wci = 0
        for n, piece in enumerate(porder):
            s = stg[n % 2]
            key = "stg%d" % (n % 2)
            pl.dma("sp", lambda e, s=s, piece=piece: e.dma_start(out=s[:], in_=win_v[:, :, piece * 512:(piece + 1) * 512]), "l_" + key, writes=[key])
            for hp in range(512 // PW):
                wb = wbuf[wci % NWB]
                wkey = "wbuf%d" % (wci % NWB)
                wci += 1
                for kh in range(2):
                    cast(wb[:, kh * 4:(kh + 1) * 4, :], s[:, kh * 4:(kh + 1) * 4, hp * PW:(hp + 1) * PW], [key], [wkey])
                sp_ = (512 // PW) * piece + hp
                pl.dma("pool", lambda e, wb=wb, sp_=sp_: e.dma_start(out=winbf_d[sp_, :, :], in_=wb[:].rearrange("p a b -> p (a b)")), "s_" + wkey, reads=[wkey], writes=["winbf%d" % sp_])
        for n in range(4):
            kg, ch = n // 2, n % 2
            s = stg[n % 2]
            key = "stg%d" % (n % 2)
            pl.dma("sp", lambda e, s=s, kg=kg, ch=ch: e.dma_start(out=s[:], in_=wout_v[:, kg * 8:(kg + 1) * 8, ch * 512:(ch + 1) * 512]), "l_" + key, writes=[key])
            for kh in range(2):
                cast(wout[:, kg * 8 + kh * 4:kg * 8 + (kh + 1) * 4, ch * 512:(ch + 1) * 512], s[:, kh * 4:(kh + 1) * 4, :], [key], ["wout"])

        wcnt = [0]

        def load_piece(piece):
            i = wcnt[0] % NWB
            wcnt[0] += 1
            wb = wbuf[i]
            pl.dma("sp", lambda e: e.dma_start(out=wb[:].rearrange("p a b -> p (a b)"), in_=winbf_d[piece, :, :]), "l_wbuf%d" % i, reads=["winbf%d" % piece], writes=["wbuf%d" % i])
            return wb, "wbuf%d" % i

        hcur = [hT, "hT"]

        def proj_chunk(wb, wkey, cc, ps_ap, pskey, Tn):
            hb, hk = hcur
            for kc in range(8):
                pl.op("pe", lambda e, kc=kc: e.matmul(ps_ap, lhsT=wb[:, kc, cc * 128:(cc + 1) * 128], rhs=hb[:, kc, 0:Tn], start=(kc == 0), stop=(kc == 7)),
                      reads=[wkey, hk], writes=[pskey], token=(kc == 7))

        pref = {}

        def load_x(row0, s):
            xb = xt[s % 2]
            xk = "xt%d" % (s % 2)
            pl.dma("sp", lambda e: e.dma_start(out=xb[:], in_=xs_d[row0 + s * 128:row0 + (s + 1) * 128, :]), "l_" + xk, writes=[xk])

        def step_a(row0, Tn, slots, hb=None, hk="hT"):
            for p0 in range(0, Tn // 128, 2):
                step_a_pair(row0, p0, Tn, slots, hT if hb is None else hb, hk)

        def step_a_pair(row0, p0, Tn, slots, hb, hk):
            nsub = Tn // 128
            if True:
                subs = list(range(p0, min(p0 + 2, nsub)))
                for s in subs:
                    xb = xt[s % 2]
                    xk = "xt%d" % (s % 2)
                    if not pref.pop((row0, s), False):
                        load_x(row0, s)
                    pl.op("pool", lambda e, s=s: e.memset(stat[:, s:s + 1], 0.0), writes=["stat"])
                    pl.op("act", lambda e, xb=xb, s=s: e.activation(out=sqj[:], in_=xb[:], func=AF.Square, accum_out=stat[:, s:s + 1]), reads=[xk, "stat"], writes=["sqj", "stat"])
                    pl.op("act", lambda e, s=s: e.activation(out=stat[:, 8 + s:9 + s], in_=stat[:, s:s + 1], func=AF.Ln, scale=1.0 / D, bias=EPS), reads=["stat"], writes=["stat"])
                    pl.op("act", lambda e, s=s: e.activation(out=stat[:, 16 + s:17 + s], in_=stat[:, 8 + s:9 + s], func=AF.Exp, scale=-0.5), reads=["stat"], writes=["stat"])
                    pl.op("dve", lambda e, xb=xb, s=s: e.tensor_scalar_mul(out=xb[:], in0=xb[:], scalar1=stat[:, 16 + s:17 + s]), reads=[xk, "stat"], writes=[xk])
                w0, w1 = subs[0] * 128, (subs[-1] + 1) * 128
                for kc in range(8):
                    pk = "psB%d" % (kc % 2)
                    for s in subs:
                        xb = xt[s % 2]
                        xk = "xt%d" % (s % 2)
                        pl.op("pe", lambda e, xb=xb, kc=kc, s=s, p0=p0: e.transpose(out=psB[:, kc % 2, (s - p0) * 128:(s - p0 + 1) * 128], in_=xb[:, kc * 128:(kc + 1) * 128], identity=ident), reads=[xk, "cst"], writes=[pk], token=(s == subs[-1]))
                    for (c0, c1, slot) in slots:
                        lo, hi = max(c0, w0), min(c1, w1)
                        if lo >= hi:
                            continue
                        pl.op("act", lambda e, kc=kc, lo=lo, hi=hi, slot=slot, w0=w0: e.activation(out=hb[:, kc, lo:hi], in_=psB[:, kc % 2, lo - w0:hi - w0], func=AF.Identity, scale=gam[:, slot, kc:kc + 1], bias=bet[:, slot, kc:kc + 1]),
                              reads=[pk, "gam", "bet"], writes=[hk])

        def conv_chunk(eng_first, src, wts, nk, dst, Tn, rk, wk, bias=None, bk=()):
            off = 3 - (nk - 1)
            if bias is None:
                pl.op("act", lambda e: e.activation(out=dst[:, 0:Tn], in_=src[:, off:off + Tn], func=AF.Copy, scale=wts[:, 0:1]), reads=rk, writes=wk)
            else:
                pl.op("act", lambda e: e.activation(out=dst[:, 0:Tn], in_=src[:, off:off + Tn], func=AF.Identity, scale=wts[:, 0:1], bias=bias), reads=rk + list(bk), writes=wk)
            for k in range(1, nk):
                pl.op("dve", lambda e, k=k: e.scalar_tensor_tensor(out=dst[:, 0:Tn], in0=src[:, off + k:off + k + Tn], scalar=wts[:, k:k + 1], in1=dst[:, 0:Tn], op0=ALU.mult, op1=ALU.add), reads=rk + wk, writes=wk)

        def tile(row0, Tn, slots, mode, gidx, yrow0, hist_from, st_mode, out_slot, next_row0=None, h_idx=0, pre_a=False, next_a=None):
            nsub = Tn // 128
            nch = Tn // 64
            hcur[0], hcur[1] = ((hT, "hT"), (ycat[:, 0:8, :], "ycat"))[h_idx]
            if not pre_a:
                step_a(row0, Tn, slots, hcur[0], hcur[1])
            light = (mode != "full")
            if mode in ("full", "halo"):
                wbs = {}
                pendA = None
                for j in range(8):
                    i = j % 2
                    need = [8 + j, 16 + j] if mode == "halo" else [8 + j, 16 + j, 0 + j, 24 + j]
                    for pc in need:
                        wbs[pc] = load_piece(pc)
                    cc = 0
                    if j % 2 == 0:
                        (pc_, pck), (ph_, phk), (pb_, pbk) = (psA[:, 0, 0:Tn], "psA0"), (psA[:, 1, 0:Tn], "psA1"), (psA[:, 2, 0:Tn], "psA2")
                    else:
                        (pc_, pck), (ph_, phk), (pb_, pbk) = (psA[:, 0, 0:Tn], "psA0"), (psA[:, 1, 0:Tn], "psA1"), (psA[:, 2, 0:Tn], "psA2")
                    wb, wk_ = wbs[8 + j]
                    proj_chunk(wb, wk_, cc, pc_, pck, Tn)
                    wb, wk_ = wbs[16 + j]
                    proj_chunk(wb, wk_, cc, ph_, phk, Tn)
                    if mode != "halo":
                        wb, wk_ = wbs[0 + j]
                        proj_chunk(wb, wk_, cc, pb_, pbk, Tn)
                        wb, wk_ = wbs[24 + j]
                        proj_chunk(wb, wk_, cc, psA[:, 3, 0:Tn], "psA3", Tn)
                        wbz, wkz = load_piece(32 + j)
                        proj_chunk(wbz, wkz, 0, psB[:, j % 2, 0:Tn], "psB%d" % (j % 2), Tn)
                        if pendA is not None:
                            pendA()
                            pendA = None
                    ub, uk = ubuf[i], "ubuf%d" % i
                    pl.op("act", lambda e, i=i, ph_=ph_: e.copy(out=hsb[i][:, 0:Tn], in_=ph_), reads=[phk], writes=["hsb%d" % i])
                    pl.op("pool", lambda e, ub=ub, j=j: e.tensor_copy(out=ub[:, 0:3], in_=uhist[:, j, :]), reads=["uhist"], writes=[uk])
                    pl.op("dve", lambda e, ub=ub, i=i, pc_=pc_: e.tensor_tensor(out=ub[:, 3:3 + Tn], in0=pc_, in1=hsb[i][:, 0:Tn], op=ALU.mult), reads=[pck, "hsb%d" % i], writes=[uk])
                    pl.op("pool", lambda e, ub=ub, j=j: e.tensor_copy(out=uhist[:, j, :], in_=ub[:, Tn:Tn + 3]), reads=[uk], writes=["uhist"])
                    if mode == "halo":
                        continue
                    pl.op("act", lambda e, i=i: e.activation(out=tq[i][:, 0:Tn], in_=psA[:, 3, 0:Tn], func=AF.Silu), reads=["psA3"], writes=["tq%d" % i])
                    pl.op("dve", lambda e, i=i, pb_=pb_: e.tensor_tensor(out=tq[i][:, 0:Tn], in0=pb_, in1=tq[i][:, 0:Tn], op=ALU.mult), reads=[pbk, "tq%d" % i], writes=["tq%d" % i])
                    c_, ck = cu[i], "cu%d" % i
                    if hist_from == "carry":
                        conv_chunk("dve", ub, caw[:, j, :], 3, c_, Tn, [uk, "caw"], [ck])
                    else:
                        for sidx in range(2):
                            pl.op("dve", lambda e, ub=ub, j=j, sidx=sidx: e.tensor_copy(out=stsb[:, sidx * 128 + 1:sidx * 128 + 3], in_=sca_sb[:, j, sidx, :]), reads=["sca"], writes=["stsb"])
                        for sidx in range(2):
                            base = sidx * 128
                            pl.op("dve", lambda e, ub=ub, base=base, sidx=sidx: e.tensor_copy(out=stsb[:, base + 3:base + 67], in_=ub[:, 3 + sidx * 64:3 + (sidx + 1) * 64]), reads=[uk], writes=["stsb"])
                            off = 1
                            pl.op("dve", lambda e, c_=c_, base=base, sidx=sidx, j=j: e.tensor_scalar_mul(out=c_[:, sidx * 64:(sidx + 1) * 64], in0=stsb[:, base + 1:base + 65], scalar1=caw[:, j, 0:1]), reads=["stsb", "caw"], writes=[ck])
                            for k in (1, 2):
                                pl.op("dve", lambda e, c_=c_, base=base, sidx=sidx, j=j, k=k: e.scalar_tensor_tensor(out=c_[:, sidx * 64:(sidx + 1) * 64], in0=stsb[:, base + 1 + k:base + 65 + k], scalar=caw[:, j, k:k + 1], in1=c_[:, sidx * 64:(sidx + 1) * 64], op0=ALU.mult, op1=ALU.add), reads=["stsb", "caw", ck], writes=[ck])
                            pl.op("dve", lambda e, base=base, sidx=sidx, j=j: e.tensor_copy(out=casb[:, j, 1 + sidx, :], in_=stsb[:, base + 65:base + 67]), reads=["stsb"], writes=["casb"])
                    def stage_b(c_=c_, ck=ck, i=i, j=j):
                        pl.op("dve", lambda e: e.tensor_tensor(out=c_[:, 0:Tn], in0=c_[:, 0:Tn], in1=tq[i][:, 0:Tn], op=ALU.mult), reads=[ck, "tq%d" % i], writes=[ck])
                        pl.op("act", lambda e: e.activation(out=sq2[i][:, 0:Tn], in_=c_[:, 0:Tn], func=AF.Square), reads=[ck], writes=["sq2%d" % i])
                        pl.op("act", lambda e: e.activation(out=ycat[:, j, 0:Tn], in_=c_[:, 0:Tn], func=AF.Copy, scale=pfc(PF_NAW, j)), reads=[ck, "pf"], writes=["ycat"])
                        for s in range(nsub):
                            pl.op("pe", lambda e, s=s: e.matmul(psC[:, 320 + s:321 + s], lhsT=sq2[i][:, s * 128:(s + 1) * 128], rhs=ones_bf[:, 0:1], start=(j == 0 and s == 0), stop=(j == 7), skip_group_check=True),
                                  reads=["sq2%d" % i, "ones"], writes=["psCa"], token=(s == nsub - 1))
                    pl.op("act", lambda e, j=j: e.activation(out=sz[:, j, 0:Tn], in_=psB[:, j % 2, 0:Tn], func=AF.Silu), reads=["psB%d" % (j % 2)], writes=["sz"])
                    pendA = stage_b
                if pendA is not None:
                    pendA()
                if mode == "full" and hist_from == "carry":
                    pl.op("dve", lambda e: e.tensor_copy(out=casb[:, :, 0, :], in_=uhist[:, :, 1:3]), reads=["uhist"], writes=["casb"])
                if mode == "halo":
                    pl.op("dve", lambda e: e.tensor_scalar_mul(out=uhist[:].rearrange("p a b -> p (a b)"), in0=uhist[:].rearrange("p a b -> p (a b)"), scalar1=msk[:, 0:1]), reads=["uhist", "msk"], writes=["uhist"])
                    wbs = {}
                    for j in range(12, 16):
                        wb, wk_ = load_piece(40 + j)
                        pa = psA[:, j % 4, 0:Tn]
                        pk = "psA%d" % (j % 4)
                        proj_chunk(wb, wk_, 0, pa, pk, Tn)
                        pl.op("act", lambda e, j=j, pa=pa: e.copy(out=xhist[:, j, :], in_=pa[:, Tn - 3:Tn]), reads=[pk], writes=["xhist"])
                    pl.op("dve", lambda e: e.tensor_scalar_mul(out=xhist[:, 12:16, :], in0=xhist[:, 12:16, :], scalar1=msk[:, 0:1]), reads=["xhist", "msk"], writes=["xhist"])
                    return

            wbs = {}
            pending = None
            for j in range(16):
                if mode == "light" and j >= 12:
                    break
                wb, wk_ = load_piece(40 + j)
                i = j % 2
                pa = psA[:, j % 4, 0:Tn]
                pk = "psA%d" % (j % 4)
                proj_chunk(wb, wk_, 0, pa, pk, Tn)
                ub, uk = ubuf[i], "ubuf%d" % i
                pl.op("pool", lambda e, ub=ub, j=j: e.tensor_copy(out=ub[:, 0:3], in_=xhist[:, j, :]), reads=["xhist"], writes=[uk])
                pl.op("act", lambda e, ub=ub, pa=pa: e.copy(out=ub[:, 3:3 + Tn], in_=pa), reads=[pk], writes=[uk])
                pl.op("pool", lambda e, ub=ub, j=j: e.tensor_copy(out=xhist[:, j, :], in_=ub[:, Tn:Tn + 3]), reads=[uk], writes=["xhist"])
                c_, ck = cu[i], "cu%d" % i
                if hist_from == "carry":
                    conv_chunk("act", ub, cbw[:, j, :], 4, c_, Tn, [uk, "cbw"], [ck], bias=cbb[:, j:j + 1], bk=["cbb"])
                else:
                    for sidx in range(2):
                        base = sidx * 128
                        pl.op("dve", lambda e, j=j, sidx=sidx, base=base: e.tensor_copy(out=stsb[:, base:base + 3], in_=scb_sb[:, j, sidx, :]), reads=["scb"], writes=["stsb"])
                        pl.op("dve", lambda e, ub=ub, base=base, sidx=sidx: e.tensor_copy(out=stsb[:, base + 3:base + 67], in_=ub[:, 3 + sidx * 64:3 + (sidx + 1) * 64]), reads=[uk], writes=["stsb"])
                        pl.op("dve", lambda e, c_=c_, base=base, sidx=sidx, j=j: e.tensor_scalar_mul(out=c_[:, sidx * 64:(sidx + 1) * 64], in0=stsb[:, base:base + 64], scalar1=cbw[:, j, 0:1]), reads=["stsb", "cbw"], writes=[ck])
                        for k in (1, 2, 3):
                            pl.op("dve", lambda e, c_=c_, base=base, sidx=sidx, j=j, k=k: e.scalar_tensor_tensor(out=c_[:, sidx * 64:(sidx + 1) * 64], in0=stsb[:, base + k:base + 64 + k], scalar=cbw[:, j, k:k + 1], in1=c_[:, sidx * 64:(sidx + 1) * 64], op0=ALU.mult, op1=ALU.add), reads=["stsb", "cbw", ck], writes=[ck])
                        pl.op("dve", lambda e, base=base, sidx=sidx, j=j: e.tensor_copy(out=cbsb[:, j, 1 + sidx, :], in_=stsb[:, base + 64:base + 67]), reads=["stsb"], writes=["cbsb"])
                    pl.op("dve", lambda e, c_=c_, j=j: e.tensor_scalar_add(out=c_[:, 0:Tn], in0=c_[:, 0:Tn], scalar1=cbb[:, j:j + 1]), reads=[ck, "cbb"], writes=[ck])
                if j < 8:
                    dst, dk = xbf[:, j, 0:Tn], "xbf"
                elif j < 12:
                    dst, dk = BT[:, j - 8, 0:Tn], "BT"
                else:
                    dst, dk = CT[:, j - 12, 0:Tn], "CT"

                def stage_b(c_=c_, ck=ck, i=i, dst=dst, dk=dk):
                    pl.op("act", lambda e: e.activation(out=dst, in_=c_[:, 0:Tn], func=AF.Silu), reads=[ck], writes=[dk])
                if pending is not None:
                    pending()
                pending = stage_b
            if pending is not None:
                pending()
            if mode == "full" and hist_from == "carry":
                pl.op("dve", lambda e: e.tensor_copy(out=cbsb[:, :, 0, :], in_=xhist[:]), reads=["xhist"], writes=["cbsb"])
            W = nsub * 16
            SMA = lambda idx: sm[:, idx, 0:W]
            v3 = lambda ap: ap.rearrange("p (b h) -> p b h", h=16)
            for b in range(nsub):
                tsl = slice(b * 128, (b + 1) * 128)
                for kc in range(8):
                    pl.op("pe", lambda e, kc=kc, tsl=tsl, b=b, hb=hcur[0]: e.matmul(psC[:, 256 + b * 16:272 + b * 16], lhsT=hb[:, kc, tsl], rhs=wdt[:, kc, :], start=(kc == 0), stop=(kc == 7)), reads=[hcur[1], "wdt"], writes=["psCd"], token=(kc == 7))
            pl.op("dve", lambda e: e.tensor_tensor(out=v3(SMA(0)), in0=v3(psC[:, 256:256 + W]), in1=bc[:, BC_DTB:BC_DTB + 16].unsqueeze(1).broadcast_to([128, nsub, 16]), op=ALU.add), reads=["psCd", "bc"], writes=["sm0"])
            pl.op("act", lambda e: e.activation(out=SMA(1), in_=SMA(0), func=AF.Abs), reads=["sm0"], writes=["sm1"])
            pl.op("act", lambda e: e.activation(out=SMA(1), in_=SMA(1), func=AF.Exp, scale=-1.0), reads=["sm1"], writes=["sm1"])
            pl.op("act", lambda e: e.activation(out=SMA(1), in_=SMA(1), func=AF.Ln, bias=1.0), reads=["sm1"], writes=["sm1"])
            pl.op("dve", lambda e: e.scalar_tensor_tensor(out=SMA(2), in0=SMA(0), scalar=0.0, in1=SMA(1), op0=ALU.max, op1=ALU.add), reads=["sm0", "sm1"], writes=["sm2"])
            pl.op("dve", lambda e: e.tensor_tensor(out=v3(SMA(3)), in0=v3(SMA(2)), in1=a_bc[:].unsqueeze(1).broadcast_to([128, nsub, 16]), op=ALU.mult), reads=["sm2", "a_bc"], writes=["sm3"])
            pl.op("pe", lambda e: e.matmul(psB[:, 0, 0:W], lhsT=tri, rhs=SMA(3), start=True, stop=True), reads=["cst", "sm3"], writes=["psB0"], token=False)
            pl.op("pe", lambda e: e.matmul(psB[:, 0, 64:64 + W], lhsT=blk, rhs=SMA(3), start=True, stop=True), reads=["cst", "sm3"], writes=["psB0"], token=False)
            pl.op("pe", lambda e: e.matmul(psB[:, 0, 128:128 + W], lhsT=chsel[0], rhs=SMA(3), start=True, stop=True), reads=["cst", "sm3"], writes=["psB0"], token=False)
            pl.op("pe", lambda e: e.matmul(psB[:, 0, 192:192 + W], lhsT=chsel[1], rhs=SMA(3), start=True, stop=True), reads=["cst", "sm3"], writes=["psB0"])
            pl.op("dve", lambda e: e.tensor_copy(out=SMA(4), in_=psB[:, 0, 0:W]), reads=["psB0"], writes=["sm4"])
            pl.op("dve", lambda e: e.tensor_tensor(out=SMA(5), in0=psB[:, 0, 64:64 + W], in1=SMA(4), op=ALU.subtract), reads=["psB0", "sm4"], writes=["sm5"])
            if mode == "light":
                sel0, sel1 = psB[:, 0, 128:128 + W], psB[:, 0, 192:192 + W]
                pl.op("dve", lambda e: e.tensor_copy(out=SMA(7), in_=sel1), reads=["psB0"], writes=["sm7"])
                pl.op("dve", lambda e: e.tensor_tensor(out=SMA(0), in0=sel0, in1=SMA(7), op=ALU.add), reads=["psB0", "sm7"], writes=["sm0"])
                pl.op("dve", lambda e: e.memset(SMA(1), 0.0), writes=["sm1"])
                for b in range(nsub - 2, -1, -1):
                    pl.op("dve", lambda e, b=b: e.tensor_tensor(out=sm[:, 1, b * 16:(b + 1) * 16], in0=sm[:, 1, (b + 1) * 16:(b + 2) * 16], in1=sm[:, 0, (b + 1) * 16:(b + 2) * 16], op=ALU.add), reads=["sm1", "sm0"], writes=["sm1"])
                pl.op("dve", lambda e: e.tensor_tensor(out=cd[:, 1, 0:16], in0=sm[:, 1, 0:16], in1=sm[:, 0, 0:16], op=ALU.add), reads=["sm1", "sm0"], writes=["cd"])
                pl.op("act", lambda e: e.activation(out=cd[:, 0, 0:16], in_=cd[:, 1, 0:16], func=AF.Exp), reads=["cd"], writes=["cd"])
                pl.op("dve", lambda e: e.scalar_tensor_tensor(out=SMA(1), in0=SMA(7), scalar=chsel[0][:, 0:1], in1=SMA(1), op0=ALU.mult, op1=ALU.add), reads=["sm7", "cst", "sm1"], writes=["sm1"])
                pl.op("dve", lambda e: e.tensor_tensor(out=SMA(5), in0=SMA(5), in1=SMA(1), op=ALU.add), reads=["sm5", "sm1"], writes=["sm5"])
            pl.op("act", lambda e: e.activation(out=SMA(5), in_=SMA(5), func=AF.Exp), reads=["sm5"], writes=["sm5"])
            pl.op("dve", lambda e: e.tensor_tensor(out=SMA(6), in0=SMA(5), in1=SMA(2), op=ALU.mult), reads=["sm5", "sm2"], writes=["sm6"])
            if mode != "light":
                pl.op("act", lambda e: e.activation(out=cd[:, 0, 0:W], in_=psB[:, 0, 128:128 + W], func=AF.Exp), reads=["psB0"], writes=["cd"])
                pl.op("act", lambda e: e.activation(out=cd[:, 1, 0:W], in_=psB[:, 0, 192:192 + W], func=AF.Exp), reads=["psB0"], writes=["cd"])
            else:
                if st_mode[0] is not None and st_mode[0][0] == "zero":
                    pl.op("dve", lambda e: e.memset(ST[:], 0.0), writes=["ST"])
                pl.op("pool", lambda e: e.tensor_tensor(out=ST[:].rearrange("p (h q) -> p h q", h=16), in0=ST[:].rearrange("p (h q) -> p h q", h=16), in1=cd[:, 0, 0:16].unsqueeze(2).broadcast_to([128, 16, 64]), op=ALU.mult), reads=["ST", "cd"], writes=["ST"])
            for b in range(nsub):
                tsl = slice(b * 128, (b + 1) * 128)
                SM = lambda idx, b=b: sm[:, idx, b * 16:(b + 1) * 16]
                bc16 = lambda idx, SM=SM: SM(idx).unsqueeze(2).broadcast_to([128, 16, 64])
                b16_2, b16_3, b16_4, b16_6 = bc16(2), bc16(3), bc16(4), bc16(6)
                for j in range(8):
                    pl.op("pe", lambda e, j=j, tsl=tsl: e.transpose(out=psT[:, j * 128:(j + 1) * 128], in_=xbf[:, j, tsl], identity=idb[:]), reads=["xbf", "idb"], writes=["psT"], token=(j == 7))
                bview = lambda ap: ap.rearrange("p (h q) -> p h q", h=16)
                if mode == "full":
                    pl.op("dve", lambda e, b16_2=b16_2: e.tensor_tensor(out=bview(xdt[:]), in0=bview(psT[:, :]), in1=b16_2, op=ALU.mult), reads=["psT", "sm2"], writes=["xdt"])
                if mode == "light":
                    xw_, xwk = ((xw, "xw"), (xdt, "xdt"))[b % 2]
                    bt_, btk = ((btok[:], "btok"), (Mt[:, 0:512], "Mt"))[b % 2]
                    bps, bpk = psC[:, 0:256].bitcast(BF16), "psCb"
                else:
                    xw_, xwk, bt_, btk, bps, bpk = xw, "xw", btok[:], "btok", psT[:, 0:512], "psT"
                pl.op("dve", lambda e, b16_6=b16_6, xw_=xw_: e.tensor_tensor(out=bview(xw_[:]), in0=bview(psT[:, :]), in1=b16_6, op=ALU.mult), reads=["psT", "sm6"], writes=[xwk])
                for g in range(4):
                    pl.op("pe", lambda e, g=g, tsl=tsl, bps=bps: e.transpose(out=bps[:, g * 128:(g + 1) * 128], in_=BT[:, g, tsl], identity=idb[:]), reads=["BT", "idb"], writes=[bpk], token=(g == 3))
                pl.op("act", lambda e, bt_=bt_, bps=bps: e.copy(out=bt_, in_=bps), reads=[bpk], writes=[btk])

                if mode == "full":
                    for g in range(4):
                        for c in range(2):
                            cs = slice(b * 128 + c * 64, b * 128 + (c + 1) * 64)
                            pl.op("pe", lambda e, g=g, c=c, cs=cs: e.matmul(psC[c * 64:(c + 1) * 64, g * 64:(g + 1) * 64], lhsT=BT[:, g, cs], rhs=CT[:, g, cs], start=True, stop=True), reads=["BT", "CT"], writes=["psCb"], token=(g == 3 and c == 1))
                    pl.op("dve", lambda e: e.tensor_tensor(out=cbtm[:], in0=psC[:, 0:256].rearrange("p (g i) -> p g i", g=4), in1=t64.unsqueeze(1).broadcast_to([128, 4, 64]), op=ALU.mult), reads=["psCb", "cst"], writes=["cbtm"])
                    pl.op("dve", lambda e, b16_3=b16_3: e.tensor_tensor(out=bview(rhs1[:]), in0=b16_3, in1=t64.unsqueeze(1).broadcast_to([128, 16, 64]), op=ALU.mult), reads=["sm3", "cst"], writes=["rhs1"])
                    for hh in range(2):
                        pl.op("pe", lambda e, hh=hh: e.matmul(psA[:, hh, :], lhsT=blk, rhs=rhs1[:, hh * 512:(hh + 1) * 512], start=True, stop=True), reads=["cst", "rhs1"], writes=["psA%d" % hh])
                    pl.op("dve", lambda e, b16_4=b16_4: e.tensor_tensor(out=bview(segs[:]), in0=psA[:, 0:2, :].rearrange("p a (h q) -> p (a h) q", q=64), in1=b16_4, op=ALU.subtract), reads=["psA0", "psA1", "sm4"], writes=["segs"])
                    pl.op("dve", lambda e: e.tensor_scalar_min(out=segs[:], in0=segs[:], scalar1=0.0), reads=["segs"], writes=["segs"])
                    pl.op("act", lambda e: e.activation(out=segs[:], in_=segs[:], func=AF.Exp), reads=["segs"], writes=["segs"])
                    for g in range(4):
                        pl.op("dve", lambda e, g=g: e.scalar_tensor_tensor(out=Mt[:, g * 256:(g + 1) * 256].rearrange("p (r q) -> p r q", r=4), in0=segs[:, g * 256:(g + 1) * 256].rearrange("p (r q) -> p r q", r=4), scalar=1.0,
                                                                           in1=cbtm[:, g, :].unsqueeze(1).broadcast_to([128, 4, 64]), op0=ALU.min, op1=ALU.mult), reads=["segs", "cbtm"], writes=["Mt"])
                    pl.op("pool", lambda e, tsl=tsl: e.tensor_tensor(out=segs[:].rearrange("p (a t) -> p a t", a=8), in0=xbf[:, :, tsl], in1=pf[:, PF_DSK:PF_DSK + 8].unsqueeze(2).broadcast_to([128, 8, 128]), op=ALU.mult), reads=["xbf", "pf", "segs"], writes=["segs", "segs2"])
                    pl.op("dve", lambda e, b16_3=b16_3: e.tensor_copy(out=bview(rhs1[:]), in_=b16_3), reads=["sm3"], writes=["rhs1"])
                    for a in range(8):
                        pl.op("pe", lambda e, a=a: e.matmul(psA[:, 2 + a // 4, (a % 4) * 128:(a % 4 + 1) * 128], lhsT=rhs1[:, a * 128:(a + 1) * 128], rhs=tri, start=True, stop=True), reads=["rhs1", "cst"], writes=["psA%d" % (2 + a // 4)], token=(a % 4 == 3))
                    pl.op("act", lambda e: e.activation(out=eac[:], in_=psA[:, 2:4, :].rearrange("p a q -> p (a q)"), func=AF.Exp), reads=["psA2", "psA3"], writes=["eac"])

                for c in range(2):
                    ch = b * 2 + c
                    cs = slice(b * 128 + c * 64, b * 128 + (c + 1) * 64)
                    ps_ = slice(c * 64, (c + 1) * 64)
                    if mode != "light" and st_mode[ch] is not None:
                        kind, val = st_mode[ch]
                        if kind == "zero":
                            pl.op("dve", lambda e: e.memset(ST[:], 0.0), writes=["ST"])
                        elif kind == "load":
                            pl.dma("sp", lambda e, val=val: e.dma_start(out=ST[:], in_=sst_d[val, :, :]), "l_ST", writes=["ST"])
                        elif kind == "keep":
                            pass
                        if mode == "full":
                            pl.op("act", lambda e: e.copy(out=STb[:], in_=ST[:]), reads=["ST"], writes=["STb"])
                    if mode != "light":
                        pl.op("pool", lambda e, c=c, b=b: e.tensor_tensor(out=bview(ST[:]), in0=bview(ST[:]), in1=cd[:, c, b * 16:(b + 1) * 16].unsqueeze(2).broadcast_to([128, 16, 64]), op=ALU.mult), reads=["ST", "cd"], writes=["ST"])
                    if mode == "full":
                        for a in range(8):
                            for hh in range(2):
                                h = 2 * a + hh
                                pl.op("pe", lambda e, a=a, hh=hh, h=h, c=c, ps_=ps_: e.matmul(psB[hh * 64:(hh + 1) * 64, a // 4, (a % 4) * 128 + c * 64:(a % 4) * 128 + (c + 1) * 64], lhsT=xdt[ps_, h * 64:(h + 1) * 64], rhs=Mt[ps_, h * 64:(h + 1) * 64], start=True, stop=True),
                                      reads=["xdt", "Mt"], writes=["psB%d" % (a // 4)], token=(hh == 1 and a % 4 == 3))
                        for a in range(8):
                            g = a // 2
                            pl.op("pe", lambda e, a=a, g=g, c=c, cs=cs: e.matmul(psA[:, a // 4, (a % 4) * 128 + c * 64:(a % 4) * 128 + (c + 1) * 64], lhsT=STb[:, a * 128:(a + 1) * 128], rhs=CT[:, g, cs], start=True, stop=True),
                                  reads=["STb", "CT"], writes=["psA%d" % (a // 4)], token=(a % 4 == 3))
                    first = (b == 0 and c == 0)
                    last = (b == nsub - 1 and c == 1)
                    for g in range(4):
                        if mode == "light":
                            bk = (2 if c == 0 else 0) + g // 2
                            pl.op("pe", lambda e, g=g, ps_=ps_, bk=bk, b=b, last=last, bt_=bt_, xw_=xw_: e.matmul(psA[:, bk, (g % 2) * 256:(g % 2 + 1) * 256], lhsT=bt_[ps_, g * 128:(g + 1) * 128], rhs=xw_[ps_, g * 256:(g + 1) * 256], start=(b == 0 and g % 2 == 0), stop=(b == nsub - 1), skip_group_check=True),
                                  reads=[btk, xwk], writes=["psA%d" % bk], token=(g % 2 == 1))
                        else:
                            pl.op("pe", lambda e, g=g, ps_=ps_: e.matmul(psA[:, 2 + g // 2, (g % 2) * 256:(g % 2 + 1) * 256], lhsT=btok[ps_, g * 128:(g + 1) * 128], rhs=xw[ps_, g * 256:(g + 1) * 256], start=True, stop=True),
                                  reads=["btok", "xw"], writes=["psA%d" % (2 + g // 2)], token=(g % 2 == 1))
                    if mode != "light" or last:
                        pl.op("dve", lambda e: e.tensor_tensor(out=ST[:], in0=ST[:], in1=psA[:, 2:4, :].rearrange("p a q -> p (a q)"), op=ALU.add), reads=["ST", "psA2", "psA3"], writes=["ST"])
                        if mode == "light":
                            pl.op("dve", lambda e: e.tensor_tensor(out=ST[:], in0=ST[:], in1=psA[:, 0:2, :].rearrange("p a q -> p (a q)"), op=ALU.add), reads=["ST", "psA0", "psA1"], writes=["ST"])
                    if mode == "full":
                        pl.op("act", lambda e: e.copy(out=STb[:], in_=ST[:]), reads=["ST"], writes=["STb"])
                        if hist_from != "carry":
                            pl.dma("pool", lambda e, ch=ch: e.dma_start(out=so_d[1 + ch, :, :], in_=ST[:]), "s_ST", reads=["ST"], writes=["so%d" % (1 + ch)])
                if next_a is not None and b < len(next_a):
                    next_a[b]()
                if mode == "full":
                    pl.op("dve", lambda e: e.tensor_tensor(out=yo[:], in0=psA[:, 0:2, :].rearrange("p a q -> p (a q)"), in1=eac[:], op=ALU.mult), reads=["psA0", "psA1", "eac"], writes=["yo"])
                    pl.op("dve", lambda e: e.tensor_tensor(out=yo[:], in0=psB[:, :, :].rearrange("p a q -> p (a q)"), in1=yo[:], op=ALU.add), reads=["psB0", "psB1", "yo"], writes=["yo"])
                    v8 = lambda ap: ap.rearrange("p (a t) -> p a t", a=8)
                    pl.op("dve", lambda e: e.tensor_tensor(out=yo[:], in0=yo[:], in1=segs[:], op=ALU.add), reads=["yo", "segs2"], writes=["yo"])
                    pl.op("dve", lambda e, tsl=tsl: e.tensor_tensor(out=v8(yo[:]), in0=v8(yo[:]), in1=sz[:, :, tsl], op=ALU.mult), reads=["yo", "sz"], writes=["yo"])
                    pl.op("act", lambda e: e.activation(out=Mt[:], in_=yo[:], func=AF.Square), reads=["yo"], writes=["Mt"])
                    pl.op("pool", lambda e, tsl=tsl: e.tensor_tensor(out=ycat[:, 8:16, tsl], in0=v8(yo[:]), in1=pf[:, PF_NBW:PF_NBW + 8].unsqueeze(2).broadcast_to([128, 8, 128]), op=ALU.mult), reads=["yo", "pf"], writes=["ycat"])
                    for a in range(8):
                        pl.op("pe", lambda e, a=a, b=b: e.matmul(psC[:, 324 + b:325 + b], lhsT=Mt[:, a * 128:(a + 1) * 128], rhs=ones_bf[:, 0:1], start=(a == 0), stop=(a == 7)), reads=["Mt", "ones"], writes=["psCs"], token=(a == 7))

            if mode != "full":
                return
            pl.op("act", lambda e: e.activation(out=stat[:, 24:32], in_=psC[:, 320:328], func=AF.Ln, scale=1.0 / D, bias=EPS), reads=["psCa", "psCs"], writes=["stat"])
            pl.op("act", lambda e: e.activation(out=stat[:, 24:32], in_=stat[:, 24:32], func=AF.Exp, scale=-0.5), reads=["stat"], writes=["stat"])
            for s in range(nsub):
                tsl = slice(s * 128, (s + 1) * 128)
                for hf in range(2):
                    for part in range(2):
                        for kc in range(8):
                            pl.op("pe", lambda e, hf=hf, part=part, kc=kc, tsl=tsl: e.matmul(psA[:, part * 2 + hf, :], lhsT=ycat[:, part * 8 + kc, tsl], rhs=wout[:, part * 8 + kc, hf * 512:(hf + 1) * 512], start=(kc == 0), stop=(kc == 7)),
                                  reads=["ycat", "wout"], writes=["psA%d" % (part * 2 + hf)], token=(kc == 7))
                xb = xt[s % 2]
                xk = "xt%d" % (s % 2)
                pl.dma("sp", lambda e, xb=xb, s=s: e.dma_start(out=xb[:], in_=xs_d[row0 + s * 128:row0 + (s + 1) * 128, :]), "l_" + xk, writes=[xk])
                ob = ost[s % 2]
                ok = "ost%d" % (s % 2)
                pl.op("act", lambda e, s=s: e.activation(out=o1[:], in_=psA[:, 0:2, :].rearrange("p a q -> p (a q)"), func=AF.Copy, scale=stat[:, 24 + s:25 + s]), reads=["psA0", "psA1", "stat"], writes=["o1"])
                pl.op("dve", lambda e, s=s: e.scalar_tensor_tensor(out=o1[:], in0=psA[:, 2:4, :].rearrange("p a q -> p (a q)"), scalar=stat[:, 28 + s:29 + s], in1=o1[:], op0=ALU.mult, op1=ALU.add), reads=["psA2", "psA3", "stat", "o1"], writes=["o1"])
                pl.op("dve", lambda e: e.tensor_tensor(out=o1[:], in0=o1[:], in1=gate[:, gidx, :], op=ALU.mult), reads=["o1", "gate"], writes=["o1"])
                pl.op("dve", lambda e, xb=xb: e.tensor_tensor(out=o1[:], in0=o1[:], in1=xb[:], op=ALU.add), reads=["o1", xk], writes=["o1"])
                pl.op("dve", lambda e, s=s: e.memset(stat[:, 32 + s:33 + s], 0.0), writes=["stat"])
                pl.op("act", lambda e, s=s: e.activation(out=sqj[:], in_=o1[:], func=AF.Square, accum_out=stat[:, 32 + s:33 + s]), reads=["o1", "stat"], writes=["sqj", "stat"])
                pl.op("act", lambda e, s=s: e.activation(out=stat[:, 36 + s:37 + s], in_=stat[:, 32 + s:33 + s], func=AF.Ln, scale=1.0 / D, bias=EPS), reads=["stat"], writes=["stat"])
                pl.op("act", lambda e, s=s: e.activation(out=stat[:, 36 + s:37 + s], in_=stat[:, 36 + s:37 + s], func=AF.Exp, scale=-0.5), reads=["stat"], writes=["stat"])
                pl.op("dve", lambda e, s=s, ob=ob: e.scalar_tensor_tensor(out=ob[:], in0=o1[:], scalar=stat[:, 36 + s:37 + s], in1=bc[:, BC_NF:BC_NF + D], op0=ALU.mult, op1=ALU.mult), reads=["o1", "stat", "bc"], writes=[ok])
                pl.dma("pool", lambda e, ob=ob, s=s: e.dma_start(out=y_d[yrow0 + s * 128:yrow0 + (s + 1) * 128, :], in_=ob[:]), "s_" + ok, reads=[ok], writes=["y"])

        ones_bf = SB("ones_bf", [128, 8], BF16)
        pl.op("dve", lambda e: e.memset(ones_bf[:], 1.0), writes=["ones"])
        sca_sb = SB("sca_sb", [128, 8, 2, 2])
        scb_sb = SB("scb_sb", [128, 16, 2, 3])
        ld("l_sca", sca_sb[:].rearrange("p a b c -> p (a b c)"), sca_d[:, :], "sca")
        ld("l_scb", scb_sb[:].rearrange("p a b c -> p (a b c)"), scb_d[:, :], "scb")
        pl.op("dve", lambda e: e.memset(uhist[:], 0.0), writes=["uhist"])
        pl.op("dve", lambda e: e.memset(xhist[:], 0.0), writes=["xhist"])
        pl.op("dve", lambda e: e.memset(atot[:], 0.0), writes=["atot"])
        pslot = [(0, T, 0)]
        hsel = ((hT, "hT"), (ycat[:, 0:8, :], "ycat"))
        for n in range(NLSEG * NT):
            stm = [None] * (T // 64)
            if n == 0:
                stm[0] = ("zero", 0)
            nxa = None
            if n + 1 < NLSEG * NT:
                r1 = 128 + (n + 1) * T
                hb1, hk1 = hsel[(n + 1) % 2]
                nxa = [(lambda r1=r1, p0=p0, hb1=hb1, hk1=hk1: step_a_pair(r1, p0, T, pslot, hb1, hk1)) for p0 in (0, 2)]
            tile(128 + n * T, T, pslot, "light", 0, 0, "carry", stm, None, h_idx=n % 2, pre_a=(n > 0), next_a=nxa)
            if n % NT == NT - 1:
                m = n // NT
                pl.op("dve", lambda e, m=m: e.tensor_scalar_mul(out=ST[:], in0=ST[:], scalar1=msk[:, 1 + m:2 + m]), reads=["ST", "msk"], writes=["ST"])
                pl.op("dve", lambda e, m=m: e.tensor_scalar_mul(out=xhist[:].rearrange("p a b -> p (a b)"), in0=xhist[:].rearrange("p a b -> p (a b)"), scalar1=msk[:, 1 + m:2 + m]), reads=["xhist", "msk"], writes=["xhist"])
        tile(0, 128, [(0, 128, 0)], "halo", 0, 0, "carry", [None, None], None)
        for n in range(NT):
            stm = [None] * (T // 64)
            if n == 0:
                stm[0] = ("keep", 0)
            tile(128 + LROWS + n * T, T, pslot, "full", 0, n * T, "carry", stm, 0)
        pl.dma("pool", lambda e: e.dma_start(out=so_d[0, :, :], in_=ST[:]), "s_ST", reads=["ST"], writes=["so0"])
        tile(128 + LROWS + SEGLEN, 128, [(0, 64, 1), (64, 128, 2)], "full", 1, SEGLEN, "state", [("load", 0), ("load", 1)], 1)
        pl.dma("pool", lambda e: e.dma_start(out=ca_d[:, :], in_=casb[:].rearrange("p a b c -> p (a b c)")), "s_ca", reads=["casb"], writes=["ca"])
        pl.dma("pool", lambda e: e.dma_start(out=cb_d[:, :], in_=cbsb[:].rearrange("p a b c -> p (a b c)")), "s_cb", reads=["cbsb"], writes=["cb"])
        if debug:
            dbg_list = [("xbf", xbf[:].rearrange("p a b -> p (a b)"), 8 * T, BF16), ("BT", BT[:].rearrange("p a b -> p (a b)"), 4 * T, BF16),
                        ("CT", CT[:].rearrange("p a b -> p (a b)"), 4 * T, BF16), ("sm", sm[:].rearrange("p a b -> p (a b)"), 256, F32),
                        ("ycat", ycat[:].rearrange("p a b -> p (a b)"), 16 * T, BF16), ("yo", yo[:], 1024, F32), ("xw", xw[:], 1024, BF16),
                        ("xdt", xdt[:], 1024, BF16), ("btok", btok[:], 512, BF16), ("Mt", Mt[:], 1024, BF16), ("eac", eac[:], 1024, F32),
                        ("cd", cd[:].rearrange("p a b -> p (a b)"), 32, F32), ("hT", hT[:].rearrange("p a b -> p (a b)"), 8 * T, BF16),
                        ("sz", sz[:].rearrange("p a b -> p (a b)"), 8 * T, BF16), ("stat", stat[:], 64, F32), ("gam", gam[:].rearrange("p a b -> p (a b)"), 24, F32)]
            allk = list(pl.bufs.keys())
            for nm, ap, w, dt_ in dbg_list:
                dd = nc.dram_tensor("dbg_" + nm, [128, w], dt_, kind="ExternalOutput").ap()
                pl.dma("pool", lambda e, dd=dd, ap=ap: e.dma_start(out=dd[:, :], in_=ap), "s_dbg", reads=allk)
        pl.wait_tokens("pool", [(s, c) for s, c in pl.dma_cnt.items() if s.startswith("s_")])
        print("planned instructions:", pl.nins, {e: len(pl.lists[e]) for e in pl.ENGS})
        pl.emit()
    return nc


def _host_consts():
    c = np.zeros((128, NCONST), np.float32)
    k = np.arange(128)
    c[:, C_ID:C_ID + 128] = np.eye(128, dtype=np.float32)
    same = (k[:, None] // 64) == (k[None, :] // 64)
    c[:, C_BLK:C_BLK + 128] = same
    c[:, C_TRI:C_TRI + 128] = same & (k[:, None] <= k[None, :])
    c[:, C_T64:C_T64 + 64] = (k[:, None] % 64) <= np.arange(64)[None, :]
    c[:, C_SEL0:C_SEL0 + 128] = (k[:, None] < 64)
    c[:, C_SEL1:C_SEL1 + 128] = (k[:, None] >= 64)
    return c


def _fm(v, nchunk):
    return np.ascontiguousarray(np.asarray(v, np.float32).reshape(nchunk, 128).T)


_NC_CACHE = {}


def kernel(x_prompt, x_sample, state_conv_a, state_conv_b, state_ssm, c_prompt, c_sample,
           w_mod, b_mod, norm_in_w, w_in, conv_a_w, norm_a_w, conv_b_w, conv_b_b,
           dt_bias, a_log, d_skip, norm_b_w, w_out, norm_f_w, _two_phase=True, _debug=False):
    f = lambda a: np.ascontiguousarray(np.asarray(a, np.float32))
    x_prompt, x_sample = f(x_prompt), f(x_sample)
    state_conv_a, state_conv_b, state_ssm = f(state_conv_a), f(state_conv_b), f(state_ssm)
    c_prompt, c_sample = f(c_prompt), f(c_sample)
    w_mod, b_mod, w_in, w_out = f(w_mod)[0], f(b_mod)[0], f(w_in)[0], f(w_out)[0]
    pf = np.zeros((128, NPF), np.float32)
    pf[:, PF_NIN:PF_NIN + 8] = _fm(f(norm_in_w)[0], 8)
    caw = f(conv_a_w)[0]
    pf[:, PF_CAW:PF_CAW + 24] = np.stack([_fm(caw[k], 8) for k in range(3)], axis=2).reshape(128, 24)
    pf[:, PF_NAW:PF_NAW + 8] = _fm(f(norm_a_w)[0], 8)
    cbw = f(conv_b_w)[0]
    pf[:, PF_CBW:PF_CBW + 64] = np.stack([_fm(cbw[k], 16) for k in range(4)], axis=2).reshape(128, 64)
    pf[:, PF_CBB:PF_CBB + 16] = _fm(f(conv_b_b)[0], 16)
    pf[:, PF_NBW:PF_NBW + 8] = _fm(f(norm_b_w)[0], 8)
    pf[:, PF_DSK:PF_DSK + 8] = _fm(np.repeat(f(d_skip)[0], 64), 8)
    pf[:, PF_BSH:PF_BSH + 8] = _fm(b_mod[0:D], 8)
    pf[:, PF_BSC:PF_BSC + 8] = _fm(b_mod[D:2 * D], 8)
    bcv = np.zeros((128, NBC), np.float32)
    bcv[:, BC_NF:BC_NF + D] = f(norm_f_w)[None, :]
    bcv[:, BC_BG:BC_BG + D] = b_mod[None, 2 * D:3 * D]
    bcv[:, BC_DTB:BC_DTB + 16] = f(dt_bias)[0][None, :]
    bcv[:, BC_ALOG:BC_ALOG + 16] = f(a_log)[0][None, :]
    cst = _host_consts()

    in_maps = []
    for k in range(NCORES):
        seq, seg = k // 4, k % 4
        start = seg * SEGLEN
        xs = np.zeros((XROWS, D), np.float32)
        if seg > 0:
            xs[0:128] = x_prompt[seq, start - 128:start]
        if seg > 0 and LROWS >= start:
            xs[128 + LROWS - start:128 + LROWS] = x_prompt[seq, 0:start]
        xs[128 + LROWS:128 + LROWS + SEGLEN] = x_prompt[seq, start:start + SEGLEN]
        xs[128 + LROWS + SEGLEN:] = x_sample[2 * k:2 * k + 2].reshape(128, D)
        cs = [c_prompt[seq], c_sample[2 * k], c_sample[2 * k + 1]]
        cT = np.stack([_fm(c, 8) for c in cs], axis=2).reshape(128, 24)
        cbc = np.zeros((2, 128, 8, 128), np.float32)
        cbc[0] = _fm(cs[0], 8)[:, :, None]
        cbc[1, :, :, 0:64] = _fm(cs[1], 8)[:, :, None]
        cbc[1, :, :, 64:128] = _fm(cs[2], 8)[:, :, None]
        sca = state_conv_a[0, 2 * k:2 * k + 2]
        sca = sca.reshape(2, 2, 8, 128).transpose(3, 2, 0, 1).reshape(128, 32)
        scb = state_conv_b[0, 2 * k:2 * k + 2]
        scb = scb.reshape(2, 3, 16, 128).transpose(3, 2, 0, 1).reshape(128, 96)
        sst = state_ssm[0, 2 * k:2 * k + 2]
        sst = sst.reshape(2, 1024, 128).transpose(0, 2, 1)
        msk = np.zeros((128, 16), np.float32)
        msk[:, 0] = 1.0 if seg > 0 else 0.0
        for m in range(NLSEG):
            msk[:, 1 + m] = 0.0 if m < NLSEG - seg else 1.0
        in_maps.append({
            "xs": xs, "w_mod": w_mod, "w_in": w_in, "w_out": w_out, "pf": pf, "bc": bcv, "cst": cst,
            "cT": np.ascontiguousarray(cT), "cbc": np.ascontiguousarray(cbc.reshape(2, 128, 1024)),
            "sca": np.ascontiguousarray(sca), "scb": np.ascontiguousarray(scb),
            "sst": np.ascontiguousarray(sst), "msk": msk,
        })
    key = (bool(_two_phase), bool(_debug))
    if key not in _NC_CACHE:
        _NC_CACHE[key] = build_nc(two_phase=key[0], debug=key[1])
    nc = _NC_CACHE[key]
    res = run_bass_kernel_spmd(nc, in_maps, core_ids=list(range(NCORES)))
    R = res.results
    if _debug:
        kernel.last_results = R
    y_prompt = np.zeros((2, SEQ, D), np.float32)
    y_sample = np.zeros((16, 64, D), np.float32)
    ca_p = np.zeros((1, 2, 2, D), np.float32)
    cb_p = np.zeros((1, 2, 3, 2 * D), np.float32)
    ss_p = np.zeros((1, 2, 16, 64, 128), np.float32)
    ca_s = np.zeros((1, 16, 2, D), np.float32)
    cb_s = np.zeros((1, 16, 3, 2 * D), np.float32)
    ss_s = np.zeros((1, 16, 16, 64, 128), np.float32)
    for k in range(NCORES):
        seq, seg = k // 4, k % 4
        r = R[k]
        y_prompt[seq, seg * SEGLEN:(seg + 1) * SEGLEN] = r["y"][0:SEGLEN]
        y_sample[2 * k:2 * k + 2] = r["y"][SEGLEN:].reshape(2, 64, D)
        ca = r["ca"].reshape(128, 8, 3, 2)
        cb = r["cb"].reshape(128, 16, 3, 3)
        so = r["so"]
        for s in range(2):
            ca_s[0, 2 * k + s] = ca[:, :, 1 + s, :].transpose(2, 1, 0).reshape(2, D)
            cb_s[0, 2 * k + s] = cb[:, :, 1 + s, :].transpose(2, 1, 0).reshape(3, 2 * D)
            ss_s[0, 2 * k + s] = so[1 + s].T.reshape(16, 64, 128)
        if seg == 3:
            ca_p[0, seq] = ca[:, :, 0, :].transpose(2, 1, 0).reshape(2, D)
            cb_p[0, seq] = cb[:, :, 0, :].transpose(2, 1, 0).reshape(3, 2 * D)
            ss_p[0, seq] = so[0].T.reshape(16, 64, 128)
    return (y_prompt, y_sample, ca_p, cb_p, ss_p, ca_s, cb_s, ss_s)
```

```python
import contextlib
import numpy as np
import concourse.bass as bass
import concourse.mybir as mybir
from concourse.bass_utils import run_bass_kernel_spmd

F32 = mybir.dt.float32
BF16 = mybir.dt.bfloat16
ALU = mybir.AluOpType
AF = mybir.ActivationFunctionType

NCORES = 8
D = 1024
SEQ = 16384
SEGLEN = 4096
T = 512
NT = SEGLEN // T
DIN = 7184
PW = 128
NPIECE = 56
NWB = 12
EPS = 1e-5
NLSEG = 3
LROWS = NLSEG * SEGLEN
XROWS = 128 + LROWS + SEGLEN + 128
YROWS = SEGLEN + 128

PF_NIN = 0
PF_CAW = 8
PF_NAW = 32
PF_CBW = 40
PF_CBB = 104
PF_NBW = 120
PF_DSK = 128
PF_BSH = 136
PF_BSC = 144
NPF = 152
BC_NF = 0
BC_BG = 1024
BC_DTB = 2048
BC_ALOG = 2064
NBC = 2080
C_ID = 0
C_BLK = 128
C_TRI = 256
C_T64 = 384
C_SEL0 = 448
C_SEL1 = 576
NCONST = 704


class Planner:
    ENGS = ("pe", "act", "dve", "pool", "sp")
    SEM_LIMIT = 30000

    def __init__(self, nc):
        self.nc = nc
        self.lists = {e: [] for e in self.ENGS}
        self.cur = {e: [e + "_0", 0] for e in self.ENGS}
        self.gen = {e: 0 for e in self.ENGS}
        self.sem_names = [e + "_0" for e in self.ENGS]
        self.dma_cnt = {}
        self.waited = {e: {} for e in self.ENGS}
        self.bufs = {}
        self.nins = 0
        self.alias = {}
        self.bank_last = {}

    @staticmethod
    def _bank(k):
        if k.startswith("psC"):
            return "psC"
        if k.startswith("psA") or k.startswith("psB") or k == "psT":
            return k
        return None

    def _bank_deps(self, eng, keys, need):
        banks = set(b for b in (self._bank(k) for k in keys) if b)
        for b in banks:
            for oe, tok in self.bank_last.get(b, {}).items():
                if oe != eng:
                    self._need(eng, tok, need)
        return banks

    def _exp(self, keys):
        out = []
        for k in keys:
            out.extend(self.alias.get(k, (k,)))
        return out

    def _need(self, eng, tok, out):
        if tok is None:
            return
        s, v = tok
        if eng == "pe" and s.startswith("pe_"):
            return
        if self.waited[eng].get(s, 0) >= v:
            return
        if out.get(s, 0) < v:
            out[s] = v

    def _deps(self, eng, reads, writes):
        reads, writes = self._exp(reads), self._exp(writes)
        need = {}
        for k in reads:
            b = self.bufs.get(k)
            if b:
                self._need(eng, b[0], need)
        for k in writes:
            b = self.bufs.get(k)
            if b:
                self._need(eng, b[0], need)
                for t in b[1]:
                    self._need(eng, t, need)
        self._cur_banks = self._bank_deps(eng, list(reads) + list(writes), need)
        self._cur_eng = eng
        for s, v in need.items():
            self.waited[eng][s] = v
            self.lists[eng].append(("wait", s, v))

    def _mark(self, tok, reads, writes):
        reads, writes = self._exp(reads), self._exp(writes)
        for b in self._cur_banks:
            self.bank_last.setdefault(b, {})[self._cur_eng] = tok
        for k in reads:
            b = self.bufs.setdefault(k, [None, []])
            b[1].append(tok)
        for k in writes:
            self.bufs[k] = [tok, []]

    def op(self, eng, fn, reads=(), writes=(), token=True):
        self._deps(eng, reads, writes)
        c = self.cur[eng]
        self.nins += 1
        if token:
            if c[1] >= self.SEM_LIMIT:
                self.gen[eng] += 1
                c[0] = "%s_%d" % (eng, self.gen[eng])
                c[1] = 0
                self.sem_names.append(c[0])
            c[1] += 1
            tok = (c[0], c[1])
            self.lists[eng].append(("ins", fn, c[0], 1))
        else:
            tok = (c[0], c[1] + 1)
            self.lists[eng].append(("ins", fn, None, 0))
        self._mark(tok, reads, writes)
        return tok

    def dma(self, eng, fn, sem, reads=(), writes=()):
        self._deps(eng, reads, writes)
        self.nins += 1
        if sem not in self.dma_cnt:
            self.dma_cnt[sem] = 0
            self.sem_names.append(sem)
        self.dma_cnt[sem] += 16
        tok = (sem, self.dma_cnt[sem])
        self.lists[eng].append(("ins", fn, sem, 16))
        self._mark(tok, reads, writes)
        return tok

    def raw(self, eng, fn, sem, inc, reads=(), writes=()):
        self._deps(eng, reads, writes)
        if sem not in self.dma_cnt:
            self.dma_cnt[sem] = 0
            self.sem_names.append(sem)
        self.dma_cnt[sem] += inc
        tok = (sem, self.dma_cnt[sem])
        self.lists[eng].append(("ins", fn, sem, -inc))
        self._mark(tok, reads, writes)
        return tok

    def wait_tokens(self, eng, toks):
        need = {}
        for t in toks:
            self._need(eng, t, need)
        for s, v in need.items():
            self.waited[eng][s] = v
            self.lists[eng].append(("wait", s, v))

    def emit(self):
        nc = self.nc
        with contextlib.ExitStack() as st:
            sems = {}
            for n in self.sem_names:
                sems[n] = st.enter_context(nc.semaphore(n))
            block = st.enter_context(nc.Block())
            engmap = {"pe": block.tensor, "act": block.scalar, "dve": block.vector,
                      "pool": block.gpsimd, "sp": block.sync}
            for e in self.ENGS:
                lst = self.lists[e]
                if not lst:
                    continue

                def body(engobj, lst=lst):
                    for it in lst:
                        if it[0] == "wait":
                            engobj.wait_ge(sems[it[1]], it[2])
                        else:
                            ins = it[1](engobj)
                            if it[2] is not None:
                                if it[3] < 0:
                                    ins.then_inc(sems[it[2]])
                                else:
                                    ins.then_inc(sems[it[2]], it[3])
                engmap[e](body)


def build_nc(two_phase=True, debug=False):
    nc = bass.Bass("TRN2", target_bir_lowering=False)
    dr = lambda n, s, k, d=F32: nc.dram_tensor(n, list(s), d, kind=k)
    xs_d = dr("xs", [XROWS, D], "ExternalInput").ap()
    wmod_d = dr("w_mod", [D, 3 * D], "ExternalInput").ap()
    win_d = dr("w_in", [D, DIN], "ExternalInput").ap()
    wout_d = dr("w_out", [2 * D, D], "ExternalInput").ap()
    pf_d = dr("pf", [128, NPF], "ExternalInput").ap()
    bc_d = dr("bc", [128, NBC], "ExternalInput").ap()
    cst_d = dr("cst", [128, NCONST], "ExternalInput").ap()
    cT_d = dr("cT", [128, 8 * 3], "ExternalInput").ap()
    cbc_d = dr("cbc", [2, 128, 8 * 128], "ExternalInput").ap()
    sca_d = dr("sca", [128, 8 * 2 * 2], "ExternalInput").ap()
    scb_d = dr("scb", [128, 16 * 2 * 3], "ExternalInput").ap()
    sst_d = dr("sst", [2, 128, D], "ExternalInput").ap()
    msk_d = dr("msk", [128, 16], "ExternalInput").ap()
    y_d = dr("y", [YROWS, D], "ExternalOutput").ap()
    ca_d = dr("ca", [128, 8 * 3 * 2], "ExternalOutput").ap()
    cb_d = dr("cb", [128, 16 * 3 * 3], "ExternalOutput").ap()
    so_d = dr("so", [3, 128, D], "ExternalOutput").ap()
    winbf_d = nc.dram_tensor("winbf", [NPIECE, 128, 8 * PW], BF16)

    pl = Planner(nc)
    pl.alias = {"stg0": ("rhs1", "segs", "eac", "yo"), "stg1": ("stsb", "o1", "ost0", "ost1"), "sqj": ("Mt",)}
    with contextlib.ExitStack() as st:
        def SB(name, shape, dt=F32):
            return st.enter_context(nc.sbuf_tensor("sb_" + name, list(shape), dt))

        cst = SB("cst", [128, NCONST])
        idb = SB("idb", [128, 128], BF16)
        pf = SB("pf", [128, NPF])
        bc = SB("bc", [128, NBC])
        msk = SB("msk", [128, 16])
        cT = SB("cT", [128, 8, 3])
        gam = SB("gam", [128, 3, 8])
        bet = SB("bet", [128, 3, 8])
        gate = SB("gate", [128, 2, D])
        caw = SB("caw", [128, 8, 3])
        cbw = SB("cbw", [128, 16, 4])
        cbb = SB("cbb", [128, 16])
        a_bc = SB("a_bc", [128, 16])
        wout = SB("wout", [128, 16, D], BF16)
        wdt = SB("wdt", [128, 8, 16], BF16)
        wbuf = [SB("wbuf%d" % i, [128, 8, PW], BF16) for i in range(NWB)]
        big = [SB("big%d" % i, [128, 4096]) for i in range(2)]
        stg = [b[:].rearrange("p (a c) -> p a c", a=8) for b in big]
        xt = [SB("xt%d" % i, [128, D]) for i in range(2)]
        hT = SB("hT", [128, 8, T], BF16)
        ubuf = [SB("ubuf%d" % i, [128, 3 + T]) for i in range(2)]
        hsb = [SB("hsb%d" % i, [128, T]) for i in range(2)]
        cu = [SB("cu%d" % i, [128, T]) for i in range(2)]
        tq = [SB("tq%d" % i, [128, T]) for i in range(2)]
        sq2 = [SB("sq2%d" % i, [128, T], BF16) for i in range(2)]
        uhist = SB("uhist", [128, 8, 3])
        xhist = SB("xhist", [128, 16, 3])
        uhist0 = SB("uhist0", [128, 8, 3])
        xhist0 = SB("xhist0", [128, 16, 3])
        ycat = SB("ycat", [128, 16, T], BF16)
        xbf = SB("xbf", [128, 8, T], BF16)
        BT = SB("BT", [128, 4, T], BF16)
        CT = SB("CT", [128, 4, T], BF16)
        sz = SB("sz", [128, 8, T], BF16)
        xdt = SB("xdt", [128, D], BF16)
        xw = SB("xw", [128, D], BF16)
        btok = SB("btok", [128, 512], BF16)
        rhs1 = big[0][:, 0:1024]
        segs = big[0][:, 1024:2048]
        Mt = SB("Mt", [128, D], BF16)
        sqj = Mt
        eac = big[0][:, 2048:3072]
        yo = big[0][:, 3072:4096]
        cbtm = SB("cbtm", [128, 4, 64])
        sm = SB("sm", [128, 8, 64])
        ST = SB("ST", [128, D])
        STb = SB("STb", [128, D], BF16)
        stsb = big[1][:, 0:1024]
        cd = SB("cd", [128, 2, 64])
        stat = SB("stat", [128, 64])
        ost = [big[1][:, 2048:3072], big[1][:, 3072:4096]]
        o1 = big[1][:, 1024:2048]
        casb = SB("casb", [128, 8, 3, 2])
        cbsb = SB("cbsb", [128, 16, 3, 3])
        atot = SB("atot", [128, 16])
        gsel = SB("gsel", [128, 8, 16])
        print("sbuf remaining after alloc:", nc.sbuf_bytes_remaining)

        psA = st.enter_context(nc.psum_tensor("psA", [128, 4, 512], F32))
        psB = st.enter_context(nc.psum_tensor("psB", [128, 2, 512], F32))
        psC = st.enter_context(nc.psum_tensor("psC", [128, 512], F32))
        psT = st.enter_context(nc.psum_tensor("psT", [128, 1024], BF16))

        ident = cst[:, C_ID:C_ID + 128]
        blk = cst[:, C_BLK:C_BLK + 128]
        tri = cst[:, C_TRI:C_TRI + 128]
        t64 = cst[:, C_T64:C_T64 + 64]
        chsel = [cst[:, C_SEL0:C_SEL0 + 128], cst[:, C_SEL1:C_SEL1 + 128]]

        def pfc(off, j):
            return pf[:, off + j:off + j + 1]

        ld = lambda name, dst, src, key: pl.dma("sp", lambda e: e.dma_start(out=dst, in_=src), name, writes=[key])
        ld("l_cst", cst[:], cst_d[:, :], "cst")
        ld("l_pf", pf[:], pf_d[:, :], "pf")
        ld("l_bc", bc[:], bc_d[:, :], "bc")
        ld("l_msk", msk[:], msk_d[:, :], "msk")
        ld("l_cT", cT[:].rearrange("p a b -> p (a b)"), cT_d[:, :], "cT")
        pl.op("dve", lambda e: e.tensor_copy(out=idb[:], in_=ident), reads=["cst"], writes=["idb"])
        pl.op("dve", lambda e: e.tensor_scalar_mul(out=caw[:].rearrange("p a b -> p (a b)"), in0=pf[:, PF_CAW:PF_CAW + 24], scalar1=1.0), reads=["pf"], writes=["caw"])
        pl.op("dve", lambda e: e.tensor_scalar_mul(out=cbw[:].rearrange("p a b -> p (a b)"), in0=pf[:, PF_CBW:PF_CBW + 64], scalar1=1.0), reads=["pf"], writes=["cbw"])
        pl.op("dve", lambda e: e.tensor_scalar_mul(out=cbb[:], in0=pf[:, PF_CBB:PF_CBB + 16], scalar1=1.0), reads=["pf"], writes=["cbb"])
        pl.op("act", lambda e: e.activation(out=a_bc[:], in_=bc[:, BC_ALOG:BC_ALOG + 16], func=AF.Exp), reads=["bc"], writes=["a_bc"])
        pl.op("dve", lambda e: e.tensor_scalar_mul(out=a_bc[:], in0=a_bc[:], scalar1=-1.0), reads=["a_bc"], writes=["a_bc"])

        wmod_v = wmod_d.rearrange("(kc p) c -> p kc c", p=128)
        for piece in range(6):
            s = stg[piece % 2]
            key = "stg%d" % (piece % 2)
            pl.dma("sp", lambda e, s=s, piece=piece: e.dma_start(out=s[:], in_=wmod_v[:, :, piece * 512:(piece + 1) * 512]), "l_" + key, writes=[key])
            if piece < 4:
                for cc in range(4):
                    j = (piece % 2) * 4 + cc
                    for kc in range(8):
                        pl.op("pe", lambda e, s=s, cc=cc, kc=kc, j=j: e.matmul(psC[:, j * 4:j * 4 + 3], lhsT=s[:, kc, cc * 128:(cc + 1) * 128], rhs=cT[:, kc, :], start=(kc == 0), stop=(kc == 7)),
                              reads=[key, "cT"], writes=["psC"], token=(kc == 7))
                if piece % 2 == 1:
                    src = psC[:, 0:32].rearrange("p (j s) -> p s j", s=4)[:, 0:3, :]
                    if piece == 1:
                        pl.op("dve", lambda e, src=src: e.tensor_tensor(out=bet[:], in0=src, in1=pf[:, PF_BSH:PF_BSH + 8].unsqueeze(1).broadcast_to([128, 3, 8]), op=ALU.add), reads=["psC", "pf"], writes=["bet"])
                    else:
                        pl.op("dve", lambda e, src=src: e.tensor_tensor(out=gam[:], in0=src, in1=pf[:, PF_BSC:PF_BSC + 8].unsqueeze(1).broadcast_to([128, 3, 8]), op=ALU.add), reads=["psC", "pf"], writes=["gam"])
                        pl.op("dve", lambda e: e.scalar_tensor_tensor(out=gam[:], in0=gam[:], scalar=1.0, in1=pf[:, PF_NIN:PF_NIN + 8].unsqueeze(1).broadcast_to([128, 3, 8]), op0=ALU.add, op1=ALU.mult), reads=["gam", "pf"], writes=["gam"])
            else:
                half = piece - 4
                for which in range(2):
                    cb_t = xt[which]
                    if half == 0:
                        pl.dma("sp", lambda e, cb_t=cb_t, which=which: e.dma_start(out=cb_t[:], in_=cbc_d[which, :, :]), "l_xt%d" % which, writes=["xt%d" % which])
                    cbv = cb_t[:].rearrange("p (k m) -> p k m", k=8)
                    for kc in range(8):
                        pl.op("pe", lambda e, s=s, kc=kc, cbv=cbv, which=which: e.matmul(psA[:, which, :], lhsT=cbv[:, kc, :], rhs=s[:, kc, :], start=(kc == 0), stop=(kc == 7)),
                              reads=[key, "xt%d" % which], writes=["psA%d" % which], token=(kc == 7))
                    pl.op("dve", lambda e, which=which, half=half: e.tensor_tensor(out=gate[:, which, half * 512:(half + 1) * 512], in0=psA[:, which, :], in1=bc[:, BC_BG + half * 512:BC_BG + (half + 1) * 512], op=ALU.add),
                          reads=["psA%d" % which, "bc"], writes=["gate"])

        win_v = win_d.rearrange("(kc p) c -> p kc c", p=128)
        wout_v = wout_d.rearrange("(kc p) c -> p kc c", p=128)
        castengs = ["dve", "act", "pool"]
        ci = 0

        def cast(dst, src, rk, wk):
            nonlocal ci
            eng = castengs[ci % 3]
            ci += 1
            if eng == "act":
                pl.op("act", lambda e: e.copy(out=dst, in_=src), reads=rk, writes=wk)
            else:
                pl.op(eng, lambda e: e.tensor_copy(out=dst, in_=src), reads=rk, writes=wk)

        porder = [10, 11, 12, 2, 3, 4, 5, 0, 1, 6, 7, 13, 8, 9]
        pl.dma("sp", lambda e: e.dma_start(out=stg[0][:, :, 0:16], in_=win_v[:, :, 7168:7184]), "l_stg0", writes=["stg0"])
        pl.op("dve", lambda e: e.tensor_copy(out=wdt[:], in_=stg[0][:, :, 0:16]), reads=["stg0"], writes=["wdt"])
        wci = 0
        for n, piece in enumerate(porder):
            s = stg[n % 2]
            key = "stg%d" % (n % 2)
            pl.dma("sp", lambda e, s=s, piece=piece: e.dma_start(out=s[:], in_=win_v[:, :, piece * 512:(piece + 1) * 512]), "l_" + key, writes=[key])
            for hp in range(512 // PW):
                wb = wbuf[wci % NWB]
                wkey = "wbuf%d" % (wci % NWB)
                wci += 1
                for kh in range(2):
                    cast(wb[:, kh * 4:(kh + 1) * 4, :], s[:, kh * 4:(kh + 1) * 4, hp * PW:(hp + 1) * PW], [key], [wkey])
                sp_ = (512 // PW) * piece + hp
                pl.dma("pool", lambda e, wb=wb, sp_=sp_: e.dma_start(out=winbf_d[sp_, :, :], in_=wb[:].rearrange("p a b -> p (a b)")), "s_" + wkey, reads=[wkey], writes=["winbf%d" % sp_])
        for n in range(4):
            kg, ch = n // 2, n % 2
            s = stg[n % 2]
            key = "stg%d" % (n % 2)
            pl.dma("sp", lambda e, s=s, kg=kg, ch=ch: e.dma_start(out=s[:], in_=wout_v[:, kg * 8:(kg + 1) * 8, ch * 512:(ch + 1) * 512]), "l_" + key, writes=[key])
            for kh in range(2):
                cast(wout[:, kg * 8 + kh * 4:kg * 8 + (kh + 1) * 4, ch * 512:(ch + 1) * 512], s[:, kh * 4:(kh + 1) * 4, :], [key], ["wout"])

        wcnt = [0]

        def load_piece(piece):
            i = wcnt[0] % NWB
            wcnt[0] += 1
            wb = wbuf[i]
            pl.dma("sp", lambda e: e.dma_start(out=wb[:].rearrange("p a b -> p (a b)"), in_=winbf_d[piece, :, :]), "l_wbuf%d" % i, reads=["winbf%d" % piece], writes=["wbuf%d" % i])
            return wb, "wbuf%d" % i

        hcur = [hT, "hT"]

        def proj_chunk(wb, wkey, cc, ps_ap, pskey, Tn):
            hb, hk = hcur
            for kc in range(8):
                pl.op("pe", lambda e, kc=kc: e.matmul(ps_ap, lhsT=wb[:, kc, cc * 128:(cc + 1) * 128], rhs=hb[:, kc, 0:Tn], start=(kc == 0), stop=(kc == 7)),
                      reads=[wkey, hk], writes=[pskey], token=(kc == 7))

        pref = {}

        def load_x(row0, s):
            xb = xt[s % 2]
            xk = "xt%d" % (s % 2)
            pl.dma("sp", lambda e: e.dma_start(out=xb[:], in_=xs_d[row0 + s * 128:row0 + (s + 1) * 128, :]), "l_" + xk, writes=[xk])

        def step_a(row0, Tn, slots, hb=None, hk="hT"):
            for p0 in range(0, Tn // 128, 2):
                step_a_pair(row0, p0, Tn, slots, hT if hb is None else hb, hk)

        def step_a_pair(row0, p0, Tn, slots, hb, hk):
            nsub = Tn // 128
            if True:
                subs = list(range(p0, min(p0 + 2, nsub)))
                for s in subs:
                    xb = xt[s % 2]
                    xk = "xt%d" % (s % 2)
                    if not pref.pop((row0, s), False):
                        load_x(row0, s)
                    pl.op("pool", lambda e, s=s: e.memset(stat[:, s:s + 1], 0.0), writes=["stat"])
                    pl.op("act", lambda e, xb=xb, s=s: e.activation(out=sqj[:], in_=xb[:], func=AF.Square, accum_out=stat[:, s:s + 1]), reads=[xk, "stat"], writes=["sqj", "stat"])
                    pl.op("act", lambda e, s=s: e.activation(out=stat[:, 8 + s:9 + s], in_=stat[:, s:s + 1], func=AF.Ln, scale=1.0 / D, bias=EPS), reads=["stat"], writes=["stat"])
                    pl.op("act", lambda e, s=s: e.activation(out=stat[:, 16 + s:17 + s], in_=stat[:, 8 + s:9 + s], func=AF.Exp, scale=-0.5), reads=["stat"], writes=["stat"])
                    pl.op("dve", lambda e, xb=xb, s=s: e.tensor_scalar_mul(out=xb[:], in0=xb[:], scalar1=stat[:, 16 + s:17 + s]), reads=[xk, "stat"], writes=[xk])
                w0, w1 = subs[0] * 128, (subs[-1] + 1) * 128
                for kc in range(8):
                    pk = "psB%d" % (kc % 2)
                    for s in subs:
                        xb = xt[s % 2]
                        xk = "xt%d" % (s % 2)
                        pl.op("pe", lambda e, xb=xb, kc=kc, s=s, p0=p0: e.transpose(out=psB[:, kc % 2, (s - p0) * 128:(s - p0 + 1) * 128], in_=xb[:, kc * 128:(kc + 1) * 128], identity=ident), reads=[xk, "cst"], writes=[pk], token=(s == subs[-1]))
                    for (c0, c1, slot) in slots:
                        lo, hi = max(c0, w0), min(c1, w1)
                        if lo >= hi:
                            continue
                        pl.op("act", lambda e, kc=kc, lo=lo, hi=hi, slot=slot, w0=w0: e.activation(out=hb[:, kc, lo:hi], in_=psB[:, kc % 2, lo - w0:hi - w0], func=AF.Identity, scale=gam[:, slot, kc:kc + 1], bias=bet[:, slot, kc:kc + 1]),
                              reads=[pk, "gam", "bet"], writes=[hk])

        def conv_chunk(eng_first, src, wts, nk, dst, Tn, rk, wk, bias=None, bk=()):
            off = 3 - (nk - 1)
            if bias is None:
                pl.op("act", lambda e: e.activation(out=dst[:, 0:Tn], in_=src[:, off:off + Tn], func=AF.Copy, scale=wts[:, 0:1]), reads=rk, writes=wk)
            else:
                pl.op("act", lambda e: e.activation(out=dst[:, 0:Tn], in_=src[:, off:off + Tn], func=AF.Identity, scale=wts[:, 0:1], bias=bias), reads=rk + list(bk), writes=wk)
            for k in range(1, nk):
                pl.op("dve", lambda e, k=k: e.scalar_tensor_tensor(out=dst[:, 0:Tn], in0=src[:, off + k:off + k + Tn], scalar=wts[:, k:k + 1], in1=dst[:, 0:Tn], op0=ALU.mult, op1=ALU.add), reads=rk + wk, writes=wk)

        def tile(row0, Tn, slots, mode, gidx, yrow0, hist_from, st_mode, out_slot, next_row0=None, h_idx=0, pre_a=False, next_a=None):
            nsub = Tn // 128
            nch = Tn // 64
            hcur[0], hcur[1] = ((hT, "hT"), (ycat[:, 0:8, :], "ycat"))[h_idx]
            if not pre_a:
                step_a(row0, Tn, slots, hcur[0], hcur[1])
            light = (mode != "full")
            if mode in ("full", "halo"):
                wbs = {}
                pendA = None
                for j in range(8):
                    i = j % 2
                    need = [8 + j, 16 + j] if mode == "halo" else [8 + j, 16 + j, 0 + j, 24 + j]
                    for pc in need:
                        wbs[pc] = load_piece(pc)
                    cc = 0
                    if j % 2 == 0:
                        (pc_, pck), (ph_, phk), (pb_, pbk) = (psA[:, 0, 0:Tn], "psA0"), (psA[:, 1, 0:Tn], "psA1"), (psA[:, 2, 0:Tn], "psA2")
                    else:
                        (pc_, pck), (ph_, phk), (pb_, pbk) = (psA[:, 0, 0:Tn], "psA0"), (psA[:, 1, 0:Tn], "psA1"), (psA[:, 2, 0:Tn], "psA2")
                    wb, wk_ = wbs[8 + j]
                    proj_chunk(wb, wk_, cc, pc_, pck, Tn)
                    wb, wk_ = wbs[16 + j]
                    proj_chunk(wb, wk_, cc, ph_, phk, Tn)
                    if mode != "halo":
                        wb, wk_ = wbs[0 + j]
                        proj_chunk(wb, wk_, cc, pb_, pbk, Tn)
                        wb, wk_ = wbs[24 + j]
                        proj_chunk(wb, wk_, cc, psA[:, 3, 0:Tn], "psA3", Tn)
                        wbz, wkz = load_piece(32 + j)
                        proj_chunk(wbz, wkz, 0, psB[:, j % 2, 0:Tn], "psB%d" % (j % 2), Tn)
                        if pendA is not None:
                            pendA()
                            pendA = None
                    ub, uk = ubuf[i], "ubuf%d" % i
                    pl.op("act", lambda e, i=i, ph_=ph_: e.copy(out=hsb[i][:, 0:Tn], in_=ph_), reads=[phk], writes=["hsb%d" % i])
                    pl.op("pool", lambda e, ub=ub, j=j: e.tensor_copy(out=ub[:, 0:3], in_=uhist[:, j, :]), reads=["uhist"], writes=[uk])
                    pl.op("dve", lambda e, ub=ub, i=i, pc_=pc_: e.tensor_tensor(out=ub[:, 3:3 + Tn], in0=pc_, in1=hsb[i][:, 0:Tn], op=ALU.mult), reads=[pck, "hsb%d" % i], writes=[uk])
                    pl.op("pool", lambda e, ub=ub, j=j: e.tensor_copy(out=uhist[:, j, :], in_=ub[:, Tn:Tn + 3]), reads=[uk], writes=["uhist"])
                    if mode == "halo":
                        continue
                    pl.op("act", lambda e, i=i: e.activation(out=tq[i][:, 0:Tn], in_=psA[:, 3, 0:Tn], func=AF.Silu), reads=["psA3"], writes=["tq%d" % i])
                    pl.op("dve", lambda e, i=i, pb_=pb_: e.tensor_tensor(out=tq[i][:, 0:Tn], in0=pb_, in1=tq[i][:, 0:Tn], op=ALU.mult), reads=[pbk, "tq%d" % i], writes=["tq%d" % i])
                    c_, ck = cu[i], "cu%d" % i
                    if hist_from == "carry":
                        conv_chunk("dve", ub, caw[:, j, :], 3, c_, Tn, [uk, "caw"], [ck])
                    else:
                        for sidx in range(2):
                            pl.op("dve", lambda e, ub=ub, j=j, sidx=sidx: e.tensor_copy(out=stsb[:, sidx * 128 + 1:sidx * 128 + 3], in_=sca_sb[:, j, sidx, :]), reads=["sca"], writes=["stsb"])
                        for sidx in range(2):
                            base = sidx * 128
                            pl.op("dve", lambda e, ub=ub, base=base, sidx=sidx: e.tensor_copy(out=stsb[:, base + 3:base + 67], in_=ub[:, 3 + sidx * 64:3 + (sidx + 1) * 64]), reads=[uk], writes=["stsb"])
                            off = 1
                            pl.op("dve", lambda e, c_=c_, base=base, sidx=sidx, j=j: e.tensor_scalar_mul(out=c_[:, sidx * 64:(sidx + 1) * 64], in0=stsb[:, base + 1:base + 65], scalar1=caw[:, j, 0:1]), reads=["stsb", "caw"], writes=[ck])
                            for k in (1, 2):
                                pl.op("dve", lambda e, c_=c_, base=base, sidx=sidx, j=j, k=k: e.scalar_tensor_tensor(out=c_[:, sidx * 64:(sidx + 1) * 64], in0=stsb[:, base + 1 + k:base + 65 + k], scalar=caw[:, j, k:k + 1], in1=c_[:, sidx * 64:(sidx + 1) * 64], op0=ALU.mult, op1=ALU.add), reads=["stsb", "caw", ck], writes=[ck])
                            pl.op("dve", lambda e, base=base, sidx=sidx, j=j: e.tensor_copy(out=casb[:, j, 1 + sidx, :], in_=stsb[:, base + 65:base + 67]), reads=["stsb"], writes=["casb"])
                    def stage_b(c_=c_, ck=ck, i=i, j=j):
                        pl.op("dve", lambda e: e.tensor_tensor(out=c_[:, 0:Tn], in0=c_[:, 0:Tn], in1=tq[i][:, 0:Tn], op=ALU.mult), reads=[ck, "tq%d" % i], writes=[ck])
                        pl.op("act", lambda e: e.activation(out=sq2[i][:, 0:Tn], in_=c_[:, 0:Tn], func=AF.Square), reads=[ck], writes=["sq2%d" % i])
                        pl.op("act", lambda e: e.activation(out=ycat[:, j, 0:Tn], in_=c_[:, 0:Tn], func=AF.Copy, scale=pfc(PF_NAW, j)), reads=[ck, "pf"], writes=["ycat"])
                        for s in range(nsub):
                            pl.op("pe", lambda e, s=s: e.matmul(psC[:, 320 + s:321 + s], lhsT=sq2[i][:, s * 128:(s + 1) * 128], rhs=ones_bf[:, 0:1], start=(j == 0 and s == 0), stop=(j == 7), skip_group_check=True),
                                  reads=["sq2%d" % i, "ones"], writes=["psCa"], token=(s == nsub - 1))
                    pl.op("act", lambda e, j=j: e.activation(out=sz[:, j, 0:Tn], in_=psB[:, j % 2, 0:Tn], func=AF.Silu), reads=["psB%d" % (j % 2)], writes=["sz"])
                    pendA = stage_b
                if pendA is not None:
                    pendA()
                if mode == "full" and hist_from == "carry":
                    pl.op("dve", lambda e: e.tensor_copy(out=casb[:, :, 0, :], in_=uhist[:, :, 1:3]), reads=["uhist"], writes=["casb"])
                if mode == "halo":
                    pl.op("dve", lambda e: e.tensor_scalar_mul(out=uhist[:].rearrange("p a b -> p (a b)"), in0=uhist[:].rearrange("p a b -> p (a b)"), scalar1=msk[:, 0:1]), reads=["uhist", "msk"], writes=["uhist"])
                    wbs = {}
                    for j in range(12, 16):
                        wb, wk_ = load_piece(40 + j)
                        pa = psA[:, j % 4, 0:Tn]
                        pk = "psA%d" % (j % 4)
                        proj_chunk(wb, wk_, 0, pa, pk, Tn)
                        pl.op("act", lambda e, j=j, pa=pa: e.copy(out=xhist[:, j, :], in_=pa[:, Tn - 3:Tn]), reads=[pk], writes=["xhist"])
                    pl.op("dve", lambda e: e.tensor_scalar_mul(out=xhist[:, 12:16, :], in0=xhist[:, 12:16, :], scalar1=msk[:, 0:1]), reads=["xhist", "msk"], writes=["xhist"])
                    return

            wbs = {}
            pending = None
            for j in range(16):
                if mode == "light" and j >= 12:
                    break
                wb, wk_ = load_piece(40 + j)
                i = j % 2
                pa = psA[:, j % 4, 0:Tn]
                pk = "psA%d" % (j % 4)
                proj_chunk(wb, wk_, 0, pa, pk, Tn)
                ub, uk = ubuf[i], "ubuf%d" % i
                pl.op("pool", lambda e, ub=ub, j=j: e.tensor_copy(out=ub[:, 0:3], in_=xhist[:, j, :]), reads=["xhist"], writes=[uk])
                pl.op("act", lambda e, ub=ub, pa=pa: e.copy(out=ub[:, 3:3 + Tn], in_=pa), reads=[pk], writes=[uk])
                pl.op("pool", lambda e, ub=ub, j=j: e.tensor_copy(out=xhist[:, j, :], in_=ub[:, Tn:Tn + 3]), reads=[uk], writes=["xhist"])
                c_, ck = cu[i], "cu%d" % i
                if hist_from == "carry":
                    conv_chunk("act", ub, cbw[:, j, :], 4, c_, Tn, [uk, "cbw"], [ck], bias=cbb[:, j:j + 1], bk=["cbb"])
                else:
                    for sidx in range(2):
                        base = sidx * 128
                        pl.op("dve", lambda e, j=j, sidx=sidx, base=base: e.tensor_copy(out=stsb[:, base:base + 3], in_=scb_sb[:, j, sidx, :]), reads=["scb"], writes=["stsb"])
                        pl.op("dve", lambda e, ub=ub, base=base, sidx=sidx: e.tensor_copy(out=stsb[:, base + 3:base + 67], in_=ub[:, 3 + sidx * 64:3 + (sidx + 1) * 64]), reads=[uk], writes=["stsb"])
                        pl.op("dve", lambda e, c_=c_, base=base, sidx=sidx, j=j: e.tensor_scalar_mul(out=c_[:, sidx * 64:(sidx + 1) * 64], in0=stsb[:, base:base + 64], scalar1=cbw[:, j, 0:1]), reads=["stsb", "cbw"], writes=[ck])
                        for k in (1, 2, 3):
                            pl.op("dve", lambda e, c_=c_, base=base, sidx=sidx, j=j, k=k: e.scalar_tensor_tensor(out=c_[:, sidx * 64:(sidx + 1) * 64], in0=stsb[:, base + k:base + 64 + k], scalar=cbw[:, j, k:k + 1], in1=c_[:, sidx * 64:(sidx + 1) * 64], op0=ALU.mult, op1=ALU.add), reads=["stsb", "cbw", ck], writes=[ck])
                        pl.op("dve", lambda e, base=base, sidx=sidx, j=j: e.tensor_copy(out=cbsb[:, j, 1 + sidx, :], in_=stsb[:, base + 64:base + 67]), reads=["stsb"], writes=["cbsb"])
                    pl.op("dve", lambda e, c_=c_, j=j: e.tensor_scalar_add(out=c_[:, 0:Tn], in0=c_[:, 0:Tn], scalar1=cbb[:, j:j + 1]), reads=[ck, "cbb"], writes=[ck])
                if j < 8:
                    dst, dk = xbf[:, j, 0:Tn], "xbf"
                elif j < 12:
                    dst, dk = BT[:, j - 8, 0:Tn], "BT"
                else:
                    dst, dk = CT[:, j - 12, 0:Tn], "CT"

                def stage_b(c_=c_, ck=ck, i=i, dst=dst, dk=dk):
                    pl.op("act", lambda e: e.activation(out=dst, in_=c_[:, 0:Tn], func=AF.Silu), reads=[ck], writes=[dk])
                if pending is not None:
                    pending()
                pending = stage_b
            if pending is not None:
                pending()
            if mode == "full" and hist_from == "carry":
                pl.op("dve", lambda e: e.tensor_copy(out=cbsb[:, :, 0, :], in_=xhist[:]), reads=["xhist"], writes=["cbsb"])
            W = nsub * 16
            SMA = lambda idx: sm[:, idx, 0:W]
            v3 = lambda ap: ap.rearrange("p (b h) -> p b h", h=16)
            for b in range(nsub):
                tsl = slice(b * 128, (b + 1) * 128)
                for kc in range(8):
                    pl.op("pe", lambda e, kc=kc, tsl=tsl, b=b, hb=hcur[0]: e.matmul(psC[:, 256 + b * 16:272 + b * 16], lhsT=hb[:, kc, tsl], rhs=wdt[:, kc, :], start=(kc == 0), stop=(kc == 7)), reads=[hcur[1], "wdt"], writes=["psCd"], token=(kc == 7))
            pl.op("dve", lambda e: e.tensor_tensor(out=v3(SMA(0)), in0=v3(psC[:, 256:256 + W]), in1=bc[:, BC_DTB:BC_DTB + 16].unsqueeze(1).broadcast_to([128, nsub, 16]), op=ALU.add), reads=["psCd", "bc"], writes=["sm0"])
            pl.op("act", lambda e: e.activation(out=SMA(1), in_=SMA(0), func=AF.Abs), reads=["sm0"], writes=["sm1"])
            pl.op("act", lambda e: e.activation(out=SMA(1), in_=SMA(1), func=AF.Exp, scale=-1.0), reads=["sm1"], writes=["sm1"])
            pl.op("act", lambda e: e.activation(out=SMA(1), in_=SMA(1), func=AF.Ln, bias=1.0), reads=["sm1"], writes=["sm1"])
            pl.op("dve", lambda e: e.scalar_tensor_tensor(out=SMA(2), in0=SMA(0), scalar=0.0, in1=SMA(1), op0=ALU.max, op1=ALU.add), reads=["sm0", "sm1"], writes=["sm2"])
            pl.op("dve", lambda e: e.tensor_tensor(out=v3(SMA(3)), in0=v3(SMA(2)), in1=a_bc[:].unsqueeze(1).broadcast_to([128, nsub, 16]), op=ALU.mult), reads=["sm2", "a_bc"], writes=["sm3"])
            pl.op("pe", lambda e: e.matmul(psB[:, 0, 0:W], lhsT=tri, rhs=SMA(3), start=True, stop=True), reads=["cst", "sm3"], writes=["psB0"], token=False)
            pl.op("pe", lambda e: e.matmul(psB[:, 0, 64:64 + W], lhsT=blk, rhs=SMA(3), start=True, stop=True), reads=["cst", "sm3"], writes=["psB0"], token=False)
            pl.op("pe", lambda e: e.matmul(psB[:, 0, 128:128 + W], lhsT=chsel[0], rhs=SMA(3), start=True, stop=True), reads=["cst", "sm3"], writes=["psB0"], token=False)
            pl.op("pe", lambda e: e.matmul(psB[:, 0, 192:192 + W], lhsT=chsel[1], rhs=SMA(3), start=True, stop=True), reads=["cst", "sm3"], writes=["psB0"])
            pl.op("dve", lambda e: e.tensor_copy(out=SMA(4), in_=psB[:, 0, 0:W]), reads=["psB0"], writes=["sm4"])
            pl.op("dve", lambda e: e.tensor_tensor(out=SMA(5), in0=psB[:, 0, 64:64 + W], in1=SMA(4), op=ALU.subtract), reads=["psB0", "sm4"], writes=["sm5"])
            if mode == "light":
                sel0, sel1 = psB[:, 0, 128:128 + W], psB[:, 0, 192:192 + W]
                pl.op("dve", lambda e: e.tensor_copy(out=SMA(7), in_=sel1), reads=["psB0"], writes=["sm7"])
                pl.op("dve", lambda e: e.tensor_tensor(out=SMA(0), in0=sel0, in1=SMA(7), op=ALU.add), reads=["psB0", "sm7"], writes=["sm0"])
                pl.op("dve", lambda e: e.memset(SMA(1), 0.0), writes=["sm1"])
                for b in range(nsub - 2, -1, -1):
                    pl.op("dve", lambda e, b=b: e.tensor_tensor(out=sm[:, 1, b * 16:(b + 1) * 16], in0=sm[:, 1, (b + 1) * 16:(b + 2) * 16], in1=sm[:, 0, (b + 1) * 16:(b + 2) * 16], op=ALU.add), reads=["sm1", "sm0"], writes=["sm1"])
                pl.op("dve", lambda e: e.tensor_tensor(out=cd[:, 1, 0:16], in0=sm[:, 1, 0:16], in1=sm[:, 0, 0:16], op=ALU.add), reads=["sm1", "sm0"], writes=["cd"])
                pl.op("act", lambda e: e.activation(out=cd[:, 0, 0:16], in_=cd[:, 1, 0:16], func=AF.Exp), reads=["cd"], writes=["cd"])
                pl.op("dve", lambda e: e.scalar_tensor_tensor(out=SMA(1), in0=SMA(7), scalar=chsel[0][:, 0:1], in1=SMA(1), op0=ALU.mult, op1=ALU.add), reads=["sm7", "cst", "sm1"], writes=["sm1"])
                pl.op("dve", lambda e: e.tensor_tensor(out=SMA(5), in0=SMA(5), in1=SMA(1), op=ALU.add), reads=["sm5", "sm1"], writes=["sm5"])
            pl.op("act", lambda e: e.activation(out=SMA(5), in_=SMA(5), func=AF.Exp), reads=["sm5"], writes=["sm5"])
            pl.op("dve", lambda e: e.tensor_tensor(out=SMA(6), in0=SMA(5), in1=SMA(2), op=ALU.mult), reads=["sm5", "sm2"], writes=["sm6"])
            if mode != "light":
                pl.op("act", lambda e: e.activation(out=cd[:, 0, 0:W], in_=psB[:, 0, 128:128 + W], func=AF.Exp), reads=["psB0"], writes=["cd"])
                pl.op("act", lambda e: e.activation(out=cd[:, 1, 0:W], in_=psB[:, 0, 192:192 + W], func=AF.Exp), reads=["psB0"], writes=["cd"])
            else:
                if st_mode[0] is not None and st_mode[0][0] == "zero":
                    pl.op("dve", lambda e: e.memset(ST[:], 0.0), writes=["ST"])
                pl.op("pool", lambda e: e.tensor_tensor(out=ST[:].rearrange("p (h q) -> p h q", h=16), in0=ST[:].rearrange("p (h q) -> p h q", h=16), in1=cd[:, 0, 0:16].unsqueeze(2).broadcast_to([128, 16, 64]), op=ALU.mult), reads=["ST", "cd"], writes=["ST"])
            for b in range(nsub):
                tsl = slice(b * 128, (b + 1) * 128)
                SM = lambda idx, b=b: sm[:, idx, b * 16:(b + 1) * 16]
                bc16 = lambda idx, SM=SM: SM(idx).unsqueeze(2).broadcast_to([128, 16, 64])
                b16_2, b16_3, b16_4, b16_6 = bc16(2), bc16(3), bc16(4), bc16(6)
                for j in range(8):
                    pl.op("pe", lambda e, j=j, tsl=tsl: e.transpose(out=psT[:, j * 128:(j + 1) * 128], in_=xbf[:, j, tsl], identity=idb[:]), reads=["xbf", "idb"], writes=["psT"], token=(j == 7))
                bview = lambda ap: ap.rearrange("p (h q) -> p h q", h=16)
                if mode == "full":
                    pl.op("dve", lambda e, b16_2=b16_2: e.tensor_tensor(out=bview(xdt[:]), in0=bview(psT[:, :]), in1=b16_2, op=ALU.mult), reads=["psT", "sm2"], writes=["xdt"])
                if mode == "light":
                    xw_, xwk = ((xw, "xw"), (xdt, "xdt"))[b % 2]
                    bt_, btk = ((btok[:], "btok"), (Mt[:, 0:512], "Mt"))[b % 2]
                    bps, bpk = psC[:, 0:256].bitcast(BF16), "psCb"
                else:
                    xw_, xwk, bt_, btk, bps, bpk = xw, "xw", btok[:], "btok", psT[:, 0:512], "psT"
                pl.op("dve", lambda e, b16_6=b16_6, xw_=xw_: e.tensor_tensor(out=bview(xw_[:]), in0=bview(psT[:, :]), in1=b16_6, op=ALU.mult), reads=["psT", "sm6"], writes=[xwk])
                for g in range(4):
                    pl.op("pe", lambda e, g=g, tsl=tsl, bps=bps: e.transpose(out=bps[:, g * 128:(g + 1) * 128], in_=BT[:, g, tsl], identity=idb[:]), reads=["BT", "idb"], writes=[bpk], token=(g == 3))
                pl.op("act", lambda e, bt_=bt_, bps=bps: e.copy(out=bt_, in_=bps), reads=[bpk], writes=[btk])

                if mode == "full":
                    for g in range(4):
                        for c in range(2):
                            cs = slice(b * 128 + c * 64, b * 128 + (c + 1) * 64)
                            pl.op("pe", lambda e, g=g, c=c, cs=cs: e.matmul(psC[c * 64:(c + 1) * 64, g * 64:(g + 1) * 64], lhsT=BT[:, g, cs], rhs=CT[:, g, cs], start=True, stop=True), reads=["BT", "CT"], writes=["psCb"], token=(g == 3 and c == 1))
                    pl.op("dve", lambda e: e.tensor_tensor(out=cbtm[:], in0=psC[:, 0:256].rearrange("p (g i) -> p g i", g=4), in1=t64.unsqueeze(1).broadcast_to([128, 4, 64]), op=ALU.mult), reads=["psCb", "cst"], writes=["cbtm"])
                    pl.op("dve", lambda e, b16_3=b16_3: e.tensor_tensor(out=bview(rhs1[:]), in0=b16_3, in1=t64.unsqueeze(1).broadcast_to([128, 16, 64]), op=ALU.mult), reads=["sm3", "cst"], writes=["rhs1"])
                    for hh in range(2):
                        pl.op("pe", lambda e, hh=hh: e.matmul(psA[:, hh, :], lhsT=blk, rhs=rhs1[:, hh * 512:(hh + 1) * 512], start=True, stop=True), reads=["cst", "rhs1"], writes=["psA%d" % hh])
                    pl.op("dve", lambda e, b16_4=b16_4: e.tensor_tensor(out=bview(segs[:]), in0=psA[:, 0:2, :].rearrange("p a (h q) -> p (a h) q", q=64), in1=b16_4, op=ALU.subtract), reads=["psA0", "psA1", "sm4"], writes=["segs"])
                    pl.op("dve", lambda e: e.tensor_scalar_min(out=segs[:], in0=segs[:], scalar1=0.0), reads=["segs"], writes=["segs"])
                    pl.op("act", lambda e: e.activation(out=segs[:], in_=segs[:], func=AF.Exp), reads=["segs"], writes=["segs"])
                    for g in range(4):
                        pl.op("dve", lambda e, g=g: e.scalar_tensor_tensor(out=Mt[:, g * 256:(g + 1) * 256].rearrange("p (r q) -> p r q", r=4), in0=segs[:, g * 256:(g + 1) * 256].rearrange("p (r q) -> p r q", r=4), scalar=1.0,
                                                                           in1=cbtm[:, g, :].unsqueeze(1).broadcast_to([128, 4, 64]), op0=ALU.min, op1=ALU.mult), reads=["segs", "cbtm"], writes=["Mt"])
                    pl.op("pool", lambda e, tsl=tsl: e.tensor_tensor(out=segs[:].rearrange("p (a t) -> p a t", a=8), in0=xbf[:, :, tsl], in1=pf[:, PF_DSK:PF_DSK + 8].unsqueeze(2).broadcast_to([128, 8, 128]), op=ALU.mult), reads=["xbf", "pf", "segs"], writes=["segs", "segs2"])
                    pl.op("dve", lambda e, b16_3=b16_3: e.tensor_copy(out=bview(rhs1[:]), in_=b16_3), reads=["sm3"], writes=["rhs1"])
                    for a in range(8):
                        pl.op("pe", lambda e, a=a: e.matmul(psA[:, 2 + a // 4, (a % 4) * 128:(a % 4 + 1) * 128], lhsT=rhs1[:, a * 128:(a + 1) * 128], rhs=tri, start=True, stop=True), reads=["rhs1", "cst"], writes=["psA%d" % (2 + a // 4)], token=(a % 4 == 3))
                    pl.op("act", lambda e: e.activation(out=eac[:], in_=psA[:, 2:4, :].rearrange("p a q -> p (a q)"), func=AF.Exp), reads=["psA2", "psA3"], writes=["eac"])

                for c in range(2):
                    ch = b * 2 + c
                    cs = slice(b * 128 + c * 64, b * 128 + (c + 1) * 64)
                    ps_ = slice(c * 64, (c + 1) * 64)
                    if mode != "light" and st_mode[ch] is not None:
                        kind, val = st_mode[ch]
                        if kind == "zero":
                            pl.op("dve", lambda e: e.memset(ST[:], 0.0), writes=["ST"])
                        elif kind == "load":
                            pl.dma("sp", lambda e, val=val: e.dma_start(out=ST[:], in_=sst_d[val, :, :]), "l_ST", writes=["ST"])
                        elif kind == "keep":
                            pass
                        if mode == "full":
                            pl.op("act", lambda e: e.copy(out=STb[:], in_=ST[:]), reads=["ST"], writes=["STb"])
                    if mode != "light":
                        pl.op("pool", lambda e, c=c, b=b: e.tensor_tensor(out=bview(ST[:]), in0=bview(ST[:]), in1=cd[:, c, b * 16:(b + 1) * 16].unsqueeze(2).broadcast_to([128, 16, 64]), op=ALU.mult), reads=["ST", "cd"], writes=["ST"])
                    if mode == "full":
                        for a in range(8):
                            for hh in range(2):
                                h = 2 * a + hh
                                pl.op("pe", lambda e, a=a, hh=hh, h=h, c=c, ps_=ps_: e.matmul(psB[hh * 64:(hh + 1) * 64, a // 4, (a % 4) * 128 + c * 64:(a % 4) * 128 + (c + 1) * 64], lhsT=xdt[ps_, h * 64:(h + 1) * 64], rhs=Mt[ps_, h * 64:(h + 1) * 64], start=True, stop=True),
                                      reads=["xdt", "Mt"], writes=["psB%d" % (a // 4)], token=(hh == 1 and a % 4 == 3))
                        for a in range(8):
                            g = a // 2
                            pl.op("pe", lambda e, a=a, g=g, c=c, cs=cs: e.matmul(psA[:, a // 4, (a % 4) * 128 + c * 64:(a % 4) * 128 + (c + 1) * 64], lhsT=STb[:, a * 128:(a + 1) * 128], rhs=CT[:, g, cs], start=True, stop=True),
                                  reads=["STb", "CT"], writes=["psA%d" % (a // 4)], token=(a % 4 == 3))
                    first = (b == 0 and c == 0)
                    last = (b == nsub - 1 and c == 1)
                    for g in range(4):
                        if mode == "light":
                            if c == 0:
                                bk = 2 + g // 2
                                pl.op("pe", lambda e, g=g, bk=bk, b=b, bt_=bt_, xw_=xw_: e.matmul(psA[:, bk, (g % 2) * 256:(g % 2 + 1) * 256], lhsT=bt_[:, g * 128:(g + 1) * 128], rhs=xw_[:, g * 256:(g + 1) * 256], start=(b == 0 and g % 2 == 0), stop=(b == nsub - 1), skip_group_check=True),
                                      reads=[btk, xwk], writes=["psA%d" % bk], token=(g % 2 == 1))
                        else:
                            pl.op("pe", lambda e, g=g, ps_=ps_: e.matmul(psA[:, 2 + g // 2, (g % 2) * 256:(g % 2 + 1) * 256], lhsT=btok[ps_, g * 128:(g + 1) * 128], rhs=xw[ps_, g * 256:(g + 1) * 256], start=True, stop=True),
                                  reads=["btok", "xw"], writes=["psA%d" % (2 + g // 2)], token=(g % 2 == 1))
                    if mode != "light" or last:
                        pl.op("dve", lambda e: e.tensor_tensor(out=ST[:], in0=ST[:], in1=psA[:, 2:4, :].rearrange("p a q -> p (a q)"), op=ALU.add), reads=["ST", "psA2", "psA3"], writes=["ST"])
                    if mode == "full":
                        pl.op("act", lambda e: e.copy(out=STb[:], in_=ST[:]), reads=["ST"], writes=["STb"])
                        if hist_from != "carry":
                            pl.dma("pool", lambda e, ch=ch: e.dma_start(out=so_d[1 + ch, :, :], in_=ST[:]), "s_ST", reads=["ST"], writes=["so%d" % (1 + ch)])
                if next_a is not None and b < len(next_a):
                    next_a[b]()
                if mode == "full":
                    pl.op("dve", lambda e: e.tensor_tensor(out=yo[:], in0=psA[:, 0:2, :].rearrange("p a q -> p (a q)"), in1=eac[:], op=ALU.mult), reads=["psA0", "psA1", "eac"], writes=["yo"])
                    pl.op("dve", lambda e: e.tensor_tensor(out=yo[:], in0=psB[:, :, :].rearrange("p a q -> p (a q)"), in1=yo[:], op=ALU.add), reads=["psB0", "psB1", "yo"], writes=["yo"])
                    v8 = lambda ap: ap.rearrange("p (a t) -> p a t", a=8)
                    pl.op("dve", lambda e: e.tensor_tensor(out=yo[:], in0=yo[:], in1=segs[:], op=ALU.add), reads=["yo", "segs2"], writes=["yo"])
                    pl.op("dve", lambda e, tsl=tsl: e.tensor_tensor(out=v8(yo[:]), in0=v8(yo[:]), in1=sz[:, :, tsl], op=ALU.mult), reads=["yo", "sz"], writes=["yo"])
                    pl.op("act", lambda e: e.activation(out=Mt[:], in_=yo[:], func=AF.Square), reads=["yo"], writes=["Mt"])
                    pl.op("pool", lambda e, tsl=tsl: e.tensor_tensor(out=ycat[:, 8:16, tsl], in0=v8(yo[:]), in1=pf[:, PF_NBW:PF_NBW + 8].unsqueeze(2).broadcast_to([128, 8, 128]), op=ALU.mult), reads=["yo", "pf"], writes=["ycat"])
                    for a in range(8):
                        pl.op("pe", lambda e, a=a, b=b: e.matmul(psC[:, 324 + b:325 + b], lhsT=Mt[:, a * 128:(a + 1) * 128], rhs=ones_bf[:, 0:1], start=(a == 0), stop=(a == 7)), reads=["Mt", "ones"], writes=["psCs"], token=(a == 7))

            if mode != "full":
                return
            pl.op("act", lambda e: e.activation(out=stat[:, 24:32], in_=psC[:, 320:328], func=AF.Ln, scale=1.0 / D, bias=EPS), reads=["psCa", "psCs"], writes=["stat"])
            pl.op("act", lambda e: e.activation(out=stat[:, 24:32], in_=stat[:, 24:32], func=AF.Exp, scale=-0.5), reads=["stat"], writes=["stat"])
            for s in range(nsub):
                tsl = slice(s * 128, (s + 1) * 128)
                for hf in range(2):
                    for part in range(2):
                        for kc in range(8):
                            pl.op("pe", lambda e, hf=hf, part=part, kc=kc, tsl=tsl: e.matmul(psA[:, part * 2 + hf, :], lhsT=ycat[:, part * 8 + kc, tsl], rhs=wout[:, part * 8 + kc, hf * 512:(hf + 1) * 512], start=(kc == 0), stop=(kc == 7)),
                                  reads=["ycat", "wout"], writes=["psA%d" % (part * 2 + hf)], token=(kc == 7))
                xb = xt[s % 2]
                xk = "xt%d" % (s % 2)
                pl.dma("sp", lambda e, xb=xb, s=s: e.dma_start(out=xb[:], in_=xs_d[row0 + s * 128:row0 + (s + 1) * 128, :]), "l_" + xk, writes=[xk])
                ob = ost[s % 2]
                ok = "ost%d" % (s % 2)
                pl.op("act", lambda e, s=s: e.activation(out=o1[:], in_=psA[:, 0:2, :].rearrange("p a q -> p (a q)"), func=AF.Copy, scale=stat[:, 24 + s:25 + s]), reads=["psA0", "psA1", "stat"], writes=["o1"])
                pl.op("dve", lambda e, s=s: e.scalar_tensor_tensor(out=o1[:], in0=psA[:, 2:4, :].rearrange("p a q -> p (a q)"), scalar=stat[:, 28 + s:29 + s], in1=o1[:], op0=ALU.mult, op1=ALU.add), reads=["psA2", "psA3", "stat", "o1"], writes=["o1"])
                pl.op("dve", lambda e: e.tensor_tensor(out=o1[:], in0=o1[:], in1=gate[:, gidx, :], op=ALU.mult), reads=["o1", "gate"], writes=["o1"])
                pl.op("dve", lambda e, xb=xb: e.tensor_tensor(out=o1[:], in0=o1[:], in1=xb[:], op=ALU.add), reads=["o1", xk], writes=["o1"])
                pl.op("dve", lambda e, s=s: e.memset(stat[:, 32 + s:33 + s], 0.0), writes=["stat"])
                pl.op("act", lambda e, s=s: e.activation(out=sqj[:], in_=o1[:], func=AF.Square, accum_out=stat[:, 32 + s:33 + s]), reads=["o1", "stat"], writes=["sqj", "stat"])
                pl.op("act", lambda e, s=s: e.activation(out=stat[:, 36 + s:37 + s], in_=stat[:, 32 + s:33 + s], func=AF.Ln, scale=1.0 / D, bias=EPS), reads=["stat"], writes=["stat"])
                pl.op("act", lambda e, s=s: e.activation(out=stat[:, 36 + s:37 + s], in_=stat[:, 36 + s:37 + s], func=AF.Exp, scale=-0.5), reads=["stat"], writes=["stat"])
                pl.op("dve", lambda e, s=s, ob=ob: e.scalar_tensor_tensor(out=ob[:], in0=o1[:], scalar=stat[:, 36 + s:37 + s], in1=bc[:, BC_NF:BC_NF + D], op0=ALU.mult, op1=ALU.mult), reads=["o1", "stat", "bc"], writes=[ok])
                pl.dma("pool", lambda e, ob=ob, s=s: e.dma_start(out=y_d[yrow0 + s * 128:yrow0 + (s + 1) * 128, :], in_=ob[:]), "s_" + ok, reads=[ok], writes=["y"])

        ones_bf = SB("ones_bf", [128, 8], BF16)
        pl.op("dve", lambda e: e.memset(ones_bf[:], 1.0), writes=["ones"])
        sca_sb = SB("sca_sb", [128, 8, 2, 2])
        scb_sb = SB("scb_sb", [128, 16, 2, 3])
        ld("l_sca", sca_sb[:].rearrange("p a b c -> p (a b c)"), sca_d[:, :], "sca")
        ld("l_scb", scb_sb[:].rearrange("p a b c -> p (a b c)"), scb_d[:, :], "scb")
        pl.op("dve", lambda e: e.memset(uhist[:], 0.0), writes=["uhist"])
        pl.op("dve", lambda e: e.memset(xhist[:], 0.0), writes=["xhist"])
        pl.op("dve", lambda e: e.memset(atot[:], 0.0), writes=["atot"])
        pslot = [(0, T, 0)]
        hsel = ((hT, "hT"), (ycat[:, 0:8, :], "ycat"))
        for n in range(NLSEG * NT):
            stm = [None] * (T // 64)
            if n == 0:
                stm[0] = ("zero", 0)
            nxa = None
            if n + 1 < NLSEG * NT:
                r1 = 128 + (n + 1) * T
                hb1, hk1 = hsel[(n + 1) % 2]
                nxa = [(lambda r1=r1, p0=p0, hb1=hb1, hk1=hk1: step_a_pair(r1, p0, T, pslot, hb1, hk1)) for p0 in (0, 2)]
            tile(128 + n * T, T, pslot, "light", 0, 0, "carry", stm, None, h_idx=n % 2, pre_a=(n > 0), next_a=nxa)
            if n % NT == NT - 1:
                m = n // NT
                pl.op("dve", lambda e, m=m: e.tensor_scalar_mul(out=ST[:], in0=ST[:], scalar1=msk[:, 1 + m:2 + m]), reads=["ST", "msk"], writes=["ST"])
                pl.op("dve", lambda e, m=m: e.tensor_scalar_mul(out=xhist[:].rearrange("p a b -> p (a b)"), in0=xhist[:].rearrange("p a b -> p (a b)"), scalar1=msk[:, 1 + m:2 + m]), reads=["xhist", "msk"], writes=["xhist"])
        tile(0, 128, [(0, 128, 0)], "halo", 0, 0, "carry", [None, None], None)
        for n in range(NT):
            stm = [None] * (T // 64)
            if n == 0:
                stm[0] = ("keep", 0)
            tile(128 + LROWS + n * T, T, pslot, "full", 0, n * T, "carry", stm, 0)
        pl.dma("pool", lambda e: e.dma_start(out=so_d[0, :, :], in_=ST[:]), "s_ST", reads=["ST"], writes=["so0"])
        tile(128 + LROWS + SEGLEN, 128, [(0, 64, 1), (64, 128, 2)], "full", 1, SEGLEN, "state", [("load", 0), ("load", 1)], 1)
        pl.dma("pool", lambda e: e.dma_start(out=ca_d[:, :], in_=casb[:].rearrange("p a b c -> p (a b c)")), "s_ca", reads=["casb"], writes=["ca"])
        pl.dma("pool", lambda e: e.dma_start(out=cb_d[:, :], in_=cbsb[:].rearrange("p a b c -> p (a b c)")), "s_cb", reads=["cbsb"], writes=["cb"])
        if debug:
            dbg_list = [("xbf", xbf[:].rearrange("p a b -> p (a b)"), 8 * T, BF16), ("BT", BT[:].rearrange("p a b -> p (a b)"), 4 * T, BF16),
                        ("CT", CT[:].rearrange("p a b -> p (a b)"), 4 * T, BF16), ("sm", sm[:].rearrange("p a b -> p (a b)"), 256, F32),
                        ("ycat", ycat[:].rearrange("p a b -> p (a b)"), 16 * T, BF16), ("yo", yo[:], 1024, F32), ("xw", xw[:], 1024, BF16),
                        ("xdt", xdt[:], 1024, BF16), ("btok", btok[:], 512, BF16), ("Mt", Mt[:], 1024, BF16), ("eac", eac[:], 1024, F32),
                        ("cd", cd[:].rearrange("p a b -> p (a b)"), 32, F32), ("hT", hT[:].rearrange("p a b -> p (a b)"), 8 * T, BF16),
                        ("sz", sz[:].rearrange("p a b -> p (a b)"), 8 * T, BF16), ("stat", stat[:], 64, F32), ("gam", gam[:].rearrange("p a b -> p (a b)"), 24, F32)]
            allk = list(pl.bufs.keys())
            for nm, ap, w, dt_ in dbg_list:
                dd = nc.dram_tensor("dbg_" + nm, [128, w], dt_, kind="ExternalOutput").ap()
                pl.dma("pool", lambda e, dd=dd, ap=ap: e.dma_start(out=dd[:, :], in_=ap), "s_dbg", reads=allk)
        pl.wait_tokens("pool", [(s, c) for s, c in pl.dma_cnt.items() if s.startswith("s_")])
        print("planned instructions:", pl.nins, {e: len(pl.lists[e]) for e in pl.ENGS})
        pl.emit()
    return nc


def _host_consts():
    c = np.zeros((128, NCONST), np.float32)
    k = np.arange(128)
    c[:, C_ID:C_ID + 128] = np.eye(128, dtype=np.float32)
    same = (k[:, None] // 64) == (k[None, :] // 64)
    c[:, C_BLK:C_BLK + 128] = same
    c[:, C_TRI:C_TRI + 128] = same & (k[:, None] <= k[None, :])
    c[:, C_T64:C_T64 + 64] = (k[:, None] % 64) <= np.arange(64)[None, :]
    c[:, C_SEL0:C_SEL0 + 128] = (k[:, None] < 64)
    c[:, C_SEL1:C_SEL1 + 128] = (k[:, None] >= 64)
    return c


def _fm(v, nchunk):
    return np.ascontiguousarray(np.asarray(v, np.float32).reshape(nchunk, 128).T)


_NC_CACHE = {}


def kernel(x_prompt, x_sample, state_conv_a, state_conv_b, state_ssm, c_prompt, c_sample,
           w_mod, b_mod, norm_in_w, w_in, conv_a_w, norm_a_w, conv_b_w, conv_b_b,
           dt_bias, a_log, d_skip, norm_b_w, w_out, norm_f_w, _two_phase=True, _debug=False):
    f = lambda a: np.ascontiguousarray(np.asarray(a, np.float32))
    x_prompt, x_sample = f(x_prompt), f(x_sample)
    state_conv_a, state_conv_b, state_ssm = f(state_conv_a), f(state_conv_b), f(state_ssm)
    c_prompt, c_sample = f(c_prompt), f(c_sample)
    w_mod, b_mod, w_in, w_out = f(w_mod)[0], f(b_mod)[0], f(w_in)[0], f(w_out)[0]
    pf = np.zeros((128, NPF), np.float32)
    pf[:, PF_NIN:PF_NIN + 8] = _fm(f(norm_in_w)[0], 8)
    caw = f(conv_a_w)[0]
    pf[:, PF_CAW:PF_CAW + 24] = np.stack([_fm(caw[k], 8) for k in range(3)], axis=2).reshape(128, 24)
    pf[:, PF_NAW:PF_NAW + 8] = _fm(f(norm_a_w)[0], 8)
    cbw = f(conv_b_w)[0]
    pf[:, PF_CBW:PF_CBW + 64] = np.stack([_fm(cbw[k], 16) for k in range(4)], axis=2).reshape(128, 64)
    pf[:, PF_CBB:PF_CBB + 16] = _fm(f(conv_b_b)[0], 16)
    pf[:, PF_NBW:PF_NBW + 8] = _fm(f(norm_b_w)[0], 8)
    pf[:, PF_DSK:PF_DSK + 8] = _fm(np.repeat(f(d_skip)[0], 64), 8)
    pf[:, PF_BSH:PF_BSH + 8] = _fm(b_mod[0:D], 8)
    pf[:, PF_BSC:PF_BSC + 8] = _fm(b_mod[D:2 * D], 8)
    bcv = np.zeros((128, NBC), np.float32)
    bcv[:, BC_NF:BC_NF + D] = f(norm_f_w)[None, :]
    bcv[:, BC_BG:BC_BG + D] = b_mod[None, 2 * D:3 * D]
    bcv[:, BC_DTB:BC_DTB + 16] = f(dt_bias)[0][None, :]
    bcv[:, BC_ALOG:BC_ALOG + 16] = f(a_log)[0][None, :]
    cst = _host_consts()

    in_maps = []
    for k in range(NCORES):
        seq, seg = k // 4, k % 4
        start = seg * SEGLEN
        xs = np.zeros((XROWS, D), np.float32)
        if seg > 0:
            xs[0:128] = x_prompt[seq, start - 128:start]
        if seg > 0 and LROWS >= start:
            xs[128 + LROWS - start:128 + LROWS] = x_prompt[seq, 0:start]
        xs[128 + LROWS:128 + LROWS + SEGLEN] = x_prompt[seq, start:start + SEGLEN]
        xs[128 + LROWS + SEGLEN:] = x_sample[2 * k:2 * k + 2].reshape(128, D)
        cs = [c_prompt[seq], c_sample[2 * k], c_sample[2 * k + 1]]
        cT = np.stack([_fm(c, 8) for c in cs], axis=2).reshape(128, 24)
        cbc = np.zeros((2, 128, 8, 128), np.float32)
        cbc[0] = _fm(cs[0], 8)[:, :, None]
        cbc[1, :, :, 0:64] = _fm(cs[1], 8)[:, :, None]
        cbc[1, :, :, 64:128] = _fm(cs[2], 8)[:, :, None]
        sca = state_conv_a[0, 2 * k:2 * k + 2]
        sca = sca.reshape(2, 2, 8, 128).transpose(3, 2, 0, 1).reshape(128, 32)
        scb = state_conv_b[0, 2 * k:2 * k + 2]
        scb = scb.reshape(2, 3, 16, 128).transpose(3, 2, 0, 1).reshape(128, 96)
        sst = state_ssm[0, 2 * k:2 * k + 2]
        sst = sst.reshape(2, 1024, 128).transpose(0, 2, 1)
        msk = np.zeros((128, 16), np.float32)
        msk[:, 0] = 1.0 if seg > 0 else 0.0
        for m in range(NLSEG):
            msk[:, 1 + m] = 0.0 if m < NLSEG - seg else 1.0
        in_maps.append({
            "xs": xs, "w_mod": w_mod, "w_in": w_in, "w_out": w_out, "pf": pf, "bc": bcv, "cst": cst,
            "cT": np.ascontiguousarray(cT), "cbc": np.ascontiguousarray(cbc.reshape(2, 128, 1024)),
            "sca": np.ascontiguousarray(sca), "scb": np.ascontiguousarray(scb),
            "sst": np.ascontiguousarray(sst), "msk": msk,
        })
    key = (bool(_two_phase), bool(_debug))
    if key not in _NC_CACHE:
        _NC_CACHE[key] = build_nc(two_phase=key[0], debug=key[1])
    nc = _NC_CACHE[key]
    res = run_bass_kernel_spmd(nc, in_maps, core_ids=list(range(NCORES)))
    R = res.results
    if _debug:
        kernel.last_results = R
    y_prompt = np.zeros((2, SEQ, D), np.float32)
    y_sample = np.zeros((16, 64, D), np.float32)
    ca_p = np.zeros((1, 2, 2, D), np.float32)
    cb_p = np.zeros((1, 2, 3, 2 * D), np.float32)
    ss_p = np.zeros((1, 2, 16, 64, 128), np.float32)
    ca_s = np.zeros((1, 16, 2, D), np.float32)
    cb_s = np.zeros((1, 16, 3, 2 * D), np.float32)
    ss_s = np.zeros((1, 16, 16, 64, 128), np.float32)
    for k in range(NCORES):
        seq, seg = k // 4, k % 4
        r = R[k]
        y_prompt[seq, seg * SEGLEN:(seg + 1) * SEGLEN] = r["y"][0:SEGLEN]
        y_sample[2 * k:2 * k + 2] = r["y"][SEGLEN:].reshape(2, 64, D)
        ca = r["ca"].reshape(128, 8, 3, 2)
        cb = r["cb"].reshape(128, 16, 3, 3)
        so = r["so"]
        for s in range(2):
            ca_s[0, 2 * k + s] = ca[:, :, 1 + s, :].transpose(2, 1, 0).reshape(2, D)
            cb_s[0, 2 * k + s] = cb[:, :, 1 + s, :].transpose(2, 1, 0).reshape(3, 2 * D)
            ss_s[0, 2 * k + s] = so[1 + s].T.reshape(16, 64, 128)
        if seg == 3:
            ca_p[0, seq] = ca[:, :, 0, :].transpose(2, 1, 0).reshape(2, D)
            cb_p[0, seq] = cb[:, :, 0, :].transpose(2, 1, 0).reshape(3, 2 * D)
            ss_p[0, seq] = so[0].T.reshape(16, 64, 128)
    return (y_prompt, y_sample, ca_p, cb_p, ss_p, ca_s, cb_s, ss_s)
```

```python
import contextlib
import numpy as np
import concourse.bass as bass
import concourse.mybir as mybir
from concourse.bass_utils import run_bass_kernel_spmd

F32 = mybir.dt.float32
BF16 = mybir.dt.bfloat16
ALU = mybir.AluOpType
AF = mybir.ActivationFunctionType

NCORES = 8
D = 1024
SEQ = 16384
SEGLEN = 4096
T = 512
NT = SEGLEN // T
DIN = 7184
PW = 128
NPIECE = 56
NWB = 12
EPS = 1e-5
NLSEG = 3
LROWS = NLSEG * SEGLEN
XROWS = 128 + LROWS + SEGLEN + 128
YROWS = SEGLEN + 128

PF_NIN = 0
PF_CAW = 8
PF_NAW = 32
PF_CBW = 40
PF_CBB = 104
PF_NBW = 120
PF_DSK = 128
PF_BSH = 136
PF_BSC = 144
NPF = 152
BC_NF = 0
BC_BG = 1024
BC_DTB = 2048
BC_ALOG = 2064
NBC = 2080
C_ID = 0
C_BLK = 128
C_TRI = 256
C_T64 = 384
C_SEL0 = 448
C_SEL1 = 576
NCONST = 704


class Planner:
    ENGS = ("pe", "act", "dve", "pool", "sp")
    SEM_LIMIT = 30000

    def __init__(self, nc):
        self.nc = nc
        self.lists = {e: [] for e in self.ENGS}
        self.cur = {e: [e + "_0", 0] for e in self.ENGS}
        self.gen = {e: 0 for e in self.ENGS}
        self.sem_names = [e + "_0" for e in self.ENGS]
        self.dma_cnt = {}
        self.waited = {e: {} for e in self.ENGS}
        self.bufs = {}
        self.nins = 0
        self.alias = {}
        self.bank_last = {}

    @staticmethod
    def _bank(k):
        if k.startswith("psC"):
            return "psC"
        if k.startswith("psA") or k.startswith("psB") or k == "psT":
            return k
        return None

    def _bank_deps(self, eng, keys, need):
        banks = set(b for b in (self._bank(k) for k in keys) if b)
        for b in banks:
            for oe, tok in self.bank_last.get(b, {}).items():
                if oe != eng:
                    self._need(eng, tok, need)
        return banks

    def _exp(self, keys):
        out = []
        for k in keys:
            out.extend(self.alias.get(k, (k,)))
        return out

    def _need(self, eng, tok, out):
        if tok is None:
            return
        s, v = tok
        if eng == "pe" and s.startswith("pe_"):
            return
        if self.waited[eng].get(s, 0) >= v:
            return
        if out.get(s, 0) < v:
            out[s] = v

    def _deps(self, eng, reads, writes):
        reads, writes = self._exp(reads), self._exp(writes)
        need = {}
        for k in reads:
            b = self.bufs.get(k)
            if b:
                self._need(eng, b[0], need)
        for k in writes:
            b = self.bufs.get(k)
            if b:
                self._need(eng, b[0], need)
                for t in b[1]:
                    self._need(eng, t, need)
        self._cur_banks = self._bank_deps(eng, list(reads) + list(writes), need)
        self._cur_eng = eng
        for s, v in need.items():
            self.waited[eng][s] = v
            self.lists[eng].append(("wait", s, v))

    def _mark(self, tok, reads, writes):
        reads, writes = self._exp(reads), self._exp(writes)
        for b in self._cur_banks:
            self.bank_last.setdefault(b, {})[self._cur_eng] = tok
        for k in reads:
            b = self.bufs.setdefault(k, [None, []])
            b[1].append(tok)
        for k in writes:
            self.bufs[k] = [tok, []]

    def op(self, eng, fn, reads=(), writes=(), token=True):
        self._deps(eng, reads, writes)
        c = self.cur[eng]
        self.nins += 1
        if token:
            if c[1] >= self.SEM_LIMIT:
                self.gen[eng] += 1
                c[0] = "%s_%d" % (eng, self.gen[eng])
                c[1] = 0
                self.sem_names.append(c[0])
            c[1] += 1
            tok = (c[0], c[1])
            self.lists[eng].append(("ins", fn, c[0], 1))
        else:
            tok = (c[0], c[1] + 1)
            self.lists[eng].append(("ins", fn, None, 0))
        self._mark(tok, reads, writes)
        return tok

    def dma(self, eng, fn, sem, reads=(), writes=()):
        self._deps(eng, reads, writes)
        self.nins += 1
        if sem not in self.dma_cnt:
            self.dma_cnt[sem] = 0
            self.sem_names.append(sem)
        self.dma_cnt[sem] += 16
        tok = (sem, self.dma_cnt[sem])
        self.lists[eng].append(("ins", fn, sem, 16))
        self._mark(tok, reads, writes)
        return tok

    def raw(self, eng, fn, sem, inc, reads=(), writes=()):
        self._deps(eng, reads, writes)
        if sem not in self.dma_cnt:
            self.dma_cnt[sem] = 0
            self.sem_names.append(sem)
        self.dma_cnt[sem] += inc
        tok = (sem, self.dma_cnt[sem])
        self.lists[eng].append(("ins", fn, sem, -inc))
        self._mark(tok, reads, writes)
        return tok

    def wait_tokens(self, eng, toks):
        need = {}
        for t in toks:
            self._need(eng, t, need)
        for s, v in need.items():
            self.waited[eng][s] = v
            self.lists[eng].append(("wait", s, v))

    def emit(self):
        nc = self.nc
        with contextlib.ExitStack() as st:
            sems = {}
            for n in self.sem_names:
                sems[n] = st.enter_context(nc.semaphore(n))
            block = st.enter_context(nc.Block())
            engmap = {"pe": block.tensor, "act": block.scalar, "dve": block.vector,
                      "pool": block.gpsimd, "sp": block.sync}
            for e in self.ENGS:
                lst = self.lists[e]
                if not lst:
                    continue

                def body(engobj, lst=lst):
                    for it in lst:
                        if it[0] == "wait":
                            engobj.wait_ge(sems[it[1]], it[2])
                        else:
                            ins = it[1](engobj)
                            if it[2] is not None:
                                if it[3] < 0:
                                    ins.then_inc(sems[it[2]])
                                else:
                                    ins.then_inc(sems[it[2]], it[3])
                engmap[e](body)


def build_nc(two_phase=True, debug=False):
    nc = bass.Bass("TRN2", target_bir_lowering=False)
    dr = lambda n, s, k, d=F32: nc.dram_tensor(n, list(s), d, kind=k)
    xs_d = dr("xs", [XROWS, D], "ExternalInput").ap()
    wmod_d = dr("w_mod", [D, 3 * D], "ExternalInput").ap()
    win_d = dr("w_in", [D, DIN], "ExternalInput").ap()
    wout_d = dr("w_out", [2 * D, D], "ExternalInput").ap()
    pf_d = dr("pf", [128, NPF], "ExternalInput").ap()
    bc_d = dr("bc", [128, NBC], "ExternalInput").ap()
    cst_d = dr("cst", [128, NCONST], "ExternalInput").ap()
    cT_d = dr("cT", [128, 8 * 3], "ExternalInput").ap()
    cbc_d = dr("cbc", [2, 128, 8 * 128], "ExternalInput").ap()
    sca_d = dr("sca", [128, 8 * 2 * 2], "ExternalInput").ap()
    scb_d = dr("scb", [128, 16 * 2 * 3], "ExternalInput").ap()
    sst_d = dr("sst", [2, 128, D], "ExternalInput").ap()
    msk_d = dr("msk", [128, 16], "ExternalInput").ap()
    y_d = dr("y", [YROWS, D], "ExternalOutput").ap()
    ca_d = dr("ca", [128, 8 * 3 * 2], "ExternalOutput").ap()
    cb_d = dr("cb", [128, 16 * 3 * 3], "ExternalOutput").ap()
    so_d = dr("so", [3, 128, D], "ExternalOutput").ap()
    winbf_d = nc.dram_tensor("winbf", [NPIECE, 128, 8 * PW], BF16)

    pl = Planner(nc)
    pl.alias = {"stg0": ("rhs1", "segs", "eac", "yo"), "stg1": ("stsb", "o1", "ost0", "ost1"), "sqj": ("Mt",)}
    with contextlib.ExitStack() as st:
        def SB(name, shape, dt=F32):
            return st.enter_context(nc.sbuf_tensor("sb_" + name, list(shape), dt))

        cst = SB("cst", [128, NCONST])
        idb = SB("idb", [128, 128], BF16)
        pf = SB("pf", [128, NPF])
        bc = SB("bc", [128, NBC])
        msk = SB("msk", [128, 16])
        cT = SB("cT", [128, 8, 3])
        gam = SB("gam", [128, 3, 8])
        bet = SB("bet", [128, 3, 8])
        gate = SB("gate", [128, 2, D])
        caw = SB("caw", [128, 8, 3])
        cbw = SB("cbw", [128, 16, 4])
        cbb = SB("cbb", [128, 16])
        a_bc = SB("a_bc", [128, 16])
        wout = SB("wout", [128, 16, D], BF16)
        wdt = SB("wdt", [128, 8, 16], BF16)
        wbuf = [SB("wbuf%d" % i, [128, 8, PW], BF16) for i in range(NWB)]
        big = [SB("big%d" % i, [128, 4096]) for i in range(2)]
        stg = [b[:].rearrange("p (a c) -> p a c", a=8) for b in big]
        xt = [SB("xt%d" % i, [128, D]) for i in range(2)]
        hT = SB("hT", [128, 8, T], BF16)
        ubuf = [SB("ubuf%d" % i, [128, 3 + T]) for i in range(2)]
        hsb = [SB("hsb%d" % i, [128, T]) for i in range(2)]
        cu = [SB("cu%d" % i, [128, T]) for i in range(2)]
        tq = [SB("tq%d" % i, [128, T]) for i in range(2)]
        sq2 = [SB("sq2%d" % i, [128, T], BF16) for i in range(2)]
        uhist = SB("uhist", [128, 8, 3])
        xhist = SB("xhist", [128, 16, 3])
        uhist0 = SB("uhist0", [128, 8, 3])
        xhist0 = SB("xhist0", [128, 16, 3])
        ycat = SB("ycat", [128, 16, T], BF16)
        xbf = SB("xbf", [128, 8, T], BF16)
        BT = SB("BT", [128, 4, T], BF16)
        CT = SB("CT", [128, 4, T], BF16)
        sz = SB("sz", [128, 8, T], BF16)
        xdt = SB("xdt", [128, D], BF16)
        xw = SB("xw", [128, D], BF16)
        btok = SB("btok", [128, 512], BF16)
        rhs1 = big[0][:, 0:1024]
        segs = big[0][:, 1024:2048]
        Mt = SB("Mt", [128, D], BF16)
        sqj = Mt
        eac = big[0][:, 2048:3072]
        yo = big[0][:, 3072:4096]
        cbtm = SB("cbtm", [128, 4, 64])
        sm = SB("sm", [128, 8, 64])
        ST = SB("ST", [128, D])
        STb = SB("STb", [128, D], BF16)
        stsb = big[1][:, 0:1024]
        cd = SB("cd", [128, 2, 64])
        stat = SB("stat", [128, 64])
        ost = [big[1][:, 2048:3072], big[1][:, 3072:4096]]
        o1 = big[1][:, 1024:2048]
        casb = SB("casb", [128, 8, 3, 2])
        cbsb = SB("cbsb", [128, 16, 3, 3])
        atot = SB("atot", [128, 16])
        gsel = SB("gsel", [128, 8, 16])
        print("sbuf remaining after alloc:", nc.sbuf_bytes_remaining)

        psA = st.enter_context(nc.psum_tensor("psA", [128, 4, 512], F32))
        psB = st.enter_context(nc.psum_tensor("psB", [128, 2, 512], F32))
        psC = st.enter_context(nc.psum_tensor("psC", [128, 512], F32))
        psT = st.enter_context(nc.psum_tensor("psT", [128, 1024], BF16))

        ident = cst[:, C_ID:C_ID + 128]
        blk = cst[:, C_BLK:C_BLK + 128]
        tri = cst[:, C_TRI:C_TRI + 128]
        t64 = cst[:, C_T64:C_T64 + 64]
        chsel = [cst[:, C_SEL0:C_SEL0 + 128], cst[:, C_SEL1:C_SEL1 + 128]]

        def pfc(off, j):
            return pf[:, off + j:off + j + 1]

        ld = lambda name, dst, src, key: pl.dma("sp", lambda e: e.dma_start(out=dst, in_=src), name, writes=[key])
        ld("l_cst", cst[:], cst_d[:, :], "cst")
        ld("l_pf", pf[:], pf_d[:, :], "pf")
        ld("l_bc", bc[:], bc_d[:, :], "bc")
        ld("l_msk", msk[:], msk_d[:, :], "msk")
        ld("l_cT", cT[:].rearrange("p a b -> p (a b)"), cT_d[:, :], "cT")
        pl.op("dve", lambda e: e.tensor_copy(out=idb[:], in_=ident), reads=["cst"], writes=["idb"])
        pl.op("dve", lambda e: e.tensor_scalar_mul(out=caw[:].rearrange("p a b -> p (a b)"), in0=pf[:, PF_CAW:PF_CAW + 24], scalar1=1.0), reads=["pf"], writes=["caw"])
        pl.op("dve", lambda e: e.tensor_scalar_mul(out=cbw[:].rearrange("p a b -> p (a b)"), in0=pf[:, PF_CBW:PF_CBW + 64], scalar1=1.0), reads=["pf"], writes=["cbw"])
        pl.op("dve", lambda e: e.tensor_scalar_mul(out=cbb[:], in0=pf[:, PF_CBB:PF_CBB + 16], scalar1=1.0), reads=["pf"], writes=["cbb"])
        pl.op("act", lambda e: e.activation(out=a_bc[:], in_=bc[:, BC_ALOG:BC_ALOG + 16], func=AF.Exp), reads=["bc"], writes=["a_bc"])
        pl.op("dve", lambda e: e.tensor_scalar_mul(out=a_bc[:], in0=a_bc[:], scalar1=-1.0), reads=["a_bc"], writes=["a_bc"])

        wmod_v = wmod_d.rearrange("(kc p) c -> p kc c", p=128)
        for piece in range(6):
            s = stg[piece % 2]
            key = "stg%d" % (piece % 2)
            pl.dma("sp", lambda e, s=s, piece=piece: e.dma_start(out=s[:], in_=wmod_v[:, :, piece * 512:(piece + 1) * 512]), "l_" + key, writes=[key])
            if piece < 4:
                for cc in range(4):
                    j = (piece % 2) * 4 + cc
                    for kc in range(8):
                        pl.op("pe", lambda e, s=s, cc=cc, kc=kc, j=j: e.matmul(psC[:, j * 4:j * 4 + 3], lhsT=s[:, kc, cc * 128:(cc + 1) * 128], rhs=cT[:, kc, :], start=(kc == 0), stop=(kc == 7)),
                              reads=[key, "cT"], writes=["psC"], token=(kc == 7))
                if piece % 2 == 1:
                    src = psC[:, 0:32].rearrange("p (j s) -> p s j", s=4)[:, 0:3, :]
                    if piece == 1:
                        pl.op("dve", lambda e, src=src: e.tensor_tensor(out=bet[:], in0=src, in1=pf[:, PF_BSH:PF_BSH + 8].unsqueeze(1).broadcast_to([128, 3, 8]), op=ALU.add), reads=["psC", "pf"], writes=["bet"])
                    else:
                        pl.op("dve", lambda e, src=src: e.tensor_tensor(out=gam[:], in0=src, in1=pf[:, PF_BSC:PF_BSC + 8].unsqueeze(1).broadcast_to([128, 3, 8]), op=ALU.add), reads=["psC", "pf"], writes=["gam"])
                        pl.op("dve", lambda e: e.scalar_tensor_tensor(out=gam[:], in0=gam[:], scalar=1.0, in1=pf[:, PF_NIN:PF_NIN + 8].unsqueeze(1).broadcast_to([128, 3, 8]), op0=ALU.add, op1=ALU.mult), reads=["gam", "pf"], writes=["gam"])
            else:
                half = piece - 4
                for which in range(2):
                    cb_t = xt[which]
                    if half == 0:
                        pl.dma("sp", lambda e, cb_t=cb_t, which=which: e.dma_start(out=cb_t[:], in_=cbc_d[which, :, :]), "l_xt%d" % which, writes=["xt%d" % which])
                    cbv = cb_t[:].rearrange("p (k m) -> p k m", k=8)
                    for kc in range(8):
                        pl.op("pe", lambda e, s=s, kc=kc, cbv=cbv, which=which: e.matmul(psA[:, which, :], lhsT=cbv[:, kc, :], rhs=s[:, kc, :], start=(kc == 0), stop=(kc == 7)),
                              reads=[key, "xt%d" % which], writes=["psA%d" % which], token=(kc == 7))
                    pl.op("dve", lambda e, which=which, half=half: e.tensor_tensor(out=gate[:, which, half * 512:(half + 1) * 512], in0=psA[:, which, :], in1=bc[:, BC_BG + half * 512:BC_BG + (half + 1) * 512], op=ALU.add),
                          reads=["psA%d" % which, "bc"], writes=["gate"])

        win_v = win_d.rearrange("(kc p) c -> p kc c", p=128)
        wout_v = wout_d.rearrange("(kc p) c -> p kc c", p=128)
        castengs = ["dve", "act", "pool"]
        ci = 0

        def cast(dst, src, rk, wk):
            nonlocal ci
            eng = castengs[ci % 3]
            ci += 1
            if eng == "act":
                pl.op("act", lambda e: e.copy(out=dst, in_=src), reads=rk, writes=wk)
            else:
                pl.op(eng, lambda e: e.tensor_copy(out=dst, in_=src), reads=rk, writes=wk)

        porder = [10, 11, 12, 2, 3, 4, 5, 0, 1, 6, 7, 13, 8, 9]
        pl.dma("sp", lambda e: e.dma_start(out=stg[0][:, :, 0:16], in_=win_v[:, :, 7168:7184]), "l_stg0", writes=["stg0"])
        pl.op("dve", lambda e: e.tensor_copy(out=wdt[:], in_=stg[0][:, :, 0:16]), reads=["stg0"], writes=["wdt"])
        wci = 0
        for n, piece in enumerate(porder):
            s = stg[n % 2]
            key = "stg%d" % (n % 2)
            pl.dma("sp", lambda e, s=s, piece=piece: e.dma_start(out=s[:], in_=win_v[:, :, piece * 512:(piece + 1) * 512]), "l_" + key, writes=[key])
            for hp in range(512 // PW):
                wb = wbuf[wci % NWB]
                wkey = "wbuf%d" % (wci % NWB)
                wci += 1
                for kh in range(2):
                    cast(wb[:, kh * 4:(kh + 1) * 4, :], s[:, kh * 4:(kh + 1) * 4, hp * PW:(hp + 1) * PW], [key], [wkey])
                sp_ = (512 // PW) * piece + hp
                pl.dma("pool", lambda e, wb=wb, sp_=sp_: e.dma_start(out=winbf_d[sp_, :, :], in_=wb[:].rearrange("p a b -> p (a b)")), "s_" + wkey, reads=[wkey], writes=["winbf%d" % sp_])
        for n in range(4):
            kg, ch = n // 2, n % 2
            s = stg[n % 2]
            key = "stg%d" % (n % 2)
            pl.dma("sp", lambda e, s=s, kg=kg, ch=ch: e.dma_start(out=s[:], in_=wout_v[:, kg * 8:(kg + 1) * 8, ch * 512:(ch + 1) * 512]), "l_" + key, writes=[key])
            for kh in range(2):
                cast(wout[:, kg * 8 + kh * 4:kg * 8 + (kh + 1) * 4, ch * 512:(ch + 1) * 512], s[:, kh * 4:(kh + 1) * 4, :], [key], ["wout"])

        wcnt = [0]

        def load_piece(piece):
            i = wcnt[0] % NWB
            wcnt[0] += 1
            wb = wbuf[i]
            pl.dma("sp", lambda e: e.dma_start(out=wb[:].rearrange("p a b -> p (a b)"), in_=winbf_d[piece, :, :]), "l_wbuf%d" % i, reads=["winbf%d" % piece], writes=["wbuf%d" % i])
            return wb, "wbuf%d" % i

        hcur = [hT, "hT"]

        def proj_chunk(wb, wkey, cc, ps_ap, pskey, Tn):
            hb, hk = hcur
            for kc in range(8):
                pl.op("pe", lambda e, kc=kc: e.matmul(ps_ap, lhsT=wb[:, kc, cc * 128:(cc + 1) * 128], rhs=hb[:, kc, 0:Tn], start=(kc == 0), stop=(kc == 7)),
                      reads=[wkey, hk], writes=[pskey], token=(kc == 7))

        pref = {}

        def load_x(row0, s):
            xb = xt[s % 2]
            xk = "xt%d" % (s % 2)
            pl.dma("sp", lambda e: e.dma_start(out=xb[:], in_=xs_d[row0 + s * 128:row0 + (s + 1) * 128, :]), "l_" + xk, writes=[xk])

        def step_a(row0, Tn, slots, hb=None, hk="hT"):
            for p0 in range(0, Tn // 128, 2):
                step_a_pair(row0, p0, Tn, slots, hT if hb is None else hb, hk)

        def step_a_pair(row0, p0, Tn, slots, hb, hk):
            nsub = Tn // 128
            if True:
                subs = list(range(p0, min(p0 + 2, nsub)))
                for s in subs:
                    xb = xt[s % 2]
                    xk = "xt%d" % (s % 2)
                    if not pref.pop((row0, s), False):
                        load_x(row0, s)
                    pl.op("pool", lambda e, s=s: e.memset(stat[:, s:s + 1], 0.0), writes=["stat"])
                    pl.op("act", lambda e, xb=xb, s=s: e.activation(out=sqj[:], in_=xb[:], func=AF.Square, accum_out=stat[:, s:s + 1]), reads=[xk, "stat"], writes=["sqj", "stat"])
                    pl.op("act", lambda e, s=s: e.activation(out=stat[:, 8 + s:9 + s], in_=stat[:, s:s + 1], func=AF.Ln, scale=1.0 / D, bias=EPS), reads=["stat"], writes=["stat"])
                    pl.op("act", lambda e, s=s: e.activation(out=stat[:, 16 + s:17 + s], in_=stat[:, 8 + s:9 + s], func=AF.Exp, scale=-0.5), reads=["stat"], writes=["stat"])
                    pl.op("dve", lambda e, xb=xb, s=s: e.tensor_scalar_mul(out=xb[:], in0=xb[:], scalar1=stat[:, 16 + s:17 + s]), reads=[xk, "stat"], writes=[xk])
                w0, w1 = subs[0] * 128, (subs[-1] + 1) * 128
                for kc in range(8):
                    pk = "psB%d" % (kc % 2)
                    for s in subs:
                        xb = xt[s % 2]
                        xk = "xt%d" % (s % 2)
                        pl.op("pe", lambda e, xb=xb, kc=kc, s=s, p0=p0: e.transpose(out=psB[:, kc % 2, (s - p0) * 128:(s - p0 + 1) * 128], in_=xb[:, kc * 128:(kc + 1) * 128], identity=ident), reads=[xk, "cst"], writes=[pk], token=(s == subs[-1]))
                    for (c0, c1, slot) in slots:
                        lo, hi = max(c0, w0), min(c1, w1)
                        if lo >= hi:
                            continue
                        pl.op("act", lambda e, kc=kc, lo=lo, hi=hi, slot=slot, w0=w0: e.activation(out=hb[:, kc, lo:hi], in_=psB[:, kc % 2, lo - w0:hi - w0], func=AF.Identity, scale=gam[:, slot, kc:kc + 1], bias=bet[:, slot, kc:kc + 1]),
                              reads=[pk, "gam", "bet"], writes=[hk])

        def conv_chunk(eng_first, src, wts, nk, dst, Tn, rk, wk, bias=None, bk=()):
            off = 3 - (nk - 1)
            if bias is None:
                pl.op("act", lambda e: e.activation(out=dst[:, 0:Tn], in_=src[:, off:off + Tn], func=AF.Copy, scale=wts[:, 0:1]), reads=rk, writes=wk)
            else:
                pl.op("act", lambda e: e.activation(out=dst[:, 0:Tn], in_=src[:, off:off + Tn], func=AF.Identity, scale=wts[:, 0:1], bias=bias), reads=rk + list(bk), writes=wk)
            for k in range(1, nk):
                pl.op("dve", lambda e, k=k: e.scalar_tensor_tensor(out=dst[:, 0:Tn], in0=src[:, off + k:off + k + Tn], scalar=wts[:, k:k + 1], in1=dst[:, 0:Tn], op0=ALU.mult, op1=ALU.add), reads=rk + wk, writes=wk)

        def tile(row0, Tn, slots, mode, gidx, yrow0, hist_from, st_mode, out_slot, next_row0=None, h_idx=0, pre_a=False, next_a=None):
            nsub = Tn // 128
            nch = Tn // 64
            hcur[0], hcur[1] = ((hT, "hT"), (ycat[:, 0:8, :], "ycat"))[h_idx]
            if not pre_a:
                step_a(row0, Tn, slots, hcur[0], hcur[1])
            light = (mode != "full")
            if mode in ("full", "halo"):
                wbs = {}
                pendA = None
                for j in range(8):
                    i = j % 2
                    need = [8 + j, 16 + j] if mode == "halo" else [8 + j, 16 + j, 0 + j, 24 + j]
                    for pc in need:
                        wbs[pc] = load_piece(pc)
                    cc = 0
                    if j % 2 == 0:
                        (pc_, pck), (ph_, phk), (pb_, pbk) = (psA[:, 0, 0:Tn], "psA0"), (psA[:, 1, 0:Tn], "psA1"), (psA[:, 2, 0:Tn], "psA2")
                    else:
                        (pc_, pck), (ph_, phk), (pb_, pbk) = (psA[:, 0, 0:Tn], "psA0"), (psA[:, 1, 0:Tn], "psA1"), (psA[:, 2, 0:Tn], "psA2")
                    wb, wk_ = wbs[8 + j]
                    proj_chunk(wb, wk_, cc, pc_, pck, Tn)
                    wb, wk_ = wbs[16 + j]
                    proj_chunk(wb, wk_, cc, ph_, phk, Tn)
                    if mode != "halo":
                        wb, wk_ = wbs[0 + j]
                        proj_chunk(wb, wk_, cc, pb_, pbk, Tn)
                        wb, wk_ = wbs[24 + j]
                        proj_chunk(wb, wk_, cc, psA[:, 3, 0:Tn], "psA3", Tn)
                        wbz, wkz = load_piece(32 + j)
                        proj_chunk(wbz, wkz, 0, psB[:, j % 2, 0:Tn], "psB%d" % (j % 2), Tn)
                        if pendA is not None:
                            pendA()
                            pendA = None
                    ub, uk = ubuf[i], "ubuf%d" % i
                    pl.op("act", lambda e, i=i, ph_=ph_: e.copy(out=hsb[i][:, 0:Tn], in_=ph_), reads=[phk], writes=["hsb%d" % i])
                    pl.op("pool", lambda e, ub=ub, j=j: e.tensor_copy(out=ub[:, 0:3], in_=uhist[:, j, :]), reads=["uhist"], writes=[uk])
                    pl.op("dve", lambda e, ub=ub, i=i, pc_=pc_: e.tensor_tensor(out=ub[:, 3:3 + Tn], in0=pc_, in1=hsb[i][:, 0:Tn], op=ALU.mult), reads=[pck, "hsb%d" % i], writes=[uk])
                    pl.op("pool", lambda e, ub=ub, j=j: e.tensor_copy(out=uhist[:, j, :], in_=ub[:, Tn:Tn + 3]), reads=[uk], writes=["uhist"])
                    if mode == "halo":
                        continue
                    pl.op("act", lambda e, i=i: e.activation(out=tq[i][:, 0:Tn], in_=psA[:, 3, 0:Tn], func=AF.Silu), reads=["psA3"], writes=["tq%d" % i])
                    pl.op("dve", lambda e, i=i, pb_=pb_: e.tensor_tensor(out=tq[i][:, 0:Tn], in0=pb_, in1=tq[i][:, 0:Tn], op=ALU.mult), reads=[pbk, "tq%d" % i], writes=["tq%d" % i])
                    c_, ck = cu[i], "cu%d" % i
                    if hist_from == "carry":
                        conv_chunk("dve", ub, caw[:, j, :], 3, c_, Tn, [uk, "caw"], [ck])
                    else:
                        for sidx in range(2):
                            pl.op("dve", lambda e, ub=ub, j=j, sidx=sidx: e.tensor_copy(out=stsb[:, sidx * 128 + 1:sidx * 128 + 3], in_=sca_sb[:, j, sidx, :]), reads=["sca"], writes=["stsb"])
                        for sidx in range(2):
                            base = sidx * 128
                            pl.op("dve", lambda e, ub=ub, base=base, sidx=sidx: e.tensor_copy(out=stsb[:, base + 3:base + 67], in_=ub[:, 3 + sidx * 64:3 + (sidx + 1) * 64]), reads=[uk], writes=["stsb"])
                            off = 1
                            pl.op("dve", lambda e, c_=c_, base=base, sidx=sidx, j=j: e.tensor_scalar_mul(out=c_[:, sidx * 64:(sidx + 1) * 64], in0=stsb[:, base + 1:base + 65], scalar1=caw[:, j, 0:1]), reads=["stsb", "caw"], writes=[ck])
                            for k in (1, 2):
                                pl.op("dve", lambda e, c_=c_, base=base, sidx=sidx, j=j, k=k: e.scalar_tensor_tensor(out=c_[:, sidx * 64:(sidx + 1) * 64], in0=stsb[:, base + 1 + k:base + 65 + k], scalar=caw[:, j, k:k + 1], in1=c_[:, sidx * 64:(sidx + 1) * 64], op0=ALU.mult, op1=ALU.add), reads=["stsb", "caw", ck], writes=[ck])
                            pl.op("dve", lambda e, base=base, sidx=sidx, j=j: e.tensor_copy(out=casb[:, j, 1 + sidx, :], in_=stsb[:, base + 65:base + 67]), reads=["stsb"], writes=["casb"])
                    def stage_b(c_=c_, ck=ck, i=i, j=j):
                        pl.op("dve", lambda e: e.tensor_tensor(out=c_[:, 0:Tn], in0=c_[:, 0:Tn], in1=tq[i][:, 0:Tn], op=ALU.mult), reads=[ck, "tq%d" % i], writes=[ck])
                        pl.op("act", lambda e: e.activation(out=sq2[i][:, 0:Tn], in_=c_[:, 0:Tn], func=AF.Square), reads=[ck], writes=["sq2%d" % i])
                        pl.op("act", lambda e: e.activation(out=ycat[:, j, 0:Tn], in_=c_[:, 0:Tn], func=AF.Copy, scale=pfc(PF_NAW, j)), reads=[ck, "pf"], writes=["ycat"])
                        for s in range(nsub):
                            pl.op("pe", lambda e, s=s: e.matmul(psC[:, 320 + s:321 + s], lhsT=sq2[i][:, s * 128:(s + 1) * 128], rhs=ones_bf[:, 0:1], start=(j == 0 and s == 0), stop=(j == 7), skip_group_check=True),
                                  reads=["sq2%d" % i, "ones"], writes=["psCa"], token=(s == nsub - 1))
                    pl.op("act", lambda e, j=j: e.activation(out=sz[:, j, 0:Tn], in_=psB[:, j % 2, 0:Tn], func=AF.Silu), reads=["psB%d" % (j % 2)], writes=["sz"])
                    pendA = stage_b
                if pendA is not None:
                    pendA()
                if mode == "full" and hist_from == "carry":
                    pl.op("dve", lambda e: e.tensor_copy(out=casb[:, :, 0, :], in_=uhist[:, :, 1:3]), reads=["uhist"], writes=["casb"])
                if mode == "halo":
                    pl.op("dve", lambda e: e.tensor_scalar_mul(out=uhist[:].rearrange("p a b -> p (a b)"), in0=uhist[:].rearrange("p a b -> p (a b)"), scalar1=msk[:, 0:1]), reads=["uhist", "msk"], writes=["uhist"])
                    wbs = {}
                    for j in range(12, 16):
                        wb, wk_ = load_piece(40 + j)
                        pa = psA[:, j % 4, 0:Tn]
                        pk = "psA%d" % (j % 4)
                        proj_chunk(wb, wk_, 0, pa, pk, Tn)
                        pl.op("act", lambda e, j=j, pa=pa: e.copy(out=xhist[:, j, :], in_=pa[:, Tn - 3:Tn]), reads=[pk], writes=["xhist"])
                    pl.op("dve", lambda e: e.tensor_scalar_mul(out=xhist[:, 12:16, :], in0=xhist[:, 12:16, :], scalar1=msk[:, 0:1]), reads=["xhist", "msk"], writes=["xhist"])
                    return

            wbs = {}
            pending = None
            for j in range(16):
                if mode == "light" and j >= 12:
                    break
                wb, wk_ = load_piece(40 + j)
                i = j % 2
                pa = psA[:, j % 4, 0:Tn]
                pk = "psA%d" % (j % 4)
                proj_chunk(wb, wk_, 0, pa, pk, Tn)
                ub, uk = ubuf[i], "ubuf%d" % i
                pl.op("pool", lambda e, ub=ub, j=j: e.tensor_copy(out=ub[:, 0:3], in_=xhist[:, j, :]), reads=["xhist"], writes=[uk])
                pl.op("act", lambda e, ub=ub, pa=pa: e.copy(out=ub[:, 3:3 + Tn], in_=pa), reads=[pk], writes=[uk])
                pl.op("pool", lambda e, ub=ub, j=j: e.tensor_copy(out=xhist[:, j, :], in_=ub[:, Tn:Tn + 3]), reads=[uk], writes=["xhist"])
                c_, ck = cu[i], "cu%d" % i
                if hist_from == "carry":
                    conv_chunk("act", ub, cbw[:, j, :], 4, c_, Tn, [uk, "cbw"], [ck], bias=cbb[:, j:j + 1], bk=["cbb"])
                else:
                    for sidx in range(2):
                        base = sidx * 128
                        pl.op("dve", lambda e, j=j, sidx=sidx, base=base: e.tensor_copy(out=stsb[:, base:base + 3], in_=scb_sb[:, j, sidx, :]), reads=["scb"], writes=["stsb"])
                        pl.op("dve", lambda e, ub=ub, base=base, sidx=sidx: e.tensor_copy(out=stsb[:, base + 3:base + 67], in_=ub[:, 3 + sidx * 64:3 + (sidx + 1) * 64]), reads=[uk], writes=["stsb"])
                        pl.op("dve", lambda e, c_=c_, base=base, sidx=sidx, j=j: e.tensor_scalar_mul(out=c_[:, sidx * 64:(sidx + 1) * 64], in0=stsb[:, base:base + 64], scalar1=cbw[:, j, 0:1]), reads=["stsb", "cbw"], writes=[ck])
                        for k in (1, 2, 3):
                            pl.op("dve", lambda e, c_=c_, base=base, sidx=sidx, j=j, k=k: e.scalar_tensor_tensor(out=c_[:, sidx * 64:(sidx + 1) * 64], in0=stsb[:, base + k:base + 64 + k], scalar=cbw[:, j, k:k + 1], in1=c_[:, sidx * 64:(sidx + 1) * 64], op0=ALU.mult, op1=ALU.add), reads=["stsb", "cbw", ck], writes=[ck])
                        pl.op("dve", lambda e, base=base, sidx=sidx, j=j: e.tensor_copy(out=cbsb[:, j, 1 + sidx, :], in_=stsb[:, base + 64:base + 67]), reads=["stsb"], writes=["cbsb"])
                    pl.op("dve", lambda e, c_=c_, j=j: e.tensor_scalar_add(out=c_[:, 0:Tn], in0=c_[:, 0:Tn], scalar1=cbb[:, j:j + 1]), reads=[ck, "cbb"], writes=[ck])
                if j < 8:
                    dst, dk = xbf[:, j, 0:Tn], "xbf"
                elif j < 12:
                    dst, dk = BT[:, j - 8, 0:Tn], "BT"
                else:
                    dst, dk = CT[:, j - 12, 0:Tn], "CT"

                def stage_b(c_=c_, ck=ck, i=i, dst=dst, dk=dk):
                    pl.op("act", lambda e: e.activation(out=dst, in_=c_[:, 0:Tn], func=AF.Silu), reads=[ck], writes=[dk])
                if pending is not None:
                    pending()
                pending = stage_b
            if pending is not None:
                pending()
            if mode == "full" and hist_from == "carry":
                pl.op("dve", lambda e: e.tensor_copy(out=cbsb[:, :, 0, :], in_=xhist[:]), reads=["xhist"], writes=["cbsb"])
            W = nsub * 16
            SMA = lambda idx: sm[:, idx, 0:W]
            v3 = lambda ap: ap.rearrange("p (b h) -> p b h", h=16)
            for b in range(nsub):
                tsl = slice(b * 128, (b + 1) * 128)
                for kc in range(8):
                    pl.op("pe", lambda e, kc=kc, tsl=tsl, b=b, hb=hcur[0]: e.matmul(psC[:, 256 + b * 16:272 + b * 16], lhsT=hb[:, kc, tsl], rhs=wdt[:, kc, :], start=(kc == 0), stop=(kc == 7)), reads=[hcur[1], "wdt"], writes=["psCd"], token=(kc == 7))
            pl.op("dve", lambda e: e.tensor_tensor(out=v3(SMA(0)), in0=v3(psC[:, 256:256 + W]), in1=bc[:, BC_DTB:BC_DTB + 16].unsqueeze(1).broadcast_to([128, nsub, 16]), op=ALU.add), reads=["psCd", "bc"], writes=["sm0"])
            pl.op("act", lambda e: e.activation(out=SMA(1), in_=SMA(0), func=AF.Abs), reads=["sm0"], writes=["sm1"])
            pl.op("act", lambda e: e.activation(out=SMA(1), in_=SMA(1), func=AF.Exp, scale=-1.0), reads=["sm1"], writes=["sm1"])
            pl.op("act", lambda e: e.activation(out=SMA(1), in_=SMA(1), func=AF.Ln, bias=1.0), reads=["sm1"], writes=["sm1"])
            pl.op("dve", lambda e: e.scalar_tensor_tensor(out=SMA(2), in0=SMA(0), scalar=0.0, in1=SMA(1), op0=ALU.max, op1=ALU.add), reads=["sm0", "sm1"], writes=["sm2"])
            pl.op("dve", lambda e: e.tensor_tensor(out=v3(SMA(3)), in0=v3(SMA(2)), in1=a_bc[:].unsqueeze(1).broadcast_to([128, nsub, 16]), op=ALU.mult), reads=["sm2", "a_bc"], writes=["sm3"])
            pl.op("pe", lambda e: e.matmul(psB[:, 0, 0:W], lhsT=tri, rhs=SMA(3), start=True, stop=True), reads=["cst", "sm3"], writes=["psB0"], token=False)
            pl.op("pe", lambda e: e.matmul(psB[:, 0, 64:64 + W], lhsT=blk, rhs=SMA(3), start=True, stop=True), reads=["cst", "sm3"], writes=["psB0"], token=False)
            pl.op("pe", lambda e: e.matmul(psB[:, 0, 128:128 + W], lhsT=chsel[0], rhs=SMA(3), start=True, stop=True), reads=["cst", "sm3"], writes=["psB0"], token=False)
            pl.op("pe", lambda e: e.matmul(psB[:, 0, 192:192 + W], lhsT=chsel[1], rhs=SMA(3), start=True, stop=True), reads=["cst", "sm3"], writes=["psB0"])
            pl.op("dve", lambda e: e.tensor_copy(out=SMA(4), in_=psB[:, 0, 0:W]), reads=["psB0"], writes=["sm4"])
            pl.op("dve", lambda e: e.tensor_tensor(out=SMA(5), in0=psB[:, 0, 64:64 + W], in1=SMA(4), op=ALU.subtract), reads=["psB0", "sm4"], writes=["sm5"])
            if mode == "light":
                sel0, sel1 = psB[:, 0, 128:128 + W], psB[:, 0, 192:192 + W]
                pl.op("dve", lambda e: e.tensor_copy(out=SMA(7), in_=sel1), reads=["psB0"], writes=["sm7"])
                pl.op("dve", lambda e: e.tensor_tensor(out=SMA(0), in0=sel0, in1=SMA(7), op=ALU.add), reads=["psB0", "sm7"], writes=["sm0"])
                pl.op("dve", lambda e: e.memset(SMA(1), 0.0), writes=["sm1"])
                for b in range(nsub - 2, -1, -1):
                    pl.op("dve", lambda e, b=b: e.tensor_tensor(out=sm[:, 1, b * 16:(b + 1) * 16], in0=sm[:, 1, (b + 1) * 16:(b + 2) * 16], in1=sm[:, 0, (b + 1) * 16:(b + 2) * 16], op=ALU.add), reads=["sm1", "sm0"], writes=["sm1"])
                pl.op("dve", lambda e: e.tensor_tensor(out=cd[:, 1, 0:16], in0=sm[:, 1, 0:16], in1=sm[:, 0, 0:16], op=ALU.add), reads=["sm1", "sm0"], writes=["cd"])
                pl.op("act", lambda e: e.activation(out=cd[:, 0, 0:16], in_=cd[:, 1, 0:16], func=AF.Exp), reads=["cd"], writes=["cd"])
                pl.op("dve", lambda e: e.scalar_tensor_tensor(out=SMA(1), in0=SMA(7), scalar=chsel[0][:, 0:1], in1=SMA(1), op0=ALU.mult, op1=ALU.add), reads=["sm7", "cst", "sm1"], writes=["sm1"])
                pl.op("dve", lambda e: e.tensor_tensor(out=SMA(5), in0=SMA(5), in1=SMA(1), op=ALU.add), reads=["sm5", "sm1"], writes=["sm5"])
            pl.op("act", lambda e: e.activation(out=SMA(5), in_=SMA(5), func=AF.Exp), reads=["sm5"], writes=["sm5"])
            pl.op("dve", lambda e: e.tensor_tensor(out=SMA(6), in0=SMA(5), in1=SMA(2), op=ALU.mult), reads=["sm5", "sm2"], writes=["sm6"])
            if mode != "light":
                pl.op("act", lambda e: e.activation(out=cd[:, 0, 0:W], in_=psB[:, 0, 128:128 + W], func=AF.Exp), reads=["psB0"], writes=["cd"])
                pl.op("act", lambda e: e.activation(out=cd[:, 1, 0:W], in_=psB[:, 0, 192:192 + W], func=AF.Exp), reads=["psB0"], writes=["cd"])
            else:
                if st_mode[0] is not None and st_mode[0][0] == "zero":
                    pl.op("dve", lambda e: e.memset(ST[:], 0.0), writes=["ST"])
                pl.op("pool", lambda e: e.tensor_tensor(out=ST[:].rearrange("p (h q) -> p h q", h=16), in0=ST[:].rearrange("p (h q) -> p h q", h=16), in1=cd[:, 0, 0:16].unsqueeze(2).broadcast_to([128, 16, 64]), op=ALU.mult), reads=["ST", "cd"], writes=["ST"])
            for b in range(nsub):
                tsl = slice(b * 128, (b + 1) * 128)
                SM = lambda idx, b=b: sm[:, idx, b * 16:(b + 1) * 16]
                bc16 = lambda idx, SM=SM: SM(idx).unsqueeze(2).broadcast_to([128, 16, 64])
                b16_2, b16_3, b16_4, b16_6 = bc16(2), bc16(3), bc16(4), bc16(6)
                bview = lambda ap: ap.rearrange("p (h q) -> p h q", h=16)
                if mode == "full":
                    pl.op("dve", lambda e, b16_3=b16_3: e.tensor_tensor(out=bview(rhs1[:]), in0=b16_3, in1=t64.unsqueeze(1).broadcast_to([128, 16, 64]), op=ALU.mult), reads=["sm3", "cst"], writes=["rhs1"])
                    pl.op("dve", lambda e, b16_3=b16_3: e.tensor_copy(out=bview(yo[:]), in_=b16_3), reads=["sm3"], writes=["yo"])
                for j in range(8):
                    pl.op("pe", lambda e, j=j, tsl=tsl: e.transpose(out=psT[:, j * 128:(j + 1) * 128], in_=xbf[:, j, tsl], identity=idb[:]), reads=["xbf", "idb"], writes=["psT"], token=(j == 7))
                bview = lambda ap: ap.rearrange("p (h q) -> p h q", h=16)
                if mode == "full":
                    pl.op("dve", lambda e, b16_2=b16_2: e.tensor_tensor(out=bview(xdt[:]), in0=bview(psT[:, :]), in1=b16_2, op=ALU.mult), reads=["psT", "sm2"], writes=["xdt"])
                if mode == "light":
                    xw_, xwk = ((xw, "xw"), (xdt, "xdt"))[b % 2]
                    bt_, btk = ((btok[:], "btok"), (Mt[:, 0:512], "Mt"))[b % 2]
                    bps, bpk = psC[:, 0:256].bitcast(BF16), "psCb"
                else:
                    xw_, xwk, bt_, btk, bps, bpk = xw, "xw", btok[:], "btok", psT[:, 0:512], "psT"
                pl.op("dve", lambda e, b16_6=b16_6, xw_=xw_: e.tensor_tensor(out=bview(xw_[:]), in0=bview(psT[:, :]), in1=b16_6, op=ALU.mult), reads=["psT", "sm6"], writes=[xwk])
                for g in range(4):
                    pl.op("pe", lambda e, g=g, tsl=tsl, bps=bps: e.transpose(out=bps[:, g * 128:(g + 1) * 128], in_=BT[:, g, tsl], identity=idb[:]), reads=["BT", "idb"], writes=[bpk], token=(g == 3))
                pl.op("act", lambda e, bt_=bt_, bps=bps: e.copy(out=bt_, in_=bps), reads=[bpk], writes=[btk])

                if mode == "full":
                    for g in range(4):
                        for c in range(2):
                            cs = slice(b * 128 + c * 64, b * 128 + (c + 1) * 64)
                            pl.op("pe", lambda e, g=g, c=c, cs=cs: e.matmul(psC[c * 64:(c + 1) * 64, g * 64:(g + 1) * 64], lhsT=BT[:, g, cs], rhs=CT[:, g, cs], start=True, stop=True), reads=["BT", "CT"], writes=["psCb"], token=(g == 3 and c == 1))
                    pl.op("dve", lambda e: e.tensor_tensor(out=cbtm[:], in0=psC[:, 0:256].rearrange("p (g i) -> p g i", g=4), in1=t64.unsqueeze(1).broadcast_to([128, 4, 64]), op=ALU.mult), reads=["psCb", "cst"], writes=["cbtm"])
                    for hh in range(2):
                        pl.op("pe", lambda e, hh=hh: e.matmul(psA[:, hh, :], lhsT=blk, rhs=rhs1[:, hh * 512:(hh + 1) * 512], start=True, stop=True), reads=["cst", "rhs1"], writes=["psA%d" % hh])
                    pl.op("dve", lambda e, b16_4=b16_4: e.tensor_tensor(out=bview(segs[:]), in0=psA[:, 0:2, :].rearrange("p a (h q) -> p (a h) q", q=64), in1=b16_4, op=ALU.subtract), reads=["psA0", "psA1", "sm4"], writes=["segs"])
                    pl.op("dve", lambda e: e.tensor_scalar_min(out=segs[:], in0=segs[:], scalar1=0.0), reads=["segs"], writes=["segs"])
                    pl.op("act", lambda e: e.activation(out=segs[:], in_=segs[:], func=AF.Exp), reads=["segs"], writes=["segs"])
                    for g in range(4):
                        pl.op("dve", lambda e, g=g: e.scalar_tensor_tensor(out=Mt[:, g * 256:(g + 1) * 256].rearrange("p (r q) -> p r q", r=4), in0=segs[:, g * 256:(g + 1) * 256].rearrange("p (r q) -> p r q", r=4), scalar=1.0,
                                                                           in1=cbtm[:, g, :].unsqueeze(1).broadcast_to([128, 4, 64]), op0=ALU.min, op1=ALU.mult), reads=["segs", "cbtm"], writes=["Mt"])
                    pl.op("pool", lambda e, tsl=tsl: e.tensor_tensor(out=segs[:].rearrange("p (a t) -> p a t", a=8), in0=xbf[:, :, tsl], in1=pf[:, PF_DSK:PF_DSK + 8].unsqueeze(2).broadcast_to([128, 8, 128]), op=ALU.mult), reads=["xbf", "pf", "segs"], writes=["segs", "segs2"])
                    for a in range(8):
                        pl.op("pe", lambda e, a=a: e.matmul(psA[:, 2 + a // 4, (a % 4) * 128:(a % 4 + 1) * 128], lhsT=yo[:, a * 128:(a + 1) * 128], rhs=tri, start=True, stop=True), reads=["yo", "cst"], writes=["psA%d" % (2 + a // 4)], token=(a % 4 == 3))
                    pl.op("act", lambda e: e.activation(out=eac[:], in_=psA[:, 2:4, :].rearrange("p a q -> p (a q)"), func=AF.Exp), reads=["psA2", "psA3"], writes=["eac"])

                for c in range(2):
                    ch = b * 2 + c
                    cs = slice(b * 128 + c * 64, b * 128 + (c + 1) * 64)
                    ps_ = slice(c * 64, (c + 1) * 64)
                    if mode != "light" and st_mode[ch] is not None:
                        kind, val = st_mode[ch]
                        if kind == "zero":
                            pl.op("dve", lambda e: e.memset(ST[:], 0.0), writes=["ST"])
                        elif kind == "load":
                            pl.dma("sp", lambda e, val=val: e.dma_start(out=ST[:], in_=sst_d[val, :, :]), "l_ST", writes=["ST"])
                        elif kind == "keep":
                            pass
                        if mode == "full":
                            pl.op("act", lambda e: e.copy(out=STb[:], in_=ST[:]), reads=["ST"], writes=["STb"])
                    if mode != "light":
                        pl.op("pool", lambda e, c=c, b=b: e.tensor_tensor(out=bview(ST[:]), in0=bview(ST[:]), in1=cd[:, c, b * 16:(b + 1) * 16].unsqueeze(2).broadcast_to([128, 16, 64]), op=ALU.mult), reads=["ST", "cd"], writes=["ST"])
                    if mode == "full":
                        for a in range(8):
                            for hh in range(2):
                                h = 2 * a + hh
                                pl.op("pe", lambda e, a=a, hh=hh, h=h, c=c, ps_=ps_: e.matmul(psB[hh * 64:(hh + 1) * 64, a // 4, (a % 4) * 128 + c * 64:(a % 4) * 128 + (c + 1) * 64], lhsT=xdt[ps_, h * 64:(h + 1) * 64], rhs=Mt[ps_, h * 64:(h + 1) * 64], start=True, stop=True),
                                      reads=["xdt", "Mt"], writes=["psB%d" % (a // 4)], token=(hh == 1 and a % 4 == 3))
                        for a in range(8):
                            g = a // 2
                            pl.op("pe", lambda e, a=a, g=g, c=c, cs=cs: e.matmul(psA[:, a // 4, (a % 4) * 128 + c * 64:(a % 4) * 128 + (c + 1) * 64], lhsT=STb[:, a * 128:(a + 1) * 128], rhs=CT[:, g, cs], start=True, stop=True),
                                  reads=["STb", "CT"], writes=["psA%d" % (a // 4)], token=(a % 4 == 3))
                    first = (b == 0 and c == 0)
                    last = (b == nsub - 1 and c == 1)
                    for g in range(4):
                        if mode == "light":
                            if c == 0:
                                bk = 2 + g // 2
                                pl.op("pe", lambda e, g=g, bk=bk, b=b, bt_=bt_, xw_=xw_: e.matmul(psA[:, bk, (g % 2) * 256:(g % 2 + 1) * 256], lhsT=bt_[:, g * 128:(g + 1) * 128], rhs=xw_[:, g * 256:(g + 1) * 256], start=(b == 0 and g % 2 == 0), stop=(b == nsub - 1), skip_group_check=True),
                                      reads=[btk, xwk], writes=["psA%d" % bk], token=(g % 2 == 1))
                        else:
                            pl.op("pe", lambda e, g=g, ps_=ps_: e.matmul(psA[:, 2 + g // 2, (g % 2) * 256:(g % 2 + 1) * 256], lhsT=btok[ps_, g * 128:(g + 1) * 128], rhs=xw[ps_, g * 256:(g + 1) * 256], start=True, stop=True),
                                  reads=["btok", "xw"], writes=["psA%d" % (2 + g // 2)], token=(g % 2 == 1))
                    if mode != "light" or last:
                        pl.op("dve", lambda e: e.tensor_tensor(out=ST[:], in0=ST[:], in1=psA[:, 2:4, :].rearrange("p a q -> p (a q)"), op=ALU.add), reads=["ST", "psA2", "psA3"], writes=["ST"])
                    if mode == "full":
                        pl.op("act", lambda e: e.copy(out=STb[:], in_=ST[:]), reads=["ST"], writes=["STb"])
                        if hist_from != "carry":
                            pl.dma("pool", lambda e, ch=ch: e.dma_start(out=so_d[1 + ch, :, :], in_=ST[:]), "s_ST", reads=["ST"], writes=["so%d" % (1 + ch)])
                if next_a is not None and b < len(next_a):
                    next_a[b]()
                if mode == "full":
                    pl.op("dve", lambda e: e.tensor_tensor(out=yo[:], in0=psA[:, 0:2, :].rearrange("p a q -> p (a q)"), in1=eac[:], op=ALU.mult), reads=["psA0", "psA1", "eac"], writes=["yo"])
                    pl.op("dve", lambda e: e.tensor_tensor(out=yo[:], in0=psB[:, :, :].rearrange("p a q -> p (a q)"), in1=yo[:], op=ALU.add), reads=["psB0", "psB1", "yo"], writes=["yo"])
                    v8 = lambda ap: ap.rearrange("p (a t) -> p a t", a=8)
                    pl.op("dve", lambda e: e.tensor_tensor(out=yo[:], in0=yo[:], in1=segs[:], op=ALU.add), reads=["yo", "segs2"], writes=["yo"])
                    pl.op("dve", lambda e, tsl=tsl: e.tensor_tensor(out=v8(yo[:]), in0=v8(yo[:]), in1=sz[:, :, tsl], op=ALU.mult), reads=["yo", "sz"], writes=["yo"])
                    pl.op("act", lambda e: e.activation(out=Mt[:], in_=yo[:], func=AF.Square), reads=["yo"], writes=["Mt"])
                    pl.op("pool", lambda e, tsl=tsl: e.tensor_tensor(out=ycat[:, 8:16, tsl], in0=v8(yo[:]), in1=pf[:, PF_NBW:PF_NBW + 8].unsqueeze(2).broadcast_to([128, 8, 128]), op=ALU.mult), reads=["yo", "pf"], writes=["ycat"])
                    for a in range(8):
                        pl.op("pe", lambda e, a=a, b=b: e.matmul(psC[:, 324 + b:325 + b], lhsT=Mt[:, a * 128:(a + 1) * 128], rhs=ones_bf[:, 0:1], start=(a == 0), stop=(a == 7)), reads=["Mt", "ones"], writes=["psCs"], token=(a == 7))

            if mode != "full":
                return
            pl.op("act", lambda e: e.activation(out=stat[:, 24:32], in_=psC[:, 320:328], func=AF.Ln, scale=1.0 / D, bias=EPS), reads=["psCa", "psCs"], writes=["stat"])
            pl.op("act", lambda e: e.activation(out=stat[:, 24:32], in_=stat[:, 24:32], func=AF.Exp, scale=-0.5), reads=["stat"], writes=["stat"])
            for s in range(nsub):
                tsl = slice(s * 128, (s + 1) * 128)
                for hf in range(2):
                    for part in range(2):
                        for kc in range(8):
                            pl.op("pe", lambda e, hf=hf, part=part, kc=kc, tsl=tsl: e.matmul(psA[:, part * 2 + hf, :], lhsT=ycat[:, part * 8 + kc, tsl], rhs=wout[:, part * 8 + kc, hf * 512:(hf + 1) * 512], start=(kc == 0), stop=(kc == 7)),
                                  reads=["ycat", "wout"], writes=["psA%d" % (part * 2 + hf)], token=(kc == 7))
                xb = xt[s % 2]
                xk = "xt%d" % (s % 2)
                pl.dma("sp", lambda e, xb=xb, s=s: e.dma_start(out=xb[:], in_=xs_d[row0 + s * 128:row0 + (s + 1) * 128, :]), "l_" + xk, writes=[xk])
                ob = ost[s % 2]
                ok = "ost%d" % (s % 2)
                pl.op("act", lambda e, s=s: e.activation(out=o1[:], in_=psA[:, 0:2, :].rearrange("p a q -> p (a q)"), func=AF.Copy, scale=stat[:, 24 + s:25 + s]), reads=["psA0", "psA1", "stat"], writes=["o1"])
                pl.op("dve", lambda e, s=s: e.scalar_tensor_tensor(out=o1[:], in0=psA[:, 2:4, :].rearrange("p a q -> p (a q)"), scalar=stat[:, 28 + s:29 + s], in1=o1[:], op0=ALU.mult, op1=ALU.add), reads=["psA2", "psA3", "stat", "o1"], writes=["o1"])
                pl.op("dve", lambda e: e.tensor_tensor(out=o1[:], in0=o1[:], in1=gate[:, gidx, :], op=ALU.mult), reads=["o1", "gate"], writes=["o1"])
                pl.op("dve", lambda e, xb=xb: e.tensor_tensor(out=o1[:], in0=o1[:], in1=xb[:], op=ALU.add), reads=["o1", xk], writes=["o1"])
                pl.op("dve", lambda e, s=s: e.memset(stat[:, 32 + s:33 + s], 0.0), writes=["stat"])
                pl.op("act", lambda e, s=s: e.activation(out=sqj[:], in_=o1[:], func=AF.Square, accum_out=stat[:, 32 + s:33 + s]), reads=["o1", "stat"], writes=["sqj", "stat"])
                pl.op("act", lambda e, s=s: e.activation(out=stat[:, 36 + s:37 + s], in_=stat[:, 32 + s:33 + s], func=AF.Ln, scale=1.0 / D, bias=EPS), reads=["stat"], writes=["stat"])
                pl.op("act", lambda e, s=s: e.activation(out=stat[:, 36 + s:37 + s], in_=stat[:, 36 + s:37 + s], func=AF.Exp, scale=-0.5), reads=["stat"], writes=["stat"])
                pl.op("dve", lambda e, s=s, ob=ob: e.scalar_tensor_tensor(out=ob[:], in0=o1[:], scalar=stat[:, 36 + s:37 + s], in1=bc[:, BC_NF:BC_NF + D], op0=ALU.mult, op1=ALU.mult), reads=["o1", "stat", "bc"], writes=[ok])
                pl.dma("pool", lambda e, ob=ob, s=s: e.dma_start(out=y_d[yrow0 + s * 128:yrow0 + (s + 1) * 128, :], in_=ob[:]), "s_" + ok, reads=[ok], writes=["y"])

        ones_bf = SB("ones_bf", [128, 8], BF16)
        pl.op("dve", lambda e: e.memset(ones_bf[:], 1.0), writes=["ones"])
        sca_sb = SB("sca_sb", [128, 8, 2, 2])
        scb_sb = SB("scb_sb", [128, 16, 2, 3])
        ld("l_sca", sca_sb[:].rearrange("p a b c -> p (a b c)"), sca_d[:, :], "sca")
        ld("l_scb", scb_sb[:].rearrange("p a b c -> p (a b c)"), scb_d[:, :], "scb")
        pl.op("dve", lambda e: e.memset(uhist[:], 0.0), writes=["uhist"])
        pl.op("dve", lambda e: e.memset(xhist[:], 0.0), writes=["xhist"])
        pl.op("dve", lambda e: e.memset(atot[:], 0.0), writes=["atot"])
        pslot = [(0, T, 0)]
        hsel = ((hT, "hT"), (ycat[:, 0:8, :], "ycat"))
        for n in range(NLSEG * NT):
            stm = [None] * (T // 64)
            if n == 0:
                stm[0] = ("zero", 0)
            nxa = None
            if n + 1 < NLSEG * NT:
                r1 = 128 + (n + 1) * T
                hb1, hk1 = hsel[(n + 1) % 2]
                nxa = [(lambda r1=r1, p0=p0, hb1=hb1, hk1=hk1: step_a_pair(r1, p0, T, pslot, hb1, hk1)) for p0 in (0, 2)]
            tile(128 + n * T, T, pslot, "light", 0, 0, "carry", stm, None, h_idx=n % 2, pre_a=(n > 0), next_a=nxa)
            if n % NT == NT - 1:
                m = n // NT
                pl.op("dve", lambda e, m=m: e.tensor_scalar_mul(out=ST[:], in0=ST[:], scalar1=msk[:, 1 + m:2 + m]), reads=["ST", "msk"], writes=["ST"])
                pl.op("dve", lambda e, m=m: e.tensor_scalar_mul(out=xhist[:].rearrange("p a b -> p (a b)"), in0=xhist[:].rearrange("p a b -> p (a b)"), scalar1=msk[:, 1 + m:2 + m]), reads=["xhist", "msk"], writes=["xhist"])
        tile(0, 128, [(0, 128, 0)], "halo", 0, 0, "carry", [None, None], None)
        for n in range(NT):
            stm = [None] * (T // 64)
            if n == 0:
                stm[0] = ("keep", 0)
            tile(128 + LROWS + n * T, T, pslot, "full", 0, n * T, "carry", stm, 0)
        pl.dma("pool", lambda e: e.dma_start(out=so_d[0, :, :], in_=ST[:]), "s_ST", reads=["ST"], writes=["so0"])
        tile(128 + LROWS + SEGLEN, 128, [(0, 64, 1), (64, 128, 2)], "full", 1, SEGLEN, "state", [("load", 0), ("load", 1)], 1)
        pl.dma("pool", lambda e: e.dma_start(out=ca_d[:, :], in_=casb[:].rearrange("p a b c -> p (a b c)")), "s_ca", reads=["casb"], writes=["ca"])
        pl.dma("pool", lambda e: e.dma_start(out=cb_d[:, :], in_=cbsb[:].rearrange("p a b c -> p (a b c)")), "s_cb", reads=["cbsb"], writes=["cb"])
        if debug:
            dbg_list = [("xbf", xbf[:].rearrange("p a b -> p (a b)"), 8 * T, BF16), ("BT", BT[:].rearrange("p a b -> p (a b)"), 4 * T, BF16),
                        ("CT", CT[:].rearrange("p a b -> p (a b)"), 4 * T, BF16), ("sm", sm[:].rearrange("p a b -> p (a b)"), 256, F32),
                        ("ycat", ycat[:].rearrange("p a b -> p (a b)"), 16 * T, BF16), ("yo", yo[:], 1024, F32), ("xw", xw[:], 1024, BF16),
                        ("xdt", xdt[:], 1024, BF16), ("btok", btok[:], 512, BF16), ("Mt", Mt[:], 1024, BF16), ("eac", eac[:], 1024, F32),
                        ("cd", cd[:].rearrange("p a b -> p (a b)"), 32, F32), ("hT", hT[:].rearrange("p a b -> p (a b)"), 8 * T, BF16),
                        ("sz", sz[:].rearrange("p a b -> p (a b)"), 8 * T, BF16), ("stat", stat[:], 64, F32), ("gam", gam[:].rearrange("p a b -> p (a b)"), 24, F32)]
            allk = list(pl.bufs.keys())
            for nm, ap, w, dt_ in dbg_list:
                dd = nc.dram_tensor("dbg_" + nm, [128, w], dt_, kind="ExternalOutput").ap()
                pl.dma("pool", lambda e, dd=dd, ap=ap: e.dma_start(out=dd[:, :], in_=ap), "s_dbg", reads=allk)
        pl.wait_tokens("pool", [(s, c) for s, c in pl.dma_cnt.items() if s.startswith("s_")])
        print("planned instructions:", pl.nins, {e: len(pl.lists[e]) for e in pl.ENGS})
        pl.emit()
    return nc


def _host_consts():
    c = np.zeros((128, NCONST), np.float32)
    k = np.arange(128)
    c[:, C_ID:C_ID + 128] = np.eye(128, dtype=np.float32)
    same = (k[:, None] // 64) == (k[None, :] // 64)
    c[:, C_BLK:C_BLK + 128] = same
    c[:, C_TRI:C_TRI + 128] = same & (k[:, None] <= k[None, :])
    c[:, C_T64:C_T64 + 64] = (k[:, None] % 64) <= np.arange(64)[None, :]
    c[:, C_SEL0:C_SEL0 + 128] = (k[:, None] < 64)
    c[:, C_SEL1:C_SEL1 + 128] = (k[:, None] >= 64)
    return c


def _fm(v, nchunk):
    return np.ascontiguousarray(np.asarray(v, np.float32).reshape(nchunk, 128).T)


_NC_CACHE = {}


def kernel(x_prompt, x_sample, state_conv_a, state_conv_b, state_ssm, c_prompt, c_sample,
           w_mod, b_mod, norm_in_w, w_in, conv_a_w, norm_a_w, conv_b_w, conv_b_b,
           dt_bias, a_log, d_skip, norm_b_w, w_out, norm_f_w, _two_phase=True, _debug=False):
    f = lambda a: np.ascontiguousarray(np.asarray(a, np.float32))
    x_prompt, x_sample = f(x_prompt), f(x_sample)
    state_conv_a, state_conv_b, state_ssm = f(state_conv_a), f(state_conv_b), f(state_ssm)
    c_prompt, c_sample = f(c_prompt), f(c_sample)
    w_mod, b_mod, w_in, w_out = f(w_mod)[0], f(b_mod)[0], f(w_in)[0], f(w_out)[0]
    pf = np.zeros((128, NPF), np.float32)
    pf[:, PF_NIN:PF_NIN + 8] = _fm(f(norm_in_w)[0], 8)
    caw = f(conv_a_w)[0]
    pf[:, PF_CAW:PF_CAW + 24] = np.stack([_fm(caw[k], 8) for k in range(3)], axis=2).reshape(128, 24)
    pf[:, PF_NAW:PF_NAW + 8] = _fm(f(norm_a_w)[0], 8)
    cbw = f(conv_b_w)[0]
    pf[:, PF_CBW:PF_CBW + 64] = np.stack([_fm(cbw[k], 16) for k in range(4)], axis=2).reshape(128, 64)
    pf[:, PF_CBB:PF_CBB + 16] = _fm(f(conv_b_b)[0], 16)
    pf[:, PF_NBW:PF_NBW + 8] = _fm(f(norm_b_w)[0], 8)
    pf[:, PF_DSK:PF_DSK + 8] = _fm(np.repeat(f(d_skip)[0], 64), 8)
    pf[:, PF_BSH:PF_BSH + 8] = _fm(b_mod[0:D], 8)
    pf[:, PF_BSC:PF_BSC + 8] = _fm(b_mod[D:2 * D], 8)
    bcv = np.zeros((128, NBC), np.float32)
    bcv[:, BC_NF:BC_NF + D] = f(norm_f_w)[None, :]
    bcv[:, BC_BG:BC_BG + D] = b_mod[None, 2 * D:3 * D]
    bcv[:, BC_DTB:BC_DTB + 16] = f(dt_bias)[0][None, :]
    bcv[:, BC_ALOG:BC_ALOG + 16] = f(a_log)[0][None, :]
    cst = _host_consts()

    in_maps = []
    for k in range(NCORES):
        seq, seg = k // 4, k % 4
        start = seg * SEGLEN
        xs = np.zeros((XROWS, D), np.float32)
        if seg > 0:
            xs[0:128] = x_prompt[seq, start - 128:start]
        if seg > 0 and LROWS >= start:
            xs[128 + LROWS - start:128 + LROWS] = x_prompt[seq, 0:start]
        xs[128 + LROWS:128 + LROWS + SEGLEN] = x_prompt[seq, start:start + SEGLEN]
        xs[128 + LROWS + SEGLEN:] = x_sample[2 * k:2 * k + 2].reshape(128, D)
        cs = [c_prompt[seq], c_sample[2 * k], c_sample[2 * k + 1]]
        cT = np.stack([_fm(c, 8) for c in cs], axis=2).reshape(128, 24)
        cbc = np.zeros((2, 128, 8, 128), np.float32)
        cbc[0] = _fm(cs[0], 8)[:, :, None]
        cbc[1, :, :, 0:64] = _fm(cs[1], 8)[:, :, None]
        cbc[1, :, :, 64:128] = _fm(cs[2], 8)[:, :, None]
        sca = state_conv_a[0, 2 * k:2 * k + 2]
        sca = sca.reshape(2, 2, 8, 128).transpose(3, 2, 0, 1).reshape(128, 32)
        scb = state_conv_b[0, 2 * k:2 * k + 2]
        scb = scb.reshape(2, 3, 16, 128).transpose(3, 2, 0, 1).reshape(128, 96)
        sst = state_ssm[0, 2 * k:2 * k + 2]
        sst = sst.reshape(2, 1024, 128).transpose(0, 2, 1)
        msk = np.zeros((128, 16), np.float32)
        msk[:, 0] = 1.0 if seg > 0 else 0.0
        for m in range(NLSEG):
            msk[:, 1 + m] = 0.0 if m < NLSEG - seg else 1.0
        in_maps.append({
            "xs": xs, "w_mod": w_mod, "w_in": w_in, "w_out": w_out, "pf": pf, "bc": bcv, "cst": cst,
            "cT": np.ascontiguousarray(cT), "cbc": np.ascontiguousarray(cbc.reshape(2, 128, 1024)),
            "sca": np.ascontiguousarray(sca), "scb": np.ascontiguousarray(scb),
            "sst": np.ascontiguousarray(sst), "msk": msk,
        })
    key = (bool(_two_phase), bool(_debug))
    if key not in _NC_CACHE:
        _NC_CACHE[key] = build_nc(two_phase=key[0], debug=key[1])
    nc = _NC_CACHE[key]
    res = run_bass_kernel_spmd(nc, in_maps, core_ids=list(range(NCORES)))
    R = res.results
    if _debug:
        kernel.last_results = R
    y_prompt = np.zeros((2, SEQ, D), np.float32)
    y_sample = np.zeros((16, 64, D), np.float32)
    ca_p = np.zeros((1, 2, 2, D), np.float32)
    cb_p = np.zeros((1, 2, 3, 2 * D), np.float32)
    ss_p = np.zeros((1, 2, 16, 64, 128), np.float32)
    ca_s = np.zeros((1, 16, 2, D), np.float32)
    cb_s = np.zeros((1, 16, 3, 2 * D), np.float32)
    ss_s = np.zeros((1, 16, 16, 64, 128), np.float32)
    for k in range(NCORES):
        seq, seg = k // 4, k % 4
        r = R[k]
        y_prompt[seq, seg * SEGLEN:(seg + 1) * SEGLEN] = r["y"][0:SEGLEN]
        y_sample[2 * k:2 * k + 2] = r["y"][SEGLEN:].reshape(2, 64, D)
        ca = r["ca"].reshape(128, 8, 3, 2)
        cb = r["cb"].reshape(128, 16, 3, 3)
        so = r["so"]
        for s in range(2):
            ca_s[0, 2 * k + s] = ca[:, :, 1 + s, :].transpose(2, 1, 0).reshape(2, D)
            cb_s[0, 2 * k + s] = cb[:, :, 1 + s, :].transpose(2, 1, 0).reshape(3, 2 * D)
            ss_s[0, 2 * k + s] = so[1 + s].T.reshape(16, 64, 128)
        if seg == 3:
            ca_p[0, seq] = ca[:, :, 0, :].transpose(2, 1, 0).reshape(2, D)
            cb_p[0, seq] = cb[:, :, 0, :].transpose(2, 1, 0).reshape(3, 2 * D)
            ss_p[0, seq] = so[0].T.reshape(16, 64, 128)
    return (y_prompt, y_sample, ca_p, cb_p, ss_p, ca_s, cb_s, ss_s)
```

```python
import contextlib
import numpy as np
import concourse.bass as bass
import concourse.mybir as mybir
from concourse.bass_utils import run_bass_kernel_spmd

F32 = mybir.dt.float32
BF16 = mybir.dt.bfloat16
ALU = mybir.AluOpType
AF = mybir.ActivationFunctionType

NCORES = 8
D = 1024
SEQ = 16384
SEGLEN = 4096
T = 512
NT = SEGLEN // T
DIN = 7184
PW = 128
NPIECE = 56
NWB = 12
EPS = 1e-5
NLSEG = 3
LROWS = NLSEG * SEGLEN
XROWS = 128 + LROWS + SEGLEN + 128
YROWS = SEGLEN + 128

PF_NIN = 0
PF_CAW = 8
PF_NAW = 32
PF_CBW = 40
PF_CBB = 104
PF_NBW = 120
PF_DSK = 128
PF_BSH = 136
PF_BSC = 144
NPF = 152
BC_NF = 0
BC_BG = 1024
BC_DTB = 2048
BC_ALOG = 2064
NBC = 2080
C_ID = 0
C_BLK = 128
C_TRI = 256
C_T64 = 384
C_SEL0 = 448
C_SEL1 = 576
NCONST = 704


class Planner:
    ENGS = ("pe", "act", "dve", "pool", "sp")
    SEM_LIMIT = 30000

    def __init__(self, nc):
        self.nc = nc
        self.lists = {e: [] for e in self.ENGS}
        self.cur = {e: [e + "_0", 0] for e in self.ENGS}
        self.gen = {e: 0 for e in self.ENGS}
        self.sem_names = [e + "_0" for e in self.ENGS]
        self.dma_cnt = {}
        self.waited = {e: {} for e in self.ENGS}
        self.bufs = {}
        self.nins = 0
        self.alias = {}
        self.bank_last = {}

    @staticmethod
    def _bank(k):
        if k.startswith("psC"):
            return "psC"
        if k.startswith("psA") or k.startswith("psB") or k == "psT":
            return k
        return None

    def _bank_deps(self, eng, keys, need):
        banks = set(b for b in (self._bank(k) for k in keys) if b)
        for b in banks:
            for oe, tok in self.bank_last.get(b, {}).items():
                if oe != eng:
                    self._need(eng, tok, need)
        return banks

    def _exp(self, keys):
        out = []
        for k in keys:
            out.extend(self.alias.get(k, (k,)))
        return out

    def _need(self, eng, tok, out):
        if tok is None:
            return
        s, v = tok
        if eng == "pe" and s.startswith("pe_"):
            return
        if self.waited[eng].get(s, 0) >= v:
            return
        if out.get(s, 0) < v:
            out[s] = v

    def _deps(self, eng, reads, writes):
        reads, writes = self._exp(reads), self._exp(writes)
        need = {}
        for k in reads:
            b = self.bufs.get(k)
            if b:
                self._need(eng, b[0], need)
        for k in writes:
            b = self.bufs.get(k)
            if b:
                self._need(eng, b[0], need)
                for t in b[1]:
                    self._need(eng, t, need)
        self._cur_banks = self._bank_deps(eng, list(reads) + list(writes), need)
        self._cur_eng = eng
        for s, v in need.items():
            self.waited[eng][s] = v
            self.lists[eng].append(("wait", s, v))

    def _mark(self, tok, reads, writes):
        reads, writes = self._exp(reads), self._exp(writes)
        for b in self._cur_banks:
            self.bank_last.setdefault(b, {})[self._cur_eng] = tok
        for k in reads:
            b = self.bufs.setdefault(k, [None, []])
            b[1].append(tok)
        for k in writes:
            self.bufs[k] = [tok, []]

    def op(self, eng, fn, reads=(), writes=(), token=True):
        self._deps(eng, reads, writes)
        c = self.cur[eng]
        self.nins += 1
        if token:
            if c[1] >= self.SEM_LIMIT:
                self.gen[eng] += 1
                c[0] = "%s_%d" % (eng, self.gen[eng])
                c[1] = 0
                self.sem_names.append(c[0])
            c[1] += 1
            tok = (c[0], c[1])
            self.lists[eng].append(("ins", fn, c[0], 1))
        else:
            tok = (c[0], c[1] + 1)
            self.lists[eng].append(("ins", fn, None, 0))
        self._mark(tok, reads, writes)
        return tok

    def dma(self, eng, fn, sem, reads=(), writes=()):
        self._deps(eng, reads, writes)
        self.nins += 1
        if sem not in self.dma_cnt:
            self.dma_cnt[sem] = 0
            self.sem_names.append(sem)
        self.dma_cnt[sem] += 16
        tok = (sem, self.dma_cnt[sem])
        self.lists[eng].append(("ins", fn, sem, 16))
        self._mark(tok, reads, writes)
        return tok

    def raw(self, eng, fn, sem, inc, reads=(), writes=()):
        self._deps(eng, reads, writes)
        if sem not in self.dma_cnt:
            self.dma_cnt[sem] = 0
            self.sem_names.append(sem)
        self.dma_cnt[sem] += inc
        tok = (sem, self.dma_cnt[sem])
        self.lists[eng].append(("ins", fn, sem, -inc))
        self._mark(tok, reads, writes)
        return tok

    def wait_tokens(self, eng, toks):
        need = {}
        for t in toks:
            self._need(eng, t, need)
        for s, v in need.items():
            self.waited[eng][s] = v
            self.lists[eng].append(("wait", s, v))

    def emit(self):
        nc = self.nc
        with contextlib.ExitStack() as st:
            sems = {}
            for n in self.sem_names:
                sems[n] = st.enter_context(nc.semaphore(n))
            block = st.enter_context(nc.Block())
            engmap = {"pe": block.tensor, "act": block.scalar, "dve": block.vector,
                      "pool": block.gpsimd, "sp": block.sync}
            for e in self.ENGS:
                lst = self.lists[e]
                if not lst:
                    continue

                def body(engobj, lst=lst):
                    for it in lst:
                        if it[0] == "wait":
                            engobj.wait_ge(sems[it[1]], it[2])
                        else:
                            ins = it[1](engobj)
                            if it[2] is not None:
                                if it[3] < 0:
                                    ins.then_inc(sems[it[2]])
                                else:
                                    ins.then_inc(sems[it[2]], it[3])
                engmap[e](body)


def build_nc(two_phase=True, debug=False):
    nc = bass.Bass("TRN2", target_bir_lowering=False)
    dr = lambda n, s, k, d=F32: nc.dram_tensor(n, list(s), d, kind=k)
    xs_d = dr("xs", [XROWS, D], "ExternalInput").ap()
    wmod_d = dr("w_mod", [D, 3 * D], "ExternalInput").ap()
    win_d = dr("w_in", [D, DIN], "ExternalInput").ap()
    wout_d = dr("w_out", [2 * D, D], "ExternalInput").ap()
    pf_d = dr("pf", [128, NPF], "ExternalInput").ap()
    bc_d = dr("bc", [128, NBC], "ExternalInput").ap()
    cst_d = dr("cst", [128, NCONST], "ExternalInput").ap()
    cT_d = dr("cT", [128, 8 * 3], "ExternalInput").ap()
    cbc_d = dr("cbc", [2, 128, 8 * 128], "ExternalInput").ap()
    sca_d = dr("sca", [128, 8 * 2 * 2], "ExternalInput").ap()
    scb_d = dr("scb", [128, 16 * 2 * 3], "ExternalInput").ap()
    sst_d = dr("sst", [2, 128, D], "ExternalInput").ap()
    msk_d = dr("msk", [128, 16], "ExternalInput").ap()
    y_d = dr("y", [YROWS, D], "ExternalOutput").ap()
    ca_d = dr("ca", [128, 8 * 3 * 2], "ExternalOutput").ap()
    cb_d = dr("cb", [128, 16 * 3 * 3], "ExternalOutput").ap()
    so_d = dr("so", [3, 128, D], "ExternalOutput").ap()
    winbf_d = nc.dram_tensor("winbf", [NPIECE, 128, 8 * PW], BF16)

    pl = Planner(nc)
    pl.alias = {"stg0": ("rhs1", "segs", "eac", "yo"), "stg1": ("stsb", "o1", "ost0", "ost1"), "sqj": ("Mt",)}
    with contextlib.ExitStack() as st:
        def SB(name, shape, dt=F32):
            return st.enter_context(nc.sbuf_tensor("sb_" + name, list(shape), dt))

        cst = SB("cst", [128, NCONST])
        idb = SB("idb", [128, 128], BF16)
        pf = SB("pf", [128, NPF])
        bc = SB("bc", [128, NBC])
        msk = SB("msk", [128, 16])
        cT = SB("cT", [128, 8, 3])
        gam = SB("gam", [128, 3, 8])
        bet = SB("bet", [128, 3, 8])
        gate = SB("gate", [128, 2, D])
        caw = SB("caw", [128, 8, 3])
        cbw = SB("cbw", [128, 16, 4])
        cbb = SB("cbb", [128, 16])
        a_bc = SB("a_bc", [128, 16])
        wout = SB("wout", [128, 16, D], BF16)
        wdt = SB("wdt", [128, 8, 16], BF16)
        wbuf = [SB("wbuf%d" % i, [128, 8, PW], BF16) for i in range(NWB)]
        big = [SB("big%d" % i, [128, 4096]) for i in range(2)]
        stg = [b[:].rearrange("p (a c) -> p a c", a=8) for b in big]
        xt = [SB("xt%d" % i, [128, D]) for i in range(2)]
        hT = SB("hT", [128, 8, T], BF16)
        ubuf = [SB("ubuf%d" % i, [128, 3 + T]) for i in range(2)]
        hsb = [SB("hsb%d" % i, [128, T]) for i in range(2)]
        cu = [SB("cu%d" % i, [128, T]) for i in range(2)]
        tq = [SB("tq%d" % i, [128, T]) for i in range(2)]
        sq2 = [SB("sq2%d" % i, [128, T], BF16) for i in range(2)]
        uhist = SB("uhist", [128, 8, 3])
        xhist = SB("xhist", [128, 16, 3])
        uhist0 = SB("uhist0", [128, 8, 3])
        xhist0 = SB("xhist0", [128, 16, 3])
        ycat = SB("ycat", [128, 16, T], BF16)
        xbf = SB("xbf", [128, 8, T], BF16)
        BT = SB("BT", [128, 4, T], BF16)
        CT = SB("CT", [128, 4, T], BF16)
        sz = SB("sz", [128, 8, T], BF16)
        xdt = SB("xdt", [128, D], BF16)
        xw = SB("xw", [128, D], BF16)
        btok = SB("btok", [128, 512], BF16)
        rhs1 = big[0][:, 0:1024]
        segs = big[0][:, 1024:2048]
        Mt = SB("Mt", [128, D], BF16)
        sqj = Mt
        eac = big[0][:, 2048:3072]
        yo = big[0][:, 3072:4096]
        cbtm = SB("cbtm", [128, 4, 64])
        sm = SB("sm", [128, 8, 64])
        ST = SB("ST", [128, D])
        STb = SB("STb", [128, D], BF16)
        stsb = big[1][:, 0:1024]
        cd = SB("cd", [128, 2, 64])
        stat = SB("stat", [128, 64])
        ost = [big[1][:, 2048:3072], big[1][:, 3072:4096]]
        o1 = big[1][:, 1024:2048]
        casb = SB("casb", [128, 8, 3, 2])
        cbsb = SB("cbsb", [128, 16, 3, 3])
        atot = SB("atot", [128, 16])
        gsel = SB("gsel", [128, 8, 16])
        print("sbuf remaining after alloc:", nc.sbuf_bytes_remaining)

        psA = st.enter_context(nc.psum_tensor("psA", [128, 4, 512], F32))
        psB = st.enter_context(nc.psum_tensor("psB", [128, 2, 512], F32))
        psC = st.enter_context(nc.psum_tensor("psC", [128, 512], F32))
        psT = st.enter_context(nc.psum_tensor("psT", [128, 1024], BF16))

        ident = cst[:, C_ID:C_ID + 128]
        blk = cst[:, C_BLK:C_BLK + 128]
        tri = cst[:, C_TRI:C_TRI + 128]
        t64 = cst[:, C_T64:C_T64 + 64]
        chsel = [cst[:, C_SEL0:C_SEL0 + 128], cst[:, C_SEL1:C_SEL1 + 128]]

        def pfc(off, j):
            return pf[:, off + j:off + j + 1]

        ld = lambda name, dst, src, key: pl.dma("sp", lambda e: e.dma_start(out=dst, in_=src), name, writes=[key])
        ld("l_cst", cst[:], cst_d[:, :], "cst")
        ld("l_pf", pf[:], pf_d[:, :], "pf")
        ld("l_bc", bc[:], bc_d[:, :], "bc")
        ld("l_msk", msk[:], msk_d[:, :], "msk")
        ld("l_cT", cT[:].rearrange("p a b -> p (a b)"), cT_d[:, :], "cT")
        pl.op("dve", lambda e: e.tensor_copy(out=idb[:], in_=ident), reads=["cst"], writes=["idb"])
        pl.op("dve", lambda e: e.tensor_scalar_mul(out=caw[:].rearrange("p a b -> p (a b)"), in0=pf[:, PF_CAW:PF_CAW + 24], scalar1=1.0), reads=["pf"], writes=["caw"])
        pl.op("dve", lambda e: e.tensor_scalar_mul(out=cbw[:].rearrange("p a b -> p (a b)"), in0=pf[:, PF_CBW:PF_CBW + 64], scalar1=1.0), reads=["pf"], writes=["cbw"])
        pl.op("dve", lambda e: e.tensor_scalar_mul(out=cbb[:], in0=pf[:, PF_CBB:PF_CBB + 16], scalar1=1.0), reads=["pf"], writes=["cbb"])
        pl.op("act", lambda e: e.activation(out=a_bc[:], in_=bc[:, BC_ALOG:BC_ALOG + 16], func=AF.Exp), reads=["bc"], writes=["a_bc"])
        pl.op("dve", lambda e: e.tensor_scalar_mul(out=a_bc[:], in0=a_bc[:], scalar1=-1.0), reads=["a_bc"], writes=["a_bc"])

        wmod_v = wmod_d.rearrange("(kc p) c -> p kc c", p=128)
        for piece in range(6):
            s = stg[piece % 2]
            key = "stg%d" % (piece % 2)
            pl.dma("sp", lambda e, s=s, piece=piece: e.dma_start(out=s[:], in_=wmod_v[:, :, piece * 512:(piece + 1) * 512]), "l_" + key, writes=[key])
            if piece < 4:
                for cc in range(4):
                    j = (piece % 2) * 4 + cc
                    for kc in range(8):
                        pl.op("pe", lambda e, s=s, cc=cc, kc=kc, j=j: e.matmul(psC[:, j * 4:j * 4 + 3], lhsT=s[:, kc, cc * 128:(cc + 1) * 128], rhs=cT[:, kc, :], start=(kc == 0), stop=(kc == 7)),
                              reads=[key, "cT"], writes=["psC"], token=(kc == 7))
                if piece % 2 == 1:
                    src = psC[:, 0:32].rearrange("p (j s) -> p s j", s=4)[:, 0:3, :]
                    if piece == 1:
                        pl.op("dve", lambda e, src=src: e.tensor_tensor(out=bet[:], in0=src, in1=pf[:, PF_BSH:PF_BSH + 8].unsqueeze(1).broadcast_to([128, 3, 8]), op=ALU.add), reads=["psC", "pf"], writes=["bet"])
                    else:
                        pl.op("dve", lambda e, src=src: e.tensor_tensor(out=gam[:], in0=src, in1=pf[:, PF_BSC:PF_BSC + 8].unsqueeze(1).broadcast_to([128, 3, 8]), op=ALU.add), reads=["psC", "pf"], writes=["gam"])
                        pl.op("dve", lambda e: e.scalar_tensor_tensor(out=gam[:], in0=gam[:], scalar=1.0, in1=pf[:, PF_NIN:PF_NIN + 8].unsqueeze(1).broadcast_to([128, 3, 8]), op0=ALU.add, op1=ALU.mult), reads=["gam", "pf"], writes=["gam"])
            else:
                half = piece - 4
                for which in range(2):
                    cb_t = xt[which]
                    if half == 0:
                        pl.dma("sp", lambda e, cb_t=cb_t, which=which: e.dma_start(out=cb_t[:], in_=cbc_d[which, :, :]), "l_xt%d" % which, writes=["xt%d" % which])
                    cbv = cb_t[:].rearrange("p (k m) -> p k m", k=8)
                    for kc in range(8):
                        pl.op("pe", lambda e, s=s, kc=kc, cbv=cbv, which=which: e.matmul(psA[:, which, :], lhsT=cbv[:, kc, :], rhs=s[:, kc, :], start=(kc == 0), stop=(kc == 7)),
                              reads=[key, "xt%d" % which], writes=["psA%d" % which], token=(kc == 7))
                    pl.op("dve", lambda e, which=which, half=half: e.tensor_tensor(out=gate[:, which, half * 512:(half + 1) * 512], in0=psA[:, which, :], in1=bc[:, BC_BG + half * 512:BC_BG + (half + 1) * 512], op=ALU.add),
                          reads=["psA%d" % which, "bc"], writes=["gate"])

        win_v = win_d.rearrange("(kc p) c -> p kc c", p=128)
        wout_v = wout_d.rearrange("(kc p) c -> p kc c", p=128)
        castengs = ["dve", "act"]
        ci = 0

        def cast(dst, src, rk, wk):
            nonlocal ci
            eng = castengs[ci % 2]
            ci += 1
            if eng == "act":
                pl.op("act", lambda e: e.copy(out=dst, in_=src), reads=rk, writes=wk)
            else:
                pl.op(eng, lambda e: e.tensor_copy(out=dst, in_=src), reads=rk, writes=wk)

        porder = [10, 11, 12, 2, 3, 4, 5, 0, 1, 6, 7, 13, 8, 9]
        pl.dma("sp", lambda e: e.dma_start(out=stg[0][:, :, 0:16], in_=win_v[:, :, 7168:7184]), "l_stg0", writes=["stg0"])
        pl.op("dve", lambda e: e.tensor_copy(out=wdt[:], in_=stg[0][:, :, 0:16]), reads=["stg0"], writes=["wdt"])
        wci = 0
        for n, piece in enumerate(porder):
            s = stg[n % 2]
            key = "stg%d" % (n % 2)
            pl.dma("sp", lambda e, s=s, piece=piece: e.dma_start(out=s[:], in_=win_v[:, :, piece * 512:(piece + 1) * 512]), "l_" + key, writes=[key])
            for hp in range(512 // PW):
                wb = wbuf[wci % NWB]
                wkey = "wbuf%d" % (wci % NWB)
                wci += 1
                for kh in range(2):
                    cast(wb[:, kh * 4:(kh + 1) * 4, :], s[:, kh * 4:(kh + 1) * 4, hp * PW:(hp + 1) * PW], [key], [wkey])
                sp_ = (512 // PW) * piece + hp
                pl.dma("pool", lambda e, wb=wb, sp_=sp_: e.dma_start(out=winbf_d[sp_, :, :], in_=wb[:].rearrange("p a b -> p (a b)")), "s_" + wkey, reads=[wkey], writes=["winbf%d" % sp_])
        for n in range(4):
            kg, ch = n // 2, n % 2
            s = stg[n % 2]
            key = "stg%d" % (n % 2)
            pl.dma("sp", lambda e, s=s, kg=kg, ch=ch: e.dma_start(out=s[:], in_=wout_v[:, kg * 8:(kg + 1) * 8, ch * 512:(ch + 1) * 512]), "l_" + key, writes=[key])
            for kh in range(2):
                cast(wout[:, kg * 8 + kh * 4:kg * 8 + (kh + 1) * 4, ch * 512:(ch + 1) * 512], s[:, kh * 4:(kh + 1) * 4, :], [key], ["wout"])

        wcnt = [0]

        def load_piece(piece):
            i = wcnt[0] % NWB
            wcnt[0] += 1
            wb = wbuf[i]
            pl.dma("sp", lambda e: e.dma_start(out=wb[:].rearrange("p a b -> p (a b)"), in_=winbf_d[piece, :, :]), "l_wbuf%d" % i, reads=["winbf%d" % piece], writes=["wbuf%d" % i])
            return wb, "wbuf%d" % i

        hcur = [hT, "hT"]

        def proj_chunk(wb, wkey, cc, ps_ap, pskey, Tn):
            hb, hk = hcur
            for kc in range(8):
                pl.op("pe", lambda e, kc=kc: e.matmul(ps_ap, lhsT=wb[:, kc, cc * 128:(cc + 1) * 128], rhs=hb[:, kc, 0:Tn], start=(kc == 0), stop=(kc == 7)),
                      reads=[wkey, hk], writes=[pskey], token=(kc == 7))

        pref = {}

        def load_x(row0, s):
            xb = xt[s % 2]
            xk = "xt%d" % (s % 2)
            pl.dma("sp", lambda e: e.dma_start(out=xb[:], in_=xs_d[row0 + s * 128:row0 + (s + 1) * 128, :]), "l_" + xk, writes=[xk])

        def step_a(row0, Tn, slots, hb=None, hk="hT"):
            for p0 in range(0, Tn // 128, 2):
                step_a_pair(row0, p0, Tn, slots, hT if hb is None else hb, hk)

        def step_a_pair(row0, p0, Tn, slots, hb, hk):
            nsub = Tn // 128
            if True:
                subs = list(range(p0, min(p0 + 2, nsub)))
                for s in subs:
                    xb = xt[s % 2]
                    xk = "xt%d" % (s % 2)
                    if not pref.pop((row0, s), False):
                        load_x(row0, s)
                    pl.op("pool", lambda e, s=s: e.memset(stat[:, s:s + 1], 0.0), writes=["stat"])
                    pl.op("act", lambda e, xb=xb, s=s: e.activation(out=sqj[:], in_=xb[:], func=AF.Square, accum_out=stat[:, s:s + 1]), reads=[xk, "stat"], writes=["sqj", "stat"])
                    pl.op("act", lambda e, s=s: e.activation(out=stat[:, 8 + s:9 + s], in_=stat[:, s:s + 1], func=AF.Ln, scale=1.0 / D, bias=EPS), reads=["stat"], writes=["stat"])
                    pl.op("act", lambda e, s=s: e.activation(out=stat[:, 16 + s:17 + s], in_=stat[:, 8 + s:9 + s], func=AF.Exp, scale=-0.5), reads=["stat"], writes=["stat"])
                    pl.op("dve", lambda e, xb=xb, s=s: e.tensor_scalar_mul(out=xb[:], in0=xb[:], scalar1=stat[:, 16 + s:17 + s]), reads=[xk, "stat"], writes=[xk])
                w0, w1 = subs[0] * 128, (subs[-1] + 1) * 128
                for kc in range(8):
                    pk = "psB%d" % (kc % 2)
                    for s in subs:
                        xb = xt[s % 2]
                        xk = "xt%d" % (s % 2)
                        pl.op("pe", lambda e, xb=xb, kc=kc, s=s, p0=p0: e.transpose(out=psB[:, kc % 2, (s - p0) * 128:(s - p0 + 1) * 128], in_=xb[:, kc * 128:(kc + 1) * 128], identity=ident), reads=[xk, "cst"], writes=[pk], token=(s == subs[-1]))
                    for (c0, c1, slot) in slots:
                        lo, hi = max(c0, w0), min(c1, w1)
                        if lo >= hi:
                            continue
                        pl.op("act", lambda e, kc=kc, lo=lo, hi=hi, slot=slot, w0=w0: e.activation(out=hb[:, kc, lo:hi], in_=psB[:, kc % 2, lo - w0:hi - w0], func=AF.Identity, scale=gam[:, slot, kc:kc + 1], bias=bet[:, slot, kc:kc + 1]),
                              reads=[pk, "gam", "bet"], writes=[hk])

        def conv_chunk(eng_first, src, wts, nk, dst, Tn, rk, wk, bias=None, bk=()):
            off = 3 - (nk - 1)
            if bias is None:
                pl.op("act", lambda e: e.activation(out=dst[:, 0:Tn], in_=src[:, off:off + Tn], func=AF.Copy, scale=wts[:, 0:1]), reads=rk, writes=wk)
            else:
                pl.op("act", lambda e: e.activation(out=dst[:, 0:Tn], in_=src[:, off:off + Tn], func=AF.Identity, scale=wts[:, 0:1], bias=bias), reads=rk + list(bk), writes=wk)
            for k in range(1, nk):
                pl.op("dve", lambda e, k=k: e.scalar_tensor_tensor(out=dst[:, 0:Tn], in0=src[:, off + k:off + k + Tn], scalar=wts[:, k:k + 1], in1=dst[:, 0:Tn], op0=ALU.mult, op1=ALU.add), reads=rk + wk, writes=wk)

        def tile(row0, Tn, slots, mode, gidx, yrow0, hist_from, st_mode, out_slot, next_row0=None, h_idx=0, pre_a=False, next_a=None):
            nsub = Tn // 128
            nch = Tn // 64
            hcur[0], hcur[1] = ((hT, "hT"), (ycat[:, 0:8, :], "ycat"))[h_idx]
            if not pre_a:
                step_a(row0, Tn, slots, hcur[0], hcur[1])
            light = (mode != "full")
            if mode in ("full", "halo"):
                wbs = {}
                pendA = None
                for j in range(8):
                    i = j % 2
                    need = [8 + j, 16 + j] if mode == "halo" else [8 + j, 16 + j, 0 + j, 24 + j]
                    for pc in need:
                        wbs[pc] = load_piece(pc)
                    cc = 0
                    if j % 2 == 0:
                        (pc_, pck), (ph_, phk), (pb_, pbk) = (psA[:, 0, 0:Tn], "psA0"), (psA[:, 1, 0:Tn], "psA1"), (psA[:, 2, 0:Tn], "psA2")
                    else:
                        (pc_, pck), (ph_, phk), (pb_, pbk) = (psA[:, 0, 0:Tn], "psA0"), (psA[:, 1, 0:Tn], "psA1"), (psA[:, 2, 0:Tn], "psA2")
                    wb, wk_ = wbs[8 + j]
                    proj_chunk(wb, wk_, cc, pc_, pck, Tn)
                    wb, wk_ = wbs[16 + j]
                    proj_chunk(wb, wk_, cc, ph_, phk, Tn)
                    if mode != "halo":
                        wb, wk_ = wbs[0 + j]
                        proj_chunk(wb, wk_, cc, pb_, pbk, Tn)
                        wb, wk_ = wbs[24 + j]
                        proj_chunk(wb, wk_, cc, psA[:, 3, 0:Tn], "psA3", Tn)
                        wbz, wkz = load_piece(32 + j)
                        proj_chunk(wbz, wkz, 0, psB[:, j % 2, 0:Tn], "psB%d" % (j % 2), Tn)
                        if pendA is not None:
                            pendA()
                            pendA = None
                    ub, uk = ubuf[i], "ubuf%d" % i
                    pl.op("act", lambda e, i=i, ph_=ph_: e.copy(out=hsb[i][:, 0:Tn], in_=ph_), reads=[phk], writes=["hsb%d" % i])
                    pl.op("pool", lambda e, ub=ub, j=j: e.tensor_copy(out=ub[:, 0:3], in_=uhist[:, j, :]), reads=["uhist"], writes=[uk])
                    pl.op("dve", lambda e, ub=ub, i=i, pc_=pc_: e.tensor_tensor(out=ub[:, 3:3 + Tn], in0=pc_, in1=hsb[i][:, 0:Tn], op=ALU.mult), reads=[pck, "hsb%d" % i], writes=[uk])
                    pl.op("pool", lambda e, ub=ub, j=j: e.tensor_copy(out=uhist[:, j, :], in_=ub[:, Tn:Tn + 3]), reads=[uk], writes=["uhist"])
                    if mode == "halo":
                        continue
                    pl.op("act", lambda e, i=i: e.activation(out=tq[i][:, 0:Tn], in_=psA[:, 3, 0:Tn], func=AF.Silu), reads=["psA3"], writes=["tq%d" % i])
                    pl.op("dve", lambda e, i=i, pb_=pb_: e.tensor_tensor(out=tq[i][:, 0:Tn], in0=pb_, in1=tq[i][:, 0:Tn], op=ALU.mult), reads=[pbk, "tq%d" % i], writes=["tq%d" % i])
                    c_, ck = cu[i], "cu%d" % i
                    if hist_from == "carry":
                        conv_chunk("dve", ub, caw[:, j, :], 3, c_, Tn, [uk, "caw"], [ck])
                    else:
                        for sidx in range(2):
                            pl.op("dve", lambda e, ub=ub, j=j, sidx=sidx: e.tensor_copy(out=stsb[:, sidx * 128 + 1:sidx * 128 + 3], in_=sca_sb[:, j, sidx, :]), reads=["sca"], writes=["stsb"])
                        for sidx in range(2):
                            base = sidx * 128
                            pl.op("dve", lambda e, ub=ub, base=base, sidx=sidx: e.tensor_copy(out=stsb[:, base + 3:base + 67], in_=ub[:, 3 + sidx * 64:3 + (sidx + 1) * 64]), reads=[uk], writes=["stsb"])
                            off = 1
                            pl.op("dve", lambda e, c_=c_, base=base, sidx=sidx, j=j: e.tensor_scalar_mul(out=c_[:, sidx * 64:(sidx + 1) * 64], in0=stsb[:, base + 1:base + 65], scalar1=caw[:, j, 0:1]), reads=["stsb", "caw"], writes=[ck])
                            for k in (1, 2):
                                pl.op("dve", lambda e, c_=c_, base=base, sidx=sidx, j=j, k=k: e.scalar_tensor_tensor(out=c_[:, sidx * 64:(sidx + 1) * 64], in0=stsb[:, base + 1 + k:base + 65 + k], scalar=caw[:, j, k:k + 1], in1=c_[:, sidx * 64:(sidx + 1) * 64], op0=ALU.mult, op1=ALU.add), reads=["stsb", "caw", ck], writes=[ck])
                            pl.op("dve", lambda e, base=base, sidx=sidx, j=j: e.tensor_copy(out=casb[:, j, 1 + sidx, :], in_=stsb[:, base + 65:base + 67]), reads=["stsb"], writes=["casb"])
                    def stage_b(c_=c_, ck=ck, i=i, j=j):
                        pl.op("dve", lambda e: e.tensor_tensor(out=c_[:, 0:Tn], in0=c_[:, 0:Tn], in1=tq[i][:, 0:Tn], op=ALU.mult), reads=[ck, "tq%d" % i], writes=[ck])
                        pl.op("act", lambda e: e.activation(out=sq2[i][:, 0:Tn], in_=c_[:, 0:Tn], func=AF.Square), reads=[ck], writes=["sq2%d" % i])
                        pl.op("act", lambda e: e.activation(out=ycat[:, j, 0:Tn], in_=c_[:, 0:Tn], func=AF.Copy, scale=pfc(PF_NAW, j)), reads=[ck, "pf"], writes=["ycat"])
                        for s in range(nsub):
                            pl.op("pe", lambda e, s=s: e.matmul(psC[:, 320 + s:321 + s], lhsT=sq2[i][:, s * 128:(s + 1) * 128], rhs=ones_bf[:, 0:1], start=(j == 0 and s == 0), stop=(j == 7), skip_group_check=True),
                                  reads=["sq2%d" % i, "ones"], writes=["psCa"], token=(s == nsub - 1))
                    pl.op("act", lambda e, j=j: e.activation(out=sz[:, j, 0:Tn], in_=psB[:, j % 2, 0:Tn], func=AF.Silu), reads=["psB%d" % (j % 2)], writes=["sz"])
                    pendA = stage_b
                if pendA is not None:
                    pendA()
                if mode == "full" and hist_from == "carry":
                    pl.op("dve", lambda e: e.tensor_copy(out=casb[:, :, 0, :], in_=uhist[:, :, 1:3]), reads=["uhist"], writes=["casb"])
                if mode == "halo":
                    pl.op("dve", lambda e: e.tensor_scalar_mul(out=uhist[:].rearrange("p a b -> p (a b)"), in0=uhist[:].rearrange("p a b -> p (a b)"), scalar1=msk[:, 0:1]), reads=["uhist", "msk"], writes=["uhist"])
                    wbs = {}
                    for j in range(12, 16):
                        wb, wk_ = load_piece(40 + j)
                        pa = psA[:, j % 4, 0:Tn]
                        pk = "psA%d" % (j % 4)
                        proj_chunk(wb, wk_, 0, pa, pk, Tn)
                        pl.op("act", lambda e, j=j, pa=pa: e.copy(out=xhist[:, j, :], in_=pa[:, Tn - 3:Tn]), reads=[pk], writes=["xhist"])
                    pl.op("dve", lambda e: e.tensor_scalar_mul(out=xhist[:, 12:16, :], in0=xhist[:, 12:16, :], scalar1=msk[:, 0:1]), reads=["xhist", "msk"], writes=["xhist"])
                    return

            wbs = {}
            pending = None
            for j in range(16):
                if mode == "light" and j >= 12:
                    break
                wb, wk_ = load_piece(40 + j)
                i = j % 2
                pa = psA[:, j % 4, 0:Tn]
                pk = "psA%d" % (j % 4)
                proj_chunk(wb, wk_, 0, pa, pk, Tn)
                ub, uk = ubuf[i], "ubuf%d" % i
                pl.op("pool", lambda e, ub=ub, j=j: e.tensor_copy(out=ub[:, 0:3], in_=xhist[:, j, :]), reads=["xhist"], writes=[uk])
                pl.op("act", lambda e, ub=ub, pa=pa: e.copy(out=ub[:, 3:3 + Tn], in_=pa), reads=[pk], writes=[uk])
                pl.op("pool", lambda e, ub=ub, j=j: e.tensor_copy(out=xhist[:, j, :], in_=ub[:, Tn:Tn + 3]), reads=[uk], writes=["xhist"])
                c_, ck = cu[i], "cu%d" % i
                if hist_from == "carry":
                    conv_chunk("act", ub, cbw[:, j, :], 4, c_, Tn, [uk, "cbw"], [ck], bias=cbb[:, j:j + 1], bk=["cbb"])
                else:
                    for sidx in range(2):
                        base = sidx * 128
                        pl.op("dve", lambda e, j=j, sidx=sidx, base=base: e.tensor_copy(out=stsb[:, base:base + 3], in_=scb_sb[:, j, sidx, :]), reads=["scb"], writes=["stsb"])
                        pl.op("dve", lambda e, ub=ub, base=base, sidx=sidx: e.tensor_copy(out=stsb[:, base + 3:base + 67], in_=ub[:, 3 + sidx * 64:3 + (sidx + 1) * 64]), reads=[uk], writes=["stsb"])
                        pl.op("dve", lambda e, c_=c_, base=base, sidx=sidx, j=j: e.tensor_scalar_mul(out=c_[:, sidx * 64:(sidx + 1) * 64], in0=stsb[:, base:base + 64], scalar1=cbw[:, j, 0:1]), reads=["stsb", "cbw"], writes=[ck])
                        for k in (1, 2, 3):
                            pl.op("dve", lambda e, c_=c_, base=base, sidx=sidx, j=j, k=k: e.scalar_tensor_tensor(out=c_[:, sidx * 64:(sidx + 1) * 64], in0=stsb[:, base + k:base + 64 + k], scalar=cbw[:, j, k:k + 1], in1=c_[:, sidx * 64:(sidx + 1) * 64], op0=ALU.mult, op1=ALU.add), reads=["stsb", "cbw", ck], writes=[ck])
                        pl.op("dve", lambda e, base=base, sidx=sidx, j=j: e.tensor_copy(out=cbsb[:, j, 1 + sidx, :], in_=stsb[:, base + 64:base + 67]), reads=["stsb"], writes=["cbsb"])
                    pl.op("dve", lambda e, c_=c_, j=j: e.tensor_scalar_add(out=c_[:, 0:Tn], in0=c_[:, 0:Tn], scalar1=cbb[:, j:j + 1]), reads=[ck, "cbb"], writes=[ck])
                if j < 8:
                    dst, dk = xbf[:, j, 0:Tn], "xbf"
                elif j < 12:
                    dst, dk = BT[:, j - 8, 0:Tn], "BT"
                else:
                    dst, dk = CT[:, j - 12, 0:Tn], "CT"

                def stage_b(c_=c_, ck=ck, i=i, dst=dst, dk=dk):
                    pl.op("act", lambda e: e.activation(out=dst, in_=c_[:, 0:Tn], func=AF.Silu), reads=[ck], writes=[dk])
                if pending is not None:
                    pending()
                pending = stage_b
            if pending is not None:
                pending()
            if mode == "full" and hist_from == "carry":
                pl.op("dve", lambda e: e.tensor_copy(out=cbsb[:, :, 0, :], in_=xhist[:]), reads=["xhist"], writes=["cbsb"])
            W = nsub * 16
            SMA = lambda idx: sm[:, idx, 0:W]
            v3 = lambda ap: ap.rearrange("p (b h) -> p b h", h=16)
            for b in range(nsub):
                tsl = slice(b * 128, (b + 1) * 128)
                for kc in range(8):
                    pl.op("pe", lambda e, kc=kc, tsl=tsl, b=b, hb=hcur[0]: e.matmul(psC[:, 256 + b * 16:272 + b * 16], lhsT=hb[:, kc, tsl], rhs=wdt[:, kc, :], start=(kc == 0), stop=(kc == 7)), reads=[hcur[1], "wdt"], writes=["psCd"], token=(kc == 7))
            pl.op("dve", lambda e: e.tensor_tensor(out=v3(SMA(0)), in0=v3(psC[:, 256:256 + W]), in1=bc[:, BC_DTB:BC_DTB + 16].unsqueeze(1).broadcast_to([128, nsub, 16]), op=ALU.add), reads=["psCd", "bc"], writes=["sm0"])
            pl.op("act", lambda e: e.activation(out=SMA(1), in_=SMA(0), func=AF.Abs), reads=["sm0"], writes=["sm1"])
            pl.op("act", lambda e: e.activation(out=SMA(1), in_=SMA(1), func=AF.Exp, scale=-1.0), reads=["sm1"], writes=["sm1"])
            pl.op("act", lambda e: e.activation(out=SMA(1), in_=SMA(1), func=AF.Ln, bias=1.0), reads=["sm1"], writes=["sm1"])
            pl.op("dve", lambda e: e.scalar_tensor_tensor(out=SMA(2), in0=SMA(0), scalar=0.0, in1=SMA(1), op0=ALU.max, op1=ALU.add), reads=["sm0", "sm1"], writes=["sm2"])
            pl.op("dve", lambda e: e.tensor_tensor(out=v3(SMA(3)), in0=v3(SMA(2)), in1=a_bc[:].unsqueeze(1).broadcast_to([128, nsub, 16]), op=ALU.mult), reads=["sm2", "a_bc"], writes=["sm3"])
            pl.op("pe", lambda e: e.matmul(psB[:, 0, 0:W], lhsT=tri, rhs=SMA(3), start=True, stop=True), reads=["cst", "sm3"], writes=["psB0"], token=False)
            pl.op("pe", lambda e: e.matmul(psB[:, 0, 64:64 + W], lhsT=blk, rhs=SMA(3), start=True, stop=True), reads=["cst", "sm3"], writes=["psB0"], token=False)
            pl.op("pe", lambda e: e.matmul(psB[:, 0, 128:128 + W], lhsT=chsel[0], rhs=SMA(3), start=True, stop=True), reads=["cst", "sm3"], writes=["psB0"], token=False)
            pl.op("pe", lambda e: e.matmul(psB[:, 0, 192:192 + W], lhsT=chsel[1], rhs=SMA(3), start=True, stop=True), reads=["cst", "sm3"], writes=["psB0"])
            pl.op("dve", lambda e: e.tensor_copy(out=SMA(4), in_=psB[:, 0, 0:W]), reads=["psB0"], writes=["sm4"])
            pl.op("dve", lambda e: e.tensor_tensor(out=SMA(5), in0=psB[:, 0, 64:64 + W], in1=SMA(4), op=ALU.subtract), reads=["psB0", "sm4"], writes=["sm5"])
            if mode == "light":
                sel0, sel1 = psB[:, 0, 128:128 + W], psB[:, 0, 192:192 + W]
                pl.op("dve", lambda e: e.tensor_copy(out=SMA(7), in_=sel1), reads=["psB0"], writes=["sm7"])
                pl.op("dve", lambda e: e.tensor_tensor(out=SMA(0), in0=sel0, in1=SMA(7), op=ALU.add), reads=["psB0", "sm7"], writes=["sm0"])
                pl.op("dve", lambda e: e.memset(SMA(1), 0.0), writes=["sm1"])
                for b in range(nsub - 2, -1, -1):
                    pl.op("dve", lambda e, b=b: e.tensor_tensor(out=sm[:, 1, b * 16:(b + 1) * 16], in0=sm[:, 1, (b + 1) * 16:(b + 2) * 16], in1=sm[:, 0, (b + 1) * 16:(b + 2) * 16], op=ALU.add), reads=["sm1", "sm0"], writes=["sm1"])
                pl.op("dve", lambda e: e.tensor_tensor(out=cd[:, 1, 0:16], in0=sm[:, 1, 0:16], in1=sm[:, 0, 0:16], op=ALU.add), reads=["sm1", "sm0"], writes=["cd"])
                pl.op("act", lambda e: e.activation(out=cd[:, 0, 0:16], in_=cd[:, 1, 0:16], func=AF.Exp), reads=["cd"], writes=["cd"])
                pl.op("dve", lambda e: e.scalar_tensor_tensor(out=SMA(1), in0=SMA(7), scalar=chsel[0][:, 0:1], in1=SMA(1), op0=ALU.mult, op1=ALU.add), reads=["sm7", "cst", "sm1"], writes=["sm1"])
                pl.op("dve", lambda e: e.tensor_tensor(out=SMA(5), in0=SMA(5), in1=SMA(1), op=ALU.add), reads=["sm5", "sm1"], writes=["sm5"])
            pl.op("act", lambda e: e.activation(out=SMA(5), in_=SMA(5), func=AF.Exp), reads=["sm5"], writes=["sm5"])
            pl.op("dve", lambda e: e.tensor_tensor(out=SMA(6), in0=SMA(5), in1=SMA(2), op=ALU.mult), reads=["sm5", "sm2"], writes=["sm6"])
            if mode != "light":
                pl.op("act", lambda e: e.activation(out=cd[:, 0, 0:W], in_=psB[:, 0, 128:128 + W], func=AF.Exp), reads=["psB0"], writes=["cd"])
                pl.op("act", lambda e: e.activation(out=cd[:, 1, 0:W], in_=psB[:, 0, 192:192 + W], func=AF.Exp), reads=["psB0"], writes=["cd"])
            else:
                if st_mode[0] is not None and st_mode[0][0] == "zero":
                    pl.op("dve", lambda e: e.memset(ST[:], 0.0), writes=["ST"])
                pl.op("pool", lambda e: e.tensor_tensor(out=ST[:].rearrange("p (h q) -> p h q", h=16), in0=ST[:].rearrange("p (h q) -> p h q", h=16), in1=cd[:, 0, 0:16].unsqueeze(2).broadcast_to([128, 16, 64]), op=ALU.mult), reads=["ST", "cd"], writes=["ST"])
            for b in range(nsub):
                tsl = slice(b * 128, (b + 1) * 128)
                SM = lambda idx, b=b: sm[:, idx, b * 16:(b + 1) * 16]
                bc16 = lambda idx, SM=SM: SM(idx).unsqueeze(2).broadcast_to([128, 16, 64])
                b16_2, b16_3, b16_4, b16_6 = bc16(2), bc16(3), bc16(4), bc16(6)
                bview = lambda ap: ap.rearrange("p (h q) -> p h q", h=16)
                if mode == "full":
                    pl.op("dve", lambda e, b16_3=b16_3: e.tensor_tensor(out=bview(rhs1[:]), in0=b16_3, in1=t64.unsqueeze(1).broadcast_to([128, 16, 64]), op=ALU.mult), reads=["sm3", "cst"], writes=["rhs1"])
                    pl.op("dve", lambda e, b16_3=b16_3: e.tensor_copy(out=bview(yo[:]), in_=b16_3), reads=["sm3"], writes=["yo"])
                for j in range(8):
                    pl.op("pe", lambda e, j=j, tsl=tsl: e.transpose(out=psT[:, j * 128:(j + 1) * 128], in_=xbf[:, j, tsl], identity=idb[:]), reads=["xbf", "idb"], writes=["psT"], token=(j == 7))
                bview = lambda ap: ap.rearrange("p (h q) -> p h q", h=16)
                if mode == "full":
                    pl.op("dve", lambda e, b16_2=b16_2: e.tensor_tensor(out=bview(xdt[:]), in0=bview(psT[:, :]), in1=b16_2, op=ALU.mult), reads=["psT", "sm2"], writes=["xdt"])
                if mode == "light":
                    xw_, xwk = ((xw, "xw"), (xdt, "xdt"))[b % 2]
                    bt_, btk = ((btok[:], "btok"), (Mt[:, 0:512], "Mt"))[b % 2]
                    bps, bpk = psC[:, 0:256].bitcast(BF16), "psCb"
                else:
                    xw_, xwk, bt_, btk, bps, bpk = xw, "xw", btok[:], "btok", psT[:, 0:512], "psT"
                pl.op("dve", lambda e, b16_6=b16_6, xw_=xw_: e.tensor_tensor(out=bview(xw_[:]), in0=bview(psT[:, :]), in1=b16_6, op=ALU.mult), reads=["psT", "sm6"], writes=[xwk])
                for g in range(4):
                    pl.op("pe", lambda e, g=g, tsl=tsl, bps=bps: e.transpose(out=bps[:, g * 128:(g + 1) * 128], in_=BT[:, g, tsl], identity=idb[:]), reads=["BT", "idb"], writes=[bpk], token=(g == 3))
                pl.op("act", lambda e, bt_=bt_, bps=bps: e.copy(out=bt_, in_=bps), reads=[bpk], writes=[btk])

                if mode == "full":
                    for g in range(4):
                        for c in range(2):
                            cs = slice(b * 128 + c * 64, b * 128 + (c + 1) * 64)
                            pl.op("pe", lambda e, g=g, c=c, cs=cs: e.matmul(psC[c * 64:(c + 1) * 64, g * 64:(g + 1) * 64], lhsT=BT[:, g, cs], rhs=CT[:, g, cs], start=True, stop=True), reads=["BT", "CT"], writes=["psCb"], token=(g == 3 and c == 1))
                    pl.op("dve", lambda e: e.tensor_tensor(out=cbtm[:], in0=psC[:, 0:256].rearrange("p (g i) -> p g i", g=4), in1=t64.unsqueeze(1).broadcast_to([128, 4, 64]), op=ALU.mult), reads=["psCb", "cst"], writes=["cbtm"])
                    for hh in range(2):
                        pl.op("pe", lambda e, hh=hh: e.matmul(psA[:, hh, :], lhsT=blk, rhs=rhs1[:, hh * 512:(hh + 1) * 512], start=True, stop=True), reads=["cst", "rhs1"], writes=["psA%d" % hh])
                    pl.op("dve", lambda e, b16_4=b16_4: e.tensor_tensor(out=bview(segs[:]), in0=psA[:, 0:2, :].rearrange("p a (h q) -> p (a h) q", q=64), in1=b16_4, op=ALU.subtract), reads=["psA0", "psA1", "sm4"], writes=["segs"])
                    pl.op("dve", lambda e: e.tensor_scalar_min(out=segs[:], in0=segs[:], scalar1=0.0), reads=["segs"], writes=["segs"])
                    pl.op("act", lambda e: e.activation(out=segs[:], in_=segs[:], func=AF.Exp), reads=["segs"], writes=["segs"])
                    for g in range(4):
                        pl.op("dve", lambda e, g=g: e.scalar_tensor_tensor(out=Mt[:, g * 256:(g + 1) * 256].rearrange("p (r q) -> p r q", r=4), in0=segs[:, g * 256:(g + 1) * 256].rearrange("p (r q) -> p r q", r=4), scalar=1.0,
                                                                           in1=cbtm[:, g, :].unsqueeze(1).broadcast_to([128, 4, 64]), op0=ALU.min, op1=ALU.mult), reads=["segs", "cbtm"], writes=["Mt"])
                    pl.op("pool", lambda e, tsl=tsl: e.tensor_tensor(out=segs[:].rearrange("p (a t) -> p a t", a=8), in0=xbf[:, :, tsl], in1=pf[:, PF_DSK:PF_DSK + 8].unsqueeze(2).broadcast_to([128, 8, 128]), op=ALU.mult), reads=["xbf", "pf", "segs"], writes=["segs", "segs2"])
                    for a in range(8):
                        pl.op("pe", lambda e, a=a: e.matmul(psA[:, 2 + a // 4, (a % 4) * 128:(a % 4 + 1) * 128], lhsT=yo[:, a * 128:(a + 1) * 128], rhs=tri, start=True, stop=True), reads=["yo", "cst"], writes=["psA%d" % (2 + a // 4)], token=(a % 4 == 3))
                    pl.op("act", lambda e: e.activation(out=eac[:], in_=psA[:, 2:4, :].rearrange("p a q -> p (a q)"), func=AF.Exp), reads=["psA2", "psA3"], writes=["eac"])

                for c in range(2):
                    ch = b * 2 + c
                    cs = slice(b * 128 + c * 64, b * 128 + (c + 1) * 64)
                    ps_ = slice(c * 64, (c + 1) * 64)
                    if mode != "light" and st_mode[ch] is not None:
                        kind, val = st_mode[ch]
                        if kind == "zero":
                            pl.op("dve", lambda e: e.memset(ST[:], 0.0), writes=["ST"])
                        elif kind == "load":
                            pl.dma("sp", lambda e, val=val: e.dma_start(out=ST[:], in_=sst_d[val, :, :]), "l_ST", writes=["ST"])
                        elif kind == "keep":
                            pass
                        if mode == "full":
                            pl.op("act", lambda e: e.copy(out=STb[:], in_=ST[:]), reads=["ST"], writes=["STb"])
                    if mode != "light":
                        pl.op("pool", lambda e, c=c, b=b: e.tensor_tensor(out=bview(ST[:]), in0=bview(ST[:]), in1=cd[:, c, b * 16:(b + 1) * 16].unsqueeze(2).broadcast_to([128, 16, 64]), op=ALU.mult), reads=["ST", "cd"], writes=["ST"])
                    if mode == "full":
                        for a in range(8):
                            for hh in range(2):
                                h = 2 * a + hh
                                pl.op("pe", lambda e, a=a, hh=hh, h=h, c=c, ps_=ps_: e.matmul(psB[hh * 64:(hh + 1) * 64, a // 4, (a % 4) * 128 + c * 64:(a % 4) * 128 + (c + 1) * 64], lhsT=xdt[ps_, h * 64:(h + 1) * 64], rhs=Mt[ps_, h * 64:(h + 1) * 64], start=True, stop=True),
                                      reads=["xdt", "Mt"], writes=["psB%d" % (a // 4)], token=(hh == 1 and a % 4 == 3))
                        for a in range(8):
                            g = a // 2
                            pl.op("pe", lambda e, a=a, g=g, c=c, cs=cs: e.matmul(psA[:, a // 4, (a % 4) * 128 + c * 64:(a % 4) * 128 + (c + 1) * 64], lhsT=STb[:, a * 128:(a + 1) * 128], rhs=CT[:, g, cs], start=True, stop=True),
                                  reads=["STb", "CT"], writes=["psA%d" % (a // 4)], token=(a % 4 == 3))
                    first = (b == 0 and c == 0)
                    last = (b == nsub - 1 and c == 1)
                    for g in range(4):
                        if mode == "light":
                            if c == 0:
                                bk = 2 + g // 2
                                pl.op("pe", lambda e, g=g, bk=bk, b=b, bt_=bt_, xw_=xw_: e.matmul(psA[:, bk, (g % 2) * 256:(g % 2 + 1) * 256], lhsT=bt_[:, g * 128:(g + 1) * 128], rhs=xw_[:, g * 256:(g + 1) * 256], start=(b == 0 and g % 2 == 0), stop=(b == nsub - 1), skip_group_check=True),
                                      reads=[btk, xwk], writes=["psA%d" % bk], token=(g % 2 == 1))
                        else:
                            pl.op("pe", lambda e, g=g, ps_=ps_: e.matmul(psA[:, 2 + g // 2, (g % 2) * 256:(g % 2 + 1) * 256], lhsT=btok[ps_, g * 128:(g + 1) * 128], rhs=xw[ps_, g * 256:(g + 1) * 256], start=True, stop=True),
                                  reads=["btok", "xw"], writes=["psA%d" % (2 + g // 2)], token=(g % 2 == 1))
                    if mode != "light" or last:
                        pl.op("dve", lambda e: e.tensor_tensor(out=ST[:], in0=ST[:], in1=psA[:, 2:4, :].rearrange("p a q -> p (a q)"), op=ALU.add), reads=["ST", "psA2", "psA3"], writes=["ST"])
                    if mode == "full":
                        pl.op("act", lambda e: e.copy(out=STb[:], in_=ST[:]), reads=["ST"], writes=["STb"])
                        if hist_from != "carry":
                            pl.dma("pool", lambda e, ch=ch: e.dma_start(out=so_d[1 + ch, :, :], in_=ST[:]), "s_ST", reads=["ST"], writes=["so%d" % (1 + ch)])
                if next_a is not None and b < len(next_a):
                    next_a[b]()
                if mode == "full":
                    pl.op("dve", lambda e: e.tensor_tensor(out=yo[:], in0=psA[:, 0:2, :].rearrange("p a q -> p (a q)"), in1=eac[:], op=ALU.mult), reads=["psA0", "psA1", "eac"], writes=["yo"])
                    pl.op("dve", lambda e: e.tensor_tensor(out=yo[:], in0=psB[:, :, :].rearrange("p a q -> p (a q)"), in1=yo[:], op=ALU.add), reads=["psB0", "psB1", "yo"], writes=["yo"])
                    v8 = lambda ap: ap.rearrange("p (a t) -> p a t", a=8)
                    pl.op("dve", lambda e: e.tensor_tensor(out=yo[:], in0=yo[:], in1=segs[:], op=ALU.add), reads=["yo", "segs2"], writes=["yo"])
                    pl.op("dve", lambda e, tsl=tsl: e.tensor_tensor(out=v8(yo[:]), in0=v8(yo[:]), in1=sz[:, :, tsl], op=ALU.mult), reads=["yo", "sz"], writes=["yo"])
                    pl.op("act", lambda e: e.activation(out=Mt[:], in_=yo[:], func=AF.Square), reads=["yo"], writes=["Mt"])
                    pl.op("pool", lambda e, tsl=tsl: e.tensor_tensor(out=ycat[:, 8:16, tsl], in0=v8(yo[:]), in1=pf[:, PF_NBW:PF_NBW + 8].unsqueeze(2).broadcast_to([128, 8, 128]), op=ALU.mult), reads=["yo", "pf"], writes=["ycat"])
                    for a in range(8):
                        pl.op("pe", lambda e, a=a, b=b: e.matmul(psC[:, 324 + b:325 + b], lhsT=Mt[:, a * 128:(a + 1) * 128], rhs=ones_bf[:, 0:1], start=(a == 0), stop=(a == 7)), reads=["Mt", "ones"], writes=["psCs"], token=(a == 7))

            if mode != "full":
                return
            pl.op("act", lambda e: e.activation(out=stat[:, 24:32], in_=psC[:, 320:328], func=AF.Ln, scale=1.0 / D, bias=EPS), reads=["psCa", "psCs"], writes=["stat"])
            pl.op("act", lambda e: e.activation(out=stat[:, 24:32], in_=stat[:, 24:32], func=AF.Exp, scale=-0.5), reads=["stat"], writes=["stat"])
            for s in range(nsub):
                tsl = slice(s * 128, (s + 1) * 128)
                for hf in range(2):
                    for part in range(2):
                        for kc in range(8):
                            pl.op("pe", lambda e, hf=hf, part=part, kc=kc, tsl=tsl: e.matmul(psA[:, part * 2 + hf, :], lhsT=ycat[:, part * 8 + kc, tsl], rhs=wout[:, part * 8 + kc, hf * 512:(hf + 1) * 512], start=(kc == 0), stop=(kc == 7)),
                                  reads=["ycat", "wout"], writes=["psA%d" % (part * 2 + hf)], token=(kc == 7))
                xb = xt[s % 2]
                xk = "xt%d" % (s % 2)
                pl.dma("sp", lambda e, xb=xb, s=s: e.dma_start(out=xb[:], in_=xs_d[row0 + s * 128:row0 + (s + 1) * 128, :]), "l_" + xk, writes=[xk])
                ob = ost[s % 2]
                ok = "ost%d" % (s % 2)
                pl.op("act", lambda e, s=s: e.activation(out=o1[:], in_=psA[:, 0:2, :].rearrange("p a q -> p (a q)"), func=AF.Copy, scale=stat[:, 24 + s:25 + s]), reads=["psA0", "psA1", "stat"], writes=["o1"])
                pl.op("dve", lambda e, s=s: e.scalar_tensor_tensor(out=o1[:], in0=psA[:, 2:4, :].rearrange("p a q -> p (a q)"), scalar=stat[:, 28 + s:29 + s], in1=o1[:], op0=ALU.mult, op1=ALU.add), reads=["psA2", "psA3", "stat", "o1"], writes=["o1"])
                pl.op("dve", lambda e: e.tensor_tensor(out=o1[:], in0=o1[:], in1=gate[:, gidx, :], op=ALU.mult), reads=["o1", "gate"], writes=["o1"])
                pl.op("dve", lambda e, xb=xb: e.tensor_tensor(out=o1[:], in0=o1[:], in1=xb[:], op=ALU.add), reads=["o1", xk], writes=["o1"])
                pl.op("dve", lambda e, s=s: e.memset(stat[:, 32 + s:33 + s], 0.0), writes=["stat"])
                pl.op("act", lambda e, s=s: e.activation(out=sqj[:], in_=o1[:], func=AF.Square, accum_out=stat[:, 32 + s:33 + s]), reads=["o1", "stat"], writes=["sqj", "stat"])
                pl.op("act", lambda e, s=s: e.activation(out=stat[:, 36 + s:37 + s], in_=stat[:, 32 + s:33 + s], func=AF.Ln, scale=1.0 / D, bias=EPS), reads=["stat"], writes=["stat"])
                pl.op("act", lambda e, s=s: e.activation(out=stat[:, 36 + s:37 + s], in_=stat[:, 36 + s:37 + s], func=AF.Exp, scale=-0.5), reads=["stat"], writes=["stat"])
                pl.op("dve", lambda e, s=s, ob=ob: e.scalar_tensor_tensor(out=ob[:], in0=o1[:], scalar=stat[:, 36 + s:37 + s], in1=bc[:, BC_NF:BC_NF + D], op0=ALU.mult, op1=ALU.mult), reads=["o1", "stat", "bc"], writes=[ok])
                pl.dma("pool", lambda e, ob=ob, s=s: e.dma_start(out=y_d[yrow0 + s * 128:yrow0 + (s + 1) * 128, :], in_=ob[:]), "s_" + ok, reads=[ok], writes=["y"])

        ones_bf = SB("ones_bf", [128, 8], BF16)
        pl.op("dve", lambda e: e.memset(ones_bf[:], 1.0), writes=["ones"])
        sca_sb = SB("sca_sb", [128, 8, 2, 2])
        scb_sb = SB("scb_sb", [128, 16, 2, 3])
        ld("l_sca", sca_sb[:].rearrange("p a b c -> p (a b c)"), sca_d[:, :], "sca")
        ld("l_scb", scb_sb[:].rearrange("p a b c -> p (a b c)"), scb_d[:, :], "scb")
        pl.op("dve", lambda e: e.memset(uhist[:], 0.0), writes=["uhist"])
        pl.op("dve", lambda e: e.memset(xhist[:], 0.0), writes=["xhist"])
        pl.op("dve", lambda e: e.memset(atot[:], 0.0), writes=["atot"])
        pslot = [(0, T, 0)]
        hsel = ((hT, "hT"), (ycat[:, 0:8, :], "ycat"))
        for n in range(NLSEG * NT):
            stm = [None] * (T // 64)
            if n == 0:
                stm[0] = ("zero", 0)
            nxa = None
            if n + 1 < NLSEG * NT:
                r1 = 128 + (n + 1) * T
                hb1, hk1 = hsel[(n + 1) % 2]
                nxa = [(lambda r1=r1, p0=p0, hb1=hb1, hk1=hk1: step_a_pair(r1, p0, T, pslot, hb1, hk1)) for p0 in (0, 2)]
            tile(128 + n * T, T, pslot, "light", 0, 0, "carry", stm, None, h_idx=n % 2, pre_a=(n > 0), next_a=nxa)
            if n % NT == NT - 1:
                m = n // NT
                pl.op("dve", lambda e, m=m: e.tensor_scalar_mul(out=ST[:], in0=ST[:], scalar1=msk[:, 1 + m:2 + m]), reads=["ST", "msk"], writes=["ST"])
                pl.op("dve", lambda e, m=m: e.tensor_scalar_mul(out=xhist[:].rearrange("p a b -> p (a b)"), in0=xhist[:].rearrange("p a b -> p (a b)"), scalar1=msk[:, 1 + m:2 + m]), reads=["xhist", "msk"], writes=["xhist"])
        tile(0, 128, [(0, 128, 0)], "halo", 0, 0, "carry", [None, None], None)
        for n in range(NT):
            stm = [None] * (T // 64)
            if n == 0:
                stm[0] = ("keep", 0)
            tile(128 + LROWS + n * T, T, pslot, "full", 0, n * T, "carry", stm, 0)
        pl.dma("pool", lambda e: e.dma_start(out=so_d[0, :, :], in_=ST[:]), "s_ST", reads=["ST"], writes=["so0"])
        tile(128 + LROWS + SEGLEN, 128, [(0, 64, 1), (64, 128, 2)], "full", 1, SEGLEN, "state", [("load", 0), ("load", 1)], 1)
        pl.dma("pool", lambda e: e.dma_start(out=ca_d[:, :], in_=casb[:].rearrange("p a b c -> p (a b c)")), "s_ca", reads=["casb"], writes=["ca"])
        pl.dma("pool", lambda e: e.dma_start(out=cb_d[:, :], in_=cbsb[:].rearrange("p a b c -> p (a b c)")), "s_cb", reads=["cbsb"], writes=["cb"])
        if debug:
            dbg_list = [("xbf", xbf[:].rearrange("p a b -> p (a b)"), 8 * T, BF16), ("BT", BT[:].rearrange("p a b -> p (a b)"), 4 * T, BF16),
                        ("CT", CT[:].rearrange("p a b -> p (a b)"), 4 * T, BF16), ("sm", sm[:].rearrange("p a b -> p (a b)"), 256, F32),
                        ("ycat", ycat[:].rearrange("p a b -> p (a b)"), 16 * T, BF16), ("yo", yo[:], 1024, F32), ("xw", xw[:], 1024, BF16),
                        ("xdt", xdt[:], 1024, BF16), ("btok", btok[:], 512, BF16), ("Mt", Mt[:], 1024, BF16), ("eac", eac[:], 1024, F32),
                        ("cd", cd[:].rearrange("p a b -> p (a b)"), 32, F32), ("hT", hT[:].rearrange("p a b -> p (a b)"), 8 * T, BF16),
                        ("sz", sz[:].rearrange("p a b -> p (a b)"), 8 * T, BF16), ("stat", stat[:], 64, F32), ("gam", gam[:].rearrange("p a b -> p (a b)"), 24, F32)]
            allk = list(pl.bufs.keys())
            for nm, ap, w, dt_ in dbg_list:
                dd = nc.dram_tensor("dbg_" + nm, [128, w], dt_, kind="ExternalOutput").ap()
                pl.dma("pool", lambda e, dd=dd, ap=ap: e.dma_start(out=dd[:, :], in_=ap), "s_dbg", reads=allk)
        pl.wait_tokens("pool", [(s, c) for s, c in pl.dma_cnt.items() if s.startswith("s_")])
        print("planned instructions:", pl.nins, {e: len(pl.lists[e]) for e in pl.ENGS})
        pl.emit()
    return nc


def _host_consts():
    c = np.zeros((128, NCONST), np.float32)
    k = np.arange(128)
    c[:, C_ID:C_ID + 128] = np.eye(128, dtype=np.float32)
    same = (k[:, None] // 64) == (k[None, :] // 64)
    c[:, C_BLK:C_BLK + 128] = same
    c[:, C_TRI:C_TRI + 128] = same & (k[:, None] <= k[None, :])
    c[:, C_T64:C_T64 + 64] = (k[:, None] % 64) <= np.arange(64)[None, :]
    c[:, C_SEL0:C_SEL0 + 128] = (k[:, None] < 64)
    c[:, C_SEL1:C_SEL1 + 128] = (k[:, None] >= 64)
    return c


def _fm(v, nchunk):
    return np.ascontiguousarray(np.asarray(v, np.float32).reshape(nchunk, 128).T)


_NC_CACHE = {}


def kernel(x_prompt, x_sample, state_conv_a, state_conv_b, state_ssm, c_prompt, c_sample,
           w_mod, b_mod, norm_in_w, w_in, conv_a_w, norm_a_w, conv_b_w, conv_b_b,
           dt_bias, a_log, d_skip, norm_b_w, w_out, norm_f_w, _two_phase=True, _debug=False):
    f = lambda a: np.ascontiguousarray(np.asarray(a, np.float32))
    x_prompt, x_sample = f(x_prompt), f(x_sample)
    state_conv_a, state_conv_b, state_ssm = f(state_conv_a), f(state_conv_b), f(state_ssm)
    c_prompt, c_sample = f(c_prompt), f(c_sample)
    w_mod, b_mod, w_in, w_out = f(w_mod)[0], f(b_mod)[0], f(w_in)[0], f(w_out)[0]
    pf = np.zeros((128, NPF), np.float32)
    pf[:, PF_NIN:PF_NIN + 8] = _fm(f(norm_in_w)[0], 8)
    caw = f(conv_a_w)[0]
    pf[:, PF_CAW:PF_CAW + 24] = np.stack([_fm(caw[k], 8) for k in range(3)], axis=2).reshape(128, 24)
    pf[:, PF_NAW:PF_NAW + 8] = _fm(f(norm_a_w)[0], 8)
    cbw = f(conv_b_w)[0]
    pf[:, PF_CBW:PF_CBW + 64] = np.stack([_fm(cbw[k], 16) for k in range(4)], axis=2).reshape(128, 64)
    pf[:, PF_CBB:PF_CBB + 16] = _fm(f(conv_b_b)[0], 16)
    pf[:, PF_NBW:PF_NBW + 8] = _fm(f(norm_b_w)[0], 8)
    pf[:, PF_DSK:PF_DSK + 8] = _fm(np.repeat(f(d_skip)[0], 64), 8)
    pf[:, PF_BSH:PF_BSH + 8] = _fm(b_mod[0:D], 8)
    pf[:, PF_BSC:PF_BSC + 8] = _fm(b_mod[D:2 * D], 8)
    bcv = np.zeros((128, NBC), np.float32)
    bcv[:, BC_NF:BC_NF + D] = f(norm_f_w)[None, :]
    bcv[:, BC_BG:BC_BG + D] = b_mod[None, 2 * D:3 * D]
    bcv[:, BC_DTB:BC_DTB + 16] = f(dt_bias)[0][None, :]
    bcv[:, BC_ALOG:BC_ALOG + 16] = f(a_log)[0][None, :]
    cst = _host_consts()

    in_maps = []
    for k in range(NCORES):
        seq, seg = k // 4, k % 4
        start = seg * SEGLEN
        xs = np.zeros((XROWS, D), np.float32)
        if seg > 0:
            xs[0:128] = x_prompt[seq, start - 128:start]
        if seg > 0 and LROWS >= start:
            xs[128 + LROWS - start:128 + LROWS] = x_prompt[seq, 0:start]
        xs[128 + LROWS:128 + LROWS + SEGLEN] = x_prompt[seq, start:start + SEGLEN]
        xs[128 + LROWS + SEGLEN:] = x_sample[2 * k:2 * k + 2].reshape(128, D)
        cs = [c_prompt[seq], c_sample[2 * k], c_sample[2 * k + 1]]
        cT = np.stack([_fm(c, 8) for c in cs], axis=2).reshape(128, 24)
        cbc = np.zeros((2, 128, 8, 128), np.float32)
        cbc[0] = _fm(cs[0], 8)[:, :, None]
        cbc[1, :, :, 0:64] = _fm(cs[1], 8)[:, :, None]
        cbc[1, :, :, 64:128] = _fm(cs[2], 8)[:, :, None]
        sca = state_conv_a[0, 2 * k:2 * k + 2]
        sca = sca.reshape(2, 2, 8, 128).transpose(3, 2, 0, 1).reshape(128, 32)
        scb = state_conv_b[0, 2 * k:2 * k + 2]
        scb = scb.reshape(2, 3, 16, 128).transpose(3, 2, 0, 1).reshape(128, 96)
        sst = state_ssm[0, 2 * k:2 * k + 2]
        sst = sst.reshape(2, 1024, 128).transpose(0, 2, 1)
        msk = np.zeros((128, 16), np.float32)
        msk[:, 0] = 1.0 if seg > 0 else 0.0
        for m in range(NLSEG):
            msk[:, 1 + m] = 0.0 if m < NLSEG - seg else 1.0
        in_maps.append({
            "xs": xs, "w_mod": w_mod, "w_in": w_in, "w_out": w_out, "pf": pf, "bc": bcv, "cst": cst,
            "cT": np.ascontiguousarray(cT), "cbc": np.ascontiguousarray(cbc.reshape(2, 128, 1024)),
            "sca": np.ascontiguousarray(sca), "scb": np.ascontiguousarray(scb),
            "sst": np.ascontiguousarray(sst), "msk": msk,
        })
    key = (bool(_two_phase), bool(_debug))
    if key not in _NC_CACHE:
        _NC_CACHE[key] = build_nc(two_phase=key[0], debug=key[1])
    nc = _NC_CACHE[key]
    res = run_bass_kernel_spmd(nc, in_maps, core_ids=list(range(NCORES)))
    R = res.results
    if _debug:
        kernel.last_results = R
    y_prompt = np.zeros((2, SEQ, D), np.float32)
    y_sample = np.zeros((16, 64, D), np.float32)
    ca_p = np.zeros((1, 2, 2, D), np.float32)
    cb_p = np.zeros((1, 2, 3, 2 * D), np.float32)
    ss_p = np.zeros((1, 2, 16, 64, 128), np.float32)
    ca_s = np.zeros((1, 16, 2, D), np.float32)
    cb_s = np.zeros((1, 16, 3, 2 * D), np.float32)
    ss_s = np.zeros((1, 16, 16, 64, 128), np.float32)
    for k in range(NCORES):
        seq, seg = k // 4, k % 4
        r = R[k]
        y_prompt[seq, seg * SEGLEN:(seg + 1) * SEGLEN] = r["y"][0:SEGLEN]
        y_sample[2 * k:2 * k + 2] = r["y"][SEGLEN:].reshape(2, 64, D)
        ca = r["ca"].reshape(128, 8, 3, 2)
        cb = r["cb"].reshape(128, 16, 3, 3)
        so = r["so"]
        for s in range(2):
            ca_s[0, 2 * k + s] = ca[:, :, 1 + s, :].transpose(2, 1, 0).reshape(2, D)
            cb_s[0, 2 * k + s] = cb[:, :, 1 + s, :].transpose(2, 1, 0).reshape(3, 2 * D)
            ss_s[0, 2 * k + s] = so[1 + s].T.reshape(16, 64, 128)
        if seg == 3:
            ca_p[0, seq] = ca[:, :, 0, :].transpose(2, 1, 0).reshape(2, D)
            cb_p[0, seq] = cb[:, :, 0, :].transpose(2, 1, 0).reshape(3, 2 * D)
            ss_p[0, seq] = so[0].T.reshape(16, 64, 128)
    return (y_prompt, y_sample, ca_p, cb_p, ss_p, ca_s, cb_s, ss_s)
```

```python
import contextlib
import numpy as np
import concourse.bass as bass
import concourse.mybir as mybir
from concourse.bass_utils import run_bass_kernel_spmd

F32 = mybir.dt.float32
BF16 = mybir.dt.bfloat16
ALU = mybir.AluOpType
AF = mybir.ActivationFunctionType

NCORES = 8
D = 1024
SEQ = 16384
SEGLEN = 4096
T = 512
NT = SEGLEN // T
DIN = 7184
PW = 128
NPIECE = 56
NWB = 12
EPS = 1e-5
NLSEG = 3
LROWS = NLSEG * SEGLEN
XROWS = 128 + LROWS + SEGLEN + 128
YROWS = SEGLEN + 128

PF_NIN = 0
PF_CAW = 8
PF_NAW = 32
PF_CBW = 40
PF_CBB = 104
PF_NBW = 120
PF_DSK = 128
PF_BSH = 136
PF_BSC = 144
NPF = 152
BC_NF = 0
BC_BG = 1024
BC_DTB = 2048
BC_ALOG = 2064
NBC = 2080
C_ID = 0
C_BLK = 128
C_TRI = 256
C_T64 = 384
C_SEL0 = 448
C_SEL1 = 576
NCONST = 704


class Planner:
    ENGS = ("pe", "act", "dve", "pool", "sp")
    SEM_LIMIT = 30000

    def __init__(self, nc):
        self.nc = nc
        self.lists = {e: [] for e in self.ENGS}
        self.cur = {e: [e + "_0", 0] for e in self.ENGS}
        self.gen = {e: 0 for e in self.ENGS}
        self.sem_names = [e + "_0" for e in self.ENGS]
        self.dma_cnt = {}
        self.waited = {e: {} for e in self.ENGS}
        self.bufs = {}
        self.nins = 0
        self.alias = {}
        self.bank_last = {}

    @staticmethod
    def _bank(k):
        if k.startswith("psC"):
            return "psC"
        if k.startswith("psA") or k.startswith("psB") or k == "psT":
            return k
        return None

    def _bank_deps(self, eng, keys, need):
        banks = set(b for b in (self._bank(k) for k in keys) if b)
        for b in banks:
            for oe, tok in self.bank_last.get(b, {}).items():
                if oe != eng:
                    self._need(eng, tok, need)
        return banks

    def _exp(self, keys):
        out = []
        for k in keys:
            out.extend(self.alias.get(k, (k,)))
        return out

    def _need(self, eng, tok, out):
        if tok is None:
            return
        s, v = tok
        if eng == "pe" and s.startswith("pe_"):
            return
        if self.waited[eng].get(s, 0) >= v:
            return
        if out.get(s, 0) < v:
            out[s] = v

    def _deps(self, eng, reads, writes):
        reads, writes = self._exp(reads), self._exp(writes)
        need = {}
        for k in reads:
            b = self.bufs.get(k)
            if b:
                self._need(eng, b[0], need)
        for k in writes:
            b = self.bufs.get(k)
            if b:
                self._need(eng, b[0], need)
                for t in b[1]:
                    self._need(eng, t, need)
        self._cur_banks = self._bank_deps(eng, list(reads) + list(writes), need)
        self._cur_eng = eng
        for s, v in need.items():
            self.waited[eng][s] = v
            self.lists[eng].append(("wait", s, v))

    def _mark(self, tok, reads, writes):
        reads, writes = self._exp(reads), self._exp(writes)
        for b in self._cur_banks:
            self.bank_last.setdefault(b, {})[self._cur_eng] = tok
        for k in reads:
            b = self.bufs.setdefault(k, [None, []])
            b[1].append(tok)
        for k in writes:
            self.bufs[k] = [tok, []]

    def op(self, eng, fn, reads=(), writes=(), token=True):
        self._deps(eng, reads, writes)
        c = self.cur[eng]
        self.nins += 1
        if token:
            if c[1] >= self.SEM_LIMIT:
                self.gen[eng] += 1
                c[0] = "%s_%d" % (eng, self.gen[eng])
                c[1] = 0
                self.sem_names.append(c[0])
            c[1] += 1
            tok = (c[0], c[1])
            self.lists[eng].append(("ins", fn, c[0], 1))
        else:
            tok = (c[0], c[1] + 1)
            self.lists[eng].append(("ins", fn, None, 0))
        self._mark(tok, reads, writes)
        return tok

    def dma(self, eng, fn, sem, reads=(), writes=()):
        self._deps(eng, reads, writes)
        self.nins += 1
        if sem not in self.dma_cnt:
            self.dma_cnt[sem] = 0
            self.sem_names.append(sem)
        self.dma_cnt[sem] += 16
        tok = (sem, self.dma_cnt[sem])
        self.lists[eng].append(("ins", fn, sem, 16))
        self._mark(tok, reads, writes)
        return tok

    def raw(self, eng, fn, sem, inc, reads=(), writes=()):
        self._deps(eng, reads, writes)
        if sem not in self.dma_cnt:
            self.dma_cnt[sem] = 0
            self.sem_names.append(sem)
        self.dma_cnt[sem] += inc
        tok = (sem, self.dma_cnt[sem])
        self.lists[eng].append(("ins", fn, sem, -inc))
        self._mark(tok, reads, writes)
        return tok

    def wait_tokens(self, eng, toks):
        need = {}
        for t in toks:
            self._need(eng, t, need)
        for s, v in need.items():
            self.waited[eng][s] = v
            self.lists[eng].append(("wait", s, v))

    def emit(self):
        nc = self.nc
        with contextlib.ExitStack() as st:
            sems = {}
            for n in self.sem_names:
                sems[n] = st.enter_context(nc.semaphore(n))
            block = st.enter_context(nc.Block())
            engmap = {"pe": block.tensor, "act": block.scalar, "dve": block.vector,
                      "pool": block.gpsimd, "sp": block.sync}
            for e in self.ENGS:
                lst = self.lists[e]
                if not lst:
                    continue

                def body(engobj, lst=lst):
                    for it in lst:
                        if it[0] == "wait":
                            engobj.wait_ge(sems[it[1]], it[2])
                        else:
                            ins = it[1](engobj)
                            if it[2] is not None:
                                if it[3] < 0:
                                    ins.then_inc(sems[it[2]])
                                else:
                                    ins.then_inc(sems[it[2]], it[3])
                engmap[e](body)


def build_nc(two_phase=True, debug=False):
    nc = bass.Bass("TRN2", target_bir_lowering=False)
    dr = lambda n, s, k, d=F32: nc.dram_tensor(n, list(s), d, kind=k)
    xs_d = dr("xs", [XROWS, D], "ExternalInput").ap()
    wmod_d = dr("w_mod", [D, 3 * D], "ExternalInput").ap()
    win_d = dr("w_in", [D, DIN], "ExternalInput").ap()
    wout_d = dr("w_out", [2 * D, D], "ExternalInput").ap()
    pf_d = dr("pf", [128, NPF], "ExternalInput").ap()
    bc_d = dr("bc", [128, NBC], "ExternalInput").ap()
    cst_d = dr("cst", [128, NCONST], "ExternalInput").ap()
    cT_d = dr("cT", [128, 8 * 3], "ExternalInput").ap()
    cbc_d = dr("cbc", [2, 128, 8 * 128], "ExternalInput").ap()
    sca_d = dr("sca", [128, 8 * 2 * 2], "ExternalInput").ap()
    scb_d = dr("scb", [128, 16 * 2 * 3], "ExternalInput").ap()
    sst_d = dr("sst", [2, 128, D], "ExternalInput").ap()
    msk_d = dr("msk", [128, 16], "ExternalInput").ap()
    y_d = dr("y", [YROWS, D], "ExternalOutput").ap()
    ca_d = dr("ca", [128, 8 * 3 * 2], "ExternalOutput").ap()
    cb_d = dr("cb", [128, 16 * 3 * 3], "ExternalOutput").ap()
    so_d = dr("so", [3, 128, D], "ExternalOutput").ap()
    winbf_d = nc.dram_tensor("winbf", [NPIECE, 128, 8 * PW], BF16)

    pl = Planner(nc)
    pl.alias = {"stg0": ("rhs1", "segs", "eac", "yo"), "stg1": ("stsb", "o1", "ost0", "ost1"), "sqj": ("Mt",)}
    with contextlib.ExitStack() as st:
        def SB(name, shape, dt=F32):
            return st.enter_context(nc.sbuf_tensor("sb_" + name, list(shape), dt))

        cst = SB("cst", [128, NCONST])
        idb = SB("idb", [128, 128], BF16)
        pf = SB("pf", [128, NPF])
        bc = SB("bc", [128, NBC])
        msk = SB("msk", [128, 16])
        cT = SB("cT", [128, 8, 3])
        gam = SB("gam", [128, 3, 8])
        bet = SB("bet", [128, 3, 8])
        gate = SB("gate", [128, 2, D])
        caw = SB("caw", [128, 8, 3])
        cbw = SB("cbw", [128, 16, 4])
        cbb = SB("cbb", [128, 16])
        a_bc = SB("a_bc", [128, 16])
        wout = SB("wout", [128, 16, D], BF16)
        wdt = SB("wdt", [128, 8, 16], BF16)
        wbuf = [SB("wbuf%d" % i, [128, 8, PW], BF16) for i in range(NWB)]
        big = [SB("big%d" % i, [128, 4096]) for i in range(2)]
        stg = [b[:].rearrange("p (a c) -> p a c", a=8) for b in big]
        xt = [SB("xt%d" % i, [128, D]) for i in range(2)]
        hT = SB("hT", [128, 8, T], BF16)
        ubuf = [SB("ubuf%d" % i, [128, 3 + T]) for i in range(2)]
        hsb = [SB("hsb%d" % i, [128, T]) for i in range(2)]
        cu = [SB("cu%d" % i, [128, T]) for i in range(2)]
        tq = [SB("tq%d" % i, [128, T]) for i in range(2)]
        sq2 = [SB("sq2%d" % i, [128, T], BF16) for i in range(2)]
        uhist = SB("uhist", [128, 8, 3])
        xhist = SB("xhist", [128, 16, 3])
        uhist0 = SB("uhist0", [128, 8, 3])
        xhist0 = SB("xhist0", [128, 16, 3])
        ycat = SB("ycat", [128, 16, T], BF16)
        xbf = SB("xbf", [128, 8, T], BF16)
        BT = SB("BT", [128, 4, T], BF16)
        CT = SB("CT", [128, 4, T], BF16)
        sz = SB("sz", [128, 8, T], BF16)
        xdt = SB("xdt", [128, D], BF16)
        xw = SB("xw", [128, D], BF16)
        btok = SB("btok", [128, 512], BF16)
        rhs1 = big[0][:, 0:1024]
        segs = big[0][:, 1024:2048]
        Mt = SB("Mt", [128, D], BF16)
        sqj = Mt
        eac = big[0][:, 2048:3072]
        yo = big[0][:, 3072:4096]
        cbtm = SB("cbtm", [128, 4, 64])
        sm = SB("sm", [128, 8, 64])
        ST = SB("ST", [128, D])
        STb = SB("STb", [128, D], BF16)
        stsb = big[1][:, 0:1024]
        cd = SB("cd", [128, 2, 64])
        stat = SB("stat", [128, 64])
        ost = [big[1][:, 2048:3072], big[1][:, 3072:4096]]
        o1 = big[1][:, 1024:2048]
        casb = SB("casb", [128, 8, 3, 2])
        cbsb = SB("cbsb", [128, 16, 3, 3])
        atot = SB("atot", [128, 16])
        gsel = SB("gsel", [128, 8, 16])
        print("sbuf remaining after alloc:", nc.sbuf_bytes_remaining)

        psA = st.enter_context(nc.psum_tensor("psA", [128, 4, 512], F32))
        psB = st.enter_context(nc.psum_tensor("psB", [128, 2, 512], F32))
        psC = st.enter_context(nc.psum_tensor("psC", [128, 512], F32))
        psT = st.enter_context(nc.psum_tensor("psT", [128, 1024], BF16))

        ident = cst[:, C_ID:C_ID + 128]
        blk = cst[:, C_BLK:C_BLK + 128]
        tri = cst[:, C_TRI:C_TRI + 128]
        t64 = cst[:, C_T64:C_T64 + 64]
        chsel = [cst[:, C_SEL0:C_SEL0 + 128], cst[:, C_SEL1:C_SEL1 + 128]]

        def pfc(off, j):
            return pf[:, off + j:off + j + 1]

        ld = lambda name, dst, src, key: pl.dma("sp", lambda e: e.dma_start(out=dst, in_=src), name, writes=[key])
        ld("l_cst", cst[:], cst_d[:, :], "cst")
        ld("l_pf", pf[:], pf_d[:, :], "pf")
        ld("l_bc", bc[:], bc_d[:, :], "bc")
        ld("l_msk", msk[:], msk_d[:, :], "msk")
        ld("l_cT", cT[:].rearrange("p a b -> p (a b)"), cT_d[:, :], "cT")
        pl.op("dve", lambda e: e.tensor_copy(out=idb[:], in_=ident), reads=["cst"], writes=["idb"])
        pl.op("dve", lambda e: e.tensor_scalar_mul(out=caw[:].rearrange("p a b -> p (a b)"), in0=pf[:, PF_CAW:PF_CAW + 24], scalar1=1.0), reads=["pf"], writes=["caw"])
        pl.op("dve", lambda e: e.tensor_scalar_mul(out=cbw[:].rearrange("p a b -> p (a b)"), in0=pf[:, PF_CBW:PF_CBW + 64], scalar1=1.0), reads=["pf"], writes=["cbw"])
        pl.op("dve", lambda e: e.tensor_scalar_mul(out=cbb[:], in0=pf[:, PF_CBB:PF_CBB + 16], scalar1=1.0), reads=["pf"], writes=["cbb"])
        pl.op("act", lambda e: e.activation(out=a_bc[:], in_=bc[:, BC_ALOG:BC_ALOG + 16], func=AF.Exp), reads=["bc"], writes=["a_bc"])
        pl.op("dve", lambda e: e.tensor_scalar_mul(out=a_bc[:], in0=a_bc[:], scalar1=-1.0), reads=["a_bc"], writes=["a_bc"])

        wmod_v = wmod_d.rearrange("(kc p) c -> p kc c", p=128)
        for piece in range(6):
            s = stg[piece % 2]
            key = "stg%d" % (piece % 2)
            pl.dma("sp", lambda e, s=s, piece=piece: e.dma_start(out=s[:], in_=wmod_v[:, :, piece * 512:(piece + 1) * 512]), "l_" + key, writes=[key])
            if piece < 4:
                for cc in range(4):
                    j = (piece % 2) * 4 + cc
                    for kc in range(8):
                        pl.op("pe", lambda e, s=s, cc=cc, kc=kc, j=j: e.matmul(psC[:, j * 4:j * 4 + 3], lhsT=s[:, kc, cc * 128:(cc + 1) * 128], rhs=cT[:, kc, :], start=(kc == 0), stop=(kc == 7)),
                              reads=[key, "cT"], writes=["psC"], token=(kc == 7))
                if piece % 2 == 1:
                    src = psC[:, 0:32].rearrange("p (j s) -> p s j", s=4)[:, 0:3, :]
                    if piece == 1:
                        pl.op("dve", lambda e, src=src: e.tensor_tensor(out=bet[:], in0=src, in1=pf[:, PF_BSH:PF_BSH + 8].unsqueeze(1).broadcast_to([128, 3, 8]), op=ALU.add), reads=["psC", "pf"], writes=["bet"])
                    else:
                        pl.op("dve", lambda e, src=src: e.tensor_tensor(out=gam[:], in0=src, in1=pf[:, PF_BSC:PF_BSC + 8].unsqueeze(1).broadcast_to([128, 3, 8]), op=ALU.add), reads=["psC", "pf"], writes=["gam"])
                        pl.op("dve", lambda e: e.scalar_tensor_tensor(out=gam[:], in0=gam[:], scalar=1.0, in1=pf[:, PF_NIN:PF_NIN + 8].unsqueeze(1).broadcast_to([128, 3, 8]), op0=ALU.add, op1=ALU.mult), reads=["gam", "pf"], writes=["gam"])
            else:
                half = piece - 4
                for which in range(2):
                    cb_t = xt[which]
                    if half == 0:
                        pl.dma("sp", lambda e, cb_t=cb_t, which=which: e.dma_start(out=cb_t[:], in_=cbc_d[which, :, :]), "l_xt%d" % which, writes=["xt%d" % which])
                    cbv = cb_t[:].rearrange("p (k m) -> p k m", k=8)
                    for kc in range(8):
                        pl.op("pe", lambda e, s=s, kc=kc, cbv=cbv, which=which: e.matmul(psA[:, which, :], lhsT=cbv[:, kc, :], rhs=s[:, kc, :], start=(kc == 0), stop=(kc == 7)),
                              reads=[key, "xt%d" % which], writes=["psA%d" % which], token=(kc == 7))
                    pl.op("dve", lambda e, which=which, half=half: e.tensor_tensor(out=gate[:, which, half * 512:(half + 1) * 512], in0=psA[:, which, :], in1=bc[:, BC_BG + half * 512:BC_BG + (half + 1) * 512], op=ALU.add),
                          reads=["psA%d" % which, "bc"], writes=["gate"])

        win_v = win_d.rearrange("(kc p) c -> p kc c", p=128)
        wout_v = wout_d.rearrange("(kc p) c -> p kc c", p=128)
        castengs = ["dve", "act"]
        ci = 0

        def cast(dst, src, rk, wk):
            nonlocal ci
            eng = castengs[ci % 2]
            ci += 1
            if eng == "act":
                pl.op("act", lambda e: e.copy(out=dst, in_=src), reads=rk, writes=wk)
            else:
                pl.op(eng, lambda e: e.tensor_copy(out=dst, in_=src), reads=rk, writes=wk)

        porder = [10, 11, 12, 2, 3, 4, 5, 0, 1, 6, 7, 13, 8, 9]
        pl.dma("sp", lambda e: e.dma_start(out=stg[0][:, :, 0:16], in_=win_v[:, :, 7168:7184]), "l_stg0", writes=["stg0"])
        pl.op("dve", lambda e: e.tensor_copy(out=wdt[:], in_=stg[0][:, :, 0:16]), reads=["stg0"], writes=["wdt"])
        wci = 0
        for n, piece in enumerate(porder):
            s = stg[n % 2]
            key = "stg%d" % (n % 2)
            pl.dma("sp", lambda e, s=s, piece=piece: e.dma_start(out=s[:], in_=win_v[:, :, piece * 512:(piece + 1) * 512]), "l_" + key, writes=[key])
            for hp in range(512 // PW):
                wb = wbuf[wci % NWB]
                wkey = "wbuf%d" % (wci % NWB)
                wci += 1
                for kh in range(2):
                    cast(wb[:, kh * 4:(kh + 1) * 4, :], s[:, kh * 4:(kh + 1) * 4, hp * PW:(hp + 1) * PW], [key], [wkey])
                sp_ = (512 // PW) * piece + hp
                pl.dma("pool", lambda e, wb=wb, sp_=sp_: e.dma_start(out=winbf_d[sp_, :, :], in_=wb[:].rearrange("p a b -> p (a b)")), "s_" + wkey, reads=[wkey], writes=["winbf%d" % sp_])
        for n in range(4):
            kg, ch = n // 2, n % 2
            s = stg[n % 2]
            key = "stg%d" % (n % 2)
            pl.dma("sp", lambda e, s=s, kg=kg, ch=ch: e.dma_start(out=s[:], in_=wout_v[:, kg * 8:(kg + 1) * 8, ch * 512:(ch + 1) * 512]), "l_" + key, writes=[key])
            for kh in range(2):
                cast(wout[:, kg * 8 + kh * 4:kg * 8 + (kh + 1) * 4, ch * 512:(ch + 1) * 512], s[:, kh * 4:(kh + 1) * 4, :], [key], ["wout"])

        wcnt = [0]

        def load_piece(piece):
            i = wcnt[0] % NWB
            wcnt[0] += 1
            wb = wbuf[i]
            pl.dma("sp", lambda e: e.dma_start(out=wb[:].rearrange("p a b -> p (a b)"), in_=winbf_d[piece, :, :]), "l_wbuf%d" % i, reads=["winbf%d" % piece], writes=["wbuf%d" % i])
            return wb, "wbuf%d" % i

        hcur = [hT, "hT"]

        def proj_chunk(wb, wkey, cc, ps_ap, pskey, Tn):
            hb, hk = hcur
            for kc in range(8):
                pl.op("pe", lambda e, kc=kc: e.matmul(ps_ap, lhsT=wb[:, kc, cc * 128:(cc + 1) * 128], rhs=hb[:, kc, 0:Tn], start=(kc == 0), stop=(kc == 7)),
                      reads=[wkey, hk], writes=[pskey], token=(kc == 7))

        pref = {}

        def load_x(row0, s):
            xb = xt[s % 2]
            xk = "xt%d" % (s % 2)
            pl.dma("sp", lambda e: e.dma_start(out=xb[:], in_=xs_d[row0 + s * 128:row0 + (s + 1) * 128, :]), "l_" + xk, writes=[xk])

        def step_a(row0, Tn, slots, hb=None, hk="hT"):
            for p0 in range(0, Tn // 128, 2):
                step_a_pair(row0, p0, Tn, slots, hT if hb is None else hb, hk)

        def step_a_pair(row0, p0, Tn, slots, hb, hk):
            nsub = Tn // 128
            if True:
                subs = list(range(p0, min(p0 + 2, nsub)))
                for s in subs:
                    xb = xt[s % 2]
                    xk = "xt%d" % (s % 2)
                    if not pref.pop((row0, s), False):
                        load_x(row0, s)
                    pl.op("pool", lambda e, s=s: e.memset(stat[:, s:s + 1], 0.0), writes=["stat"])
                    pl.op("act", lambda e, xb=xb, s=s: e.activation(out=sqj[:], in_=xb[:], func=AF.Square, accum_out=stat[:, s:s + 1]), reads=[xk, "stat"], writes=["sqj", "stat"])
                    pl.op("act", lambda e, s=s: e.activation(out=stat[:, 8 + s:9 + s], in_=stat[:, s:s + 1], func=AF.Ln, scale=1.0 / D, bias=EPS), reads=["stat"], writes=["stat"])
                    pl.op("act", lambda e, s=s: e.activation(out=stat[:, 16 + s:17 + s], in_=stat[:, 8 + s:9 + s], func=AF.Exp, scale=-0.5), reads=["stat"], writes=["stat"])
                    pl.op("dve", lambda e, xb=xb, s=s: e.tensor_scalar_mul(out=xb[:], in0=xb[:], scalar1=stat[:, 16 + s:17 + s]), reads=[xk, "stat"], writes=[xk])
                w0, w1 = subs[0] * 128, (subs[-1] + 1) * 128
                for kc in range(8):
                    pk = "psB%d" % (kc % 2)
                    for s in subs:
                        xb = xt[s % 2]
                        xk = "xt%d" % (s % 2)
                        pl.op("pe", lambda e, xb=xb, kc=kc, s=s, p0=p0: e.transpose(out=psB[:, kc % 2, (s - p0) * 128:(s - p0 + 1) * 128], in_=xb[:, kc * 128:(kc + 1) * 128], identity=ident), reads=[xk, "cst"], writes=[pk], token=(s == subs[-1]))
                    for (c0, c1, slot) in slots:
                        lo, hi = max(c0, w0), min(c1, w1)
                        if lo >= hi:
                            continue
                        pl.op("act", lambda e, kc=kc, lo=lo, hi=hi, slot=slot, w0=w0: e.activation(out=hb[:, kc, lo:hi], in_=psB[:, kc % 2, lo - w0:hi - w0], func=AF.Identity, scale=gam[:, slot, kc:kc + 1], bias=bet[:, slot, kc:kc + 1]),
                              reads=[pk, "gam", "bet"], writes=[hk])

        def conv_chunk(eng_first, src, wts, nk, dst, Tn, rk, wk, bias=None, bk=()):
            off = 3 - (nk - 1)
            if bias is None:
                pl.op("act", lambda e: e.activation(out=dst[:, 0:Tn], in_=src[:, off:off + Tn], func=AF.Copy, scale=wts[:, 0:1]), reads=rk, writes=wk)
            else:
                pl.op("act", lambda e: e.activation(out=dst[:, 0:Tn], in_=src[:, off:off + Tn], func=AF.Identity, scale=wts[:, 0:1], bias=bias), reads=rk + list(bk), writes=wk)
            for k in range(1, nk):
                pl.op("dve", lambda e, k=k: e.scalar_tensor_tensor(out=dst[:, 0:Tn], in0=src[:, off + k:off + k + Tn], scalar=wts[:, k:k + 1], in1=dst[:, 0:Tn], op0=ALU.mult, op1=ALU.add), reads=rk + wk, writes=wk)

        def tile(row0, Tn, slots, mode, gidx, yrow0, hist_from, st_mode, out_slot, next_row0=None, h_idx=0, pre_a=False, next_a=None):
            nsub = Tn // 128
            nch = Tn // 64
            hcur[0], hcur[1] = ((hT, "hT"), (ycat[:, 0:8, :], "ycat"))[h_idx]
            if not pre_a:
                step_a(row0, Tn, slots, hcur[0], hcur[1])
            light = (mode != "full")
            if mode in ("full", "halo"):
                wbs = {}
                pendA = None
                for j in range(8):
                    i = j % 2
                    need = [8 + j, 16 + j] if mode == "halo" else [8 + j, 16 + j, 0 + j, 24 + j]
                    for pc in need:
                        wbs[pc] = load_piece(pc)
                    cc = 0
                    if j % 2 == 0:
                        (pc_, pck), (ph_, phk), (pb_, pbk) = (psA[:, 0, 0:Tn], "psA0"), (psA[:, 1, 0:Tn], "psA1"), (psA[:, 2, 0:Tn], "psA2")
                    else:
                        (pc_, pck), (ph_, phk), (pb_, pbk) = (psA[:, 0, 0:Tn], "psA0"), (psA[:, 1, 0:Tn], "psA1"), (psA[:, 2, 0:Tn], "psA2")
                    wb, wk_ = wbs[8 + j]
                    proj_chunk(wb, wk_, cc, pc_, pck, Tn)
                    wb, wk_ = wbs[16 + j]
                    proj_chunk(wb, wk_, cc, ph_, phk, Tn)
                    if mode != "halo":
                        wb, wk_ = wbs[0 + j]
                        proj_chunk(wb, wk_, cc, pb_, pbk, Tn)
                        wb, wk_ = wbs[24 + j]
                        proj_chunk(wb, wk_, cc, psA[:, 3, 0:Tn], "psA3", Tn)
                        wbz, wkz = load_piece(32 + j)
                        proj_chunk(wbz, wkz, 0, psB[:, j % 2, 0:Tn], "psB%d" % (j % 2), Tn)
                        if pendA is not None:
                            pendA()
                            pendA = None
                    ub, uk = ubuf[i], "ubuf%d" % i
                    pl.op("act", lambda e, i=i, ph_=ph_: e.copy(out=hsb[i][:, 0:Tn], in_=ph_), reads=[phk], writes=["hsb%d" % i])
                    pl.op("pool", lambda e, ub=ub, j=j: e.tensor_copy(out=ub[:, 0:3], in_=uhist[:, j, :]), reads=["uhist"], writes=[uk])
                    pl.op("dve", lambda e, ub=ub, i=i, pc_=pc_: e.tensor_tensor(out=ub[:, 3:3 + Tn], in0=pc_, in1=hsb[i][:, 0:Tn], op=ALU.mult), reads=[pck, "hsb%d" % i], writes=[uk])
                    pl.op("pool", lambda e, ub=ub, j=j: e.tensor_copy(out=uhist[:, j, :], in_=ub[:, Tn:Tn + 3]), reads=[uk], writes=["uhist"])
                    if mode == "halo":
                        continue
                    pl.op("act", lambda e, i=i: e.activation(out=tq[i][:, 0:Tn], in_=psA[:, 3, 0:Tn], func=AF.Silu), reads=["psA3"], writes=["tq%d" % i])
                    pl.op("dve", lambda e, i=i, pb_=pb_: e.tensor_tensor(out=tq[i][:, 0:Tn], in0=pb_, in1=tq[i][:, 0:Tn], op=ALU.mult), reads=[pbk, "tq%d" % i], writes=["tq%d" % i])
                    c_, ck = cu[i], "cu%d" % i
                    if hist_from == "carry":
                        conv_chunk("dve", ub, caw[:, j, :], 3, c_, Tn, [uk, "caw"], [ck])
                    else:
                        for sidx in range(2):
                            pl.op("dve", lambda e, ub=ub, j=j, sidx=sidx: e.tensor_copy(out=stsb[:, sidx * 128 + 1:sidx * 128 + 3], in_=sca_sb[:, j, sidx, :]), reads=["sca"], writes=["stsb"])
                        for sidx in range(2):
                            base = sidx * 128
                            pl.op("dve", lambda e, ub=ub, base=base, sidx=sidx: e.tensor_copy(out=stsb[:, base + 3:base + 67], in_=ub[:, 3 + sidx * 64:3 + (sidx + 1) * 64]), reads=[uk], writes=["stsb"])
                            off = 1
                            pl.op("dve", lambda e, c_=c_, base=base, sidx=sidx, j=j: e.tensor_scalar_mul(out=c_[:, sidx * 64:(sidx + 1) * 64], in0=stsb[:, base + 1:base + 65], scalar1=caw[:, j, 0:1]), reads=["stsb", "caw"], writes=[ck])
                            for k in (1, 2):
                                pl.op("dve", lambda e, c_=c_, base=base, sidx=sidx, j=j, k=k: e.scalar_tensor_tensor(out=c_[:, sidx * 64:(sidx + 1) * 64], in0=stsb[:, base + 1 + k:base + 65 + k], scalar=caw[:, j, k:k + 1], in1=c_[:, sidx * 64:(sidx + 1) * 64], op0=ALU.mult, op1=ALU.add), reads=["stsb", "caw", ck], writes=[ck])
                            pl.op("dve", lambda e, base=base, sidx=sidx, j=j: e.tensor_copy(out=casb[:, j, 1 + sidx, :], in_=stsb[:, base + 65:base + 67]), reads=["stsb"], writes=["casb"])
                    def stage_b(c_=c_, ck=ck, i=i, j=j):
                        pl.op("dve", lambda e: e.tensor_tensor(out=c_[:, 0:Tn], in0=c_[:, 0:Tn], in1=tq[i][:, 0:Tn], op=ALU.mult), reads=[ck, "tq%d" % i], writes=[ck])
                        pl.op("act", lambda e: e.activation(out=sq2[i][:, 0:Tn], in_=c_[:, 0:Tn], func=AF.Square), reads=[ck], writes=["sq2%d" % i])
                        pl.op("act", lambda e: e.activation(out=ycat[:, j, 0:Tn], in_=c_[:, 0:Tn], func=AF.Copy, scale=pfc(PF_NAW, j)), reads=[ck, "pf"], writes=["ycat"])
                        for s in range(nsub):
                            pl.op("pe", lambda e, s=s: e.matmul(psC[:, 320 + s:321 + s], lhsT=sq2[i][:, s * 128:(s + 1) * 128], rhs=ones_bf[:, 0:1], start=(j == 0 and s == 0), stop=(j == 7), skip_group_check=True),
                                  reads=["sq2%d" % i, "ones"], writes=["psCa"], token=(s == nsub - 1))
                    pl.op("act", lambda e, j=j: e.activation(out=sz[:, j, 0:Tn], in_=psB[:, j % 2, 0:Tn], func=AF.Silu), reads=["psB%d" % (j % 2)], writes=["sz"])
                    pendA = stage_b
                if pendA is not None:
                    pendA()
                if mode == "full" and hist_from == "carry":
                    pl.op("dve", lambda e: e.tensor_copy(out=casb[:, :, 0, :], in_=uhist[:, :, 1:3]), reads=["uhist"], writes=["casb"])
                if mode == "halo":
                    pl.op("dve", lambda e: e.tensor_scalar_mul(out=uhist[:].rearrange("p a b -> p (a b)"), in0=uhist[:].rearrange("p a b -> p (a b)"), scalar1=msk[:, 0:1]), reads=["uhist", "msk"], writes=["uhist"])
                    wbs = {}
                    for j in range(12, 16):
                        wb, wk_ = load_piece(40 + j)
                        pa = psA[:, j % 4, 0:Tn]
                        pk = "psA%d" % (j % 4)
                        proj_chunk(wb, wk_, 0, pa, pk, Tn)
                        pl.op("act", lambda e, j=j, pa=pa: e.copy(out=xhist[:, j, :], in_=pa[:, Tn - 3:Tn]), reads=[pk], writes=["xhist"])
                    pl.op("dve", lambda e: e.tensor_scalar_mul(out=xhist[:, 12:16, :], in0=xhist[:, 12:16, :], scalar1=msk[:, 0:1]), reads=["xhist", "msk"], writes=["xhist"])
                    return

            wbs = {}
            pending = None
            for j in range(16):
                if mode == "light" and j >= 12:
                    break
                wb, wk_ = load_piece(40 + j)
                i = j % 2
                pa = psA[:, j % 4, 0:Tn]
                pk = "psA%d" % (j % 4)
                proj_chunk(wb, wk_, 0, pa, pk, Tn)
                ub, uk = ubuf[i], "ubuf%d" % i
                pl.op("pool", lambda e, ub=ub, j=j: e.tensor_copy(out=ub[:, 0:3], in_=xhist[:, j, :]), reads=["xhist"], writes=[uk])
                pl.op("act", lambda e, ub=ub, pa=pa: e.copy(out=ub[:, 3:3 + Tn], in_=pa), reads=[pk], writes=[uk])
                pl.op("pool", lambda e, ub=ub, j=j: e.tensor_copy(out=xhist[:, j, :], in_=ub[:, Tn:Tn + 3]), reads=[uk], writes=["xhist"])
                c_, ck = cu[i], "cu%d" % i
                if hist_from == "carry":
                    conv_chunk("act", ub, cbw[:, j, :], 4, c_, Tn, [uk, "cbw"], [ck], bias=cbb[:, j:j + 1], bk=["cbb"])
                else:
                    for sidx in range(2):
                        base = sidx * 128
                        pl.op("dve", lambda e, j=j, sidx=sidx, base=base: e.tensor_copy(out=stsb[:, base:base + 3], in_=scb_sb[:, j, sidx, :]), reads=["scb"], writes=["stsb"])
                        pl.op("dve", lambda e, ub=ub, base=base, sidx=sidx: e.tensor_copy(out=stsb[:, base + 3:base + 67], in_=ub[:, 3 + sidx * 64:3 + (sidx + 1) * 64]), reads=[uk], writes=["stsb"])
                        pl.op("dve", lambda e, c_=c_, base=base, sidx=sidx, j=j: e.tensor_scalar_mul(out=c_[:, sidx * 64:(sidx + 1) * 64], in0=stsb[:, base:base + 64], scalar1=cbw[:, j, 0:1]), reads=["stsb", "cbw"], writes=[ck])
                        for k in (1, 2, 3):
                            pl.op("dve", lambda e, c_=c_, base=base, sidx=sidx, j=j, k=k: e.scalar_tensor_tensor(out=c_[:, sidx * 64:(sidx + 1) * 64], in0=stsb[:, base + k:base + 64 + k], scalar=cbw[:, j, k:k + 1], in1=c_[:, sidx * 64:(sidx + 1) * 64], op0=ALU.mult, op1=ALU.add), reads=["stsb", "cbw", ck], writes=[ck])
                        pl.op("dve", lambda e, base=base, sidx=sidx, j=j: e.tensor_copy(out=cbsb[:, j, 1 + sidx, :], in_=stsb[:, base + 64:base + 67]), reads=["stsb"], writes=["cbsb"])
                    pl.op("dve", lambda e, c_=c_, j=j: e.tensor_scalar_add(out=c_[:, 0:Tn], in0=c_[:, 0:Tn], scalar1=cbb[:, j:j + 1]), reads=[ck, "cbb"], writes=[ck])
                if j < 8:
                    dst, dk = xbf[:, j, 0:Tn], "xbf"
                elif j < 12:
                    dst, dk = BT[:, j - 8, 0:Tn], "BT"
                else:
                    dst, dk = CT[:, j - 12, 0:Tn], "CT"

                def stage_b(c_=c_, ck=ck, i=i, dst=dst, dk=dk):
                    pl.op("act", lambda e: e.activation(out=dst, in_=c_[:, 0:Tn], func=AF.Silu), reads=[ck], writes=[dk])
                if pending is not None:
                    pending()
                pending = stage_b
            if pending is not None:
                pending()
            if mode == "full" and hist_from == "carry":
                pl.op("dve", lambda e: e.tensor_copy(out=cbsb[:, :, 0, :], in_=xhist[:]), reads=["xhist"], writes=["cbsb"])
            W = nsub * 16
            SMA = lambda idx: sm[:, idx, 0:W]
            v3 = lambda ap: ap.rearrange("p (b h) -> p b h", h=16)
            for b in range(nsub):
                tsl = slice(b * 128, (b + 1) * 128)
                for kc in range(8):
                    pl.op("pe", lambda e, kc=kc, tsl=tsl, b=b, hb=hcur[0]: e.matmul(psC[:, 256 + b * 16:272 + b * 16], lhsT=hb[:, kc, tsl], rhs=wdt[:, kc, :], start=(kc == 0), stop=(kc == 7)), reads=[hcur[1], "wdt"], writes=["psCd"], token=(kc == 7))
            pl.op("dve", lambda e: e.tensor_tensor(out=v3(SMA(0)), in0=v3(psC[:, 256:256 + W]), in1=bc[:, BC_DTB:BC_DTB + 16].unsqueeze(1).broadcast_to([128, nsub, 16]), op=ALU.add), reads=["psCd", "bc"], writes=["sm0"])
            pl.op("act", lambda e: e.activation(out=SMA(1), in_=SMA(0), func=AF.Abs), reads=["sm0"], writes=["sm1"])
            pl.op("act", lambda e: e.activation(out=SMA(1), in_=SMA(1), func=AF.Exp, scale=-1.0), reads=["sm1"], writes=["sm1"])
            pl.op("act", lambda e: e.activation(out=SMA(1), in_=SMA(1), func=AF.Ln, bias=1.0), reads=["sm1"], writes=["sm1"])
            pl.op("dve", lambda e: e.scalar_tensor_tensor(out=SMA(2), in0=SMA(0), scalar=0.0, in1=SMA(1), op0=ALU.max, op1=ALU.add), reads=["sm0", "sm1"], writes=["sm2"])
            pl.op("dve", lambda e: e.tensor_tensor(out=v3(SMA(3)), in0=v3(SMA(2)), in1=a_bc[:].unsqueeze(1).broadcast_to([128, nsub, 16]), op=ALU.mult), reads=["sm2", "a_bc"], writes=["sm3"])
            pl.op("pe", lambda e: e.matmul(psB[:, 0, 0:W], lhsT=tri, rhs=SMA(3), start=True, stop=True), reads=["cst", "sm3"], writes=["psB0"], token=False)
            pl.op("pe", lambda e: e.matmul(psB[:, 0, 64:64 + W], lhsT=blk, rhs=SMA(3), start=True, stop=True), reads=["cst", "sm3"], writes=["psB0"], token=False)
            pl.op("pe", lambda e: e.matmul(psB[:, 0, 128:128 + W], lhsT=chsel[0], rhs=SMA(3), start=True, stop=True), reads=["cst", "sm3"], writes=["psB0"], token=False)
            pl.op("pe", lambda e: e.matmul(psB[:, 0, 192:192 + W], lhsT=chsel[1], rhs=SMA(3), start=True, stop=True), reads=["cst", "sm3"], writes=["psB0"])
            pl.op("dve", lambda e: e.tensor_copy(out=SMA(4), in_=psB[:, 0, 0:W]), reads=["psB0"], writes=["sm4"])
            pl.op("dve", lambda e: e.tensor_tensor(out=SMA(5), in0=psB[:, 0, 64:64 + W], in1=SMA(4), op=ALU.subtract), reads=["psB0", "sm4"], writes=["sm5"])
            if mode == "light":
                sel0, sel1 = psB[:, 0, 128:128 + W], psB[:, 0, 192:192 + W]
                pl.op("dve", lambda e: e.tensor_copy(out=SMA(7), in_=sel1), reads=["psB0"], writes=["sm7"])
                pl.op("dve", lambda e: e.tensor_tensor(out=SMA(0), in0=sel0, in1=SMA(7), op=ALU.add), reads=["psB0", "sm7"], writes=["sm0"])
                pl.op("dve", lambda e: e.memset(SMA(1), 0.0), writes=["sm1"])
                for b in range(nsub - 2, -1, -1):
                    pl.op("dve", lambda e, b=b: e.tensor_tensor(out=sm[:, 1, b * 16:(b + 1) * 16], in0=sm[:, 1, (b + 1) * 16:(b + 2) * 16], in1=sm[:, 0, (b + 1) * 16:(b + 2) * 16], op=ALU.add), reads=["sm1", "sm0"], writes=["sm1"])
                pl.op("dve", lambda e: e.tensor_tensor(out=cd[:, 1, 0:16], in0=sm[:, 1, 0:16], in1=sm[:, 0, 0:16], op=ALU.add), reads=["sm1", "sm0"], writes=["cd"])
                pl.op("act", lambda e: e.activation(out=cd[:, 0, 0:16], in_=cd[:, 1, 0:16], func=AF.Exp), reads=["cd"], writes=["cd"])
                pl.op("dve", lambda e: e.scalar_tensor_tensor(out=SMA(1), in0=SMA(7), scalar=chsel[0][:, 0:1], in1=SMA(1), op0=ALU.mult, op1=ALU.add), reads=["sm7", "cst", "sm1"], writes=["sm1"])
                pl.op("dve", lambda e: e.tensor_tensor(out=SMA(5), in0=SMA(5), in1=SMA(1), op=ALU.add), reads=["sm5", "sm1"], writes=["sm5"])
            pl.op("act", lambda e: e.activation(out=SMA(5), in_=SMA(5), func=AF.Exp), reads=["sm5"], writes=["sm5"])
            pl.op("dve", lambda e: e.tensor_tensor(out=SMA(6), in0=SMA(5), in1=SMA(2), op=ALU.mult), reads=["sm5", "sm2"], writes=["sm6"])
            if mode != "light":
                pl.op("act", lambda e: e.activation(out=cd[:, 0, 0:W], in_=psB[:, 0, 128:128 + W], func=AF.Exp), reads=["psB0"], writes=["cd"])
                pl.op("act", lambda e: e.activation(out=cd[:, 1, 0:W], in_=psB[:, 0, 192:192 + W], func=AF.Exp), reads=["psB0"], writes=["cd"])
            else:
                if st_mode[0] is not None and st_mode[0][0] == "zero":
                    pl.op("dve", lambda e: e.memset(ST[:], 0.0), writes=["ST"])
                pl.op("pool", lambda e: e.tensor_tensor(out=ST[:].rearrange("p (h q) -> p h q", h=16), in0=ST[:].rearrange("p (h q) -> p h q", h=16), in1=cd[:, 0, 0:16].unsqueeze(2).broadcast_to([128, 16, 64]), op=ALU.mult), reads=["ST", "cd"], writes=["ST"])
            for b in range(nsub):
                tsl = slice(b * 128, (b + 1) * 128)
                SM = lambda idx, b=b: sm[:, idx, b * 16:(b + 1) * 16]
                bc16 = lambda idx, SM=SM: SM(idx).unsqueeze(2).broadcast_to([128, 16, 64])
                b16_2, b16_3, b16_4, b16_6 = bc16(2), bc16(3), bc16(4), bc16(6)
                bview = lambda ap: ap.rearrange("p (h q) -> p h q", h=16)
                if mode == "full":
                    pl.op("dve", lambda e, b16_3=b16_3: e.tensor_tensor(out=bview(rhs1[:]), in0=b16_3, in1=t64.unsqueeze(1).broadcast_to([128, 16, 64]), op=ALU.mult), reads=["sm3", "cst"], writes=["rhs1"])
                    pl.op("dve", lambda e, b16_3=b16_3: e.tensor_copy(out=bview(yo[:]), in_=b16_3), reads=["sm3"], writes=["yo"])
                for j in range(8):
                    pl.op("pe", lambda e, j=j, tsl=tsl: e.transpose(out=psT[:, j * 128:(j + 1) * 128], in_=xbf[:, j, tsl], identity=idb[:]), reads=["xbf", "idb"], writes=["psT"], token=(j == 7))
                bview = lambda ap: ap.rearrange("p (h q) -> p h q", h=16)
                if mode == "full":
                    pl.op("dve", lambda e, b16_2=b16_2: e.tensor_tensor(out=bview(xdt[:]), in0=bview(psT[:, :]), in1=b16_2, op=ALU.mult), reads=["psT", "sm2"], writes=["xdt"])
                if mode == "light":
                    xw_, xwk = ((xw, "xw"), (xdt, "xdt"))[b % 2]
                    bt_, btk = ((btok[:], "btok"), (Mt[:, 0:512], "Mt"))[b % 2]
                    bps, bpk = psC[:, 0:256].bitcast(BF16), "psCb"
                else:
                    xw_, xwk, bt_, btk, bps, bpk = xw, "xw", btok[:], "btok", psT[:, 0:512], "psT"
                pl.op("dve", lambda e, b16_6=b16_6, xw_=xw_: e.tensor_tensor(out=bview(xw_[:]), in0=bview(psT[:, :]), in1=b16_6, op=ALU.mult), reads=["psT", "sm6"], writes=[xwk])
                for g in range(4):
                    pl.op("pe", lambda e, g=g, tsl=tsl, bps=bps: e.transpose(out=bps[:, g * 128:(g + 1) * 128], in_=BT[:, g, tsl], identity=idb[:]), reads=["BT", "idb"], writes=[bpk], token=(g == 3))
                pl.op("act", lambda e, bt_=bt_, bps=bps: e.copy(out=bt_, in_=bps), reads=[bpk], writes=[btk])

                if mode == "full":
                    for g in range(4):
                        for c in range(2):
                            cs = slice(b * 128 + c * 64, b * 128 + (c + 1) * 64)
                            pl.op("pe", lambda e, g=g, c=c, cs=cs: e.matmul(psC[c * 64:(c + 1) * 64, g * 64:(g + 1) * 64], lhsT=BT[:, g, cs], rhs=CT[:, g, cs], start=True, stop=True), reads=["BT", "CT"], writes=["psCb"], token=(g == 3 and c == 1))
                    pl.op("dve", lambda e: e.tensor_tensor(out=cbtm[:], in0=psC[:, 0:256].rearrange("p (g i) -> p g i", g=4), in1=t64.unsqueeze(1).broadcast_to([128, 4, 64]), op=ALU.mult), reads=["psCb", "cst"], writes=["cbtm"])
                    for hh in range(2):
                        pl.op("pe", lambda e, hh=hh: e.matmul(psA[:, hh, :], lhsT=blk, rhs=rhs1[:, hh * 512:(hh + 1) * 512], start=True, stop=True), reads=["cst", "rhs1"], writes=["psA%d" % hh])
                    pl.op("dve", lambda e, b16_4=b16_4: e.tensor_tensor(out=bview(segs[:]), in0=psA[:, 0:2, :].rearrange("p a (h q) -> p (a h) q", q=64), in1=b16_4, op=ALU.subtract), reads=["psA0", "psA1", "sm4"], writes=["segs"])
                    pl.op("dve", lambda e: e.tensor_scalar_min(out=segs[:], in0=segs[:], scalar1=0.0), reads=["segs"], writes=["segs"])
                    pl.op("act", lambda e: e.activation(out=segs[:], in_=segs[:], func=AF.Exp), reads=["segs"], writes=["segs"])
                    for g in range(4):
                        pl.op("dve", lambda e, g=g: e.scalar_tensor_tensor(out=Mt[:, g * 256:(g + 1) * 256].rearrange("p (r q) -> p r q", r=4), in0=segs[:, g * 256:(g + 1) * 256].rearrange("p (r q) -> p r q", r=4), scalar=1.0,
                                                                           in1=cbtm[:, g, :].unsqueeze(1).broadcast_to([128, 4, 64]), op0=ALU.min, op1=ALU.mult), reads=["segs", "cbtm"], writes=["Mt"])
                    pl.op("pool", lambda e, tsl=tsl: e.tensor_tensor(out=segs[:].rearrange("p (a t) -> p a t", a=8), in0=xbf[:, :, tsl], in1=pf[:, PF_DSK:PF_DSK + 8].unsqueeze(2).broadcast_to([128, 8, 128]), op=ALU.mult), reads=["xbf", "pf", "segs"], writes=["segs", "segs2"])
                    for a in range(8):
                        pl.op("pe", lambda e, a=a: e.matmul(psA[:, 2 + a // 4, (a % 4) * 128:(a % 4 + 1) * 128], lhsT=yo[:, a * 128:(a + 1) * 128], rhs=tri, start=True, stop=True), reads=["yo", "cst"], writes=["psA%d" % (2 + a // 4)], token=(a % 4 == 3))
                    pl.op("act", lambda e: e.activation(out=eac[:], in_=psA[:, 2:4, :].rearrange("p a q -> p (a q)"), func=AF.Exp), reads=["psA2", "psA3"], writes=["eac"])

                for c in range(2):
                    ch = b * 2 + c
                    cs = slice(b * 128 + c * 64, b * 128 + (c + 1) * 64)
                    ps_ = slice(c * 64, (c + 1) * 64)
                    if mode != "light" and st_mode[ch] is not None:
                        kind, val = st_mode[ch]
                        if kind == "zero":
                            pl.op("dve", lambda e: e.memset(ST[:], 0.0), writes=["ST"])
                        elif kind == "load":
                            pl.dma("sp", lambda e, val=val: e.dma_start(out=ST[:], in_=sst_d[val, :, :]), "l_ST", writes=["ST"])
                        elif kind == "keep":
                            pass
                        if mode == "full":
                            pl.op("act", lambda e: e.copy(out=STb[:], in_=ST[:]), reads=["ST"], writes=["STb"])
                    if mode != "light":
                        pl.op("dve", lambda e, c=c, b=b: e.tensor_tensor(out=bview(ST[:]), in0=bview(ST[:]), in1=cd[:, c, b * 16:(b + 1) * 16].unsqueeze(2).broadcast_to([128, 16, 64]), op=ALU.mult), reads=["ST", "cd"], writes=["ST"])
                    if mode == "full":
                        for a in range(8):
                            for hh in range(2):
                                h = 2 * a + hh
                                pl.op("pe", lambda e, a=a, hh=hh, h=h, c=c, ps_=ps_: e.matmul(psB[hh * 64:(hh + 1) * 64, a // 4, (a % 4) * 128 + c * 64:(a % 4) * 128 + (c + 1) * 64], lhsT=xdt[ps_, h * 64:(h + 1) * 64], rhs=Mt[ps_, h * 64:(h + 1) * 64], start=True, stop=True),
                                      reads=["xdt", "Mt"], writes=["psB%d" % (a // 4)], token=(hh == 1 and a % 4 == 3))
                        for a in range(8):
                            g = a // 2
                            pl.op("pe", lambda e, a=a, g=g, c=c, cs=cs: e.matmul(psA[:, a // 4, (a % 4) * 128 + c * 64:(a % 4) * 128 + (c + 1) * 64], lhsT=STb[:, a * 128:(a + 1) * 128], rhs=CT[:, g, cs], start=True, stop=True),
                                  reads=["STb", "CT"], writes=["psA%d" % (a // 4)], token=(a % 4 == 3))
                    first = (b == 0 and c == 0)
                    last = (b == nsub - 1 and c == 1)
                    for g in range(4):
                        if mode == "light":
                            if c == 0:
                                bk = 2 + g // 2
                                pl.op("pe", lambda e, g=g, bk=bk, b=b, bt_=bt_, xw_=xw_: e.matmul(psA[:, bk, (g % 2) * 256:(g % 2 + 1) * 256], lhsT=bt_[:, g * 128:(g + 1) * 128], rhs=xw_[:, g * 256:(g + 1) * 256], start=(b == 0 and g % 2 == 0), stop=(b == nsub - 1), skip_group_check=True),
                                      reads=[btk, xwk], writes=["psA%d" % bk], token=(g % 2 == 1))
                        else:
                            pl.op("pe", lambda e, g=g, ps_=ps_: e.matmul(psA[:, 2 + g // 2, (g % 2) * 256:(g % 2 + 1) * 256], lhsT=btok[ps_, g * 128:(g + 1) * 128], rhs=xw[ps_, g * 256:(g + 1) * 256], start=True, stop=True),
                                  reads=["btok", "xw"], writes=["psA%d" % (2 + g // 2)], token=(g % 2 == 1))
                    if mode != "light" or last:
                        pl.op("dve", lambda e: e.tensor_tensor(out=ST[:], in0=ST[:], in1=psA[:, 2:4, :].rearrange("p a q -> p (a q)"), op=ALU.add), reads=["ST", "psA2", "psA3"], writes=["ST"])
                    if mode == "full":
                        pl.op("act", lambda e: e.copy(out=STb[:], in_=ST[:]), reads=["ST"], writes=["STb"])
                        if hist_from != "carry":
                            pl.dma("pool", lambda e, ch=ch: e.dma_start(out=so_d[1 + ch, :, :], in_=ST[:]), "s_ST", reads=["ST"], writes=["so%d" % (1 + ch)])
                if next_a is not None and b < len(next_a):
                    next_a[b]()
                if mode == "full":
                    pl.op("dve", lambda e: e.tensor_tensor(out=yo[:], in0=psA[:, 0:2, :].rearrange("p a q -> p (a q)"), in1=eac[:], op=ALU.mult), reads=["psA0", "psA1", "eac"], writes=["yo"])
                    pl.op("dve", lambda e: e.tensor_tensor(out=yo[:], in0=psB[:, :, :].rearrange("p a q -> p (a q)"), in1=yo[:], op=ALU.add), reads=["psB0", "psB1", "yo"], writes=["yo"])
                    v8 = lambda ap: ap.rearrange("p (a t) -> p a t", a=8)
                    pl.op("dve", lambda e: e.tensor_tensor(out=yo[:], in0=yo[:], in1=segs[:], op=ALU.add), reads=["yo", "segs2"], writes=["yo"])
                    pl.op("dve", lambda e, tsl=tsl: e.tensor_tensor(out=v8(yo[:]), in0=v8(yo[:]), in1=sz[:, :, tsl], op=ALU.mult), reads=["yo", "sz"], writes=["yo"])
                    pl.op("act", lambda e: e.activation(out=Mt[:], in_=yo[:], func=AF.Square), reads=["yo"], writes=["Mt"])
                    pl.op("pool", lambda e, tsl=tsl: e.tensor_tensor(out=ycat[:, 8:16, tsl], in0=v8(yo[:]), in1=pf[:, PF_NBW:PF_NBW + 8].unsqueeze(2).broadcast_to([128, 8, 128]), op=ALU.mult), reads=["yo", "pf"], writes=["ycat"])
                    for a in range(8):
                        pl.op("pe", lambda e, a=a, b=b: e.matmul(psC[:, 324 + b:325 + b], lhsT=Mt[:, a * 128:(a + 1) * 128], rhs=ones_bf[:, 0:1], start=(a == 0), stop=(a == 7)), reads=["Mt", "ones"], writes=["psCs"], token=(a == 7))

            if mode != "full":
                return
            pl.op("act", lambda e: e.activation(out=stat[:, 24:32], in_=psC[:, 320:328], func=AF.Ln, scale=1.0 / D, bias=EPS), reads=["psCa", "psCs"], writes=["stat"])
            pl.op("act", lambda e: e.activation(out=stat[:, 24:32], in_=stat[:, 24:32], func=AF.Exp, scale=-0.5), reads=["stat"], writes=["stat"])
            for s in range(nsub):
                tsl = slice(s * 128, (s + 1) * 128)
                for hf in range(2):
                    for part in range(2):
                        for kc in range(8):
                            pl.op("pe", lambda e, hf=hf, part=part, kc=kc, tsl=tsl: e.matmul(psA[:, part * 2 + hf, :], lhsT=ycat[:, part * 8 + kc, tsl], rhs=wout[:, part * 8 + kc, hf * 512:(hf + 1) * 512], start=(kc == 0), stop=(kc == 7)),
                                  reads=["ycat", "wout"], writes=["psA%d" % (part * 2 + hf)], token=(kc == 7))
                xb = xt[s % 2]
                xk = "xt%d" % (s % 2)
                pl.dma("sp", lambda e, xb=xb, s=s: e.dma_start(out=xb[:], in_=xs_d[row0 + s * 128:row0 + (s + 1) * 128, :]), "l_" + xk, writes=[xk])
                ob = ost[s % 2]
                ok = "ost%d" % (s % 2)
                pl.op("act", lambda e, s=s: e.activation(out=o1[:], in_=psA[:, 0:2, :].rearrange("p a q -> p (a q)"), func=AF.Copy, scale=stat[:, 24 + s:25 + s]), reads=["psA0", "psA1", "stat"], writes=["o1"])
                pl.op("dve", lambda e, s=s: e.scalar_tensor_tensor(out=o1[:], in0=psA[:, 2:4, :].rearrange("p a q -> p (a q)"), scalar=stat[:, 28 + s:29 + s], in1=o1[:], op0=ALU.mult, op1=ALU.add), reads=["psA2", "psA3", "stat", "o1"], writes=["o1"])
                pl.op("dve", lambda e: e.tensor_tensor(out=o1[:], in0=o1[:], in1=gate[:, gidx, :], op=ALU.mult), reads=["o1", "gate"], writes=["o1"])
                pl.op("dve", lambda e, xb=xb: e.tensor_tensor(out=o1[:], in0=o1[:], in1=xb[:], op=ALU.add), reads=["o1", xk], writes=["o1"])
                pl.op("dve", lambda e, s=s: e.memset(stat[:, 32 + s:33 + s], 0.0), writes=["stat"])
                pl.op("act", lambda e, s=s: e.activation(out=sqj[:], in_=o1[:], func=AF.Square, accum_out=stat[:, 32 + s:33 + s]), reads=["o1", "stat"], writes=["sqj", "stat"])
                pl.op("act", lambda e, s=s: e.activation(out=stat[:, 36 + s:37 + s], in_=stat[:, 32 + s:33 + s], func=AF.Ln, scale=1.0 / D, bias=EPS), reads=["stat"], writes=["stat"])
                pl.op("act", lambda e, s=s: e.activation(out=stat[:, 36 + s:37 + s], in_=stat[:, 36 + s:37 + s], func=AF.Exp, scale=-0.5), reads=["stat"], writes=["stat"])
                pl.op("dve", lambda e, s=s, ob=ob: e.scalar_tensor_tensor(out=ob[:], in0=o1[:], scalar=stat[:, 36 + s:37 + s], in1=bc[:, BC_NF:BC_NF + D], op0=ALU.mult, op1=ALU.mult), reads=["o1", "stat", "bc"], writes=[ok])
                pl.dma("pool", lambda e, ob=ob, s=s: e.dma_start(out=y_d[yrow0 + s * 128:yrow0 + (s + 1) * 128, :], in_=ob[:]), "s_" + ok, reads=[ok], writes=["y"])

        ones_bf = SB("ones_bf", [128, 8], BF16)
        pl.op("dve", lambda e: e.memset(ones_bf[:], 1.0), writes=["ones"])
        sca_sb = SB("sca_sb", [128, 8, 2, 2])
        scb_sb = SB("scb_sb", [128, 16, 2, 3])
        ld("l_sca", sca_sb[:].rearrange("p a b c -> p (a b c)"), sca_d[:, :], "sca")
        ld("l_scb", scb_sb[:].rearrange("p a b c -> p (a b c)"), scb_d[:, :], "scb")
        pl.op("dve", lambda e: e.memset(uhist[:], 0.0), writes=["uhist"])
        pl.op("dve", lambda e: e.memset(xhist[:], 0.0), writes=["xhist"])
        pl.op("dve", lambda e: e.memset(atot[:], 0.0), writes=["atot"])
        pslot = [(0, T, 0)]
        hsel = ((hT, "hT"), (ycat[:, 0:8, :], "ycat"))
        for n in range(NLSEG * NT):
            stm = [None] * (T // 64)
            if n == 0:
                stm[0] = ("zero", 0)
            nxa = None
            if n + 1 < NLSEG * NT:
                r1 = 128 + (n + 1) * T
                hb1, hk1 = hsel[(n + 1) % 2]
                nxa = [(lambda r1=r1, p0=p0, hb1=hb1, hk1=hk1: step_a_pair(r1, p0, T, pslot, hb1, hk1)) for p0 in (0, 2)]
            tile(128 + n * T, T, pslot, "light", 0, 0, "carry", stm, None, h_idx=n % 2, pre_a=(n > 0), next_a=nxa)
            if n % NT == NT - 1:
                m = n // NT
                pl.op("dve", lambda e, m=m: e.tensor_scalar_mul(out=ST[:], in0=ST[:], scalar1=msk[:, 1 + m:2 + m]), reads=["ST", "msk"], writes=["ST"])
                pl.op("dve", lambda e, m=m: e.tensor_scalar_mul(out=xhist[:].rearrange("p a b -> p (a b)"), in0=xhist[:].rearrange("p a b -> p (a b)"), scalar1=msk[:, 1 + m:2 + m]), reads=["xhist", "msk"], writes=["xhist"])
        tile(0, 128, [(0, 128, 0)], "halo", 0, 0, "carry", [None, None], None)
        for n in range(NT):
            stm = [None] * (T // 64)
            if n == 0:
                stm[0] = ("keep", 0)
            tile(128 + LROWS + n * T, T, pslot, "full", 0, n * T, "carry", stm, 0)
        pl.dma("pool", lambda e: e.dma_start(out=so_d[0, :, :], in_=ST[:]), "s_ST", reads=["ST"], writes=["so0"])
        tile(128 + LROWS + SEGLEN, 128, [(0, 64, 1), (64, 128, 2)], "full", 1, SEGLEN, "state", [("load", 0), ("load", 1)], 1)
        pl.dma("pool", lambda e: e.dma_start(out=ca_d[:, :], in_=casb[:].rearrange("p a b c -> p (a b c)")), "s_ca", reads=["casb"], writes=["ca"])
        pl.dma("pool", lambda e: e.dma_start(out=cb_d[:, :], in_=cbsb[:].rearrange("p a b c -> p (a b c)")), "s_cb", reads=["cbsb"], writes=["cb"])
        if debug:
            dbg_list = [("xbf", xbf[:].rearrange("p a b -> p (a b)"), 8 * T, BF16), ("BT", BT[:].rearrange("p a b -> p (a b)"), 4 * T, BF16),
                        ("CT", CT[:].rearrange("p a b -> p (a b)"), 4 * T, BF16), ("sm", sm[:].rearrange("p a b -> p (a b)"), 256, F32),
                        ("ycat", ycat[:].rearrange("p a b -> p (a b)"), 16 * T, BF16), ("yo", yo[:], 1024, F32), ("xw", xw[:], 1024, BF16),
                        ("xdt", xdt[:], 1024, BF16), ("btok", btok[:], 512, BF16), ("Mt", Mt[:], 1024, BF16), ("eac", eac[:], 1024, F32),
                        ("cd", cd[:].rearrange("p a b -> p (a b)"), 32, F32), ("hT", hT[:].rearrange("p a b -> p (a b)"), 8 * T, BF16),
                        ("sz", sz[:].rearrange("p a b -> p (a b)"), 8 * T, BF16), ("stat", stat[:], 64, F32), ("gam", gam[:].rearrange("p a b -> p (a b)"), 24, F32)]
            allk = list(pl.bufs.keys())
            for nm, ap, w, dt_ in dbg_list:
                dd = nc.dram_tensor("dbg_" + nm, [128, w], dt_, kind="ExternalOutput").ap()
                pl.dma("pool", lambda e, dd=dd, ap=ap: e.dma_start(out=dd[:, :], in_=ap), "s_dbg", reads=allk)
        pl.wait_tokens("pool", [(s, c) for s, c in pl.dma_cnt.items() if s.startswith("s_")])
        print("planned instructions:", pl.nins, {e: len(pl.lists[e]) for e in pl.ENGS})
        pl.emit()
    return nc


def _host_consts():
    c = np.zeros((128, NCONST), np.float32)
    k = np.arange(128)
    c[:, C_ID:C_ID + 128] = np.eye(128, dtype=np.float32)
    same = (k[:, None] // 64) == (k[None, :] // 64)
    c[:, C_BLK:C_BLK + 128] = same
    c[:, C_TRI:C_TRI + 128] = same & (k[:, None] <= k[None, :])
    c[:, C_T64:C_T64 + 64] = (k[:, None] % 64) <= np.arange(64)[None, :]
    c[:, C_SEL0:C_SEL0 + 128] = (k[:, None] < 64)
    c[:, C_SEL1:C_SEL1 + 128] = (k[:, None] >= 64)
    return c


def _fm(v, nchunk):
    return np.ascontiguousarray(np.asarray(v, np.float32).reshape(nchunk, 128).T)


_NC_CACHE = {}


def kernel(x_prompt, x_sample, state_conv_a, state_conv_b, state_ssm, c_prompt, c_sample,
           w_mod, b_mod, norm_in_w, w_in, conv_a_w, norm_a_w, conv_b_w, conv_b_b,
           dt_bias, a_log, d_skip, norm_b_w, w_out, norm_f_w, _two_phase=True, _debug=False):
    f = lambda a: np.ascontiguousarray(np.asarray(a, np.float32))
    x_prompt, x_sample = f(x_prompt), f(x_sample)
    state_conv_a, state_conv_b, state_ssm = f(state_conv_a), f(state_conv_b), f(state_ssm)
    c_prompt, c_sample = f(c_prompt), f(c_sample)
    w_mod, b_mod, w_in, w_out = f(w_mod)[0], f(b_mod)[0], f(w_in)[0], f(w_out)[0]
    pf = np.zeros((128, NPF), np.float32)
    pf[:, PF_NIN:PF_NIN + 8] = _fm(f(norm_in_w)[0], 8)
    caw = f(conv_a_w)[0]
    pf[:, PF_CAW:PF_CAW + 24] = np.stack([_fm(caw[k], 8) for k in range(3)], axis=2).reshape(128, 24)
    pf[:, PF_NAW:PF_NAW + 8] = _fm(f(norm_a_w)[0], 8)
    cbw = f(conv_b_w)[0]
    pf[:, PF_CBW:PF_CBW + 64] = np.stack([_fm(cbw[k], 16) for k in range(4)], axis=2).reshape(128, 64)
    pf[:, PF_CBB:PF_CBB + 16] = _fm(f(conv_b_b)[0], 16)
    pf[:, PF_NBW:PF_NBW + 8] = _fm(f(norm_b_w)[0], 8)
    pf[:, PF_DSK:PF_DSK + 8] = _fm(np.repeat(f(d_skip)[0], 64), 8)
    pf[:, PF_BSH:PF_BSH + 8] = _fm(b_mod[0:D], 8)
    pf[:, PF_BSC:PF_BSC + 8] = _fm(b_mod[D:2 * D], 8)
    bcv = np.zeros((128, NBC), np.float32)
    bcv[:, BC_NF:BC_NF + D] = f(norm_f_w)[None, :]
    bcv[:, BC_BG:BC_BG + D] = b_mod[None, 2 * D:3 * D]
    bcv[:, BC_DTB:BC_DTB + 16] = f(dt_bias)[0][None, :]
    bcv[:, BC_ALOG:BC_ALOG + 16] = f(a_log)[0][None, :]
    cst = _host_consts()

    in_maps = []
    for k in range(NCORES):
        seq, seg = k // 4, k % 4
        start = seg * SEGLEN
        xs = np.zeros((XROWS, D), np.float32)
        if seg > 0:
            xs[0:128] = x_prompt[seq, start - 128:start]
        if seg > 0 and LROWS >= start:
            xs[128 + LROWS - start:128 + LROWS] = x_prompt[seq, 0:start]
        xs[128 + LROWS:128 + LROWS + SEGLEN] = x_prompt[seq, start:start + SEGLEN]
        xs[128 + LROWS + SEGLEN:] = x_sample[2 * k:2 * k + 2].reshape(128, D)
        cs = [c_prompt[seq], c_sample[2 * k], c_sample[2 * k + 1]]
        cT = np.stack([_fm(c, 8) for c in cs], axis=2).reshape(128, 24)
        cbc = np.zeros((2, 128, 8, 128), np.float32)
        cbc[0] = _fm(cs[0], 8)[:, :, None]
        cbc[1, :, :, 0:64] = _fm(cs[1], 8)[:, :, None]
        cbc[1, :, :, 64:128] = _fm(cs[2], 8)[:, :, None]
        sca = state_conv_a[0, 2 * k:2 * k + 2]
        sca = sca.reshape(2, 2, 8, 128).transpose(3, 2, 0, 1).reshape(128, 32)
        scb = state_conv_b[0, 2 * k:2 * k + 2]
        scb = scb.reshape(2, 3, 16, 128).transpose(3, 2, 0, 1).reshape(128, 96)
        sst = state_ssm[0, 2 * k:2 * k + 2]
        sst = sst.reshape(2, 1024, 128).transpose(0, 2, 1)
        msk = np.zeros((128, 16), np.float32)
        msk[:, 0] = 1.0 if seg > 0 else 0.0
        for m in range(NLSEG):
            msk[:, 1 + m] = 0.0 if m < NLSEG - seg else 1.0
        in_maps.append({
            "xs": xs, "w_mod": w_mod, "w_in": w_in, "w_out": w_out, "pf": pf, "bc": bcv, "cst": cst,
            "cT": np.ascontiguousarray(cT), "cbc": np.ascontiguousarray(cbc.reshape(2, 128, 1024)),
            "sca": np.ascontiguousarray(sca), "scb": np.ascontiguousarray(scb),
            "sst": np.ascontiguousarray(sst), "msk": msk,
        })
    key = (bool(_two_phase), bool(_debug))
    if key not in _NC_CACHE:
        _NC_CACHE[key] = build_nc(two_phase=key[0], debug=key[1])
    nc = _NC_CACHE[key]
    res = run_bass_kernel_spmd(nc, in_maps, core_ids=list(range(NCORES)))
    R = res.results
    if _debug:
        kernel.last_results = R
    y_prompt = np.zeros((2, SEQ, D), np.float32)
    y_sample = np.zeros((16, 64, D), np.float32)
    ca_p = np.zeros((1, 2, 2, D), np.float32)
    cb_p = np.zeros((1, 2, 3, 2 * D), np.float32)
    ss_p = np.zeros((1, 2, 16, 64, 128), np.float32)
    ca_s = np.zeros((1, 16, 2, D), np.float32)
    cb_s = np.zeros((1, 16, 3, 2 * D), np.float32)
    ss_s = np.zeros((1, 16, 16, 64, 128), np.float32)
    for k in range(NCORES):
        seq, seg = k // 4, k % 4
        r = R[k]
        y_prompt[seq, seg * SEGLEN:(seg + 1) * SEGLEN] = r["y"][0:SEGLEN]
        y_sample[2 * k:2 * k + 2] = r["y"][SEGLEN:].reshape(2, 64, D)
        ca = r["ca"].reshape(128, 8, 3, 2)
        cb = r["cb"].reshape(128, 16, 3, 3)
        so = r["so"]
        for s in range(2):
            ca_s[0, 2 * k + s] = ca[:, :, 1 + s, :].transpose(2, 1, 0).reshape(2, D)
            cb_s[0, 2 * k + s] = cb[:, :, 1 + s, :].transpose(2, 1, 0).reshape(3, 2 * D)
            ss_s[0, 2 * k + s] = so[1 + s].T.reshape(16, 64, 128)
        if seg == 3:
            ca_p[0, seq] = ca[:, :, 0, :].transpose(2, 1, 0).reshape(2, D)
            cb_p[0, seq] = cb[:, :, 0, :].transpose(2, 1, 0).reshape(3, 2 * D)
            ss_p[0, seq] = so[0].T.reshape(16, 64, 128)
    return (y_prompt, y_sample, ca_p, cb_p, ss_p, ca_s, cb_s, ss_s)
```
